# Optimizing a Trainium2 kernel written in Bass

```python
import jax, jax.numpy as jnp
from jax import lax
import numpy as np

D_MODEL = 1024
BATCH = 8
SEQ = 2048
DEPTH = 2

GRID_W = 64
CTX_LEN = 256
EPS = 1e-6

F_GROUPS = 4
F_GROUP_DIM = D_MODEL // 16
F_WIDTH = F_GROUPS * F_GROUP_DIM
M_HEADS = 4
M_HEAD_DIM = 3 * D_MODEL // 32
M_WIDTH = M_HEADS * M_HEAD_DIM
M_CHUNK = 64
K_CONV = 5
A_HEADS = 4
A_NOPE = 64
A_ROPE = 32
A_V = 3 * D_MODEL // 32
A_WIDTH = A_HEADS * A_V
Q_LORA = D_MODEL // 4
KV_LORA = D_MODEL // 8
Q_BLOCK = 128
ROPE_THETA = 10000.0
MLP_HIDDEN = 4 * D_MODEL
IN_WIDTHS = (F_WIDTH, M_WIDTH, M_WIDTH, M_WIDTH, M_WIDTH, 4 * M_HEADS, Q_LORA, KV_LORA, A_ROPE)
IN_WIDTH = F_WIDTH + 4 * M_WIDTH + 4 * M_HEADS + Q_LORA + KV_LORA + A_ROPE
MIX_WIDTH = F_WIDTH + M_WIDTH + A_WIDTH

kernel_name = "hybrid_fourier_mlstm_mla_dit_block"


def rmsnorm(x, g):
    xf = x.astype(jnp.float32)
    y = xf * lax.rsqrt(jnp.mean(xf * xf, axis=-1, keepdims=True) + EPS)
    return (y * g.astype(jnp.float32)).astype(x.dtype)


def modulate(h, shift, scale):
    return h * (1 + scale) + shift


def split_proj(p):
    out, off = [], 0
    for w in IN_WIDTHS:
        out.append(p[..., off:off + w])
        off += w
    return out


def axial_rope(n):
    rows = n // GRID_W
    row = jnp.repeat(jnp.arange(rows, dtype=jnp.float32), GRID_W)
    col = jnp.tile(jnp.arange(GRID_W, dtype=jnp.float32), rows)
    nf = A_ROPE // 4
    freqs = ROPE_THETA ** (-jnp.arange(nf, dtype=jnp.float32) / nf)
    ang = jnp.stack([row[:, None] * freqs, col[:, None] * freqs], axis=1)
    return jnp.cos(ang), jnp.sin(ang)


def apply_rope(x, cos, sin):
    xs = x.reshape(x.shape[:-1] + (2, 2, A_ROPE // 4)).astype(jnp.float32)
    x1, x2 = xs[..., 0, :], xs[..., 1, :]
    out = jnp.stack([x1 * cos - x2 * sin, x1 * sin + x2 * cos], axis=-2)
    return out.reshape(x.shape).astype(x.dtype)


def fourier_mix(u):
    B, N, _ = u.shape
    ug = u.astype(jnp.float32).reshape(B, N, F_GROUPS, F_GROUP_DIM)
    y = jnp.fft.fft2(ug, axes=(1, 3), norm="ortho").real
    return y.reshape(B, N, F_WIDTH).astype(u.dtype)


def dwconv_centred(u, w):
    C = u.shape[-1]
    pad = w.shape[0] // 2
    return lax.conv_general_dilated(u, w[:, None, :].astype(u.dtype), (1,), [(pad, pad)],
                                    dimension_numbers=("NWC", "WIO", "NWC"), feature_group_count=C)


def mlstm_inputs(mq, mk, mv, mg, conv_w, b_g):
    B, N, _ = mq.shape
    qk = jax.nn.silu(dwconv_centred(jnp.concatenate([mq, mk], axis=-1), conv_w))
    heads = lambda a: a.reshape(B, N, M_HEADS, M_HEAD_DIM).transpose(0, 2, 1, 3).astype(jnp.float32)
    q = heads(qk[..., :M_WIDTH]) * (M_HEAD_DIM ** -0.5)
    k = heads(qk[..., M_WIDTH:])
    v = heads(mv)
    g = (mg.astype(jnp.float32) + b_g.astype(jnp.float32)).reshape(B, N, 4, M_HEADS).transpose(2, 0, 3, 1)
    fwd = (q, k, v, g[0], jax.nn.log_sigmoid(g[1]))
    bwd = (q, k, v, g[2], jax.nn.log_sigmoid(g[3]))
    return fwd, bwd


def mlstm_zero_state(B):
    return (jnp.zeros((B, M_HEADS, M_HEAD_DIM, M_HEAD_DIM), jnp.float32),
            jnp.zeros((B, M_HEADS, M_HEAD_DIM), jnp.float32),
            jnp.zeros((B, M_HEADS), jnp.float32))


def mlstm_chunkwise(q, k, v, li, lf, state):
    B, H, N, _ = q.shape
    L = M_CHUNK
    nc = N // L
    to_chunks = lambda a: jnp.moveaxis(a.reshape((B, H, nc, L) + a.shape[3:]), 2, 0)
    tri = jnp.tril(jnp.ones((L, L), dtype=bool))

    def step(carry, inp):
        C, n, m = carry
        qb, kb, vb, ib, fb = inp
        b = jnp.cumsum(fb, axis=-1)
        dmat = jnp.where(tri, b[..., :, None] - b[..., None, :] + ib[..., None, :], -jnp.inf)
        inter = b + m[..., None]
        m_t = jnp.maximum(inter, dmat.max(axis=-1))
        w_intra = jnp.exp(dmat - m_t[..., None])
        w_inter = jnp.exp(inter - m_t)
        s = jnp.einsum("bhtd,bhsd->bhts", qb, kb) * w_intra
        num = jnp.einsum("bhts,bhsv->bhtv", s, vb) + w_inter[..., None] * jnp.einsum("bhtd,bhdv->bhtv", qb, C)
        den = s.sum(axis=-1) + w_inter * jnp.einsum("bhtd,bhd->bht", qb, n)
        h = num / jnp.maximum(jnp.abs(den), jnp.exp(-m_t))[..., None]
        b_last = b[..., -1]
        g = b_last[..., None] - b + ib
        m_new = jnp.maximum(b_last + m, g.max(axis=-1))
        decay = jnp.exp(b_last + m - m_new)
        wk = jnp.exp(g - m_new[..., None])
        C_new = decay[..., None, None] * C + jnp.einsum("bhs,bhsd,bhsv->bhdv", wk, kb, vb)
        n_new = decay[..., None] * n + jnp.einsum("bhs,bhsd->bhd", wk, kb)
        return (C_new, n_new, m_new), h

    state, hc = lax.scan(step, state, tuple(to_chunks(a) for a in (q, k, v, li, lf)))
    h = jnp.moveaxis(hc, 0, 2).reshape(B, H, N, -1)
    return h, state


def mlstm_direction(seq, state, reverse):
    if reverse:
        seq = tuple(jnp.flip(a, axis=2) for a in seq)
    h, state = mlstm_chunkwise(*seq, state)
    if reverse:
        h = jnp.flip(h, axis=2)
    return h, state


def mlstm_output(h_f, h_b, mo, g_m):
    B, H, N, dv = h_f.shape
    h = (h_f + h_b).transpose(0, 2, 1, 3)
    h = h * lax.rsqrt(jnp.mean(h * h, axis=-1, keepdims=True) + EPS)
    h = h.reshape(B, N, M_WIDTH) * g_m.astype(jnp.float32) * jax.nn.sigmoid(mo.astype(jnp.float32))
    return h.astype(mo.dtype)


def mla_queries(cq, g_qn, w_uq, rope):
    B, N, _ = cq.shape
    q = (rmsnorm(cq, g_qn) @ w_uq).reshape(B, N, A_HEADS, A_NOPE + A_ROPE).transpose(0, 2, 1, 3)
    qn, qr = q[..., :A_NOPE], q[..., A_NOPE:]
    if rope is not None:
        qr = apply_rope(qr, *rope)
    return qn, qr


def mla_keys(ckv, kr, g_kvn, w_ukv, rope):
    B, N, _ = ckv.shape
    kv = (rmsnorm(ckv, g_kvn) @ w_ukv).reshape(B, N, A_HEADS, A_NOPE + A_V).transpose(0, 2, 1, 3)
    kn, v = kv[..., :A_NOPE], kv[..., A_NOPE:]
    if rope is not None:
        kr = apply_rope(kr, *rope)
    return kn, kr, v


def softmax_attend(qn, qr, key_sets):
    scale = (A_NOPE + A_ROPE) ** -0.5
    s = jnp.concatenate([jnp.einsum("bhqd,bhkd->bhqk", qn, kn) + jnp.einsum("bhqr,bkr->bhqk", qr, kr)
                         for kn, kr, _ in key_sets], axis=-1)
    p = jax.nn.softmax(s.astype(jnp.float32) * scale, axis=-1)
    out, off = None, 0
    for _, _, v in key_sets:
        nk = v.shape[2]
        o = jnp.einsum("bhqk,bhkd->bhqd", p[..., off:off + nk].astype(v.dtype), v)
        out = o if out is None else out + o
        off += nk
    return out


def blocked_attend(qn, qr, key_sets):
    B, H, N, _ = qn.shape
    nb = N // Q_BLOCK
    blk = lambda a: jnp.moveaxis(a.reshape(B, H, nb, Q_BLOCK, a.shape[-1]), 2, 0)
    out = lax.map(lambda qs: softmax_attend(qs[0], qs[1], key_sets), (blk(qn), blk(qr)))
    return jnp.moveaxis(out, 0, 2).reshape(B, H, N, -1)


def merge_heads(y):
    B, H, N, dv = y.shape
    return y.transpose(0, 2, 1, 3).reshape(B, N, H * dv)


def sqrelu_mlp(h, w_up, w_down):
    return jnp.square(jax.nn.relu(h @ w_up)) @ w_down


def setup_inputs(seed: int = 0) -> dict:
    key = jax.random.key(seed)
    ks = jax.random.split(key, 24)
    nrm = lambda k, shape, s: jax.random.normal(k, shape, jnp.float32) * s
    gain = lambda k, shape: 1.0 + 0.02 * jax.random.normal(k, shape, jnp.float32)
    f_bias = jnp.linspace(3.0, 6.0, M_HEADS, dtype=jnp.float32)
    z = jnp.zeros((M_HEADS,), jnp.float32)
    gate_base = jnp.concatenate([z, f_bias, z, f_bias])
    return {
        "x": nrm(ks[0], (BATCH, SEQ, D_MODEL), 1.0),
        "c": nrm(ks[1], (BATCH, D_MODEL), 1.0),
        "ctx": nrm(ks[2], (BATCH, CTX_LEN, D_MODEL), 1.0),
        "c_ctx": nrm(ks[3], (D_MODEL,), 1.0),
        "w_mod": nrm(ks[4], (DEPTH, D_MODEL, 6 * D_MODEL), 0.5 * D_MODEL ** -0.5),
        "b_mod": nrm(ks[5], (DEPTH, 6 * D_MODEL), 0.01),
        "g_norm1": gain(ks[6], (DEPTH, D_MODEL)),
        "g_norm2": gain(ks[7], (DEPTH, D_MODEL)),
        "w_in": nrm(ks[8], (DEPTH, D_MODEL, IN_WIDTH), D_MODEL ** -0.5),
        "b_gates": gate_base[None, :] + nrm(ks[9], (DEPTH, 4 * M_HEADS), 0.1),
        "conv_qk": nrm(ks[10], (DEPTH, K_CONV, 2 * M_WIDTH), K_CONV ** -0.5),
        "g_mlstm": gain(ks[11], (DEPTH, M_WIDTH)),
        "g_q_norm": gain(ks[12], (DEPTH, Q_LORA)),
        "g_kv_norm": gain(ks[13], (DEPTH, KV_LORA)),
        "w_uq": nrm(ks[14], (DEPTH, Q_LORA, A_HEADS * (A_NOPE + A_ROPE)), Q_LORA ** -0.5),
        "w_ukv": nrm(ks[15], (DEPTH, KV_LORA, A_HEADS * (A_NOPE + A_V)), KV_LORA ** -0.5),
        "w_out": nrm(ks[16], (DEPTH, MIX_WIDTH, D_MODEL), MIX_WIDTH ** -0.5),
        "w_up": nrm(ks[17], (DEPTH, D_MODEL, MLP_HIDDEN), D_MODEL ** -0.5),
        "w_down": nrm(ks[18], (DEPTH, MLP_HIDDEN, D_MODEL), MLP_HIDDEN ** -0.5),
        "g_final": gain(ks[19], (D_MODEL,)),
    }


def reference(x, c, ctx, c_ctx, w_mod, b_mod, g_norm1, g_norm2, w_in, b_gates, conv_qk, g_mlstm,
              g_q_norm, g_kv_norm, w_uq, w_ukv, w_out, w_up, w_down, g_final):
    B, S, _ = x.shape
    rope_lat = axial_rope(S)
    xc = ctx
    for l in range(DEPTH):
        last = l == DEPTH - 1
        mod = (jax.nn.silu(c) @ w_mod[l] + b_mod[l])[:, None, :]
        mod_c = jax.nn.silu(c_ctx) @ w_mod[l] + b_mod[l]
        sh1, sc1, ga1, sh2, sc2, ga2 = jnp.split(mod, 6, axis=-1)
        sh1c, sc1c, ga1c, sh2c, sc2c, ga2c = jnp.split(mod_c, 6, axis=-1)

        h = modulate(rmsnorm(x, g_norm1[l]), sh1, sc1)
        hc = modulate(rmsnorm(xc, g_norm1[l]), sh1c, sc1c)
        pf, mq, mk, mv, mo, mg, cq, ckv, kr = split_proj(h @ w_in[l])
        pfc, mqc, mkc, mvc, moc, mgc, cqc, ckvc, krc = split_proj(hc @ w_in[l])

        y_f = fourier_mix(pf)

        fwd_c, bwd_c = mlstm_inputs(mqc, mkc, mvc, mgc, conv_qk[l], b_gates[l])
        fwd, bwd = mlstm_inputs(mq, mk, mv, mg, conv_qk[l], b_gates[l])
        zero = mlstm_zero_state(B)
        hcf, st_f = mlstm_direction(fwd_c, zero, False)
        hcb, st_b = mlstm_direction(bwd_c, zero, True)
        hf, _ = mlstm_direction(fwd, st_f, False)
        hb, _ = mlstm_direction(bwd, st_b, True)
        y_m = mlstm_output(hf, hb, mo, g_mlstm[l])

        ctx_keys = mla_keys(ckvc, krc, g_kv_norm[l], w_ukv[l], None)
        lat_keys = mla_keys(ckv, kr, g_kv_norm[l], w_ukv[l], rope_lat)
        qn, qr = mla_queries(cq, g_q_norm[l], w_uq[l], rope_lat)
        y_a = merge_heads(blocked_attend(qn, qr, (lat_keys, ctx_keys)))

        x = x + ga1 * (jnp.concatenate([y_f, y_m, y_a], axis=-1) @ w_out[l])
        x = x + ga2 * sqrelu_mlp(modulate(rmsnorm(x, g_norm2[l]), sh2, sc2), w_up[l], w_down[l])

        if not last:
            y_fc = fourier_mix(pfc)
            y_mc = mlstm_output(hcf, hcb, moc, g_mlstm[l])
            qn_c, qr_c = mla_queries(cqc, g_q_norm[l], w_uq[l], None)
            y_ac = merge_heads(softmax_attend(qn_c, qr_c, (ctx_keys,)))
            xc = xc + ga1c * (jnp.concatenate([y_fc, y_mc, y_ac], axis=-1) @ w_out[l])
            xc = xc + ga2c * sqrelu_mlp(modulate(rmsnorm(xc, g_norm2[l]), sh2c, sc2c), w_up[l], w_down[l])

    return rmsnorm(x, g_final)
```

```python
import contextlib, bisect, math
import numpy as np
import ml_dtypes
import concourse.bass as bass
import concourse.mybir as mybir
from concourse.bass_utils import run_bass_kernel_spmd

F32 = mybir.dt.float32
BF16 = mybir.dt.bfloat16
AF = mybir.ActivationFunctionType
OP = mybir.AluOpType
AX = mybir.AxisListType

NT = 18
TOK = 2304
DM = 1024
EPS = 1e-6
DEPTH = 2
STAGE = 99
DEBUG = []
BRANCHES = 'FAM'
DBGXS = False


class Dummy:
    def __getitem__(self, k):
        return self

    def __getattr__(self, n):
        return lambda *a, **k: self


class Prog:
    ENGS = ('pe', 'dve', 'act', 'pool', 'sp')

    def __init__(self, nc):
        self.nc = nc
        self.dry = True
        self.rec = []
        self.i = 0

    def op(self, eng, fn, r=(), w=(), dma=None):
        if self.dry:
            self.rec.append(('op', eng, tuple(r), tuple(w), dma))
        else:
            i = self.i
            E = self.H[eng]
            for (s, v) in self.waits[i]:
                E.wait_ge(self.sems[s], v)
            if fn is not None:
                ins = fn()
                inc = self.incs[i]
                if inc is not None:
                    ins.then_inc(self.sems[inc[0]], inc[1])
        self.i += 1

    def barrier(self):
        if self.dry:
            self.rec.append(('bar',))
        self.i += 1

    def plan(self):
        rec = self.rec
        n = len(rec)
        lastw, readers, last_on_eng, dma_ops = {}, {}, {}, {}
        deps = [None] * n
        pend = {e: None for e in self.ENGS}
        for i, r in enumerate(rec):
            if r[0] == 'bar':
                bd = set(last_on_eng.values())
                for s, l in dma_ops.items():
                    if l:
                        bd.add(l[-1])
                for e in self.ENGS:
                    pend[e] = set(bd) | (pend[e] or set())
                lastw.clear()
                readers.clear()
                continue
            _, eng, R, W, dma = r
            d = set()
            for k in R:
                if k in lastw:
                    d.add(lastw[k])
            for k in W:
                if k in lastw:
                    d.add(lastw[k])
                d.update(readers.get(k, ()))
            if pend[eng]:
                d |= pend[eng]
                pend[eng] = None
            d.discard(i)
            if eng == 'pe':
                d = {j for j in d if not (rec[j][1] == 'pe' and rec[j][4] is None)}
            deps[i] = d
            for k in W:
                lastw[k] = i
                readers[k] = []
            for k in R:
                lst = readers.setdefault(k, [])
                if dma is None:
                    lst[:] = [j for j in lst if not (rec[j][1] == eng and rec[j][4] is None)]
                lst.append(i)
            last_on_eng[eng] = i
            if dma:
                dma_ops.setdefault(dma, []).append(i)
        sig = [False] * n
        for i in range(n):
            if deps[i]:
                for j in deps[i]:
                    if rec[j][4] is None:
                        sig[j] = True
        cnt = {e: 0 for e in self.ENGS}
        ev = [None] * n
        for i, r in enumerate(rec):
            if r[0] != 'op':
                continue
            if r[4] is None and sig[i]:
                cnt[r[1]] += 1
                ev[i] = ('E_' + r[1], cnt[r[1]])
        self.waits = [()] * n
        self.incs = [None] * n
        seen = {e: {} for e in self.ENGS}
        for i, r in enumerate(rec):
            if r[0] != 'op':
                continue
            eng = r[1]
            wl = {}
            for j in deps[i]:
                if rec[j][4] is None:
                    s, v = ev[j]
                else:
                    s = 'D_' + rec[j][4]
                    v = 16 * bisect.bisect_left(dma_ops[rec[j][4]], i)
                if v > wl.get(s, 0):
                    wl[s] = v
            ws = []
            for s, v in wl.items():
                if v > seen[eng].get(s, 0):
                    seen[eng][s] = v
                    ws.append((s, v))
            self.waits[i] = tuple(ws)
            if r[4] is not None:
                self.incs[i] = ('D_' + r[4], 16)
            elif sig[i]:
                self.incs[i] = (ev[i][0], 1)
        self.semnames = ['E_' + e for e in self.ENGS] + ['D_' + s for s in dma_ops]
        self.stats = dict(n=n, cnt=cnt, nw=sum(len(w) for w in self.waits))

    def start_real(self, es):
        nc = self.nc
        self.dry = False
        self.i = 0
        self.H = {'pe': nc.tensor, 'dve': nc.vector, 'act': nc.scalar, 'pool': nc.gpsimd, 'sp': nc.sync}
        self.sems = {s: es.enter_context(nc.semaphore(s)) for s in self.semnames}


class Scope:
    _n = 0

    def __init__(self, K):
        self.K = K
        self.es = contextlib.ExitStack()

    def __enter__(self):
        return self

    def sb(self, name, shape, dt):
        if self.K.P.dry:
            return Dummy()
        Scope._n += 1
        return self.es.enter_context(self.K.nc.sbuf_tensor("%s_%d" % (name, Scope._n), list(shape), dt))

    def __exit__(self, *a):
        self.K.P.barrier()
        self.es.close()
        return False


def which(t):
    return 1 if t < 2 else 0


class K:
    def __init__(self, nc, P, dram):
        self.nc, self.P, self.d = nc, P, dram

    def mm(self, out, lhsT, rhs, start, stop, r, w):
        nc = self.nc
        self.P.op('pe', lambda: nc.tensor.matmul(out, lhsT=lhsT, rhs=rhs, start=start, stop=stop), r, w)

    def tr(self, out, in_, ident, r, w):
        nc = self.nc
        self.P.op('pe', lambda: nc.tensor.transpose(out, in_, ident), r, w)

    def act(self, out, in_, func, r, w, bias=None, scale=None, accum=None):
        nc = self.nc
        kw = {}
        if bias is not None:
            kw['bias'] = bias
        if scale is not None:
            kw['scale'] = scale
        if accum is not None:
            kw['accum_out'] = accum
        self.P.op('act', lambda: nc.scalar.activation(out=out, in_=in_, func=func, **kw), r, w)

    def v(self, eng, name, r, w, *a, **kw):
        nc = self.nc
        E = nc.vector if eng == 'dve' else nc.gpsimd
        self.P.op(eng, lambda: getattr(E, name)(*a, **kw), r, w)

    def dma(self, q, out, in_, r, w, sem):
        nc = self.nc
        E = nc.sync if q == 'sp' else nc.gpsimd
        self.P.op(q, lambda: E.dma_start(out=out, in_=in_), r, w, dma=sem)

    def build(self):
        nc, P, d = self.nc, self.P, self.d
        with Scope(self) as S:
            self.S0 = S
            self.xs = S.sb("xs", [128, NT, DM], F32)
            self.hT = S.sb("hT", [128, 8, TOK], BF16)
            self.identb = S.sb("identb", [128, 128], BF16)
            self.identf = S.sb("identf", [128, 128], F32)
            self.onesf = S.sb("onesf", [128, 128], F32)
            self.c2 = S.sb("c2", [128, 16], F32)
            self.scb = S.sb("scb", [128, 16], BF16)
            self.modv = S.sb("modv", [128, 48, 2], F32)
            self.bmod = S.sb("bmod", [128, 2, 48], F32)
            self.gn = S.sb("gn", [128, 2, 2, 8], F32)
            self.G = S.sb("G", [128, 2, 8, 2], F32)
            self.gaB = S.sb("gaB", [128, 2, DM], F32)
            self.ssq = S.sb("ssq", [128, NT], F32)
            self.rstd = S.sb("rstd", [128, NT], F32)
            self.epsT = S.sb("epsT", [128, 1], F32)
            self.xn = [S.sb("xn%d" % i, [128, DM], BF16) for i in range(2)]
            self.junk = S.sb("junk", [128, DM], BF16)
            self.dg = [S.sb("dg%d" % i, [128, 128], F32) for i in range(2)]
            self.xt = [S.sb("xt%d" % i, [128, 512], F32) for i in range(2)]
            if not P.dry:
                es = S.es
                self.pb = [es.enter_context(nc.psum_tensor("pb%d" % i, [128, 512], F32)) for i in range(6)]
                self.ptr = [es.enter_context(nc.psum_tensor("ptr%d" % i, [128, 8, 128], BF16)) for i in range(2)]
            else:
                self.pb = [Dummy()] * 6
                self.ptr = [Dummy()] * 2
            self.cnt = 0
            self.dma('sp', self.identb[:], d['identb'][:, :], [], ['identb'], 'cst')
            self.dma('sp', self.identf[:], d['identf'][:, :], [], ['identf'], 'cst')
            self.dma('sp', self.c2[:], d['c2'][:, :], [], ['c2'], 'cst')
            self.dma('sp', self.bmod[:], d['bmodfm'][:, :, :], [], ['bmod'], 'cst')
            self.dma('sp', self.gn[:], d['gnfm'][:, :, :, :], [], ['gn'], 'cst')
            self.v('dve', 'memset', [], ['onesf'], self.onesf[:], 1.0)
            self.v('dve', 'memset', [], ['epsT'], self.epsT[:], EPS)
            self.v('dve', 'memset', [], ['ssq'], self.ssq[:], 0.0)
            self.act(self.scb[:], self.c2[:], AF.Silu, ['c2'], ['scb'])
            for t in range(NT):
                self.dma('sp', self.xs[:, t, :], d['xin'][t * 128:(t + 1) * 128, :], [], [('xs', t)], 'xin')
            for l in range(DEPTH):
                last = (l == DEPTH - 1)
                self.mod_phase(l)
                if STAGE <= 1:
                    break
                self.make_gaB(2)
                self.phaseA(l, 0, range(NT))
                if STAGE >= 3:
                    self.mixers(l, last)
                if DBGXS and l == 0:
                    for t in range(NT):
                        self.dma('sp', d['dbgxs'][t * 128:(t + 1) * 128, :], self.xs[:, t, :], [('xs', t)], [('dbgxs', t)], 'out')
                    break
                self.make_gaB(5)
                tiles = list(range(2, NT)) if last else list(range(NT))
                self.phaseA(l, 1, tiles)
                self.mlp(l, tiles)
            self.final()
        P.barrier()
        P.op('sp', None)

    def mod_phase(self, l):
        nc, P, d = self.nc, self.P, self.d
        with Scope(self) as S:
            wm = [S.sb("wm%d" % i, [128, 8, 1024], BF16) for i in range(2)]
            modps = self.pb[0]
            for blk in range(6):
                s = blk % 2
                self.dma('pool', wm[s][:], d['w_mod'][l][:, blk * 1024:(blk + 1) * 1024].rearrange("(k p) c -> p k c", p=128),
                         [], [('wm', s)], 'wm%d' % s)
                for jj in range(8):
                    j = blk * 8 + jj
                    for k in range(8):
                        self.mm(modps[:, 2 * j:2 * j + 2], wm[s][:, k, jj * 128:(jj + 1) * 128], self.scb[:, 2 * k:2 * k + 2],
                                k == 0, k == 7, [('wm', s), 'scb'], [('pb', 0)])
            self.v('dve', 'tensor_tensor', [('pb', 0), 'bmod'], ['modv'], out=self.modv[:],
                   in0=modps[:, 0:96].rearrange("p (j w) -> p j w", w=2),
                   in1=self.bmod[:, l, :].unsqueeze(2).to_broadcast([128, 48, 2]), op=OP.add)
            for ni, vi in ((0, 1), (1, 4)):
                self.v('dve', 'scalar_tensor_tensor', ['modv', 'gn'], ['G'], out=self.G[:, ni, :, :],
                       in0=self.modv[:, vi * 8:(vi + 1) * 8, :], scalar=1.0,
                       in1=self.gn[:, l, ni, :].unsqueeze(2).to_broadcast([128, 8, 2]), op0=OP.add, op1=OP.mult)

    def make_gaB(self, vi):
        for w in range(2):
            for kk in range(8):
                self.cnt += 1
                dgi = self.cnt % 2
                self.v('dve', 'tensor_scalar', ['modv', 'identf'], [('dg', dgi)], out=self.dg[dgi][:], in0=self.identf[:],
                       scalar1=self.modv[:, vi * 8 + kk, w:w + 1], scalar2=None, op0=OP.mult)
                bank = kk // 4
                self.mm(self.pb[bank][:, (kk % 4) * 128:(kk % 4 + 1) * 128], self.onesf[:], self.dg[dgi][:], True, True,
                        ['onesf', ('dg', dgi)], [('pb', bank)])
            for bank in range(2):
                self.act(self.gaB[:, w, bank * 512:(bank + 1) * 512], self.pb[bank][:], AF.Identity, [('pb', bank)], ['gaB'])

    def rms(self, t):
        self.cnt += 1
        self.act(self.junk[:], self.xs[:, t, :], AF.Square, [('xs', t)], [('junk', self.cnt), ('ssq', t)],
                 accum=self.ssq[:, t:t + 1])
        self.act(self.rstd[:, t:t + 1], self.ssq[:, t:t + 1], AF.Sqrt, [('ssq', t), 'epsT'], [('rstd', t)],
                 bias=self.epsT[:], scale=1.0 / DM)
        self.v('dve', 'reciprocal', [('rstd', t)], [('rstd', t)], out=self.rstd[:, t:t + 1], in_=self.rstd[:, t:t + 1])
        self.v('dve', 'memset', [('ssq', t)], [('ssq', t)], self.ssq[:, t:t + 1], 0.0)

    def phaseA(self, l, ni, tiles):
        vi = 0 if ni == 0 else 3
        for t in tiles:
            w = which(t)
            self.rms(t)
            self.cnt += 1
            xi = self.cnt % 2
            self.act(self.xn[xi][:], self.xs[:, t, :], AF.Identity, [('xs', t), ('rstd', t)], [('xn', xi)],
                     scale=self.rstd[:, t:t + 1])
            for k in range(8):
                self.tr(self.ptr[xi][:, k, :], self.xn[xi][:, k * 128:(k + 1) * 128], self.identb[:],
                        [('xn', xi), 'identb'], [('ptr', xi)])
            for k in range(8):
                self.act(self.hT[:, k, t * 128:(t + 1) * 128], self.ptr[xi][:, k, :], AF.Identity,
                         [('ptr', xi), 'G', 'modv'], [('hT', t)],
                         bias=self.modv[:, vi * 8 + k, w:w + 1], scale=self.G[:, ni, k, w:w + 1])

    def xupd(self, t, nh, b):
        w = which(t)
        self.cnt += 1
        mi = self.cnt % 2
        self.v('dve', 'tensor_tensor', [('pb', b), 'gaB'], [('xt', mi)], out=self.xt[mi][:], in0=self.pb[b][:],
               in1=self.gaB[:, w, nh * 512:(nh + 1) * 512], op=OP.mult)
        self.v('pool', 'tensor_tensor', [('xt', mi), ('xs', t)], [('xs', t)],
               out=self.xs[:, t, nh * 512:(nh + 1) * 512], in0=self.xs[:, t, nh * 512:(nh + 1) * 512],
               in1=self.xt[mi][:], op=OP.add)

    def blocks(self, lo=0, hi=TOK, step=512):
        return [(o, min(step, hi - o)) for o in range(lo, hi, step)]

    def mixers(self, l, last):
        if 'F' in BRANCHES:
            self.fourier(l, last)
        if 'M' in BRANCHES:
            self.mlstm(l, last)
        if 'A' in BRANCHES:
            self.mla(l, last)

    def mlstm(self, l, last):
        nc, P, d = self.nc, self.P, self.d
        win = d['w_in'][l]
        with Scope(self) as SO:
            acol = SO.sb("acol", [128, NT, 8], F32)
            Mcol = SO.sb("Mcol", [128, NT, 8], F32)
            T4 = SO.sb("T4", [128, 4, NT, 8], F32)
            gm = SO.sb("gm", [128, 384], F32)
            E2 = SO.sb("E2", [8, 2, 8], F32)
            negid = SO.sb("negid", [128, 128], F32)
            mlmask = SO.sb("mlmask", [128, 2, 256], BF16)
            self.dma('sp', gm[:], d['gmB'][:, l, :], [], ['gm'], 'mw')
            self.dma('sp', E2[:], d['E2'][:, :, :], [], ['E2'], 'mw')
            self.dma('sp', negid[:], d['negidf'][:, :], [], ['negid'], 'mw')
            self.dma('sp', mlmask[:], d['mlmask'][:, :, :], [], ['mlmask'], 'mw')
            with Scope(self) as S:
                wg = S.sb("wg", [128, 8, 16], BF16)
                bg = S.sb("bg", [8, 2], F32)
                nbf = S.sb("nbf", [8, 1], F32)
                IG = S.sb("IG", [8, TOK], F32)
                LF = S.sb("LF", [8, TOK], F32)
                FF = S.sb("FF", [8, TOK], F32)
                FB = S.sb("FB", [8, TOK], F32)
                AFr = S.sb("AFr", [8, TOK], F32)
                ABr = S.sb("ABr", [8, TOK], F32)
                R1 = S.sb("R1", [8, NT, 8], F32)
                R2 = S.sb("R2", [8, NT, 8], F32)
                mcol = S.sb("mcol", [128, NT, 8], F32)
                MendB = S.sb("MendB", [128, NT, 8], F32)
                MP = S.sb("MP", [128, NT, 8], F32)
                ex = S.sb("ex", [128, 4, NT, 8], F32)
                self.dma('pool', wg[:], d['w_g'][l].rearrange("(k p) c -> p k c", p=128), [], ['wg'], 'gw')
                self.dma('sp', bg[:], d['bg'][l], [], ['bg'], 'gw')
                self.v('dve', 'tensor_scalar', ['bg'], ['nbf'], out=nbf[:], in0=bg[:, 1:2], scalar1=-1.0, scalar2=None, op0=OP.mult)
                nb = 0
                for (o, n) in self.blocks():
                    hk = [('hT', t) for t in range(o // 128, (o + n) // 128)]
                    bi, bf_ = nb % 6, (nb + 1) % 6
                    nb += 2
                    for k in range(8):
                        self.mm(self.pb[bi][0:8, 0:n], wg[:, k, 0:8], self.hT[:, k, o:o + n], k == 0, k == 7, hk + ['wg'], [('pb', bi)])
                    for k in range(8):
                        self.mm(self.pb[bf_][0:8, 0:n], wg[:, k, 8:16], self.hT[:, k, o:o + n], k == 0, k == 7, hk + ['wg'], [('pb', bf_)])
                    self.act(IG[:, o:o + n], self.pb[bi][0:8, 0:n], AF.Identity, [('pb', bi), 'bg'], ['IG'], bias=bg[:, 0:1])
                    self.act(LF[:, o:o + n], self.pb[bf_][0:8, 0:n], AF.Exp, [('pb', bf_), 'nbf'], ['LF'], bias=nbf[:], scale=-1.0)
                self.act(LF[:], LF[:], AF.Ln, ['LF'], ['LF'], bias=1.0)
                self.v('dve', 'tensor_scalar', ['LF'], ['LF'], out=LF[:], in0=LF[:], scalar1=-1.0, scalar2=None, op0=OP.mult)
                one_b = lambda n: self.onesf[0:8, 0:1].to_broadcast([8, n])
                self.v('dve', 'tensor_tensor_scan', ['LF', 'onesf'], ['FF'], out=FF[:], data0=one_b(TOK), data1=LF[:], initial=0.0,
                       op0=OP.mult, op1=OP.add)
                self.v('dve', 'tensor_tensor_scan', ['LF', 'onesf'], ['FB'], out=FB[:, 255::-1], data0=one_b(256), data1=LF[:, 255::-1],
                       initial=0.0, op0=OP.mult, op1=OP.add)
                self.v('dve', 'tensor_tensor_scan', ['LF', 'onesf', 'FB'], ['FB'], out=FB[:, 2303:255:-1], data0=one_b(2048),
                       data1=LF[:, 2303:255:-1], initial=FB[:, 0:1], op0=OP.mult, op1=OP.add)
                self.v('dve', 'tensor_tensor', ['IG', 'FF'], ['AFr'], out=AFr[:], in0=IG[:], in1=FF[:], op=OP.subtract)
                self.v('dve', 'tensor_tensor', ['IG', 'FB'], ['ABr'], out=ABr[:], in0=IG[:], in1=FB[:], op=OP.subtract)
                MF, MB = IG, LF
                self.v('dve', 'tensor_tensor_scan', ['AFr'], ['IG'], out=MF[:], data0=AFr[:], data1=AFr[:], initial=0.0, op0=OP.max, op1=OP.max)
                self.v('dve', 'tensor_tensor_scan', ['ABr'], ['LF'], out=MB[:, 255::-1], data0=ABr[:, 255::-1], data1=ABr[:, 255::-1],
                       initial=0.0, op0=OP.max, op1=OP.max)
                self.v('dve', 'tensor_tensor_scan', ['ABr', 'LF'], ['LF'], out=MB[:, 2303:255:-1], data0=ABr[:, 2303:255:-1],
                       data1=ABr[:, 2303:255:-1], initial=MB[:, 0:1], op0=OP.max, op1=OP.max)
                self.v('dve', 'tensor_tensor', ['FF', 'IG'], ['FF'], out=FF[:], in0=FF[:], in1=MF[:], op=OP.add)
                self.v('dve', 'tensor_tensor', ['FB', 'LF'], ['FB'], out=FB[:], in0=FB[:], in1=MB[:], op=OP.add)
                for qi, (XF, XB, kf, kb) in enumerate(((AFr, ABr, 'AFr', 'ABr'), (MF, MB, 'IG', 'LF'), (FF, FB, 'FF', 'FB'))):
                    for c in range(NT):
                        self.mm(self.pb[qi][:, c * 8:(c + 1) * 8], XF[:, c * 128:(c + 1) * 128], E2[:, 0, :], True, False, [kf, 'E2'], [('pb', qi)])
                        self.mm(self.pb[qi][:, c * 8:(c + 1) * 8], XB[:, c * 128:(c + 1) * 128], E2[:, 1, :], False, True, [kb, 'E2'], [('pb', qi)])
                for qi, (dst, kd) in enumerate(((acol, 'acol'), (Mcol, 'Mcol'), (mcol, 'mcol'))):
                    self.act(dst[:].rearrange("p c e -> p (c e)"), self.pb[qi][:, 0:NT * 8], AF.Identity, [('pb', qi)], [kd])
                self.v('dve', 'tensor_tensor', ['E2', 'IG'], ['R1'], out=R1[:], in0=E2[:, 0, :].unsqueeze(1).to_broadcast([8, NT, 8]),
                       in1=MF[:, 127::128].unsqueeze(2).to_broadcast([8, NT, 8]), op=OP.mult)
                self.v('dve', 'tensor_tensor', ['E2', 'LF'], ['R2'], out=R2[:], in0=E2[:, 1, :].unsqueeze(1).to_broadcast([8, NT, 8]),
                       in1=MB[:, 0::128].unsqueeze(2).to_broadcast([8, NT, 8]), op=OP.mult)
                self.v('dve', 'tensor_tensor', ['R1', 'R2'], ['R1'], out=R1[:], in0=R1[:], in1=R2[:], op=OP.add)
                self.mm(self.pb[3][:, 0:NT * 8], self.onesf[0:8, :], R1[:].rearrange("p c e -> p (c e)"), True, True, ['R1', 'onesf'], [('pb', 3)])
                self.act(MendB[:].rearrange("p c e -> p (c e)"), self.pb[3][:, 0:NT * 8], AF.Identity, [('pb', 3)], ['MendB'])
                self.v('dve', 'memset', [], ['MP'], MP[:], 0.0)
                self.v('dve', 'tensor_copy', ['MendB', 'MP'], ['MP'], out=MP[:, 1:NT, 0:4], in_=MendB[:, 0:NT - 1, 0:4])
                self.v('dve', 'tensor_copy', ['MendB', 'MP'], ['MP'], out=MP[:, 0:1, 4:8], in_=MendB[:, 1:2, 4:8])
                self.v('dve', 'tensor_copy', ['MendB', 'MP'], ['MP'], out=MP[:, 2:NT - 1, 4:8], in_=MendB[:, 3:NT, 4:8])
                self.v('dve', 'tensor_copy', ['MendB', 'MP'], ['MP'], out=MP[:, NT - 1:NT, 4:8], in_=MendB[:, 0:1, 4:8])
                self.v('dve', 'tensor_tensor', ['MP', 'Mcol'], ['ex'], out=ex[:, 0], in0=MP[:], in1=Mcol[:], op=OP.subtract)
                self.v('dve', 'tensor_tensor', ['acol', 'MendB'], ['ex'], out=ex[:, 1], in0=acol[:], in1=MendB[:], op=OP.subtract)
                self.v('dve', 'tensor_tensor', ['MP', 'MendB'], ['ex'], out=ex[:, 2], in0=MP[:], in1=MendB[:], op=OP.subtract)
                self.v('dve', 'tensor_scalar', ['mcol'], ['ex'], out=ex[:, 3], in0=mcol[:], scalar1=-1.0, scalar2=None, op0=OP.mult)
                self.act(T4[:].rearrange("p a c e -> p (a c e)"), ex[:].rearrange("p a c e -> p (a c e)"), AF.Exp, ['ex'], ['T4'])
            for hp in range(2):
                self.mlstm_pair(l, last, hp, acol, Mcol, T4, gm, negid, mlmask)

    def mlstm_pair(self, l, last, hp, acol, Mcol, T4, gm, negid, mlmask):
        nc, P, d = self.nc, self.P, self.d
        win = d['w_in'][l]
        QS = 96.0 ** -0.5
        colof = lambda T: T + 2 if T < 256 else T + 6
        cblocks = [(0, 256)] + [(256 + i * 512, 512) for i in range(4)]
        with Scope(self) as SP:
            post = SP.sb("post", [96, 4, TOK], BF16)
            vp = SP.sb("vp", [128, NT, 2, 97], BF16)
            sg = SP.sb("sg", [128, NT, 192], BF16)
            S = Scope(self)
            wvo = S.sb("wvo", [128, 8, 384], BF16)
            self.dma('pool', wvo[:, :, 0:192], win[:, 1024 + 192 * hp:1024 + 192 * (hp + 1)].rearrange("(k p) c -> p k c", p=128), [], ['wvo'], 'mw')
            self.dma('pool', wvo[:, :, 192:384], win[:, 1408 + 192 * hp:1408 + 192 * (hp + 1)].rearrange("(k p) c -> p k c", p=128), [], ['wvo'], 'mw')
            self.v('dve', 'memset', [], ['vp'], vp[:], 1.0)
            with S:
                wqk = S.sb("wqk", [128, 8, 4, 96], BF16)
                dgw = S.sb("dgw", [96, 4, 5, 96], BF16)
                pre = S.sb("pre", [96, 4, 2312], BF16)
                sgt = [S.sb("sgt%d" % i, [96, 512], F32) for i in range(1)]
                for jj in range(4):
                    c0 = (256 if jj < 2 else 640) + 96 * (2 * hp + jj % 2)
                    self.dma('pool', wqk[:, :, jj, :], win[:, c0:c0 + 96].rearrange("(k p) c -> p k c", p=128), [], ['wqk'], 'mw')
                    jg = (0 if jj < 2 else 4) + 2 * hp + jj % 2
                    self.dma('sp', dgw[:, jj, :, :], d['convdg'][l, :, jg, :, :], [], ['dgw'], 'mw')
                self.v('dve', 'memset', [], ['pre'], pre[:], 0.0)
                nb = 0
                for jj in range(4):
                    for (o, n) in cblocks:
                        b = nb % 6
                        nb += 1
                        hk = [('hT', t) for t in range(o // 128, (o + n) // 128)]
                        for k in range(8):
                            self.mm(self.pb[b][0:96, 0:n], wqk[:, k, jj, :], self.hT[:, k, o:o + n], k == 0, k == 7, hk + ['wqk'], [('pb', b)])
                        self.act(pre[:, jj, colof(o):colof(o) + n], self.pb[b][0:96, 0:n], AF.Identity, [('pb', b), 'pre'], [('pre', jj)])
                for t in range(NT):
                    b = nb % 6
                    nb += 1
                    for k in range(8):
                        self.mm(self.pb[b][:, 0:384], self.hT[:, k, t * 128:(t + 1) * 128], wvo[:, k, :], k == 0, k == 7, [('hT', t), 'wvo'], [('pb', b)])
                    self.act(vp[:, t, :, 0:96], self.pb[b][:, 0:192].rearrange("p (h c) -> p h c", c=96), AF.Identity, [('pb', b), 'vp'], [('vp', t)])
                    self.act(sg[:, t, :], self.pb[b][:, 192:384], AF.Sigmoid, [('pb', b)], [('sg', t)])
                for jj in range(4):
                    for (o, n) in cblocks:
                        b = nb % 6
                        nb += 1
                        c0 = colof(o)
                        for tap in range(5):
                            self.mm(self.pb[b][0:96, 0:n], dgw[:, jj, tap, :], pre[:, jj, c0 + tap - 2:c0 + tap - 2 + n], tap == 0, tap == 4,
                                    [('pre', jj), 'dgw'], [('pb', b)])
                        tk = [('post', jj, t) for t in range(o // 128, (o + n) // 128)]
                        if jj < 2:
                            si = 0
                            self.act(sgt[si][:, 0:n], self.pb[b][0:96, 0:n], AF.Sigmoid, [('pb', b)], [('sgt', si)])
                            self.v('dve', 'scalar_tensor_tensor', [('pb', b), ('sgt', si)], tk, out=post[:, jj, o:o + n], in0=self.pb[b][0:96, 0:n],
                                   scalar=QS, in1=sgt[si][:, 0:n], op0=OP.mult, op1=OP.mult)
                        else:
                            self.act(post[:, jj, o:o + n], self.pb[b][0:96, 0:n], AF.Silu, [('pb', b)], tk)
            ktok = SP.sb("ktok", [128, NT, 2, 96], BF16)
            for t in range(NT):
                pi = t % 2
                for hh in range(2):
                    self.tr(self.ptr[pi][:, hh, 0:96], post[:, 2 + hh, t * 128:(t + 1) * 128], self.identb[0:96, 0:96],
                            [('post', 2 + hh, t), 'identb'], [('ptr', pi)])
                self.act(ktok[:, t, :, :], self.ptr[pi][:, 0:2, 0:96], AF.Identity, [('ptr', pi)], [('ktok', t)])
            with Scope(self) as S:
                hF = S.sb("hF", [128, NT, 2, 96], F32)
                woM = S.sb("woM", [96, 2, DM], BF16)
                dgM = [S.sb("dgM%d" % i, [128, 2, 128], F32) for i in range(2)]
                WT = [S.sb("WT%d" % i, [128, 2, 128], F32) for i in range(2)]
                PT = [S.sb("PT%d" % i, [128, 2, 128], BF16) for i in range(2)]
                isb = [S.sb("isb%d" % i, [128, 2, 97], F32) for i in range(2)]
                tmpn = [S.sb("tmpn%d" % i, [128, 2, 97], F32) for i in range(2)]
                dn = [S.sb("dn%d" % i, [128, 2], F32) for i in range(2)]
                vw = [S.sb("vw%d" % i, [128, 2, 97], BF16) for i in range(2)]
                Cst = S.sb("Cst", [96, 2, 97], F32)
                Cb = S.sb("Cb", [96, 2, 97], BF16)
                hs = [S.sb("hs%d" % i, [128, 2, 96], F32) for i in range(2)]
                hq = [S.sb("hq%d" % i, [128, 2, 96], F32) for i in range(2)]
                ss = [S.sb("ss%d" % i, [128, 2], F32) for i in range(2)]
                yb = [S.sb("yb%d" % i, [128, 192], BF16) for i in range(2)]
                yhT = [S.sb("yhT%d" % i, [96, 2, 128], BF16) for i in range(2)]
                self.dma('pool', woM[:], d['w_out'][l][256 + 192 * hp:256 + 192 * (hp + 1), :].rearrange("(h p) c -> p h c", p=96), [], ['woM'], 'mw')
                for dr in range(2):
                    order = list(range(NT)) if dr == 0 else [1, 0] + list(range(NT - 1, 1, -1))
                    cf0 = dr * 4 + 2 * hp
                    self.v('dve', 'memset', ['Cst'], ['Cst'], Cst[:], 0.0)
                    self.v('dve', 'memset', ['Cb'], ['Cb'], Cb[:], 0.0)
                    for it, c in enumerate(order):
                        i2 = it % 2
                        need_out = not (last and c < 2)
                        tsl = slice(c * 128, (c + 1) * 128)
                        if need_out:
                            self.v('dve', 'tensor_tensor', ['negid', 'Mcol'], [('dgM', i2)], out=dgM[i2][:],
                                   in0=negid[:].unsqueeze(1).to_broadcast([128, 2, 128]),
                                   in1=Mcol[:, c, cf0:cf0 + 2].unsqueeze(2).to_broadcast([128, 2, 128]), op=OP.mult)
                            self.mm(self.pb[1][:, 0:256], self.onesf[:], dgM[i2][:].rearrange("p h t -> p (h t)"), True, False,
                                    [('dgM', i2), 'onesf'], [('pb', 1)])
                            self.mm(self.pb[1][:, 0:256], self.identb[:], mlmask[:, dr, :], False, True, ['mlmask', 'identb'], [('pb', 1)])
                            for hh in range(2):
                                self.act(WT[i2][:, hh, :], self.pb[1][:, hh * 128:(hh + 1) * 128], AF.Exp, [('pb', 1), 'acol'], [('WT', i2)],
                                         bias=acol[:, c, cf0 + hh:cf0 + hh + 1])
                            for hh in range(2):
                                self.mm(self.pb[2][:, hh * 128:(hh + 1) * 128], post[:, 2 + hh, tsl], post[:, hh, tsl], True, True,
                                        [('post', 2 + hh, c), ('post', hh, c)], [('pb', 2)])
                            self.v('dve', 'tensor_tensor', [('pb', 2), ('WT', i2)], [('PT', i2)], out=PT[i2][:].rearrange("p h t -> p (h t)"),
                                   in0=self.pb[2][:, 0:256], in1=WT[i2][:].rearrange("p h t -> p (h t)"), op=OP.mult)
                            for hh in range(2):
                                self.mm(self.pb[3][:, hh * 97:(hh + 1) * 97], PT[i2][:, hh, :], vp[:, c, hh, :], True, True,
                                        [('PT', i2), ('vp', c)], [('pb', 3)])
                            for hh in range(2):
                                self.mm(self.pb[4][:, hh * 97:(hh + 1) * 97], post[:, hh, tsl], Cb[:, hh, :], True, True,
                                        [('post', hh, c), 'Cb'], [('pb', 4)])
                            self.act(isb[i2][:].rearrange("p h e -> p (h e)"), self.pb[3][:, 0:194], AF.Identity, [('pb', 3)], [('isb', i2)])
                            self.v('dve', 'tensor_tensor', [('pb', 4), 'T4'], [('tmpn', i2)], out=tmpn[i2][:],
                                   in0=self.pb[4][:, 0:194].rearrange("p (h e) -> p h e", e=97),
                                   in1=T4[:, 0, c, cf0:cf0 + 2].unsqueeze(2).to_broadcast([128, 2, 97]), op=OP.mult)
                            self.v('dve', 'tensor_tensor', [('tmpn', i2), ('isb', i2)], [('tmpn', i2)], out=tmpn[i2][:], in0=tmpn[i2][:],
                                   in1=isb[i2][:], op=OP.add)
                            self.v('dve', 'scalar_tensor_tensor', [('tmpn', i2)], [('dn', i2)], out=dn[i2][:], in0=tmpn[i2][:, :, 96], scalar=-1.0,
                                   in1=tmpn[i2][:, :, 96], op0=OP.mult, op1=OP.max)
                            self.v('dve', 'tensor_tensor', [('dn', i2), 'T4'], [('dn', i2)], out=dn[i2][:], in0=dn[i2][:],
                                   in1=T4[:, 3, c, cf0:cf0 + 2], op=OP.max)
                            self.v('dve', 'reciprocal', [('dn', i2)], [('dn', i2)], out=dn[i2][:], in_=dn[i2][:])
                            hdst = hF[:, c] if dr == 0 else hs[i2][:]
                            hkey = ('hF', c) if dr == 0 else ('hs', i2)
                            self.v('dve', 'tensor_tensor', [('tmpn', i2), ('dn', i2)], [hkey], out=hdst, in0=tmpn[i2][:, :, 0:96],
                                   in1=dn[i2][:].unsqueeze(2).to_broadcast([128, 2, 96]), op=OP.mult)
                        self.v('dve', 'tensor_tensor', [('vp', c), 'T4'], [('vw', i2)], out=vw[i2][:], in0=vp[:, c],
                               in1=T4[:, 1, c, cf0:cf0 + 2].unsqueeze(2).to_broadcast([128, 2, 97]), op=OP.mult)
                        for hh in range(2):
                            self.mm(self.pb[5][0:96, hh * 97:(hh + 1) * 97], ktok[:, c, hh, :], vw[i2][:, hh, :], True, True,
                                    [('ktok', c), ('vw', i2)], [('pb', 5)])
                        self.v('dve', 'tensor_tensor', ['Cst', 'T4'], ['Cst'], out=Cst[:], in0=Cst[:],
                               in1=T4[0:96, 2, c, cf0:cf0 + 2].unsqueeze(2).to_broadcast([96, 2, 97]), op=OP.mult)
                        self.v('dve', 'tensor_tensor', ['Cst', ('pb', 5)], ['Cst'], out=Cst[:], in0=Cst[:],
                               in1=self.pb[5][0:96, 0:194].rearrange("p (h e) -> p h e", e=97), op=OP.add)
                        self.act(Cb[:], Cst[:], AF.Identity, ['Cst'], ['Cb'])
                        if dr == 1 and need_out:
                            self.v('dve', 'tensor_tensor', [('hs', i2), ('hF', c)], [('hs', i2)], out=hs[i2][:], in0=hs[i2][:], in1=hF[:, c], op=OP.add)
                            self.v('dve', 'tensor_tensor', [('hs', i2)], [('hq', i2)], out=hq[i2][:], in0=hs[i2][:], in1=hs[i2][:], op=OP.mult)
                            self.v('dve', 'tensor_reduce', [('hq', i2)], [('ss', i2)], out=ss[i2][:], in_=hq[i2][:], axis=AX.X, op=OP.add)
                            self.act(ss[i2][:], ss[i2][:], AF.Sqrt, [('ss', i2), 'epsT'], [('ss', i2)], bias=self.epsT[:], scale=1.0 / 96)
                            self.v('dve', 'reciprocal', [('ss', i2)], [('ss', i2)], out=ss[i2][:], in_=ss[i2][:])
                            self.v('dve', 'tensor_tensor', [('hs', i2), ('ss', i2)], [('hq', i2)], out=hq[i2][:], in0=hs[i2][:],
                                   in1=ss[i2][:].unsqueeze(2).to_broadcast([128, 2, 96]), op=OP.mult)
                            self.v('dve', 'tensor_tensor', [('hq', i2), 'gm'], [('hq', i2)], out=hq[i2][:].rearrange("p h e -> p (h e)"),
                                   in0=hq[i2][:].rearrange("p h e -> p (h e)"), in1=gm[:, 192 * hp:192 * (hp + 1)], op=OP.mult)
                            self.v('dve', 'tensor_tensor', [('hq', i2), ('sg', c)], [('yb', i2)], out=yb[i2][:],
                                   in0=hq[i2][:].rearrange("p h e -> p (h e)"), in1=sg[:, c, :], op=OP.mult)
                            for hh in range(2):
                                self.tr(self.ptr[i2][0:96, hh, :], yb[i2][:, hh * 96:(hh + 1) * 96], self.identb[:], [('yb', i2), 'identb'], [('ptr', i2)])
                            self.act(yhT[i2][:], self.ptr[i2][0:96, 0:2, :], AF.Identity, [('ptr', i2)], [('yhT', i2)])
                            for nh in range(2):
                                for hh in range(2):
                                    self.mm(self.pb[0][:], yhT[i2][:, hh, :], woM[:, hh, nh * 512:(nh + 1) * 512], hh == 0, hh == 1,
                                            [('yhT', i2), 'woM'], [('pb', 0)])
                                self.xupd(c, nh, 0)

    def mla(self, l, last):
        nc, P, d = self.nc, self.P, self.d
        SCALE = 96.0 ** -0.5
        with Scope(self) as S:
            wA = S.sb("wA", [128, 8, 416], BF16)
            wuq = S.sb("wuq", [128, 2, 384], BF16)
            wuqs = S.sb("wuqs", [128, 2, 4, 96], BF16)
            wkr = S.sb("wkr", [128, 8, 96], BF16)
            wkrs = S.sb("wkrs", [128, 8, 96], BF16)
            wukv = S.sb("wukv", [128, 640], BF16)
            woA = S.sb("woA", [96, 4, DM], BF16)
            ropeC = S.sb("ropeC", [96, TOK], BF16)
            ropeS = S.sb("ropeS", [96, TOK], BF16)
            Vt = S.sb("Vt", [128, NT, 384], BF16)
            cqnT = S.sb("cqnT", [128, 2, 512], BF16)
            ckvnT = S.sb("ckvnT", [128, 512], BF16)
            cqn = [S.sb("cqn%d" % i, [128, 256], BF16) for i in range(2)]
            ckvn = [S.sb("ckvn%d" % i, [128, 128], BF16) for i in range(2)]
            rt1 = S.sb("rt1", [96, 512], F32)
            rt2 = S.sb("rt2", [96, 512], F32)
            pT = [S.sb("pT%d" % i, [128, 512], BF16) for i in range(3)]
            rc = S.sb("rc", [96, 512], F32)
            yaT = S.sb("yaT", [96, 4, 512], BF16)
            onesb = S.sb("onesb", [128, 96], BF16)
            st = [S.sb("st%d" % i, [128, 2], F32) for i in range(2)]
            rs = [S.sb("rs%d" % i, [128, 2], F32) for i in range(2)]
            gq = S.sb("gq", [128, 2, 2], F32)
            gkv = S.sb("gkv", [128, 2], F32)
            qT = self.hT[0:96, 0:4, :]
            kT = self.hT[0:96, 4:8, :]
            win = d['w_in'][l]
            self.dma('pool', wA[:], win[:, 1808:2224].rearrange("(k p) c -> p k c", p=128), [], ['wA'], 'aw')
            self.dma('pool', wuq[:], d['w_uq'][l].rearrange("(k p) c -> p k c", p=128), [], ['wuq'], 'aw')
            self.v('dve', 'memset', [], ['wuqs'], wuqs[:], 0.0)
            self.v('dve', 'memset', [], ['wkr'], wkr[:], 0.0)
            self.v('dve', 'memset', [], ['wkrs'], wkrs[:], 0.0)
            self.v('dve', 'memset', [], ['onesb'], onesb[:], 1.0)
            for c in range(2):
                self.dma('pool', wuqs[:, c, :, 64:96], d['w_uq_sw'][l][c * 128:(c + 1) * 128, :, :], ['wuqs'], ['wuqs'], 'aw')
            self.dma('pool', wkr[:, :, 64:96], win[:, 2192:2224].rearrange("(k p) c -> p k c", p=128), ['wkr'], ['wkr'], 'aw')
            self.dma('pool', wkrs[:, :, 64:96], d['w_kr_sw'][l].rearrange("(k p) c -> p k c", p=128), ['wkrs'], ['wkrs'], 'aw')
            self.dma('pool', wukv[:], d['w_ukv'][l], [], ['wukv'], 'aw')
            self.dma('pool', woA[:], d['w_out'][l][640:1024, :].rearrange("(h p) c -> p h c", p=96), [], ['woA'], 'aw')
            self.dma('sp', ropeC[64:96, :], d['ropeC'][:, :], [], ['rope'], 'aw')
            self.dma('sp', ropeS[64:96, :], d['ropeS'][:, :], [], ['rope'], 'aw')
            self.dma('sp', gq[:], d['gqfm'][:, :, :], [], ['gq'], 'aw')
            self.dma('sp', gkv[:], d['gkvfm'][:, :], [], ['gq'], 'aw')
            wv = wukv[:, :].rearrange("p (h c) -> p h c", c=160)[:, :, 64:160]
            for (o, n) in self.blocks():
                tl = list(range(o // 128, (o + n) // 128))
                hk = [('hT', t) for t in tl]
                for ti, t in enumerate(tl):
                    b1 = t % 2
                    for k in range(8):
                        self.mm(self.pb[b1][:, 0:416], self.hT[:, k, t * 128:(t + 1) * 128], wA[:, k, :], k == 0, k == 7,
                                [('hT', t), 'wA'], [('pb', b1)])
                    si = t % 2
                    self.v('dve', 'memset', [], [('st', si)], st[si][:], 0.0)
                    self.cnt += 1
                    self.act(self.junk[:, 0:256], self.pb[b1][:, 0:256], AF.Square, [('pb', b1)], [('junk', self.cnt), ('st', si)],
                             accum=st[si][:, 0:1])
                    self.cnt += 1
                    self.act(self.junk[:, 256:384], self.pb[b1][:, 256:384], AF.Square, [('pb', b1)], [('junk', self.cnt), ('st', si)],
                             accum=st[si][:, 1:2])
                    self.act(rs[si][:, 0:1], st[si][:, 0:1], AF.Sqrt, [('st', si), 'epsT'], [('rs', si)], bias=self.epsT[:], scale=1.0 / 256)
                    self.act(rs[si][:, 1:2], st[si][:, 1:2], AF.Sqrt, [('st', si), 'epsT'], [('rs', si)], bias=self.epsT[:], scale=1.0 / 128)
                    self.v('dve', 'reciprocal', [('rs', si)], [('rs', si)], out=rs[si][:], in_=rs[si][:])
                    self.v('dve', 'tensor_scalar', [('pb', b1), ('rs', si)], [('cqn', si)], out=cqn[si][:], in0=self.pb[b1][:, 0:256],
                           scalar1=rs[si][:, 0:1], scalar2=None, op0=OP.mult)
                    self.v('dve', 'tensor_scalar', [('pb', b1), ('rs', si)], [('ckvn', si)], out=ckvn[si][:], in0=self.pb[b1][:, 256:384],
                           scalar1=rs[si][:, 1:2], scalar2=None, op0=OP.mult)
                    pi = t % 2
                    for c in range(2):
                        self.tr(self.ptr[pi][:, c, :], cqn[si][:, c * 128:(c + 1) * 128], self.identb[:], [('cqn', si), 'identb'], [('ptr', pi)])
                    self.tr(self.ptr[pi][:, 2, :], ckvn[si][:], self.identb[:], [('ckvn', si), 'identb'], [('ptr', pi)])
                    for c in range(2):
                        self.act(cqnT[:, c, ti * 128:(ti + 1) * 128], self.ptr[pi][:, c, :], AF.Identity, [('ptr', pi), 'gq'], ['cqnT'],
                                 scale=gq[:, l, c:c + 1])
                    self.act(ckvnT[:, ti * 128:(ti + 1) * 128], self.ptr[pi][:, 2, :], AF.Identity, [('ptr', pi), 'gq'], ['ckvnT'],
                             scale=gkv[:, l:l + 1])
                    bv = 2 + t % 2
                    self.mm(self.pb[bv][:, 0:384], ckvnT[:, ti * 128:(ti + 1) * 128], wv, True, True, ['ckvnT', 'wukv'], [('pb', bv)])
                    self.act(Vt[:, t, :], self.pb[bv][:, 0:384], AF.Identity, [('pb', bv)], [('Vt', t)])
                for k in range(8):
                    self.mm(self.pb[4][0:96, 0:n], wkr[:, k, :], self.hT[:, k, o:o + n], k == 0, k == 7, hk + ['wkr'], [('pb', 4)])
                for k in range(8):
                    self.mm(self.pb[5][0:96, 0:n], wkrs[:, k, :], self.hT[:, k, o:o + n], k == 0, k == 7, hk + ['wkrs'], [('pb', 5)])
                for h in range(4):
                    for c in range(2):
                        self.mm(self.pb[0][0:96, 0:n], wuq[:, c, h * 96:(h + 1) * 96], cqnT[:, c, 0:n], c == 0, c == 1, ['wuq', 'cqnT'], [('pb', 0)])
                    for c in range(2):
                        self.mm(self.pb[1][0:96, 0:n], wuqs[:, c, h, :], cqnT[:, c, 0:n], c == 0, c == 1, ['wuqs', 'cqnT'], [('pb', 1)])
                    bk = 2 + h % 2
                    self.mm(self.pb[bk][0:64, 0:n], wukv[:, h * 160:h * 160 + 64], ckvnT[:, 0:n], True, True, ['wukv', 'ckvnT'], [('pb', bk)])
                    self.act(qT[0:64, h, o:o + n], self.pb[0][0:64, 0:n], AF.Identity, [('pb', 0)], hk)
                    self.act(kT[0:64, h, o:o + n], self.pb[bk][0:64, 0:n], AF.Identity, [('pb', bk)], hk)
                    self.v('dve', 'tensor_tensor', [('pb', 0), 'rope'], ['rt1'], out=rt1[64:96, 0:n], in0=self.pb[0][64:96, 0:n],
                           in1=ropeC[64:96, o:o + n], op=OP.mult)
                    self.v('dve', 'tensor_tensor', [('pb', 1), 'rope'], ['rt2'], out=rt2[64:96, 0:n], in0=self.pb[1][64:96, 0:n],
                           in1=ropeS[64:96, o:o + n], op=OP.mult)
                    self.v('pool', 'tensor_tensor', ['rt1', 'rt2'], hk, out=qT[64:96, h, o:o + n], in0=rt1[64:96, 0:n],
                           in1=rt2[64:96, 0:n], op=OP.add)
                self.v('dve', 'tensor_tensor', [('pb', 4), 'rope'], ['rt1'], out=rt1[64:96, 0:n], in0=self.pb[4][64:96, 0:n],
                       in1=ropeC[64:96, o:o + n], op=OP.mult)
                self.v('dve', 'tensor_tensor', [('pb', 5), 'rope'], ['rt2'], out=rt2[64:96, 0:n], in0=self.pb[5][64:96, 0:n],
                       in1=ropeS[64:96, o:o + n], op=OP.mult)
                for h in range(4):
                    self.v('pool', 'tensor_tensor', ['rt1', 'rt2'], hk, out=kT[64:96, h, o:o + n], in0=rt1[64:96, 0:n],
                           in1=rt2[64:96, 0:n], op=OP.add)
            qblocks = [(256 + i * 512, 512, list(range(NT))) for i in range(4)]
            if not last:
                qblocks.append((0, 256, [0, 1]))
            for (qo, qn, ktiles) in qblocks:
                qk = [('hT', t) for t in range(qo // 128, (qo + qn) // 128)]
                for h in range(4):
                    for i, kt in enumerate(ktiles):
                        sb_ = i % 2
                        pi = i % 3
                        self.mm(self.pb[sb_][:, 0:qn], kT[:, h, kt * 128:(kt + 1) * 128], qT[:, h, qo:qo + qn], True, True,
                                qk + [('hT', kt)], [('pb', sb_)])
                        self.act(pT[pi][:, 0:qn], self.pb[sb_][:, 0:qn], AF.Exp, [('pb', sb_)], [('pT', pi)], scale=SCALE)
                        self.mm(self.pb[2][0:96, 0:qn], Vt[:, kt, h * 96:(h + 1) * 96], pT[pi][:, 0:qn], i == 0, i == len(ktiles) - 1,
                                [('Vt', kt), ('pT', pi)], [('pb', 2)])
                        self.mm(self.pb[3][0:96, 0:qn], onesb[:, :], pT[pi][:, 0:qn], i == 0, i == len(ktiles) - 1,
                                ['onesb', ('pT', pi)], [('pb', 3)])
                    self.v('dve', 'reciprocal', [('pb', 3)], ['rc'], out=rc[:, 0:qn], in_=self.pb[3][0:96, 0:qn])
                    self.v('dve', 'tensor_tensor', [('pb', 2), 'rc'], [('yaT', h)], out=yaT[:, h, 0:qn], in0=self.pb[2][0:96, 0:qn],
                           in1=rc[:, 0:qn], op=OP.mult)
                for ti, t in enumerate(range(qo // 128, (qo + qn) // 128)):
                    for nh in range(2):
                        b = 4 + nh
                        for h in range(4):
                            self.mm(self.pb[b][:], yaT[:, h, ti * 128:(ti + 1) * 128], woA[:, h, nh * 512:(nh + 1) * 512], h == 0, h == 3,
                                    [('yaT', h), 'woA'], [('pb', b)])
                        self.xupd(t, nh, b)

    def fourier(self, l, last):
        nc, P, d = self.nc, self.P, self.d
        with Scope(self) as S:
            wpf = S.sb("wpf", [128, 8, 256], BF16)
            pfT = S.sb("pfT", [128, 2, TOK], BF16)
            bd = S.sb("bd", [128, 256], BF16)
            pfCS = S.sb("pfCS", [128, NT, 2, 256], BF16)
            tw = [S.sb("tw%d" % i, [128, 2, 1024], BF16) for i in range(3)]
            yfT = S.sb("yfT", [128, 2, TOK], BF16)
            wo = S.sb("wo", [128, 2, DM], BF16)
            c256 = S.sb("c256", [128, 2, 2, 256], BF16)
            self.dma('pool', wpf[:], d['w_in'][l][:, 0:256].rearrange("(k p) c -> p k c", p=128), [], ['wpf'], 'fw')
            self.dma('pool', wo[:], d['w_out'][l][0:256, :].rearrange("(k p) c -> p k c", p=128), [], ['wo'], 'fw')
            self.dma('sp', bd[:], d['bdcs'][:, :], [], ['bd'], 'fw')
            self.dma('sp', c256[:], d['c256'][:, :, :, :], [], ['c256'], 'fw')
            nb = 0
            for j in range(2):
                for (o, n) in self.blocks():
                    b = nb % 6
                    nb += 1
                    for k in range(8):
                        self.mm(self.pb[b][:, 0:n], wpf[:, k, j * 128:(j + 1) * 128], self.hT[:, k, o:o + n], k == 0, k == 7,
                                ['wpf'] + [('hT', t) for t in range(o // 128, (o + n) // 128)], [('pb', b)])
                    self.act(pfT[:, j, o:o + n], self.pb[b][:, 0:n], AF.Identity, [('pb', b)], [('pfT', j, o)])
            for t in range(NT):
                for j in range(2):
                    b = nb % 6
                    nb += 1
                    self.mm(self.pb[b][:, 0:256], pfT[:, j, t * 128:(t + 1) * 128], bd[:], True, True,
                            [('pfT', j, (t // 4) * 512), 'bd'], [('pb', b)])
                    self.act(pfCS[:, t, j, :], self.pb[b][:, 0:256], AF.Identity, [('pb', b)], [('pfCS', t)])
            ntw = 0
            for half in range(2):
                for nt in range(16):
                    t = nt + 2
                    s = ntw % 3
                    ntw += 1
                    self.dma('sp', tw[s][:, 0, :], d['twC'][nt * 128:(nt + 1) * 128, half * 1024:(half + 1) * 1024], [], [('tw', s)], 'tw%d' % s)
                    self.dma('sp', tw[s][:, 1, :], d['twS'][nt * 128:(nt + 1) * 128, half * 1024:(half + 1) * 1024], [], [('tw', s)], 'tw%d' % s)
                    for j in range(2):
                        for q in range(2):
                            b = j * 2 + q
                            self.mm(self.pb[b][:], pfCS[:, t, j, 0:128], tw[s][:, 0, q * 512:(q + 1) * 512], nt == 0, False,
                                    [('pfCS', t), ('tw', s)], [('pb', b)])
                            self.mm(self.pb[b][:], pfCS[:, t, j, 128:256], tw[s][:, 1, q * 512:(q + 1) * 512], False, nt == 15,
                                    [('pfCS', t), ('tw', s)], [('pb', b)])
                for j in range(2):
                    for q in range(2):
                        b = j * 2 + q
                        o = 256 + half * 1024 + q * 512
                        self.act(yfT[:, j, o:o + 512], self.pb[b][:], AF.Identity, [('pb', b)], [('yfT', o // 128)])
            if not last:
                for j in range(2):
                    b = 4 + j
                    for nt in range(2):
                        self.mm(self.pb[b][:, 0:256], pfCS[:, nt, j, 0:128], c256[:, nt, 0, :], nt == 0, False,
                                [('pfCS', nt), 'c256'], [('pb', b)])
                        self.mm(self.pb[b][:, 0:256], pfCS[:, nt, j, 128:256], c256[:, nt, 1, :], False, nt == 1,
                                [('pfCS', nt), 'c256'], [('pb', b)])
                    self.act(yfT[:, j, 0:256], self.pb[b][:, 0:256], AF.Identity, [('pb', b)], [('yfT', 0)])
            nb = 0
            for t in (range(2, NT) if last else range(NT)):
                for nh in range(2):
                    b = nb % 6
                    nb += 1
                    for j in range(2):
                        self.mm(self.pb[b][:], yfT[:, j, t * 128:(t + 1) * 128], wo[:, j, nh * 512:(nh + 1) * 512], j == 0, j == 1,
                                [('yfT', 0 if t < 2 else 2 + ((t - 2) // 4) * 4), 'wo'], [('pb', b)])
                    self.xupd(t, nh, b)

    def mlp(self, l, tiles):
        nc, P, d = self.nc, self.P, self.d
        groups = [tiles[i:i + 6] for i in range(0, len(tiles), 6)]
        with Scope(self) as S:
            uT = S.sb("uT", [128, 32, 768], BF16)
            wu = [S.sb("wu%d" % i, [128, 8, 512], BF16) for i in range(2)]
            wd = [S.sb("wd%d" % i, [128, 4, 512], BF16) for i in range(2)]
            tmp = [S.sb("mt%d" % i, [128, 512], F32) for i in range(2)]
            nu = nd = nb = 0
            for g in groups:
                G = 128 * len(g)
                t0 = g[0] * 128
                subs = [(0, G)] if G <= 512 else [(0, G // 2), (G // 2, G // 2)]
                for hb in range(8):
                    s = nu % 2
                    nu += 1
                    self.dma('pool', wu[s][:], d['w_up'][l][:, hb * 512:(hb + 1) * 512].rearrange("(k p) c -> p k c", p=128),
                             [], [('wu', s)], 'wu%d' % s)
                    for cc in range(4):
                        j = hb * 4 + cc
                        for (o, n) in subs:
                            b = nb % 6
                            nb += 1
                            for k in range(8):
                                self.mm(self.pb[b][:, 0:n], wu[s][:, k, cc * 128:(cc + 1) * 128],
                                        self.hT[:, k, t0 + o:t0 + o + n], k == 0, k == 7,
                                        [('wu', s)] + [('hT', t) for t in g], [('pb', b)])
                            ri = nb % 2
                            self.act(tmp[ri][:, 0:n], self.pb[b][:, 0:n], AF.Relu, [('pb', b)], [('mt', ri)])
                            self.v('dve', 'tensor_tensor', [('mt', ri)], [('uT', j)], out=uT[:, j, o:o + n],
                                   in0=tmp[ri][:, 0:n], in1=tmp[ri][:, 0:n], op=OP.mult)
                for nh in range(2):
                    for jb in range(8):
                        s = nd % 2
                        nd += 1
                        self.dma('pool', wd[s][:],
                                 d['w_down'][l][jb * 512:(jb + 1) * 512, nh * 512:(nh + 1) * 512].rearrange("(j p) c -> p j c", p=128),
                                 [], [('wd', s)], 'wd%d' % s)
                        for jj in range(4):
                            j = jb * 4 + jj
                            for ti, t in enumerate(g):
                                self.mm(self.pb[ti][:], uT[:, j, ti * 128:(ti + 1) * 128], wd[s][:, jj, :], j == 0, j == 31,
                                        [('wd', s), ('uT', j)], [('pb', ti)])
                    for ti, t in enumerate(g):
                        self.xupd(t, nh, ti)

    def final(self):
        nc, P, d = self.nc, self.P, self.d
        with Scope(self) as S:
            gf = S.sb("gf", [128, DM], F32)
            ob = [S.sb("ob%d" % i, [128, DM], F32) for i in range(2)]
            self.dma('sp', gf[:], d['gfin'][:, :], [], ['gf'], 'cst')
            for t in range(2, NT):
                self.rms(t)
                i = t % 2
                self.act(ob[i][:], self.xs[:, t, :], AF.Identity, [('xs', t), ('rstd', t)], [('ob', i)], scale=self.rstd[:, t:t + 1])
                self.v('dve', 'tensor_tensor', [('ob', i), 'gf'], [('ob', i)], out=ob[i][:], in0=ob[i][:], in1=gf[:], op=OP.mult)
                self.dma('sp', d['out'][(t - 2) * 128:(t - 1) * 128, :], ob[i][:], [('ob', i)], [('out', t)], 'out')


def dram_decls(nc):
    d = {}

    def inp(name, shape, dt=F32):
        d[name] = nc.dram_tensor(name, list(shape), dt, kind="ExternalInput").ap()

    inp('xin', [TOK, DM])
    inp('c2', [128, 16])
    inp('w_mod', [2, 1024, 6144])
    inp('bmodfm', [128, 2, 48])
    inp('gnfm', [128, 2, 2, 8])
    inp('w_in', [2, 1024, 2224])
    inp('w_out', [2, 1024, 1024])
    inp('w_up', [2, 1024, 4096])
    inp('w_down', [2, 4096, 1024])
    inp('gfin', [128, DM])
    inp('identb', [128, 128], BF16)
    inp('identf', [128, 128])
    inp('w_uq', [2, 256, 384])
    inp('w_ukv', [2, 128, 640])
    inp('w_uq_sw', [2, 256, 4, 32])
    inp('w_kr_sw', [2, 1024, 32])
    inp('ropeC', [32, TOK], BF16)
    inp('ropeS', [32, TOK], BF16)
    inp('gqfm', [128, 2, 2])
    inp('gkvfm', [128, 2])
    inp('w_g', [2, 1024, 16])
    inp('bg', [2, 8, 2])
    inp('convdg', [2, 96, 8, 5, 96], BF16)
    inp('gmB', [128, 2, 384])
    inp('E2', [8, 2, 8])
    inp('negidf', [128, 128])
    inp('mlmask', [128, 2, 256], BF16)
    inp('bdcs', [128, 256], BF16)
    inp('twC', [2048, 2048], BF16)
    inp('twS', [2048, 2048], BF16)
    inp('c256', [128, 2, 2, 256], BF16)
    d['out'] = nc.dram_tensor('out', [2048, DM], F32, kind="ExternalOutput").ap()
    if DBGXS:
        d['dbgxs'] = nc.dram_tensor('dbgxs', [TOK, DM], F32, kind="ExternalOutput").ap()
    return d


def build_program():
    nc = bass.Bass("TRN2", target_bir_lowering=False)
    d = dram_decls(nc)
    P = Prog(nc)
    kb = K(nc, P, d)
    kb.build()
    P.plan()
    es = contextlib.ExitStack()
    with es:
        P.start_real(es)
        kb.build()
    return nc, P


def host_inputs(inputs, b):
    f = lambda a: np.ascontiguousarray(np.asarray(a, dtype=np.float32))
    x, c, ctx, c_ctx = inputs['x'], inputs['c'], inputs['ctx'], inputs['c_ctx']
    m = {}
    m['xin'] = f(np.concatenate([ctx[b], x[b]], axis=0))
    c2 = np.stack([np.asarray(c[b]).reshape(8, 128).T, np.asarray(c_ctx).reshape(8, 128).T], axis=-1)
    m['c2'] = f(c2.reshape(128, 16))
    m['w_mod'] = f(inputs['w_mod'])
    m['bmodfm'] = f(np.asarray(inputs['b_mod']).reshape(2, 48, 128).transpose(2, 0, 1))
    gn = np.stack([np.asarray(inputs['g_norm1']), np.asarray(inputs['g_norm2'])], axis=1)
    m['gnfm'] = f(gn.reshape(2, 2, 8, 128).transpose(3, 0, 1, 2))
    m['w_in'] = f(inputs['w_in'])
    m['w_out'] = f(inputs['w_out'])
    m['w_up'] = f(inputs['w_up'])
    m['w_down'] = f(inputs['w_down'])
    m['gfin'] = f(np.tile(np.asarray(inputs['g_final'])[None, :], (128, 1)))
    m['identb'] = np.eye(128, dtype=np.float32).astype(ml_dtypes.bfloat16)
    m['identf'] = np.eye(128, dtype=np.float32)
    m.update(host_consts())
    perm = np.array([a * 16 + (1 - b_) * 8 + j for a in range(2) for b_ in range(2) for j in range(8)])
    wuq = np.asarray(inputs['w_uq'])
    m['w_uq'] = f(wuq)
    m['w_ukv'] = f(inputs['w_ukv'])
    m['w_uq_sw'] = f(wuq.reshape(2, 256, 4, 96)[:, :, :, 64:96][..., perm])
    m['w_kr_sw'] = f(np.asarray(inputs['w_in'])[:, :, 2192:2224][..., perm])
    m['gqfm'] = f(np.asarray(inputs['g_q_norm']).reshape(2, 2, 128).transpose(2, 0, 1))
    m['gkvfm'] = f(np.asarray(inputs['g_kv_norm']).T)
    gp = [0, 1, 2, 3, 8, 9, 10, 11, 4, 5, 6, 7, 12, 13, 14, 15]
    m['w_g'] = f(np.asarray(inputs['w_in'])[:, :, 1792:1808][..., gp])
    m['bg'] = f(np.asarray(inputs['b_gates'])[:, gp].reshape(2, 2, 8).transpose(0, 2, 1))
    cv = np.asarray(inputs['conv_qk'], dtype=np.float32).reshape(2, 5, 8, 96)
    dgc = np.zeros((2, 96, 8, 5, 96), np.float32)
    pi_ = np.arange(96)
    dgc[:, pi_, :, :, pi_] = cv.transpose(3, 0, 2, 1)
    m['convdg'] = dgc.astype(ml_dtypes.bfloat16)
    m['gmB'] = f(np.tile(np.asarray(inputs['g_mlstm'])[None, :, :], (128, 1, 1)))
    return m


def host_consts():
    bf = lambda a: np.ascontiguousarray(a.astype(np.float32)).astype(ml_dtypes.bfloat16)
    m = {}
    i64 = np.arange(64)
    a64 = 2 * np.pi * np.outer(i64, i64) / 64
    C64, S64 = np.cos(a64) / 8.0, np.sin(a64) / 8.0
    z = np.zeros((64, 64))
    m['bdcs'] = bf(np.block([[C64, z, S64, z], [z, C64, z, S64]]))
    n = np.arange(2048, dtype=np.float64)
    aN = 2 * np.pi * (np.outer(n, n) % 2048) / 2048
    m['twC'] = bf(np.cos(aN) / np.sqrt(2048.0))
    m['twS'] = bf(-np.sin(aN) / np.sqrt(2048.0))
    E2 = np.zeros((8, 2, 8), np.float32)
    for k_ in range(4):
        E2[k_, 0, k_] = 1.0
        E2[4 + k_, 1, 4 + k_] = 1.0
    m['E2'] = E2
    m['negidf'] = -np.eye(128, dtype=np.float32)
    sidx, tidx = np.arange(128)[:, None], np.arange(128)[None, :]
    mf = np.where(sidx <= tidx, 0.0, -30000.0)
    mb_ = np.where(sidx >= tidx, 0.0, -30000.0)
    m['mlmask'] = bf(np.stack([np.concatenate([mf, mf], axis=1), np.concatenate([mb_, mb_], axis=1)], axis=1))
    tt = np.arange(2048)
    freqs = 10000.0 ** (-np.arange(8, dtype=np.float64) / 8)
    ang = np.stack([np.outer(tt // 64, freqs), np.outer(tt % 64, freqs)], axis=1)
    cosT = np.ones((TOK, 2, 2, 8)); sinT = np.zeros((TOK, 2, 2, 8))
    cosT[256:] = np.cos(ang)[:, :, None, :]
    sinT[256:, :, 0, :] = -np.sin(ang)
    sinT[256:, :, 1, :] = np.sin(ang)
    m['ropeC'] = bf(cosT.reshape(TOK, 32).T)
    m['ropeS'] = bf(sinT.reshape(TOK, 32).T)
    n = np.arange(256, dtype=np.float64)
    a2 = 2 * np.pi * (np.outer(n, n) % 256) / 256
    c = np.stack([np.cos(a2) / 16.0, -np.sin(a2) / 16.0], axis=1)
    m['c256'] = bf(c.reshape(2, 128, 2, 256).transpose(1, 0, 2, 3))
    return m


_CACHE = {}


def kernel(**inputs):
    if 'nc' not in _CACHE:
        _CACHE['nc'] = build_program()
    nc, P = _CACHE['nc']
    shared = None
    in_maps = []
    for b in range(8):
        m = host_inputs(inputs, b) if shared is None else None
        if shared is None:
            shared = m
        else:
            m = dict(shared)
            f = lambda a: np.ascontiguousarray(np.asarray(a, dtype=np.float32))
            m['xin'] = f(np.concatenate([inputs['ctx'][b], inputs['x'][b]], axis=0))
            c2 = np.stack([np.asarray(inputs['c'][b]).reshape(8, 128).T, np.asarray(inputs['c_ctx']).reshape(8, 128).T], axis=-1)
            m['c2'] = f(c2.reshape(128, 16))
        in_maps.append(m)
    res = run_bass_kernel_spmd(nc, in_maps, core_ids=list(range(8)))
    _CACHE['res'] = res
    return np.stack([np.asarray(r['out'], dtype=np.float32) for r in res.results], axis=0)
```

```python
import contextlib, bisect, math
import numpy as np
import ml_dtypes
import concourse.bass as bass
import concourse.mybir as mybir
from concourse.bass_utils import run_bass_kernel_spmd

F32 = mybir.dt.float32
BF16 = mybir.dt.bfloat16
AF = mybir.ActivationFunctionType
OP = mybir.AluOpType
AX = mybir.AxisListType

NT = 18
TOK = 2304
DM = 1024
EPS = 1e-6
DEPTH = 2
STAGE = 99
DEBUG = []
BRANCHES = 'FAM'
DBGXS = False


class Dummy:
    def __getitem__(self, k):
        return self

    def __getattr__(self, n):
        return lambda *a, **k: self


class Prog:
    ENGS = ('pe', 'dve', 'act', 'pool', 'sp')

    def __init__(self, nc):
        self.nc = nc
        self.dry = True
        self.rec = []
        self.i = 0

    def op(self, eng, fn, r=(), w=(), dma=None):
        if self.dry:
            self.rec.append(('op', eng, tuple(r), tuple(w), dma))
        else:
            i = self.i
            E = self.H[eng]
            for (s, v) in self.waits[i]:
                E.wait_ge(self.sems[s], v)
            if fn is not None:
                ins = fn()
                inc = self.incs[i]
                if inc is not None:
                    ins.then_inc(self.sems[inc[0]], inc[1])
        self.i += 1

    def barrier(self):
        if self.dry:
            self.rec.append(('bar',))
        self.i += 1

    def plan(self):
        rec = self.rec
        n = len(rec)
        lastw, readers, last_on_eng, dma_ops = {}, {}, {}, {}
        deps = [None] * n
        pend = {e: None for e in self.ENGS}
        for i, r in enumerate(rec):
            if r[0] == 'bar':
                bd = set(last_on_eng.values())
                for s, l in dma_ops.items():
                    if l:
                        bd.add(l[-1])
                for e in self.ENGS:
                    pend[e] = set(bd) | (pend[e] or set())
                lastw.clear()
                readers.clear()
                continue
            _, eng, R, W, dma = r
            d = set()
            for k in R:
                if k in lastw:
                    d.add(lastw[k])
            for k in W:
                if k in lastw:
                    d.add(lastw[k])
                d.update(readers.get(k, ()))
            if pend[eng]:
                d |= pend[eng]
                pend[eng] = None
            d.discard(i)
            if eng == 'pe':
                d = {j for j in d if not (rec[j][1] == 'pe' and rec[j][4] is None)}
            deps[i] = d
            for k in W:
                lastw[k] = i
                readers[k] = []
            for k in R:
                lst = readers.setdefault(k, [])
                if dma is None:
                    lst[:] = [j for j in lst if not (rec[j][1] == eng and rec[j][4] is None)]
                lst.append(i)
            last_on_eng[eng] = i
            if dma:
                dma_ops.setdefault(dma, []).append(i)
        sig = [False] * n
        for i in range(n):
            if deps[i]:
                for j in deps[i]:
                    if rec[j][4] is None:
                        sig[j] = True
        cnt = {e: 0 for e in self.ENGS}
        ev = [None] * n
        for i, r in enumerate(rec):
            if r[0] != 'op':
                continue
            if r[4] is None and sig[i]:
                cnt[r[1]] += 1
                ev[i] = ('E_' + r[1], cnt[r[1]])
        self.waits = [()] * n
        self.incs = [None] * n
        seen = {e: {} for e in self.ENGS}
        for i, r in enumerate(rec):
            if r[0] != 'op':
                continue
            eng = r[1]
            wl = {}
            for j in deps[i]:
                if rec[j][4] is None:
                    s, v = ev[j]
                else:
                    s = 'D_' + rec[j][4]
                    v = 16 * bisect.bisect_left(dma_ops[rec[j][4]], i)
                if v > wl.get(s, 0):
                    wl[s] = v
            ws = []
            for s, v in wl.items():
                if v > seen[eng].get(s, 0):
                    seen[eng][s] = v
                    ws.append((s, v))
            self.waits[i] = tuple(ws)
            if r[4] is not None:
                self.incs[i] = ('D_' + r[4], 16)
            elif sig[i]:
                self.incs[i] = (ev[i][0], 1)
        self.semnames = ['E_' + e for e in self.ENGS] + ['D_' + s for s in dma_ops]
        self.stats = dict(n=n, cnt=cnt, nw=sum(len(w) for w in self.waits))

    def start_real(self, es):
        nc = self.nc
        self.dry = False
        self.i = 0
        self.H = {'pe': nc.tensor, 'dve': nc.vector, 'act': nc.scalar, 'pool': nc.gpsimd, 'sp': nc.sync}
        self.sems = {s: es.enter_context(nc.semaphore(s)) for s in self.semnames}


class Scope:
    _n = 0

    def __init__(self, K):
        self.K = K
        self.es = contextlib.ExitStack()

    def __enter__(self):
        return self

    def sb(self, name, shape, dt):
        if self.K.P.dry:
            return Dummy()
        Scope._n += 1
        return self.es.enter_context(self.K.nc.sbuf_tensor("%s_%d" % (name, Scope._n), list(shape), dt))

    def __exit__(self, *a):
        self.K.P.barrier()
        self.es.close()
        return False


def which(t):
    return 1 if t < 2 else 0


class K:
    def __init__(self, nc, P, dram):
        self.nc, self.P, self.d = nc, P, dram

    def mm(self, out, lhsT, rhs, start, stop, r, w):
        nc = self.nc
        self.P.op('pe', lambda: nc.tensor.matmul(out, lhsT=lhsT, rhs=rhs, start=start, stop=stop), r, w)

    def tr(self, out, in_, ident, r, w):
        nc = self.nc
        self.P.op('pe', lambda: nc.tensor.transpose(out, in_, ident), r, w)

    def act(self, out, in_, func, r, w, bias=None, scale=None, accum=None):
        nc = self.nc
        kw = {}
        if bias is not None:
            kw['bias'] = bias
        if scale is not None:
            kw['scale'] = scale
        if accum is not None:
            kw['accum_out'] = accum
        self.P.op('act', lambda: nc.scalar.activation(out=out, in_=in_, func=func, **kw), r, w)

    def v(self, eng, name, r, w, *a, **kw):
        nc = self.nc
        E = nc.vector if eng == 'dve' else nc.gpsimd
        self.P.op(eng, lambda: getattr(E, name)(*a, **kw), r, w)

    def dma(self, q, out, in_, r, w, sem):
        nc = self.nc
        E = nc.sync if q == 'sp' else nc.gpsimd
        self.P.op(q, lambda: E.dma_start(out=out, in_=in_), r, w, dma=sem + '_' + q)

    def build(self):
        nc, P, d = self.nc, self.P, self.d
        with Scope(self) as S:
            self.S0 = S
            self.xs = S.sb("xs", [128, NT, DM], F32)
            self.hT = S.sb("hT", [128, 8, TOK], BF16)
            self.identb = S.sb("identb", [128, 128], BF16)
            self.identf = S.sb("identf", [128, 128], F32)
            self.onesf = S.sb("onesf", [128, 128], F32)
            self.c2 = S.sb("c2", [128, 16], F32)
            self.scb = S.sb("scb", [128, 16], BF16)
            self.modv = S.sb("modv", [128, 48, 2], F32)
            self.bmod = S.sb("bmod", [128, 2, 48], F32)
            self.gn = S.sb("gn", [128, 2, 2, 8], F32)
            self.G = S.sb("G", [128, 2, 8, 2], F32)
            self.gaB = S.sb("gaB", [128, 2, DM], F32)
            self.ssq = S.sb("ssq", [128, NT], F32)
            self.rstd = S.sb("rstd", [128, NT], F32)
            self.epsT = S.sb("epsT", [128, 1], F32)
            self.xn = [S.sb("xn%d" % i, [128, DM], BF16) for i in range(2)]
            self.junk = S.sb("junk", [128, DM], BF16)
            self.dg = [S.sb("dg%d" % i, [128, 128], F32) for i in range(2)]
            self.xt = [S.sb("xt%d" % i, [128, 512], F32) for i in range(2)]
            if not P.dry:
                es = S.es
                self.pb = [es.enter_context(nc.psum_tensor("pb%d" % i, [128, 512], F32)) for i in range(6)]
                self.ptr = [es.enter_context(nc.psum_tensor("ptr%d" % i, [128, 8, 128], BF16)) for i in range(2)]
            else:
                self.pb = [Dummy()] * 6
                self.ptr = [Dummy()] * 2
            self.cnt = 0
            self.dma('sp', self.identb[:], d['identb'][:, :], [], ['identb'], 'cst')
            self.dma('sp', self.identf[:], d['identf'][:, :], [], ['identf'], 'cst')
            self.dma('sp', self.c2[:], d['c2'][:, :], [], ['c2'], 'cst')
            self.dma('sp', self.bmod[:], d['bmodfm'][:, :, :], [], ['bmod'], 'cst')
            self.dma('sp', self.gn[:], d['gnfm'][:, :, :, :], [], ['gn'], 'cst')
            self.v('dve', 'memset', [], ['onesf'], self.onesf[:], 1.0)
            self.v('dve', 'memset', [], ['epsT'], self.epsT[:], EPS)
            self.v('dve', 'memset', [], ['ssq'], self.ssq[:], 0.0)
            self.act(self.scb[:], self.c2[:], AF.Silu, ['c2'], ['scb'])
            for t in range(NT):
                self.dma('sp', self.xs[:, t, :], d['xin'][t * 128:(t + 1) * 128, :], [], [('xs', t)], 'xin')
            for l in range(DEPTH):
                last = (l == DEPTH - 1)
                self.mod_phase(l)
                if STAGE <= 1:
                    break
                self.make_gaB(2)
                self.phaseA(l, 0, range(NT))
                if STAGE >= 3:
                    self.mixers(l, last)
                if DBGXS and l == 0:
                    for t in range(NT):
                        self.dma('sp', d['dbgxs'][t * 128:(t + 1) * 128, :], self.xs[:, t, :], [('xs', t)], [('dbgxs', t)], 'out')
                    break
                self.make_gaB(5)
                tiles = list(range(2, NT)) if last else list(range(NT))
                self.phaseA(l, 1, tiles)
                self.mlp(l, tiles)
            self.final()
        P.barrier()
        P.op('sp', None)

    def mod_phase(self, l):
        nc, P, d = self.nc, self.P, self.d
        with Scope(self) as S:
            wm = [S.sb("wm%d" % i, [128, 8, 1024], BF16) for i in range(2)]
            modps = self.pb[0]
            for blk in range(6):
                s = blk % 2
                self.dma('pool', wm[s][:], d['w_mod'][l][:, blk * 1024:(blk + 1) * 1024].rearrange("(k p) c -> p k c", p=128),
                         [], [('wm', s)], 'wm%d' % s)
                for jj in range(8):
                    j = blk * 8 + jj
                    for k in range(8):
                        self.mm(modps[:, 2 * j:2 * j + 2], wm[s][:, k, jj * 128:(jj + 1) * 128], self.scb[:, 2 * k:2 * k + 2],
                                k == 0, k == 7, [('wm', s), 'scb'], [('pb', 0)])
            self.v('dve', 'tensor_tensor', [('pb', 0), 'bmod'], ['modv'], out=self.modv[:],
                   in0=modps[:, 0:96].rearrange("p (j w) -> p j w", w=2),
                   in1=self.bmod[:, l, :].unsqueeze(2).to_broadcast([128, 48, 2]), op=OP.add)
            for ni, vi in ((0, 1), (1, 4)):
                self.v('dve', 'scalar_tensor_tensor', ['modv', 'gn'], ['G'], out=self.G[:, ni, :, :],
                       in0=self.modv[:, vi * 8:(vi + 1) * 8, :], scalar=1.0,
                       in1=self.gn[:, l, ni, :].unsqueeze(2).to_broadcast([128, 8, 2]), op0=OP.add, op1=OP.mult)

    def make_gaB(self, vi):
        for w in range(2):
            for kk in range(8):
                self.cnt += 1
                dgi = self.cnt % 2
                self.v('dve', 'tensor_scalar', ['modv', 'identf'], [('dg', dgi)], out=self.dg[dgi][:], in0=self.identf[:],
                       scalar1=self.modv[:, vi * 8 + kk, w:w + 1], scalar2=None, op0=OP.mult)
                bank = kk // 4
                self.mm(self.pb[bank][:, (kk % 4) * 128:(kk % 4 + 1) * 128], self.onesf[:], self.dg[dgi][:], True, True,
                        ['onesf', ('dg', dgi)], [('pb', bank)])
            for bank in range(2):
                self.act(self.gaB[:, w, bank * 512:(bank + 1) * 512], self.pb[bank][:], AF.Identity, [('pb', bank)], ['gaB'])

    def rms(self, t):
        self.cnt += 1
        self.act(self.junk[:], self.xs[:, t, :], AF.Square, [('xs', t)], [('junk', self.cnt), ('ssq', t)],
                 accum=self.ssq[:, t:t + 1])
        self.act(self.rstd[:, t:t + 1], self.ssq[:, t:t + 1], AF.Sqrt, [('ssq', t), 'epsT'], [('rstd', t)],
                 bias=self.epsT[:], scale=1.0 / DM)
        self.v('dve', 'reciprocal', [('rstd', t)], [('rstd', t)], out=self.rstd[:, t:t + 1], in_=self.rstd[:, t:t + 1])
        self.v('dve', 'memset', [('ssq', t)], [('ssq', t)], self.ssq[:, t:t + 1], 0.0)

    def phaseA(self, l, ni, tiles):
        vi = 0 if ni == 0 else 3
        for t in tiles:
            w = which(t)
            self.rms(t)
            self.cnt += 1
            xi = self.cnt % 2
            self.act(self.xn[xi][:], self.xs[:, t, :], AF.Identity, [('xs', t), ('rstd', t)], [('xn', xi)],
                     scale=self.rstd[:, t:t + 1])
            for k in range(8):
                self.tr(self.ptr[xi][:, k, :], self.xn[xi][:, k * 128:(k + 1) * 128], self.identb[:],
                        [('xn', xi), 'identb'], [('ptr', xi)])
            for k in range(8):
                self.act(self.hT[:, k, t * 128:(t + 1) * 128], self.ptr[xi][:, k, :], AF.Identity,
                         [('ptr', xi), 'G', 'modv'], [('hT', t)],
                         bias=self.modv[:, vi * 8 + k, w:w + 1], scale=self.G[:, ni, k, w:w + 1])

    def xupd(self, t, nh, b):
        w = which(t)
        self.cnt += 1
        mi = self.cnt % 2
        self.v('dve', 'tensor_tensor', [('pb', b), 'gaB'], [('xt', mi)], out=self.xt[mi][:], in0=self.pb[b][:],
               in1=self.gaB[:, w, nh * 512:(nh + 1) * 512], op=OP.mult)
        self.v('pool', 'tensor_tensor', [('xt', mi), ('xs', t)], [('xs', t)],
               out=self.xs[:, t, nh * 512:(nh + 1) * 512], in0=self.xs[:, t, nh * 512:(nh + 1) * 512],
               in1=self.xt[mi][:], op=OP.add)

    def blocks(self, lo=0, hi=TOK, step=512):
        return [(o, min(step, hi - o)) for o in range(lo, hi, step)]

    def mixers(self, l, last):
        if 'F' in BRANCHES:
            self.fourier(l, last)
        if 'M' in BRANCHES:
            self.mlstm(l, last)
        if 'A' in BRANCHES:
            self.mla(l, last)

    def mlstm(self, l, last):
        nc, P, d = self.nc, self.P, self.d
        win = d['w_in'][l]
        with Scope(self) as SO:
            acol = SO.sb("acol", [128, NT, 8], F32)
            Mcol = SO.sb("Mcol", [128, NT, 8], F32)
            T4 = SO.sb("T4", [128, 4, NT, 8], F32)
            gm = SO.sb("gm", [128, 384], F32)
            E2 = SO.sb("E2", [8, 2, 8], F32)
            negid = SO.sb("negid", [128, 128], F32)
            mlmask = SO.sb("mlmask", [128, 2, 256], BF16)
            self.dma('sp', gm[:], d['gmB'][:, l, :], [], ['gm'], 'mw')
            self.dma('sp', E2[:], d['E2'][:, :, :], [], ['E2'], 'mw')
            self.dma('sp', negid[:], d['negidf'][:, :], [], ['negid'], 'mw')
            self.dma('sp', mlmask[:], d['mlmask'][:, :, :], [], ['mlmask'], 'mw')
            with Scope(self) as S:
                wg = S.sb("wg", [128, 8, 16], BF16)
                bg = S.sb("bg", [8, 2], F32)
                nbf = S.sb("nbf", [8, 1], F32)
                IG = S.sb("IG", [8, TOK], F32)
                LF = S.sb("LF", [8, TOK], F32)
                FF = S.sb("FF", [8, TOK], F32)
                FB = S.sb("FB", [8, TOK], F32)
                AFr = S.sb("AFr", [8, TOK], F32)
                ABr = S.sb("ABr", [8, TOK], F32)
                R1 = S.sb("R1", [8, NT, 8], F32)
                R2 = S.sb("R2", [8, NT, 8], F32)
                mcol = S.sb("mcol", [128, NT, 8], F32)
                MendB = S.sb("MendB", [128, NT, 8], F32)
                MP = S.sb("MP", [128, NT, 8], F32)
                ex = S.sb("ex", [128, 4, NT, 8], F32)
                self.dma('pool', wg[:], d['w_g'][l].rearrange("(k p) c -> p k c", p=128), [], ['wg'], 'gw')
                self.dma('sp', bg[:], d['bg'][l], [], ['bg'], 'gw')
                self.v('dve', 'tensor_scalar', ['bg'], ['nbf'], out=nbf[:], in0=bg[:, 1:2], scalar1=-1.0, scalar2=None, op0=OP.mult)
                nb = 0
                for (o, n) in self.blocks():
                    hk = [('hT', t) for t in range(o // 128, (o + n) // 128)]
                    bi, bf_ = nb % 6, (nb + 1) % 6
                    nb += 2
                    for k in range(8):
                        self.mm(self.pb[bi][0:8, 0:n], wg[:, k, 0:8], self.hT[:, k, o:o + n], k == 0, k == 7, hk + ['wg'], [('pb', bi)])
                    for k in range(8):
                        self.mm(self.pb[bf_][0:8, 0:n], wg[:, k, 8:16], self.hT[:, k, o:o + n], k == 0, k == 7, hk + ['wg'], [('pb', bf_)])
                    self.act(IG[:, o:o + n], self.pb[bi][0:8, 0:n], AF.Identity, [('pb', bi), 'bg'], ['IG'], bias=bg[:, 0:1])
                    self.act(LF[:, o:o + n], self.pb[bf_][0:8, 0:n], AF.Exp, [('pb', bf_), 'nbf'], ['LF'], bias=nbf[:], scale=-1.0)
                self.act(LF[:], LF[:], AF.Ln, ['LF'], ['LF'], bias=1.0)
                self.v('dve', 'tensor_scalar', ['LF'], ['LF'], out=LF[:], in0=LF[:], scalar1=-1.0, scalar2=None, op0=OP.mult)
                one_b = lambda n: self.onesf[0:8, 0:1].to_broadcast([8, n])
                self.v('dve', 'tensor_tensor_scan', ['LF', 'onesf'], ['FF'], out=FF[:], data0=one_b(TOK), data1=LF[:], initial=0.0,
                       op0=OP.mult, op1=OP.add)
                self.v('dve', 'tensor_tensor_scan', ['LF', 'onesf'], ['FB'], out=FB[:, 255::-1], data0=one_b(256), data1=LF[:, 255::-1],
                       initial=0.0, op0=OP.mult, op1=OP.add)
                self.v('dve', 'tensor_tensor_scan', ['LF', 'onesf', 'FB'], ['FB'], out=FB[:, 2303:255:-1], data0=one_b(2048),
                       data1=LF[:, 2303:255:-1], initial=FB[:, 0:1], op0=OP.mult, op1=OP.add)
                self.v('dve', 'tensor_tensor', ['IG', 'FF'], ['AFr'], out=AFr[:], in0=IG[:], in1=FF[:], op=OP.subtract)
                self.v('dve', 'tensor_tensor', ['IG', 'FB'], ['ABr'], out=ABr[:], in0=IG[:], in1=FB[:], op=OP.subtract)
                MF, MB = IG, LF
                self.v('dve', 'tensor_tensor_scan', ['AFr'], ['IG'], out=MF[:], data0=AFr[:], data1=AFr[:], initial=0.0, op0=OP.max, op1=OP.max)
                self.v('dve', 'tensor_tensor_scan', ['ABr'], ['LF'], out=MB[:, 255::-1], data0=ABr[:, 255::-1], data1=ABr[:, 255::-1],
                       initial=0.0, op0=OP.max, op1=OP.max)
                self.v('dve', 'tensor_tensor_scan', ['ABr', 'LF'], ['LF'], out=MB[:, 2303:255:-1], data0=ABr[:, 2303:255:-1],
                       data1=ABr[:, 2303:255:-1], initial=MB[:, 0:1], op0=OP.max, op1=OP.max)
                self.v('dve', 'tensor_tensor', ['FF', 'IG'], ['FF'], out=FF[:], in0=FF[:], in1=MF[:], op=OP.add)
                self.v('dve', 'tensor_tensor', ['FB', 'LF'], ['FB'], out=FB[:], in0=FB[:], in1=MB[:], op=OP.add)
                for qi, (XF, XB, kf, kb) in enumerate(((AFr, ABr, 'AFr', 'ABr'), (MF, MB, 'IG', 'LF'), (FF, FB, 'FF', 'FB'))):
                    for c in range(NT):
                        self.mm(self.pb[qi][:, c * 8:(c + 1) * 8], XF[:, c * 128:(c + 1) * 128], E2[:, 0, :], True, False, [kf, 'E2'], [('pb', qi)])
                        self.mm(self.pb[qi][:, c * 8:(c + 1) * 8], XB[:, c * 128:(c + 1) * 128], E2[:, 1, :], False, True, [kb, 'E2'], [('pb', qi)])
                for qi, (dst, kd) in enumerate(((acol, 'acol'), (Mcol, 'Mcol'), (mcol, 'mcol'))):
                    self.act(dst[:].rearrange("p c e -> p (c e)"), self.pb[qi][:, 0:NT * 8], AF.Identity, [('pb', qi)], [kd])
                self.v('dve', 'tensor_tensor', ['E2', 'IG'], ['R1'], out=R1[:], in0=E2[:, 0, :].unsqueeze(1).to_broadcast([8, NT, 8]),
                       in1=MF[:, 127::128].unsqueeze(2).to_broadcast([8, NT, 8]), op=OP.mult)
                self.v('dve', 'tensor_tensor', ['E2', 'LF'], ['R2'], out=R2[:], in0=E2[:, 1, :].unsqueeze(1).to_broadcast([8, NT, 8]),
                       in1=MB[:, 0::128].unsqueeze(2).to_broadcast([8, NT, 8]), op=OP.mult)
                self.v('dve', 'tensor_tensor', ['R1', 'R2'], ['R1'], out=R1[:], in0=R1[:], in1=R2[:], op=OP.add)
                self.mm(self.pb[3][:, 0:NT * 8], self.onesf[0:8, :], R1[:].rearrange("p c e -> p (c e)"), True, True, ['R1', 'onesf'], [('pb', 3)])
                self.act(MendB[:].rearrange("p c e -> p (c e)"), self.pb[3][:, 0:NT * 8], AF.Identity, [('pb', 3)], ['MendB'])
                self.v('dve', 'memset', [], ['MP'], MP[:], 0.0)
                self.v('dve', 'tensor_copy', ['MendB', 'MP'], ['MP'], out=MP[:, 1:NT, 0:4], in_=MendB[:, 0:NT - 1, 0:4])
                self.v('dve', 'tensor_copy', ['MendB', 'MP'], ['MP'], out=MP[:, 0:1, 4:8], in_=MendB[:, 1:2, 4:8])
                self.v('dve', 'tensor_copy', ['MendB', 'MP'], ['MP'], out=MP[:, 2:NT - 1, 4:8], in_=MendB[:, 3:NT, 4:8])
                self.v('dve', 'tensor_copy', ['MendB', 'MP'], ['MP'], out=MP[:, NT - 1:NT, 4:8], in_=MendB[:, 0:1, 4:8])
                self.v('dve', 'tensor_tensor', ['MP', 'Mcol'], ['ex'], out=ex[:, 0], in0=MP[:], in1=Mcol[:], op=OP.subtract)
                self.v('dve', 'tensor_tensor', ['acol', 'MendB'], ['ex'], out=ex[:, 1], in0=acol[:], in1=MendB[:], op=OP.subtract)
                self.v('dve', 'tensor_tensor', ['MP', 'MendB'], ['ex'], out=ex[:, 2], in0=MP[:], in1=MendB[:], op=OP.subtract)
                self.v('dve', 'tensor_scalar', ['mcol'], ['ex'], out=ex[:, 3], in0=mcol[:], scalar1=-1.0, scalar2=None, op0=OP.mult)
                self.act(T4[:].rearrange("p a c e -> p (a c e)"), ex[:].rearrange("p a c e -> p (a c e)"), AF.Exp, ['ex'], ['T4'])
            for hp in range(2):
                self.mlstm_pair(l, last, hp, acol, Mcol, T4, gm, negid, mlmask)

    def mlstm_pair(self, l, last, hp, acol, Mcol, T4, gm, negid, mlmask):
        nc, P, d = self.nc, self.P, self.d
        win = d['w_in'][l]
        QS = 96.0 ** -0.5
        colof = lambda T: T + 2 if T < 256 else T + 6
        cblocks = [(0, 256)] + [(256 + i * 512, 512) for i in range(4)]
        with Scope(self) as SP:
            post = SP.sb("post", [96, 4, TOK], BF16)
            vp = SP.sb("vp", [128, NT, 2, 97], BF16)
            sg = SP.sb("sg", [128, NT, 192], BF16)
            S = Scope(self)
            wvo = S.sb("wvo", [128, 8, 384], BF16)
            self.dma('pool', wvo[:, :, 0:192], win[:, 1024 + 192 * hp:1024 + 192 * (hp + 1)].rearrange("(k p) c -> p k c", p=128), [], ['wvo'], 'mw')
            self.dma('pool', wvo[:, :, 192:384], win[:, 1408 + 192 * hp:1408 + 192 * (hp + 1)].rearrange("(k p) c -> p k c", p=128), [], ['wvo'], 'mw')
            self.v('dve', 'memset', [], ['vp'], vp[:], 1.0)
            with S:
                wqk = S.sb("wqk", [128, 8, 4, 96], BF16)
                dgw = S.sb("dgw", [96, 4, 5, 96], BF16)
                pre = S.sb("pre", [96, 4, 2312], BF16)
                sgt = [S.sb("sgt%d" % i, [96, 512], F32) for i in range(1)]
                for jj in range(4):
                    c0 = (256 if jj < 2 else 640) + 96 * (2 * hp + jj % 2)
                    self.dma('pool', wqk[:, :, jj, :], win[:, c0:c0 + 96].rearrange("(k p) c -> p k c", p=128), [], ['wqk'], 'mw')
                    jg = (0 if jj < 2 else 4) + 2 * hp + jj % 2
                    self.dma('sp', dgw[:, jj, :, :], d['convdg'][l, :, jg, :, :], [], ['dgw'], 'mw')
                self.v('dve', 'memset', [], ['pre'], pre[:], 0.0)
                nb = 0
                for jj in range(4):
                    for (o, n) in cblocks:
                        b = nb % 6
                        nb += 1
                        hk = [('hT', t) for t in range(o // 128, (o + n) // 128)]
                        for k in range(8):
                            self.mm(self.pb[b][0:96, 0:n], wqk[:, k, jj, :], self.hT[:, k, o:o + n], k == 0, k == 7, hk + ['wqk'], [('pb', b)])
                        self.act(pre[:, jj, colof(o):colof(o) + n], self.pb[b][0:96, 0:n], AF.Identity, [('pb', b), 'pre'], [('pre', jj)])
                for t in range(NT):
                    b = nb % 6
                    nb += 1
                    for k in range(8):
                        self.mm(self.pb[b][:, 0:384], self.hT[:, k, t * 128:(t + 1) * 128], wvo[:, k, :], k == 0, k == 7, [('hT', t), 'wvo'], [('pb', b)])
                    self.act(vp[:, t, :, 0:96], self.pb[b][:, 0:192].rearrange("p (h c) -> p h c", c=96), AF.Identity, [('pb', b), 'vp'], [('vp', t)])
                    self.act(sg[:, t, :], self.pb[b][:, 192:384], AF.Sigmoid, [('pb', b)], [('sg', t)])
                for jj in range(4):
                    for (o, n) in cblocks:
                        b = nb % 6
                        nb += 1
                        c0 = colof(o)
                        for tap in range(5):
                            self.mm(self.pb[b][0:96, 0:n], dgw[:, jj, tap, :], pre[:, jj, c0 + tap - 2:c0 + tap - 2 + n], tap == 0, tap == 4,
                                    [('pre', jj), 'dgw'], [('pb', b)])
                        tk = [('post', jj, t) for t in range(o // 128, (o + n) // 128)]
                        if jj < 2:
                            si = 0
                            self.act(sgt[si][:, 0:n], self.pb[b][0:96, 0:n], AF.Sigmoid, [('pb', b)], [('sgt', si)])
                            self.v('dve', 'scalar_tensor_tensor', [('pb', b), ('sgt', si)], tk, out=post[:, jj, o:o + n], in0=self.pb[b][0:96, 0:n],
                                   scalar=QS, in1=sgt[si][:, 0:n], op0=OP.mult, op1=OP.mult)
                        else:
                            self.act(post[:, jj, o:o + n], self.pb[b][0:96, 0:n], AF.Silu, [('pb', b)], tk)
            ktok = SP.sb("ktok", [128, NT, 2, 96], BF16)
            for t in range(NT):
                pi = t % 2
                for hh in range(2):
                    self.tr(self.ptr[pi][:, hh, 0:96], post[:, 2 + hh, t * 128:(t + 1) * 128], self.identb[0:96, 0:96],
                            [('post', 2 + hh, t), 'identb'], [('ptr', pi)])
                self.act(ktok[:, t, :, :], self.ptr[pi][:, 0:2, 0:96], AF.Identity, [('ptr', pi)], [('ktok', t)])
            with Scope(self) as S:
                hF = S.sb("hF", [128, NT, 2, 96], F32)
                woM = S.sb("woM", [96, 2, DM], BF16)
                dgM = [S.sb("dgM%d" % i, [128, 2, 128], F32) for i in range(2)]
                WT = [S.sb("WT%d" % i, [128, 2, 128], F32) for i in range(2)]
                PT = [S.sb("PT%d" % i, [128, 2, 128], BF16) for i in range(2)]
                isb = [S.sb("isb%d" % i, [128, 2, 97], F32) for i in range(2)]
                tmpn = [S.sb("tmpn%d" % i, [128, 2, 97], F32) for i in range(2)]
                dn = [S.sb("dn%d" % i, [128, 2], F32) for i in range(2)]
                vw = [S.sb("vw%d" % i, [128, 2, 97], BF16) for i in range(2)]
                Cst = S.sb("Cst", [96, 2, 97], F32)
                Cb = S.sb("Cb", [96, 2, 97], BF16)
                hs = [S.sb("hs%d" % i, [128, 2, 96], F32) for i in range(2)]
                hq = [S.sb("hq%d" % i, [128, 2, 96], F32) for i in range(2)]
                ss = [S.sb("ss%d" % i, [128, 2], F32) for i in range(2)]
                yb = [S.sb("yb%d" % i, [128, 192], BF16) for i in range(2)]
                yhT = [S.sb("yhT%d" % i, [96, 2, 128], BF16) for i in range(2)]
                self.dma('pool', woM[:], d['w_out'][l][256 + 192 * hp:256 + 192 * (hp + 1), :].rearrange("(h p) c -> p h c", p=96), [], ['woM'], 'mw')
                for dr in range(2):
                    order = list(range(NT)) if dr == 0 else [1, 0] + list(range(NT - 1, 1, -1))
                    cf0 = dr * 4 + 2 * hp
                    self.v('dve', 'memset', ['Cst'], ['Cst'], Cst[:], 0.0)
                    self.v('dve', 'memset', ['Cb'], ['Cb'], Cb[:], 0.0)

                    def part_i(it):
                        c = order[it]
                        i2 = it % 2
                        if last and c < 2:
                            return
                        tsl = slice(c * 128, (c + 1) * 128)
                        bo = 3 + i2
                        self.v('dve', 'tensor_tensor', ['negid', 'Mcol'], [('dgM', i2)], out=dgM[i2][:],
                               in0=negid[:].unsqueeze(1).to_broadcast([128, 2, 128]),
                               in1=Mcol[:, c, cf0:cf0 + 2].unsqueeze(2).to_broadcast([128, 2, 128]), op=OP.mult)
                        self.mm(self.pb[1][:, 0:256], self.onesf[:], dgM[i2][:].rearrange("p h t -> p (h t)"), True, False,
                                [('dgM', i2), 'onesf'], [('pb', 1)])
                        self.mm(self.pb[1][:, 0:256], self.identb[:], mlmask[:, dr, :], False, True, ['mlmask', 'identb'], [('pb', 1)])
                        for hh in range(2):
                            self.act(WT[i2][:, hh, :], self.pb[1][:, hh * 128:(hh + 1) * 128], AF.Exp, [('pb', 1), 'acol'], [('WT', i2)],
                                     bias=acol[:, c, cf0 + hh:cf0 + hh + 1])
                        for hh in range(2):
                            self.mm(self.pb[2][:, hh * 128:(hh + 1) * 128], post[:, 2 + hh, tsl], post[:, hh, tsl], True, True,
                                    [('post', 2 + hh, c), ('post', hh, c)], [('pb', 2)])
                        self.v('dve', 'tensor_tensor', [('pb', 2), ('WT', i2)], [('PT', i2)], out=PT[i2][:].rearrange("p h t -> p (h t)"),
                               in0=self.pb[2][:, 0:256], in1=WT[i2][:].rearrange("p h t -> p (h t)"), op=OP.mult)
                        for hh in range(2):
                            self.mm(self.pb[bo][:, hh * 97:(hh + 1) * 97], PT[i2][:, hh, :], vp[:, c, hh, :], True, True,
                                    [('PT', i2), ('vp', c)], [('pb', bo)])

                    def part_d(it):
                        c = order[it]
                        i2 = it % 2
                        need_out = not (last and c < 2)
                        tsl = slice(c * 128, (c + 1) * 128)
                        bo = 3 + i2
                        if need_out:
                            for hh in range(2):
                                self.mm(self.pb[5][:, hh * 97:(hh + 1) * 97], post[:, hh, tsl], Cb[:, hh, :], True, True,
                                        [('post', hh, c), 'Cb'], [('pb', 5)])
                            self.v('dve', 'tensor_tensor', [('pb', 5), 'T4'], [('tmpn', i2)], out=tmpn[i2][:],
                                   in0=self.pb[5][:, 0:194].rearrange("p (h e) -> p h e", e=97),
                                   in1=T4[:, 0, c, cf0:cf0 + 2].unsqueeze(2).to_broadcast([128, 2, 97]), op=OP.mult)
                            self.v('dve', 'tensor_tensor', [('tmpn', i2), ('pb', bo)], [('tmpn', i2)], out=tmpn[i2][:], in0=tmpn[i2][:],
                                   in1=self.pb[bo][:, 0:194].rearrange("p (h e) -> p h e", e=97), op=OP.add)
                            self.v('dve', 'scalar_tensor_tensor', [('tmpn', i2)], [('dn', i2)], out=dn[i2][:], in0=tmpn[i2][:, :, 96], scalar=-1.0,
                                   in1=tmpn[i2][:, :, 96], op0=OP.mult, op1=OP.max)
                            self.v('dve', 'tensor_tensor', [('dn', i2), 'T4'], [('dn', i2)], out=dn[i2][:], in0=dn[i2][:],
                                   in1=T4[:, 3, c, cf0:cf0 + 2], op=OP.max)
                            self.v('dve', 'reciprocal', [('dn', i2)], [('dn', i2)], out=dn[i2][:], in_=dn[i2][:])
                            hdst = hF[:, c] if dr == 0 else hs[i2][:]
                            hkey = ('hF', c) if dr == 0 else ('hs', i2)
                            self.v('dve', 'tensor_tensor', [('tmpn', i2), ('dn', i2)], [hkey], out=hdst, in0=tmpn[i2][:, :, 0:96],
                                   in1=dn[i2][:].unsqueeze(2).to_broadcast([128, 2, 96]), op=OP.mult)
                        self.v('dve', 'tensor_tensor', [('vp', c), 'T4'], [('vw', i2)], out=vw[i2][:], in0=vp[:, c],
                               in1=T4[:, 1, c, cf0:cf0 + 2].unsqueeze(2).to_broadcast([128, 2, 97]), op=OP.mult)
                        for hh in range(2):
                            self.mm(self.pb[5][0:96, 256 + hh * 97:256 + (hh + 1) * 97], ktok[:, c, hh, :], vw[i2][:, hh, :], True, True,
                                    [('ktok', c), ('vw', i2)], [('pb', 5)])
                        self.v('dve', 'tensor_tensor', ['Cst', 'T4'], ['Cst'], out=Cst[:], in0=Cst[:],
                               in1=T4[0:96, 2, c, cf0:cf0 + 2].unsqueeze(2).to_broadcast([96, 2, 97]), op=OP.mult)
                        self.v('dve', 'tensor_tensor', ['Cst', ('pb', 5)], ['Cst'], out=Cst[:], in0=Cst[:],
                               in1=self.pb[5][0:96, 256:450].rearrange("p (h e) -> p h e", e=97), op=OP.add)
                        self.v('dve', 'tensor_copy', ['Cst'], ['Cb'], out=Cb[:], in_=Cst[:])
                        if dr == 1 and need_out:
                            self.v('dve', 'tensor_tensor', [('hs', i2), ('hF', c)], [('hs', i2)], out=hs[i2][:], in0=hs[i2][:], in1=hF[:, c], op=OP.add)
                            self.v('dve', 'tensor_tensor', [('hs', i2)], [('hq', i2)], out=hq[i2][:], in0=hs[i2][:], in1=hs[i2][:], op=OP.mult)
                            self.v('dve', 'tensor_reduce', [('hq', i2)], [('ss', i2)], out=ss[i2][:], in_=hq[i2][:], axis=AX.X, op=OP.add)
                            self.act(ss[i2][:], ss[i2][:], AF.Ln, [('ss', i2), 'epsT'], [('ss', i2)], bias=self.epsT[:], scale=1.0 / 96)
                            self.act(ss[i2][:], ss[i2][:], AF.Exp, [('ss', i2)], [('ss', i2)], scale=-0.5)
                            self.v('dve', 'tensor_tensor', [('hs', i2), ('ss', i2)], [('hq', i2)], out=hq[i2][:], in0=hs[i2][:],
                                   in1=ss[i2][:].unsqueeze(2).to_broadcast([128, 2, 96]), op=OP.mult)
                            self.v('dve', 'tensor_tensor', [('hq', i2), 'gm'], [('hq', i2)], out=hq[i2][:].rearrange("p h e -> p (h e)"),
                                   in0=hq[i2][:].rearrange("p h e -> p (h e)"), in1=gm[:, 192 * hp:192 * (hp + 1)], op=OP.mult)
                            self.v('dve', 'tensor_tensor', [('hq', i2), ('sg', c)], [('yb', i2)], out=yb[i2][:],
                                   in0=hq[i2][:].rearrange("p h e -> p (h e)"), in1=sg[:, c, :], op=OP.mult)
                            for hh in range(2):
                                self.tr(self.ptr[i2][0:96, hh, :], yb[i2][:, hh * 96:(hh + 1) * 96], self.identb[:], [('yb', i2), 'identb'], [('ptr', i2)])
                            self.v('dve', 'tensor_copy', [('ptr', i2)], [('yhT', i2)], out=yhT[i2][:], in_=self.ptr[i2][0:96, 0:2, :])
                            for nh in range(2):
                                for hh in range(2):
                                    self.mm(self.pb[0][:], yhT[i2][:, hh, :], woM[:, hh, nh * 512:(nh + 1) * 512], hh == 0, hh == 1,
                                            [('yhT', i2), 'woM'], [('pb', 0)])
                                self.xupd(c, nh, 0)

                    part_i(0)
                    for it in range(NT):
                        if it + 1 < NT:
                            part_i(it + 1)
                        part_d(it)

    def mla(self, l, last):
        nc, P, d = self.nc, self.P, self.d
        SCALE = 96.0 ** -0.5
        with Scope(self) as S:
            wA = S.sb("wA", [128, 8, 416], BF16)
            wuq = S.sb("wuq", [128, 2, 384], BF16)
            wuqs = S.sb("wuqs", [128, 2, 4, 96], BF16)
            wkr = S.sb("wkr", [128, 8, 96], BF16)
            wkrs = S.sb("wkrs", [128, 8, 96], BF16)
            wukv = S.sb("wukv", [128, 640], BF16)
            woA = S.sb("woA", [96, 4, DM], BF16)
            ropeC = S.sb("ropeC", [96, TOK], BF16)
            ropeS = S.sb("ropeS", [96, TOK], BF16)
            Vt = S.sb("Vt", [128, NT, 384], BF16)
            cqnT = S.sb("cqnT", [128, 2, 512], BF16)
            ckvnT = S.sb("ckvnT", [128, 512], BF16)
            cqn = [S.sb("cqn%d" % i, [128, 256], BF16) for i in range(2)]
            ckvn = [S.sb("ckvn%d" % i, [128, 128], BF16) for i in range(2)]
            rt1 = S.sb("rt1", [96, 512], F32)
            rt2 = S.sb("rt2", [96, 512], F32)
            pT = [S.sb("pT%d" % i, [128, 512], BF16) for i in range(3)]
            rc = S.sb("rc", [96, 512], F32)
            yaT = S.sb("yaT", [96, 4, 512], BF16)
            onesb = S.sb("onesb", [128, 96], BF16)
            st = [S.sb("st%d" % i, [128, 2], F32) for i in range(2)]
            rs = [S.sb("rs%d" % i, [128, 2], F32) for i in range(2)]
            gq = S.sb("gq", [128, 2, 2], F32)
            gkv = S.sb("gkv", [128, 2], F32)
            qT = self.hT[0:96, 0:4, :]
            kT = self.hT[0:96, 4:8, :]
            win = d['w_in'][l]
            self.dma('pool', wA[:], win[:, 1808:2224].rearrange("(k p) c -> p k c", p=128), [], ['wA'], 'aw')
            self.dma('pool', wuq[:], d['w_uq'][l].rearrange("(k p) c -> p k c", p=128), [], ['wuq'], 'aw')
            self.v('dve', 'memset', [], ['wuqs'], wuqs[:], 0.0)
            self.v('dve', 'memset', [], ['wkr'], wkr[:], 0.0)
            self.v('dve', 'memset', [], ['wkrs'], wkrs[:], 0.0)
            self.v('dve', 'memset', [], ['onesb'], onesb[:], 1.0)
            for c in range(2):
                self.dma('pool', wuqs[:, c, :, 64:96], d['w_uq_sw'][l][c * 128:(c + 1) * 128, :, :], ['wuqs'], ['wuqs'], 'aw')
            self.dma('pool', wkr[:, :, 64:96], win[:, 2192:2224].rearrange("(k p) c -> p k c", p=128), ['wkr'], ['wkr'], 'aw')
            self.dma('pool', wkrs[:, :, 64:96], d['w_kr_sw'][l].rearrange("(k p) c -> p k c", p=128), ['wkrs'], ['wkrs'], 'aw')
            self.dma('pool', wukv[:], d['w_ukv'][l], [], ['wukv'], 'aw')
            self.dma('pool', woA[:], d['w_out'][l][640:1024, :].rearrange("(h p) c -> p h c", p=96), [], ['woA'], 'aw')
            self.dma('sp', ropeC[64:96, :], d['ropeC'][:, :], [], ['rope'], 'aw')
            self.dma('sp', ropeS[64:96, :], d['ropeS'][:, :], [], ['rope'], 'aw')
            self.dma('sp', gq[:], d['gqfm'][:, :, :], [], ['gq'], 'aw')
            self.dma('sp', gkv[:], d['gkvfm'][:, :], [], ['gq'], 'aw')
            wv = wukv[:, :].rearrange("p (h c) -> p h c", c=160)[:, :, 64:160]
            for (o, n) in self.blocks():
                tl = list(range(o // 128, (o + n) // 128))
                hk = [('hT', t) for t in tl]
                for ti, t in enumerate(tl):
                    b1 = t % 2
                    for k in range(8):
                        self.mm(self.pb[b1][:, 0:416], self.hT[:, k, t * 128:(t + 1) * 128], wA[:, k, :], k == 0, k == 7,
                                [('hT', t), 'wA'], [('pb', b1)])
                    si = t % 2
                    self.v('dve', 'memset', [], [('st', si)], st[si][:], 0.0)
                    self.cnt += 1
                    self.act(self.junk[:, 0:256], self.pb[b1][:, 0:256], AF.Square, [('pb', b1)], [('junk', self.cnt), ('st', si)],
                             accum=st[si][:, 0:1])
                    self.cnt += 1
                    self.act(self.junk[:, 256:384], self.pb[b1][:, 256:384], AF.Square, [('pb', b1)], [('junk', self.cnt), ('st', si)],
                             accum=st[si][:, 1:2])
                    self.act(rs[si][:, 0:1], st[si][:, 0:1], AF.Sqrt, [('st', si), 'epsT'], [('rs', si)], bias=self.epsT[:], scale=1.0 / 256)
                    self.act(rs[si][:, 1:2], st[si][:, 1:2], AF.Sqrt, [('st', si), 'epsT'], [('rs', si)], bias=self.epsT[:], scale=1.0 / 128)
                    self.v('dve', 'reciprocal', [('rs', si)], [('rs', si)], out=rs[si][:], in_=rs[si][:])
                    self.v('dve', 'tensor_scalar', [('pb', b1), ('rs', si)], [('cqn', si)], out=cqn[si][:], in0=self.pb[b1][:, 0:256],
                           scalar1=rs[si][:, 0:1], scalar2=None, op0=OP.mult)
                    self.v('dve', 'tensor_scalar', [('pb', b1), ('rs', si)], [('ckvn', si)], out=ckvn[si][:], in0=self.pb[b1][:, 256:384],
                           scalar1=rs[si][:, 1:2], scalar2=None, op0=OP.mult)
                    pi = t % 2
                    for c in range(2):
                        self.tr(self.ptr[pi][:, c, :], cqn[si][:, c * 128:(c + 1) * 128], self.identb[:], [('cqn', si), 'identb'], [('ptr', pi)])
                    self.tr(self.ptr[pi][:, 2, :], ckvn[si][:], self.identb[:], [('ckvn', si), 'identb'], [('ptr', pi)])
                    for c in range(2):
                        self.act(cqnT[:, c, ti * 128:(ti + 1) * 128], self.ptr[pi][:, c, :], AF.Identity, [('ptr', pi), 'gq'], ['cqnT'],
                                 scale=gq[:, l, c:c + 1])
                    self.act(ckvnT[:, ti * 128:(ti + 1) * 128], self.ptr[pi][:, 2, :], AF.Identity, [('ptr', pi), 'gq'], ['ckvnT'],
                             scale=gkv[:, l:l + 1])
                    bv = 2 + t % 2
                    self.mm(self.pb[bv][:, 0:384], ckvnT[:, ti * 128:(ti + 1) * 128], wv, True, True, ['ckvnT', 'wukv'], [('pb', bv)])
                    self.act(Vt[:, t, :], self.pb[bv][:, 0:384], AF.Identity, [('pb', bv)], [('Vt', t)])
                for k in range(8):
                    self.mm(self.pb[4][0:96, 0:n], wkr[:, k, :], self.hT[:, k, o:o + n], k == 0, k == 7, hk + ['wkr'], [('pb', 4)])
                for k in range(8):
                    self.mm(self.pb[5][0:96, 0:n], wkrs[:, k, :], self.hT[:, k, o:o + n], k == 0, k == 7, hk + ['wkrs'], [('pb', 5)])
                for h in range(4):
                    for c in range(2):
                        self.mm(self.pb[0][0:96, 0:n], wuq[:, c, h * 96:(h + 1) * 96], cqnT[:, c, 0:n], c == 0, c == 1, ['wuq', 'cqnT'], [('pb', 0)])
                    for c in range(2):
                        self.mm(self.pb[1][0:96, 0:n], wuqs[:, c, h, :], cqnT[:, c, 0:n], c == 0, c == 1, ['wuqs', 'cqnT'], [('pb', 1)])
                    bk = 2 + h % 2
                    self.mm(self.pb[bk][0:64, 0:n], wukv[:, h * 160:h * 160 + 64], ckvnT[:, 0:n], True, True, ['wukv', 'ckvnT'], [('pb', bk)])
                    self.act(qT[0:64, h, o:o + n], self.pb[0][0:64, 0:n], AF.Identity, [('pb', 0)], hk)
                    self.act(kT[0:64, h, o:o + n], self.pb[bk][0:64, 0:n], AF.Identity, [('pb', bk)], hk)
                    self.v('dve', 'tensor_tensor', [('pb', 0), 'rope'], ['rt1'], out=rt1[64:96, 0:n], in0=self.pb[0][64:96, 0:n],
                           in1=ropeC[64:96, o:o + n], op=OP.mult)
                    self.v('dve', 'tensor_tensor', [('pb', 1), 'rope'], ['rt2'], out=rt2[64:96, 0:n], in0=self.pb[1][64:96, 0:n],
                           in1=ropeS[64:96, o:o + n], op=OP.mult)
                    self.v('pool', 'tensor_tensor', ['rt1', 'rt2'], hk, out=qT[64:96, h, o:o + n], in0=rt1[64:96, 0:n],
                           in1=rt2[64:96, 0:n], op=OP.add)
                self.v('dve', 'tensor_tensor', [('pb', 4), 'rope'], ['rt1'], out=rt1[64:96, 0:n], in0=self.pb[4][64:96, 0:n],
                       in1=ropeC[64:96, o:o + n], op=OP.mult)
                self.v('dve', 'tensor_tensor', [('pb', 5), 'rope'], ['rt2'], out=rt2[64:96, 0:n], in0=self.pb[5][64:96, 0:n],
                       in1=ropeS[64:96, o:o + n], op=OP.mult)
                for h in range(4):
                    self.v('pool', 'tensor_tensor', ['rt1', 'rt2'], hk, out=kT[64:96, h, o:o + n], in0=rt1[64:96, 0:n],
                           in1=rt2[64:96, 0:n], op=OP.add)
            qblocks = [(256 + i * 512, 512, list(range(NT))) for i in range(4)]
            if not last:
                qblocks.append((0, 256, [0, 1]))
            for (qo, qn, ktiles) in qblocks:
                qk = [('hT', t) for t in range(qo // 128, (qo + qn) // 128)]
                for h in range(4):
                    def st_mm(i):
                        kt_ = ktiles[i]
                        self.mm(self.pb[i % 2][:, 0:qn], kT[:, h, kt_ * 128:(kt_ + 1) * 128], qT[:, h, qo:qo + qn], True, True,
                                qk + [('hT', kt_)], [('pb', i % 2)])
                    st_mm(0)
                    for i, kt in enumerate(ktiles):
                        sb_ = i % 2
                        pi = i % 3
                        if i + 1 < len(ktiles):
                            st_mm(i + 1)
                        self.act(pT[pi][:, 0:qn], self.pb[sb_][:, 0:qn], AF.Exp, [('pb', sb_)], [('pT', pi)], scale=SCALE)
                        self.mm(self.pb[2][0:96, 0:qn], Vt[:, kt, h * 96:(h + 1) * 96], pT[pi][:, 0:qn], i == 0, i == len(ktiles) - 1,
                                [('Vt', kt), ('pT', pi)], [('pb', 2)])
                        self.mm(self.pb[3][0:96, 0:qn], onesb[:, :], pT[pi][:, 0:qn], i == 0, i == len(ktiles) - 1,
                                ['onesb', ('pT', pi)], [('pb', 3)])
                    self.v('dve', 'reciprocal', [('pb', 3)], ['rc'], out=rc[:, 0:qn], in_=self.pb[3][0:96, 0:qn])
                    self.v('dve', 'tensor_tensor', [('pb', 2), 'rc'], [('yaT', h)], out=yaT[:, h, 0:qn], in0=self.pb[2][0:96, 0:qn],
                           in1=rc[:, 0:qn], op=OP.mult)
                for ti, t in enumerate(range(qo // 128, (qo + qn) // 128)):
                    for nh in range(2):
                        b = 4 + nh
                        for h in range(4):
                            self.mm(self.pb[b][:], yaT[:, h, ti * 128:(ti + 1) * 128], woA[:, h, nh * 512:(nh + 1) * 512], h == 0, h == 3,
                                    [('yaT', h), 'woA'], [('pb', b)])
                        self.xupd(t, nh, b)

    def fourier(self, l, last):
        nc, P, d = self.nc, self.P, self.d
        with Scope(self) as S:
            wpf = S.sb("wpf", [128, 8, 256], BF16)
            pfT = S.sb("pfT", [128, 2, TOK], BF16)
            bd = S.sb("bd", [128, 256], BF16)
            pfCS = S.sb("pfCS", [128, NT, 2, 256], BF16)
            tw = [S.sb("tw%d" % i, [128, 2, 1024], BF16) for i in range(3)]
            yfT = S.sb("yfT", [128, 2, TOK], BF16)
            wo = S.sb("wo", [128, 2, DM], BF16)
            c256 = S.sb("c256", [128, 2, 2, 256], BF16)
            self.dma('pool', wpf[:], d['w_in'][l][:, 0:256].rearrange("(k p) c -> p k c", p=128), [], ['wpf'], 'fw')
            self.dma('pool', wo[:], d['w_out'][l][0:256, :].rearrange("(k p) c -> p k c", p=128), [], ['wo'], 'fw')
            self.dma('sp', bd[:], d['bdcs'][:, :], [], ['bd'], 'fw')
            self.dma('sp', c256[:], d['c256'][:, :, :, :], [], ['c256'], 'fw')
            nb = 0
            for j in range(2):
                for (o, n) in self.blocks():
                    b = nb % 6
                    nb += 1
                    for k in range(8):
                        self.mm(self.pb[b][:, 0:n], wpf[:, k, j * 128:(j + 1) * 128], self.hT[:, k, o:o + n], k == 0, k == 7,
                                ['wpf'] + [('hT', t) for t in range(o // 128, (o + n) // 128)], [('pb', b)])
                    self.act(pfT[:, j, o:o + n], self.pb[b][:, 0:n], AF.Identity, [('pb', b)], [('pfT', j, o)])
            for t in range(NT):
                for j in range(2):
                    b = nb % 6
                    nb += 1
                    self.mm(self.pb[b][:, 0:256], pfT[:, j, t * 128:(t + 1) * 128], bd[:], True, True,
                            [('pfT', j, (t // 4) * 512), 'bd'], [('pb', b)])
                    self.act(pfCS[:, t, j, :], self.pb[b][:, 0:256], AF.Identity, [('pb', b)], [('pfCS', t)])
            ntw = 0
            for half in range(2):
                for nt in range(16):
                    t = nt + 2
                    s = ntw % 3
                    ntw += 1
                    self.dma('sp', tw[s][:, 0, :], d['twC'][nt * 128:(nt + 1) * 128, half * 1024:(half + 1) * 1024], [], [('tw', s)], 'tw%d' % s)
                    self.dma('sp', tw[s][:, 1, :], d['twS'][nt * 128:(nt + 1) * 128, half * 1024:(half + 1) * 1024], [], [('tw', s)], 'tw%d' % s)
                    for j in range(2):
                        for q in range(2):
                            b = j * 2 + q
                            self.mm(self.pb[b][:], pfCS[:, t, j, 0:128], tw[s][:, 0, q * 512:(q + 1) * 512], nt == 0, False,
                                    [('pfCS', t), ('tw', s)], [('pb', b)])
                            self.mm(self.pb[b][:], pfCS[:, t, j, 128:256], tw[s][:, 1, q * 512:(q + 1) * 512], False, nt == 15,
                                    [('pfCS', t), ('tw', s)], [('pb', b)])
                for j in range(2):
                    for q in range(2):
                        b = j * 2 + q
                        o = 256 + half * 1024 + q * 512
                        self.act(yfT[:, j, o:o + 512], self.pb[b][:], AF.Identity, [('pb', b)], [('yfT', o // 128)])
            if not last:
                for j in range(2):
                    b = 4 + j
                    for nt in range(2):
                        self.mm(self.pb[b][:, 0:256], pfCS[:, nt, j, 0:128], c256[:, nt, 0, :], nt == 0, False,
                                [('pfCS', nt), 'c256'], [('pb', b)])
                        self.mm(self.pb[b][:, 0:256], pfCS[:, nt, j, 128:256], c256[:, nt, 1, :], False, nt == 1,
                                [('pfCS', nt), 'c256'], [('pb', b)])
                    self.act(yfT[:, j, 0:256], self.pb[b][:, 0:256], AF.Identity, [('pb', b)], [('yfT', 0)])
            nb = 0
            for t in (range(2, NT) if last else range(NT)):
                for nh in range(2):
                    b = nb % 6
                    nb += 1
                    for j in range(2):
                        self.mm(self.pb[b][:], yfT[:, j, t * 128:(t + 1) * 128], wo[:, j, nh * 512:(nh + 1) * 512], j == 0, j == 1,
                                [('yfT', 0 if t < 2 else 2 + ((t - 2) // 4) * 4), 'wo'], [('pb', b)])
                    self.xupd(t, nh, b)

    def mlp(self, l, tiles):
        nc, P, d = self.nc, self.P, self.d
        groups = [tiles[i:i + 6] for i in range(0, len(tiles), 6)]
        with Scope(self) as S:
            uT = S.sb("uT", [128, 32, 768], BF16)
            wu = [S.sb("wu%d" % i, [128, 8, 512], BF16) for i in range(2)]
            wd = [S.sb("wd%d" % i, [128, 4, 512], BF16) for i in range(2)]
            tmp = [S.sb("mt%d" % i, [128, 512], F32) for i in range(2)]
            nu = nd = nb = 0
            for g in groups:
                G = 128 * len(g)
                t0 = g[0] * 128
                subs = [(0, G)] if G <= 512 else [(0, G // 2), (G // 2, G // 2)]
                for hb in range(8):
                    s = nu % 2
                    nu += 1
                    self.dma('pool', wu[s][:], d['w_up'][l][:, hb * 512:(hb + 1) * 512].rearrange("(k p) c -> p k c", p=128),
                             [], [('wu', s)], 'wu%d' % s)
                    for cc in range(4):
                        j = hb * 4 + cc
                        for (o, n) in subs:
                            b = nb % 6
                            nb += 1
                            for k in range(8):
                                self.mm(self.pb[b][:, 0:n], wu[s][:, k, cc * 128:(cc + 1) * 128],
                                        self.hT[:, k, t0 + o:t0 + o + n], k == 0, k == 7,
                                        [('wu', s)] + [('hT', t) for t in g], [('pb', b)])
                            ri = nb % 2
                            self.act(tmp[ri][:, 0:n], self.pb[b][:, 0:n], AF.Relu, [('pb', b)], [('mt', ri)])
                            self.v('dve', 'tensor_tensor', [('mt', ri)], [('uT', j)], out=uT[:, j, o:o + n],
                                   in0=tmp[ri][:, 0:n], in1=tmp[ri][:, 0:n], op=OP.mult)
                for nh in range(2):
                    for jb in range(8):
                        s = nd % 2
                        nd += 1
                        self.dma('pool', wd[s][:],
                                 d['w_down'][l][jb * 512:(jb + 1) * 512, nh * 512:(nh + 1) * 512].rearrange("(j p) c -> p j c", p=128),
                                 [], [('wd', s)], 'wd%d' % s)
                        for jj in range(4):
                            j = jb * 4 + jj
                            for ti, t in enumerate(g):
                                self.mm(self.pb[ti][:], uT[:, j, ti * 128:(ti + 1) * 128], wd[s][:, jj, :], j == 0, j == 31,
                                        [('wd', s), ('uT', j)], [('pb', ti)])
                    for ti, t in enumerate(g):
                        self.xupd(t, nh, ti)

    def final(self):
        nc, P, d = self.nc, self.P, self.d
        with Scope(self) as S:
            gf = S.sb("gf", [128, DM], F32)
            ob = [S.sb("ob%d" % i, [128, DM], F32) for i in range(2)]
            self.dma('sp', gf[:], d['gfin'][:, :], [], ['gf'], 'cst')
            for t in range(2, NT):
                self.rms(t)
                i = t % 2
                self.act(ob[i][:], self.xs[:, t, :], AF.Identity, [('xs', t), ('rstd', t)], [('ob', i)], scale=self.rstd[:, t:t + 1])
                self.v('dve', 'tensor_tensor', [('ob', i), 'gf'], [('ob', i)], out=ob[i][:], in0=ob[i][:], in1=gf[:], op=OP.mult)
                self.dma('sp', d['out'][(t - 2) * 128:(t - 1) * 128, :], ob[i][:], [('ob', i)], [('out', t)], 'out')


def dram_decls(nc):
    d = {}

    def inp(name, shape, dt=F32):
        d[name] = nc.dram_tensor(name, list(shape), dt, kind="ExternalInput").ap()

    inp('xin', [TOK, DM])
    inp('c2', [128, 16])
    inp('w_mod', [2, 1024, 6144])
    inp('bmodfm', [128, 2, 48])
    inp('gnfm', [128, 2, 2, 8])
    inp('w_in', [2, 1024, 2224])
    inp('w_out', [2, 1024, 1024])
    inp('w_up', [2, 1024, 4096])
    inp('w_down', [2, 4096, 1024])
    inp('gfin', [128, DM])
    inp('identb', [128, 128], BF16)
    inp('identf', [128, 128])
    inp('w_uq', [2, 256, 384])
    inp('w_ukv', [2, 128, 640])
    inp('w_uq_sw', [2, 256, 4, 32])
    inp('w_kr_sw', [2, 1024, 32])
    inp('ropeC', [32, TOK], BF16)
    inp('ropeS', [32, TOK], BF16)
    inp('gqfm', [128, 2, 2])
    inp('gkvfm', [128, 2])
    inp('w_g', [2, 1024, 16])
    inp('bg', [2, 8, 2])
    inp('convdg', [2, 96, 8, 5, 96], BF16)
    inp('gmB', [128, 2, 384])
    inp('E2', [8, 2, 8])
    inp('negidf', [128, 128])
    inp('mlmask', [128, 2, 256], BF16)
    inp('bdcs', [128, 256], BF16)
    inp('twC', [2048, 2048], BF16)
    inp('twS', [2048, 2048], BF16)
    inp('c256', [128, 2, 2, 256], BF16)
    d['out'] = nc.dram_tensor('out', [2048, DM], F32, kind="ExternalOutput").ap()
    if DBGXS:
        d['dbgxs'] = nc.dram_tensor('dbgxs', [TOK, DM], F32, kind="ExternalOutput").ap()
    return d


def build_program():
    nc = bass.Bass("TRN2", target_bir_lowering=False)
    d = dram_decls(nc)
    P = Prog(nc)
    kb = K(nc, P, d)
    kb.build()
    P.plan()
    es = contextlib.ExitStack()
    with es:
        P.start_real(es)
        kb.build()
    return nc, P


def host_inputs(inputs, b):
    f = lambda a: np.ascontiguousarray(np.asarray(a, dtype=np.float32))
    x, c, ctx, c_ctx = inputs['x'], inputs['c'], inputs['ctx'], inputs['c_ctx']
    m = {}
    m['xin'] = f(np.concatenate([ctx[b], x[b]], axis=0))
    c2 = np.stack([np.asarray(c[b]).reshape(8, 128).T, np.asarray(c_ctx).reshape(8, 128).T], axis=-1)
    m['c2'] = f(c2.reshape(128, 16))
    m['w_mod'] = f(inputs['w_mod'])
    m['bmodfm'] = f(np.asarray(inputs['b_mod']).reshape(2, 48, 128).transpose(2, 0, 1))
    gn = np.stack([np.asarray(inputs['g_norm1']), np.asarray(inputs['g_norm2'])], axis=1)
    m['gnfm'] = f(gn.reshape(2, 2, 8, 128).transpose(3, 0, 1, 2))
    m['w_in'] = f(inputs['w_in'])
    m['w_out'] = f(inputs['w_out'])
    m['w_up'] = f(inputs['w_up'])
    m['w_down'] = f(inputs['w_down'])
    m['gfin'] = f(np.tile(np.asarray(inputs['g_final'])[None, :], (128, 1)))
    m['identb'] = np.eye(128, dtype=np.float32).astype(ml_dtypes.bfloat16)
    m['identf'] = np.eye(128, dtype=np.float32)
    m.update(host_consts())
    perm = np.array([a * 16 + (1 - b_) * 8 + j for a in range(2) for b_ in range(2) for j in range(8)])
    wuq = np.asarray(inputs['w_uq'])
    m['w_uq'] = f(wuq)
    m['w_ukv'] = f(inputs['w_ukv'])
    m['w_uq_sw'] = f(wuq.reshape(2, 256, 4, 96)[:, :, :, 64:96][..., perm])
    m['w_kr_sw'] = f(np.asarray(inputs['w_in'])[:, :, 2192:2224][..., perm])
    m['gqfm'] = f(np.asarray(inputs['g_q_norm']).reshape(2, 2, 128).transpose(2, 0, 1))
    m['gkvfm'] = f(np.asarray(inputs['g_kv_norm']).T)
    gp = [0, 1, 2, 3, 8, 9, 10, 11, 4, 5, 6, 7, 12, 13, 14, 15]
    m['w_g'] = f(np.asarray(inputs['w_in'])[:, :, 1792:1808][..., gp])
    m['bg'] = f(np.asarray(inputs['b_gates'])[:, gp].reshape(2, 2, 8).transpose(0, 2, 1))
    cv = np.asarray(inputs['conv_qk'], dtype=np.float32).reshape(2, 5, 8, 96)
    dgc = np.zeros((2, 96, 8, 5, 96), np.float32)
    pi_ = np.arange(96)
    dgc[:, pi_, :, :, pi_] = cv.transpose(3, 0, 2, 1)
    m['convdg'] = dgc.astype(ml_dtypes.bfloat16)
    m['gmB'] = f(np.tile(np.asarray(inputs['g_mlstm'])[None, :, :], (128, 1, 1)))
    return m


def host_consts():
    bf = lambda a: np.ascontiguousarray(a.astype(np.float32)).astype(ml_dtypes.bfloat16)
    m = {}
    i64 = np.arange(64)
    a64 = 2 * np.pi * np.outer(i64, i64) / 64
    C64, S64 = np.cos(a64) / 8.0, np.sin(a64) / 8.0
    z = np.zeros((64, 64))
    m['bdcs'] = bf(np.block([[C64, z, S64, z], [z, C64, z, S64]]))
    n = np.arange(2048, dtype=np.float64)
    aN = 2 * np.pi * (np.outer(n, n) % 2048) / 2048
    m['twC'] = bf(np.cos(aN) / np.sqrt(2048.0))
    m['twS'] = bf(-np.sin(aN) / np.sqrt(2048.0))
    E2 = np.zeros((8, 2, 8), np.float32)
    for k_ in range(4):
        E2[k_, 0, k_] = 1.0
        E2[4 + k_, 1, 4 + k_] = 1.0
    m['E2'] = E2
    m['negidf'] = -np.eye(128, dtype=np.float32)
    sidx, tidx = np.arange(128)[:, None], np.arange(128)[None, :]
    mf = np.where(sidx <= tidx, 0.0, -30000.0)
    mb_ = np.where(sidx >= tidx, 0.0, -30000.0)
    m['mlmask'] = bf(np.stack([np.concatenate([mf, mf], axis=1), np.concatenate([mb_, mb_], axis=1)], axis=1))
    tt = np.arange(2048)
    freqs = 10000.0 ** (-np.arange(8, dtype=np.float64) / 8)
    ang = np.stack([np.outer(tt // 64, freqs), np.outer(tt % 64, freqs)], axis=1)
    cosT = np.ones((TOK, 2, 2, 8)); sinT = np.zeros((TOK, 2, 2, 8))
    cosT[256:] = np.cos(ang)[:, :, None, :]
    sinT[256:, :, 0, :] = -np.sin(ang)
    sinT[256:, :, 1, :] = np.sin(ang)
    m['ropeC'] = bf(cosT.reshape(TOK, 32).T)
    m['ropeS'] = bf(sinT.reshape(TOK, 32).T)
    n = np.arange(256, dtype=np.float64)
    a2 = 2 * np.pi * (np.outer(n, n) % 256) / 256
    c = np.stack([np.cos(a2) / 16.0, -np.sin(a2) / 16.0], axis=1)
    m['c256'] = bf(c.reshape(2, 128, 2, 256).transpose(1, 0, 2, 3))
    return m


_CACHE = {}


def kernel(**inputs):
    if 'nc' not in _CACHE:
        _CACHE['nc'] = build_program()
    nc, P = _CACHE['nc']
    shared = None
    in_maps = []
    for b in range(8):
        m = host_inputs(inputs, b) if shared is None else None
        if shared is None:
            shared = m
        else:
            m = dict(shared)
            f = lambda a: np.ascontiguousarray(np.asarray(a, dtype=np.float32))
            m['xin'] = f(np.concatenate([inputs['ctx'][b], inputs['x'][b]], axis=0))
            c2 = np.stack([np.asarray(inputs['c'][b]).reshape(8, 128).T, np.asarray(inputs['c_ctx']).reshape(8, 128).T], axis=-1)
            m['c2'] = f(c2.reshape(128, 16))
        in_maps.append(m)
    res = run_bass_kernel_spmd(nc, in_maps, core_ids=list(range(8)))
    _CACHE['res'] = res
    return np.stack([np.asarray(r['out'], dtype=np.float32) for r in res.results], axis=0)
```

```python
import contextlib, bisect, math
import numpy as np
import ml_dtypes
import concourse.bass as bass
import concourse.mybir as mybir
from concourse.bass_utils import run_bass_kernel_spmd

F32 = mybir.dt.float32
BF16 = mybir.dt.bfloat16
AF = mybir.ActivationFunctionType
OP = mybir.AluOpType
AX = mybir.AxisListType

NT = 18
TOK = 2304
DM = 1024
EPS = 1e-6
DEPTH = 2
STAGE = 99
DEBUG = []
BRANCHES = 'FAM'
DBGXS = False


class Dummy:
    def __getitem__(self, k):
        return self

    def __getattr__(self, n):
        return lambda *a, **k: self


class Prog:
    ENGS = ('pe', 'dve', 'act', 'pool', 'sp')

    def __init__(self, nc):
        self.nc = nc
        self.dry = True
        self.rec = []
        self.i = 0

    def op(self, eng, fn, r=(), w=(), dma=None):
        if self.dry:
            self.rec.append(('op', eng, tuple(r), tuple(w), dma))
        else:
            i = self.i
            E = self.H[eng]
            for (s, v) in self.waits[i]:
                E.wait_ge(self.sems[s], v)
            if fn is not None:
                ins = fn()
                inc = self.incs[i]
                if inc is not None:
                    ins.then_inc(self.sems[inc[0]], inc[1])
        self.i += 1

    def barrier(self):
        if self.dry:
            self.rec.append(('bar',))
        self.i += 1

    def plan(self):
        rec = self.rec
        n = len(rec)
        lastw, readers, last_on_eng, dma_ops = {}, {}, {}, {}
        deps = [None] * n
        pend = {e: None for e in self.ENGS}
        for i, r in enumerate(rec):
            if r[0] == 'bar':
                bd = set(last_on_eng.values())
                for s, l in dma_ops.items():
                    if l:
                        bd.add(l[-1])
                for e in self.ENGS:
                    pend[e] = set(bd) | (pend[e] or set())
                lastw.clear()
                readers.clear()
                continue
            _, eng, R, W, dma = r
            d = set()
            for k in R:
                if k in lastw:
                    d.add(lastw[k])
            for k in W:
                if k in lastw:
                    d.add(lastw[k])
                d.update(readers.get(k, ()))
            if pend[eng]:
                d |= pend[eng]
                pend[eng] = None
            d.discard(i)
            if eng == 'pe':
                d = {j for j in d if not (rec[j][1] == 'pe' and rec[j][4] is None)}
            deps[i] = d
            for k in W:
                lastw[k] = i
                readers[k] = []
            for k in R:
                lst = readers.setdefault(k, [])
                if dma is None:
                    lst[:] = [j for j in lst if not (rec[j][1] == eng and rec[j][4] is None)]
                lst.append(i)
            last_on_eng[eng] = i
            if dma:
                dma_ops.setdefault(dma, []).append(i)
        sig = [False] * n
        for i in range(n):
            if deps[i]:
                for j in deps[i]:
                    if rec[j][4] is None:
                        sig[j] = True
        cnt = {e: 0 for e in self.ENGS}
        ev = [None] * n
        for i, r in enumerate(rec):
            if r[0] != 'op':
                continue
            if r[4] is None and sig[i]:
                cnt[r[1]] += 1
                ev[i] = ('E_' + r[1], cnt[r[1]])
        self.waits = [()] * n
        self.incs = [None] * n
        seen = {e: {} for e in self.ENGS}
        for i, r in enumerate(rec):
            if r[0] != 'op':
                continue
            eng = r[1]
            wl = {}
            for j in deps[i]:
                if rec[j][4] is None:
                    s, v = ev[j]
                else:
                    s = 'D_' + rec[j][4]
                    v = 16 * bisect.bisect_left(dma_ops[rec[j][4]], i)
                if v > wl.get(s, 0):
                    wl[s] = v
            ws = []
            for s, v in wl.items():
                if v > seen[eng].get(s, 0):
                    seen[eng][s] = v
                    ws.append((s, v))
            self.waits[i] = tuple(ws)
            if r[4] is not None:
                self.incs[i] = ('D_' + r[4], 16)
            elif sig[i]:
                self.incs[i] = (ev[i][0], 1)
        self.semnames = ['E_' + e for e in self.ENGS] + ['D_' + s for s in dma_ops]
        self.stats = dict(n=n, cnt=cnt, nw=sum(len(w) for w in self.waits))

    def start_real(self, es):
        nc = self.nc
        self.dry = False
        self.i = 0
        self.H = {'pe': nc.tensor, 'dve': nc.vector, 'act': nc.scalar, 'pool': nc.gpsimd, 'sp': nc.sync}
        self.sems = {s: es.enter_context(nc.semaphore(s)) for s in self.semnames}


class Scope:
    _n = 0

    def __init__(self, K):
        self.K = K
        self.es = contextlib.ExitStack()

    def __enter__(self):
        return self

    def sb(self, name, shape, dt):
        if self.K.P.dry:
            return Dummy()
        Scope._n += 1
        return self.es.enter_context(self.K.nc.sbuf_tensor("%s_%d" % (name, Scope._n), list(shape), dt))

    def __exit__(self, *a):
        self.K.P.barrier()
        self.K.semmap = {}
        self.es.close()
        return False


def which(t):
    return 1 if t < 2 else 0


class K:
    def __init__(self, nc, P, dram):
        self.nc, self.P, self.d = nc, P, dram
        self.semmap = {}

    def mm(self, out, lhsT, rhs, start, stop, r, w):
        nc = self.nc
        self.P.op('pe', lambda: nc.tensor.matmul(out, lhsT=lhsT, rhs=rhs, start=start, stop=stop), r, w)

    def tr(self, out, in_, ident, r, w):
        nc = self.nc
        self.P.op('pe', lambda: nc.tensor.transpose(out, in_, ident), r, w)

    def act(self, out, in_, func, r, w, bias=None, scale=None, accum=None):
        nc = self.nc
        kw = {}
        if bias is not None:
            kw['bias'] = bias
        if scale is not None:
            kw['scale'] = scale
        if accum is not None:
            kw['accum_out'] = accum
        self.P.op('act', lambda: nc.scalar.activation(out=out, in_=in_, func=func, **kw), r, w)

    def v(self, eng, name, r, w, *a, **kw):
        nc = self.nc
        E = nc.vector if eng == 'dve' else nc.gpsimd
        self.P.op(eng, lambda: getattr(E, name)(*a, **kw), r, w)

    def dma(self, q, out, in_, r, w, sem):
        nc = self.nc
        E = nc.sync if q == 'sp' else nc.gpsimd
        k0 = w[0] if w else sem
        if isinstance(k0, tuple) and k0[0] in ('out', 'dbgxs'):
            name = sem
        elif isinstance(k0, tuple) and k0[0] == 'xs':
            name = 'xs'
        else:
            name = "b%d" % self.semmap.setdefault((k0, q), len(self.semmap))
        self.P.op(q, lambda: E.dma_start(out=out, in_=in_), r, w, dma=name + '_' + q)

    def build(self):
        nc, P, d = self.nc, self.P, self.d
        self.semmap = {}
        with Scope(self) as S:
            self.S0 = S
            self.xs = S.sb("xs", [128, NT, DM], F32)
            self.hT = S.sb("hT", [128, 8, TOK], BF16)
            self.identb = S.sb("identb", [128, 128], BF16)
            self.identf = S.sb("identf", [128, 128], F32)
            self.onesf = S.sb("onesf", [128, 128], F32)
            self.c2 = S.sb("c2", [128, 16], F32)
            self.scb = S.sb("scb", [128, 16], BF16)
            self.modv = S.sb("modv", [128, 48, 2], F32)
            self.bmod = S.sb("bmod", [128, 2, 48], F32)
            self.gn = S.sb("gn", [128, 2, 2, 8], F32)
            self.G = S.sb("G", [128, 2, 8, 2], F32)
            self.gaB = S.sb("gaB", [128, 2, DM], F32)
            self.ssq = S.sb("ssq", [128, NT], F32)
            self.rstd = S.sb("rstd", [128, NT], F32)
            self.epsT = S.sb("epsT", [128, 1], F32)
            self.xn = [S.sb("xn%d" % i, [128, DM], BF16) for i in range(2)]
            self.junk = S.sb("junk", [128, DM], BF16)
            self.dg = [S.sb("dg%d" % i, [128, 128], F32) for i in range(2)]
            self.xt = [S.sb("xt%d" % i, [128, 512], F32) for i in range(2)]
            if not P.dry:
                es = S.es
                self.pb = [es.enter_context(nc.psum_tensor("pb%d" % i, [128, 512], F32)) for i in range(6)]
                self.ptr = [es.enter_context(nc.psum_tensor("ptr%d" % i, [128, 8, 128], BF16)) for i in range(2)]
            else:
                self.pb = [Dummy()] * 6
                self.ptr = [Dummy()] * 2
            self.cnt = 0
            self.dma('sp', self.identb[:], d['identb'][:, :], [], ['identb'], 'cst')
            self.dma('sp', self.identf[:], d['identf'][:, :], [], ['identf'], 'cst')
            self.dma('sp', self.c2[:], d['c2'][:, :], [], ['c2'], 'cst')
            self.dma('sp', self.bmod[:], d['bmodfm'][:, :, :], [], ['bmod'], 'cst')
            self.dma('sp', self.gn[:], d['gnfm'][:, :, :, :], [], ['gn'], 'cst')
            self.v('dve', 'memset', [], ['onesf'], self.onesf[:], 1.0)
            self.v('dve', 'memset', [], ['epsT'], self.epsT[:], EPS)
            self.v('dve', 'memset', [], ['ssq'], self.ssq[:], 0.0)
            self.act(self.scb[:], self.c2[:], AF.Silu, ['c2'], ['scb'])
            for t in range(NT):
                self.dma('sp', self.xs[:, t, :], d['xin'][t * 128:(t + 1) * 128, :], [], [('xs', t)], 'xin')
            for l in range(DEPTH):
                last = (l == DEPTH - 1)
                self.mod_phase(l)
                if STAGE <= 1:
                    break
                self.make_gaB(2)
                self.phaseA(l, 0, range(NT))
                if STAGE >= 3:
                    self.mixers(l, last)
                if DBGXS and l == 0:
                    for t in range(NT):
                        self.dma('sp', d['dbgxs'][t * 128:(t + 1) * 128, :], self.xs[:, t, :], [('xs', t)], [('dbgxs', t)], 'out')
                    break
                self.make_gaB(5)
                tiles = list(range(2, NT)) if last else list(range(NT))
                self.phaseA(l, 1, tiles)
                self.mlp(l, tiles)
            self.final()
        P.barrier()
        P.op('sp', None)

    def mod_phase(self, l):
        nc, P, d = self.nc, self.P, self.d
        with Scope(self) as S:
            wm = [S.sb("wm%d" % i, [128, 8, 1024], BF16) for i in range(2)]
            modps = self.pb[0]
            for blk in range(6):
                s = blk % 2
                self.dma('pool', wm[s][:], d['w_mod'][l][:, blk * 1024:(blk + 1) * 1024].rearrange("(k p) c -> p k c", p=128),
                         [], [('wm', s)], 'wm%d' % s)
                for jj in range(8):
                    j = blk * 8 + jj
                    for k in range(8):
                        self.mm(modps[:, 2 * j:2 * j + 2], wm[s][:, k, jj * 128:(jj + 1) * 128], self.scb[:, 2 * k:2 * k + 2],
                                k == 0, k == 7, [('wm', s), 'scb'], [('pb', 0)])
            self.v('dve', 'tensor_tensor', [('pb', 0), 'bmod'], ['modv'], out=self.modv[:],
                   in0=modps[:, 0:96].rearrange("p (j w) -> p j w", w=2),
                   in1=self.bmod[:, l, :].unsqueeze(2).to_broadcast([128, 48, 2]), op=OP.add)
            for ni, vi in ((0, 1), (1, 4)):
                self.v('dve', 'scalar_tensor_tensor', ['modv', 'gn'], ['G'], out=self.G[:, ni, :, :],
                       in0=self.modv[:, vi * 8:(vi + 1) * 8, :], scalar=1.0,
                       in1=self.gn[:, l, ni, :].unsqueeze(2).to_broadcast([128, 8, 2]), op0=OP.add, op1=OP.mult)

    def make_gaB(self, vi):
        for w in range(2):
            for kk in range(8):
                self.cnt += 1
                dgi = self.cnt % 2
                self.v('dve', 'tensor_scalar', ['modv', 'identf'], [('dg', dgi)], out=self.dg[dgi][:], in0=self.identf[:],
                       scalar1=self.modv[:, vi * 8 + kk, w:w + 1], scalar2=None, op0=OP.mult)
                bank = kk // 4
                self.mm(self.pb[bank][:, (kk % 4) * 128:(kk % 4 + 1) * 128], self.onesf[:], self.dg[dgi][:], True, True,
                        ['onesf', ('dg', dgi)], [('pb', bank)])
            for bank in range(2):
                self.act(self.gaB[:, w, bank * 512:(bank + 1) * 512], self.pb[bank][:], AF.Identity, [('pb', bank)], ['gaB'])

    def rms(self, t):
        self.cnt += 1
        self.act(self.junk[:], self.xs[:, t, :], AF.Square, [('xs', t)], [('junk', self.cnt), ('ssq', t)],
                 accum=self.ssq[:, t:t + 1])
        self.act(self.rstd[:, t:t + 1], self.ssq[:, t:t + 1], AF.Sqrt, [('ssq', t), 'epsT'], [('rstd', t)],
                 bias=self.epsT[:], scale=1.0 / DM)
        self.v('dve', 'reciprocal', [('rstd', t)], [('rstd', t)], out=self.rstd[:, t:t + 1], in_=self.rstd[:, t:t + 1])
        self.v('dve', 'memset', [('ssq', t)], [('ssq', t)], self.ssq[:, t:t + 1], 0.0)

    def phaseA(self, l, ni, tiles):
        vi = 0 if ni == 0 else 3
        for t in tiles:
            w = which(t)
            self.rms(t)
            self.cnt += 1
            xi = self.cnt % 2
            self.act(self.xn[xi][:], self.xs[:, t, :], AF.Identity, [('xs', t), ('rstd', t)], [('xn', xi)],
                     scale=self.rstd[:, t:t + 1])
            for k in range(8):
                self.tr(self.ptr[xi][:, k, :], self.xn[xi][:, k * 128:(k + 1) * 128], self.identb[:],
                        [('xn', xi), 'identb'], [('ptr', xi)])
            for k in range(8):
                self.act(self.hT[:, k, t * 128:(t + 1) * 128], self.ptr[xi][:, k, :], AF.Identity,
                         [('ptr', xi), 'G', 'modv'], [('hT', t)],
                         bias=self.modv[:, vi * 8 + k, w:w + 1], scale=self.G[:, ni, k, w:w + 1])

    def xupd(self, t, nh, b):
        w = which(t)
        self.cnt += 1
        mi = self.cnt % 2
        self.v('dve', 'tensor_tensor', [('pb', b), 'gaB'], [('xt', mi)], out=self.xt[mi][:], in0=self.pb[b][:],
               in1=self.gaB[:, w, nh * 512:(nh + 1) * 512], op=OP.mult)
        self.v('pool', 'tensor_tensor', [('xt', mi), ('xs', t)], [('xs', t)],
               out=self.xs[:, t, nh * 512:(nh + 1) * 512], in0=self.xs[:, t, nh * 512:(nh + 1) * 512],
               in1=self.xt[mi][:], op=OP.add)

    def blocks(self, lo=0, hi=TOK, step=512):
        return [(o, min(step, hi - o)) for o in range(lo, hi, step)]

    def mixers(self, l, last):
        if 'F' in BRANCHES:
            self.fourier(l, last)
        if 'M' in BRANCHES:
            self.mlstm(l, last)
        if 'A' in BRANCHES:
            self.mla(l, last)

    def mlstm(self, l, last):
        nc, P, d = self.nc, self.P, self.d
        win = d['w_in'][l]
        with Scope(self) as SO:
            acol = SO.sb("acol", [128, NT, 8], F32)
            Mcol = SO.sb("Mcol", [128, NT, 8], F32)
            T4 = SO.sb("T4", [128, 4, NT, 8], F32)
            gm = SO.sb("gm", [128, 384], F32)
            E2 = SO.sb("E2", [8, 2, 8], F32)
            negid = SO.sb("negid", [128, 128], F32)
            mlmask = SO.sb("mlmask", [128, 2, 256], BF16)
            self.dma('sp', gm[:], d['gmB'][:, l, :], [], ['gm'], 'mw')
            self.dma('sp', E2[:], d['E2'][:, :, :], [], ['E2'], 'mw')
            self.dma('sp', negid[:], d['negidf'][:, :], [], ['negid'], 'mw')
            self.dma('sp', mlmask[:], d['mlmask'][:, :, :], [], ['mlmask'], 'mw')
            with Scope(self) as S:
                wg = S.sb("wg", [128, 8, 16], BF16)
                bg = S.sb("bg", [8, 2], F32)
                nbf = S.sb("nbf", [8, 1], F32)
                IG = S.sb("IG", [8, TOK], F32)
                LF = S.sb("LF", [8, TOK], F32)
                FF = S.sb("FF", [8, TOK], F32)
                FB = S.sb("FB", [8, TOK], F32)
                AFr = S.sb("AFr", [8, TOK], F32)
                ABr = S.sb("ABr", [8, TOK], F32)
                R1 = S.sb("R1", [8, NT, 8], F32)
                R2 = S.sb("R2", [8, NT, 8], F32)
                mcol = S.sb("mcol", [128, NT, 8], F32)
                MendB = S.sb("MendB", [128, NT, 8], F32)
                MP = S.sb("MP", [128, NT, 8], F32)
                ex = S.sb("ex", [128, 4, NT, 8], F32)
                self.dma('pool', wg[:], d['w_g'][l].rearrange("(k p) c -> p k c", p=128), [], ['wg'], 'gw')
                self.dma('sp', bg[:], d['bg'][l], [], ['bg'], 'gw')
                self.v('dve', 'tensor_scalar', ['bg'], ['nbf'], out=nbf[:], in0=bg[:, 1:2], scalar1=-1.0, scalar2=None, op0=OP.mult)
                nb = 0
                for (o, n) in self.blocks():
                    hk = [('hT', t) for t in range(o // 128, (o + n) // 128)]
                    bi, bf_ = nb % 6, (nb + 1) % 6
                    nb += 2
                    for k in range(8):
                        self.mm(self.pb[bi][0:8, 0:n], wg[:, k, 0:8], self.hT[:, k, o:o + n], k == 0, k == 7, hk + ['wg'], [('pb', bi)])
                    for k in range(8):
                        self.mm(self.pb[bf_][0:8, 0:n], wg[:, k, 8:16], self.hT[:, k, o:o + n], k == 0, k == 7, hk + ['wg'], [('pb', bf_)])
                    self.act(IG[:, o:o + n], self.pb[bi][0:8, 0:n], AF.Identity, [('pb', bi), 'bg'], ['IG'], bias=bg[:, 0:1])
                    self.act(LF[:, o:o + n], self.pb[bf_][0:8, 0:n], AF.Exp, [('pb', bf_), 'nbf'], ['LF'], bias=nbf[:], scale=-1.0)
                self.act(LF[:], LF[:], AF.Ln, ['LF'], ['LF'], bias=1.0)
                self.v('dve', 'tensor_scalar', ['LF'], ['LF'], out=LF[:], in0=LF[:], scalar1=-1.0, scalar2=None, op0=OP.mult)
                one_b = lambda n: self.onesf[0:8, 0:1].to_broadcast([8, n])
                self.v('dve', 'tensor_tensor_scan', ['LF', 'onesf'], ['FF'], out=FF[:], data0=one_b(TOK), data1=LF[:], initial=0.0,
                       op0=OP.mult, op1=OP.add)
                self.v('dve', 'tensor_tensor_scan', ['LF', 'onesf'], ['FB'], out=FB[:, 255::-1], data0=one_b(256), data1=LF[:, 255::-1],
                       initial=0.0, op0=OP.mult, op1=OP.add)
                self.v('dve', 'tensor_tensor_scan', ['LF', 'onesf', 'FB'], ['FB'], out=FB[:, 2303:255:-1], data0=one_b(2048),
                       data1=LF[:, 2303:255:-1], initial=FB[:, 0:1], op0=OP.mult, op1=OP.add)
                self.v('dve', 'tensor_tensor', ['IG', 'FF'], ['AFr'], out=AFr[:], in0=IG[:], in1=FF[:], op=OP.subtract)
                self.v('dve', 'tensor_tensor', ['IG', 'FB'], ['ABr'], out=ABr[:], in0=IG[:], in1=FB[:], op=OP.subtract)
                MF, MB = IG, LF
                self.v('dve', 'tensor_tensor_scan', ['AFr'], ['IG'], out=MF[:], data0=AFr[:], data1=AFr[:], initial=0.0, op0=OP.max, op1=OP.max)
                self.v('dve', 'tensor_tensor_scan', ['ABr'], ['LF'], out=MB[:, 255::-1], data0=ABr[:, 255::-1], data1=ABr[:, 255::-1],
                       initial=0.0, op0=OP.max, op1=OP.max)
                self.v('dve', 'tensor_tensor_scan', ['ABr', 'LF'], ['LF'], out=MB[:, 2303:255:-1], data0=ABr[:, 2303:255:-1],
                       data1=ABr[:, 2303:255:-1], initial=MB[:, 0:1], op0=OP.max, op1=OP.max)
                self.v('dve', 'tensor_tensor', ['FF', 'IG'], ['FF'], out=FF[:], in0=FF[:], in1=MF[:], op=OP.add)
                self.v('dve', 'tensor_tensor', ['FB', 'LF'], ['FB'], out=FB[:], in0=FB[:], in1=MB[:], op=OP.add)
                for qi, (XF, XB, kf, kb) in enumerate(((AFr, ABr, 'AFr', 'ABr'), (MF, MB, 'IG', 'LF'), (FF, FB, 'FF', 'FB'))):
                    for c in range(NT):
                        self.mm(self.pb[qi][:, c * 8:(c + 1) * 8], XF[:, c * 128:(c + 1) * 128], E2[:, 0, :], True, False, [kf, 'E2'], [('pb', qi)])
                        self.mm(self.pb[qi][:, c * 8:(c + 1) * 8], XB[:, c * 128:(c + 1) * 128], E2[:, 1, :], False, True, [kb, 'E2'], [('pb', qi)])
                for qi, (dst, kd) in enumerate(((acol, 'acol'), (Mcol, 'Mcol'), (mcol, 'mcol'))):
                    self.act(dst[:].rearrange("p c e -> p (c e)"), self.pb[qi][:, 0:NT * 8], AF.Identity, [('pb', qi)], [kd])
                self.v('dve', 'tensor_tensor', ['E2', 'IG'], ['R1'], out=R1[:], in0=E2[:, 0, :].unsqueeze(1).to_broadcast([8, NT, 8]),
                       in1=MF[:, 127::128].unsqueeze(2).to_broadcast([8, NT, 8]), op=OP.mult)
                self.v('dve', 'tensor_tensor', ['E2', 'LF'], ['R2'], out=R2[:], in0=E2[:, 1, :].unsqueeze(1).to_broadcast([8, NT, 8]),
                       in1=MB[:, 0::128].unsqueeze(2).to_broadcast([8, NT, 8]), op=OP.mult)
                self.v('dve', 'tensor_tensor', ['R1', 'R2'], ['R1'], out=R1[:], in0=R1[:], in1=R2[:], op=OP.add)
                self.mm(self.pb[3][:, 0:NT * 8], self.onesf[0:8, :], R1[:].rearrange("p c e -> p (c e)"), True, True, ['R1', 'onesf'], [('pb', 3)])
                self.act(MendB[:].rearrange("p c e -> p (c e)"), self.pb[3][:, 0:NT * 8], AF.Identity, [('pb', 3)], ['MendB'])
                self.v('dve', 'memset', [], ['MP'], MP[:], 0.0)
                self.v('dve', 'tensor_copy', ['MendB', 'MP'], ['MP'], out=MP[:, 1:NT, 0:4], in_=MendB[:, 0:NT - 1, 0:4])
                self.v('dve', 'tensor_copy', ['MendB', 'MP'], ['MP'], out=MP[:, 0:1, 4:8], in_=MendB[:, 1:2, 4:8])
                self.v('dve', 'tensor_copy', ['MendB', 'MP'], ['MP'], out=MP[:, 2:NT - 1, 4:8], in_=MendB[:, 3:NT, 4:8])
                self.v('dve', 'tensor_copy', ['MendB', 'MP'], ['MP'], out=MP[:, NT - 1:NT, 4:8], in_=MendB[:, 0:1, 4:8])
                self.v('dve', 'tensor_tensor', ['MP', 'Mcol'], ['ex'], out=ex[:, 0], in0=MP[:], in1=Mcol[:], op=OP.subtract)
                self.v('dve', 'tensor_tensor', ['acol', 'MendB'], ['ex'], out=ex[:, 1], in0=acol[:], in1=MendB[:], op=OP.subtract)
                self.v('dve', 'tensor_tensor', ['MP', 'MendB'], ['ex'], out=ex[:, 2], in0=MP[:], in1=MendB[:], op=OP.subtract)
                self.v('dve', 'tensor_scalar', ['mcol'], ['ex'], out=ex[:, 3], in0=mcol[:], scalar1=-1.0, scalar2=None, op0=OP.mult)
                self.act(T4[:].rearrange("p a c e -> p (a c e)"), ex[:].rearrange("p a c e -> p (a c e)"), AF.Exp, ['ex'], ['T4'])
            for hp in range(2):
                self.mlstm_pair(l, last, hp, acol, Mcol, T4, gm, negid, mlmask)

    def mlstm_pair(self, l, last, hp, acol, Mcol, T4, gm, negid, mlmask):
        nc, P, d = self.nc, self.P, self.d
        win = d['w_in'][l]
        QS = 96.0 ** -0.5
        colof = lambda T: T + 2 if T < 256 else T + 6
        cblocks = [(0, 256)] + [(256 + i * 512, 512) for i in range(4)]
        with Scope(self) as SP:
            post = SP.sb("post", [96, 4, TOK], BF16)
            vp = SP.sb("vp", [128, NT, 2, 97], BF16)
            sg = SP.sb("sg", [128, NT, 192], BF16)
            S = Scope(self)
            wvo = S.sb("wvo", [128, 8, 384], BF16)
            self.dma('pool', wvo[:, :, 0:192], win[:, 1024 + 192 * hp:1024 + 192 * (hp + 1)].rearrange("(k p) c -> p k c", p=128), [], ['wvo'], 'mw')
            self.dma('pool', wvo[:, :, 192:384], win[:, 1408 + 192 * hp:1408 + 192 * (hp + 1)].rearrange("(k p) c -> p k c", p=128), [], ['wvo'], 'mw')
            self.v('dve', 'memset', [], ['vp'], vp[:], 1.0)
            with S:
                wqk = S.sb("wqk", [128, 8, 4, 96], BF16)
                dgw = S.sb("dgw", [96, 4, 5, 96], BF16)
                pre = S.sb("pre", [96, 4, 2312], BF16)
                sgt = [S.sb("sgt%d" % i, [96, 512], F32) for i in range(1)]
                for jj in range(4):
                    c0 = (256 if jj < 2 else 640) + 96 * (2 * hp + jj % 2)
                    self.dma('pool', wqk[:, :, jj, :], win[:, c0:c0 + 96].rearrange("(k p) c -> p k c", p=128), [], ['wqk'], 'mw')
                    jg = (0 if jj < 2 else 4) + 2 * hp + jj % 2
                    self.dma('sp', dgw[:, jj, :, :], d['convdg'][l, :, jg, :, :], [], ['dgw'], 'mw')
                self.v('dve', 'memset', [], ['pre'], pre[:], 0.0)
                nb = 0
                for jj in range(4):
                    for (o, n) in cblocks:
                        b = nb % 6
                        nb += 1
                        hk = [('hT', t) for t in range(o // 128, (o + n) // 128)]
                        for k in range(8):
                            self.mm(self.pb[b][0:96, 0:n], wqk[:, k, jj, :], self.hT[:, k, o:o + n], k == 0, k == 7, hk + ['wqk'], [('pb', b)])
                        self.act(pre[:, jj, colof(o):colof(o) + n], self.pb[b][0:96, 0:n], AF.Identity, [('pb', b), 'pre'], [('pre', jj)])
                for t in range(NT):
                    b = nb % 6
                    nb += 1
                    for k in range(8):
                        self.mm(self.pb[b][:, 0:384], self.hT[:, k, t * 128:(t + 1) * 128], wvo[:, k, :], k == 0, k == 7, [('hT', t), 'wvo'], [('pb', b)])
                    self.act(vp[:, t, :, 0:96], self.pb[b][:, 0:192].rearrange("p (h c) -> p h c", c=96), AF.Identity, [('pb', b), 'vp'], [('vp', t)])
                    self.act(sg[:, t, :], self.pb[b][:, 192:384], AF.Sigmoid, [('pb', b)], [('sg', t)])
                for jj in range(4):
                    for (o, n) in cblocks:
                        b = nb % 6
                        nb += 1
                        c0 = colof(o)
                        for tap in range(5):
                            self.mm(self.pb[b][0:96, 0:n], dgw[:, jj, tap, :], pre[:, jj, c0 + tap - 2:c0 + tap - 2 + n], tap == 0, tap == 4,
                                    [('pre', jj), 'dgw'], [('pb', b)])
                        tk = [('post', jj, t) for t in range(o // 128, (o + n) // 128)]
                        if jj < 2:
                            si = 0
                            self.act(sgt[si][:, 0:n], self.pb[b][0:96, 0:n], AF.Sigmoid, [('pb', b)], [('sgt', si)])
                            self.v('dve', 'scalar_tensor_tensor', [('pb', b), ('sgt', si)], tk, out=post[:, jj, o:o + n], in0=self.pb[b][0:96, 0:n],
                                   scalar=QS, in1=sgt[si][:, 0:n], op0=OP.mult, op1=OP.mult)
                        else:
                            self.act(post[:, jj, o:o + n], self.pb[b][0:96, 0:n], AF.Silu, [('pb', b)], tk)
            ktok = SP.sb("ktok", [128, NT, 2, 96], BF16)
            for t in range(NT):
                pi = t % 2
                for hh in range(2):
                    self.tr(self.ptr[pi][:, hh, 0:96], post[:, 2 + hh, t * 128:(t + 1) * 128], self.identb[0:96, 0:96],
                            [('post', 2 + hh, t), 'identb'], [('ptr', pi)])
                self.act(ktok[:, t, :, :], self.ptr[pi][:, 0:2, 0:96], AF.Identity, [('ptr', pi)], [('ktok', t)])
            with Scope(self) as S:
                hF = S.sb("hF", [128, NT, 2, 96], BF16)
                woM = S.sb("woM", [96, 2, DM], BF16)
                dgM = [S.sb("dgM%d" % i, [128, 2, 128], F32) for i in range(2)]
                WT = [S.sb("WT%d" % i, [128, 2, 128], F32) for i in range(2)]
                PT = [S.sb("PT%d" % i, [128, 2, 128], BF16) for i in range(2)]
                tmpn = [S.sb("tmpn%d" % i, [128, 2, 97], F32) for i in range(2)]
                dn = [S.sb("dn%d" % i, [128, 2], F32) for i in range(2)]
                vw = [S.sb("vw%d" % i, [128, 2, 97], BF16) for i in range(2)]
                Cst = S.sb("Cst", [96, 2, 97], F32)
                Cball = S.sb("Cball", [96, NT, 2, 97], BF16)
                hs = [S.sb("hs%d" % i, [128, 2, 96], F32) for i in range(2)]
                hq = [S.sb("hq%d" % i, [128, 2, 96], F32) for i in range(2)]
                ss = [S.sb("ss%d" % i, [128, 2], F32) for i in range(2)]
                yb = [S.sb("yb%d" % i, [128, 192], BF16) for i in range(2)]
                yhT = [S.sb("yhT%d" % i, [96, 2, 128], BF16) for i in range(2)]
                self.dma('pool', woM[:], d['w_out'][l][256 + 192 * hp:256 + 192 * (hp + 1), :].rearrange("(h p) c -> p h c", p=96), [], ['woM'], 'mw')
                for dr in range(2):
                    order = list(range(NT)) if dr == 0 else [1, 0] + list(range(NT - 1, 1, -1))
                    cf0 = dr * 4 + 2 * hp
                    self.v('dve', 'memset', ['Cst'], ['Cst'], Cst[:], 0.0)
                    self.v('dve', 'memset', [('Cball', 0)], [('Cball', 0)], Cball[:, 0], 0.0)

                    def kv_part(it):
                        c = order[it]
                        i2 = it % 2
                        self.v('dve', 'tensor_tensor', [('vp', c), 'T4'], [('vw', i2)], out=vw[i2][:], in0=vp[:, c],
                               in1=T4[:, 1, c, cf0:cf0 + 2].unsqueeze(2).to_broadcast([128, 2, 97]), op=OP.mult)
                        for hh in range(2):
                            self.mm(self.pb[4 + i2][0:96, hh * 97:(hh + 1) * 97], ktok[:, c, hh, :], vw[i2][:, hh, :], True, True,
                                    [('ktok', c), ('vw', i2)], [('pb', 4 + i2)])

                    kv_part(0)
                    for it in range(NT - 1):
                        c = order[it]
                        i2 = it % 2
                        if it + 1 < NT - 1:
                            kv_part(it + 1)
                        self.v('dve', 'tensor_tensor', ['Cst', 'T4'], ['Cst'], out=Cst[:], in0=Cst[:],
                               in1=T4[0:96, 2, c, cf0:cf0 + 2].unsqueeze(2).to_broadcast([96, 2, 97]), op=OP.mult)
                        self.v('dve', 'tensor_tensor', ['Cst', ('pb', 4 + i2)], ['Cst'], out=Cst[:], in0=Cst[:],
                               in1=self.pb[4 + i2][0:96, 0:194].rearrange("p (h e) -> p h e", e=97), op=OP.add)
                        self.act(Cball[:, it + 1], Cst[:], AF.Identity, ['Cst'], [('Cball', it + 1)])

                    def part_i(it):
                        c = order[it]
                        i2 = it % 2
                        if last and c < 2:
                            return
                        tsl = slice(c * 128, (c + 1) * 128)
                        bo = 3 + i2
                        self.v('dve', 'tensor_tensor', ['negid', 'Mcol'], [('dgM', i2)], out=dgM[i2][:],
                               in0=negid[:].unsqueeze(1).to_broadcast([128, 2, 128]),
                               in1=Mcol[:, c, cf0:cf0 + 2].unsqueeze(2).to_broadcast([128, 2, 128]), op=OP.mult)
                        self.mm(self.pb[1][:, 0:256], self.onesf[:], dgM[i2][:].rearrange("p h t -> p (h t)"), True, False,
                                [('dgM', i2), 'onesf'], [('pb', 1)])
                        self.mm(self.pb[1][:, 0:256], self.identb[:], mlmask[:, dr, :], False, True, ['mlmask', 'identb'], [('pb', 1)])
                        for hh in range(2):
                            self.act(WT[i2][:, hh, :], self.pb[1][:, hh * 128:(hh + 1) * 128], AF.Exp, [('pb', 1), 'acol'], [('WT', i2)],
                                     bias=acol[:, c, cf0 + hh:cf0 + hh + 1])
                        for hh in range(2):
                            self.mm(self.pb[2][:, hh * 128:(hh + 1) * 128], post[:, 2 + hh, tsl], post[:, hh, tsl], True, True,
                                    [('post', 2 + hh, c), ('post', hh, c)], [('pb', 2)])
                        self.v('dve', 'tensor_tensor', [('pb', 2), ('WT', i2)], [('PT', i2)], out=PT[i2][:].rearrange("p h t -> p (h t)"),
                               in0=self.pb[2][:, 0:256], in1=WT[i2][:].rearrange("p h t -> p (h t)"), op=OP.mult)
                        for hh in range(2):
                            self.mm(self.pb[bo][:, hh * 97:(hh + 1) * 97], PT[i2][:, hh, :], vp[:, c, hh, :], True, True,
                                    [('PT', i2), ('vp', c)], [('pb', bo)])
                        for hh in range(2):
                            self.mm(self.pb[bo][:, 256 + hh * 97:256 + (hh + 1) * 97], post[:, hh, tsl], Cball[:, it, hh, :], True, True,
                                    [('post', hh, c), ('Cball', it)], [('pb', bo)])

                    def part_d(it):
                        c = order[it]
                        i2 = it % 2
                        if last and c < 2:
                            return
                        bo = 3 + i2
                        for hh in range(2):
                            self.act(tmpn[i2][:, hh, :], self.pb[bo][:, 256 + hh * 97:256 + (hh + 1) * 97], AF.Identity, [('pb', bo), 'T4'],
                                     [('tmpn', i2)], scale=T4[:, 0, c, cf0 + hh:cf0 + hh + 1])
                        self.v('dve', 'tensor_tensor', [('tmpn', i2), ('pb', bo)], [('tmpn', i2)], out=tmpn[i2][:], in0=tmpn[i2][:],
                               in1=self.pb[bo][:, 0:194].rearrange("p (h e) -> p h e", e=97), op=OP.add)
                        self.v('dve', 'scalar_tensor_tensor', [('tmpn', i2)], [('dn', i2)], out=dn[i2][:], in0=tmpn[i2][:, :, 96], scalar=-1.0,
                               in1=tmpn[i2][:, :, 96], op0=OP.mult, op1=OP.max)
                        self.v('dve', 'tensor_tensor', [('dn', i2), 'T4'], [('dn', i2)], out=dn[i2][:], in0=dn[i2][:],
                               in1=T4[:, 3, c, cf0:cf0 + 2], op=OP.max)
                        self.v('dve', 'reciprocal', [('dn', i2)], [('dn', i2)], out=dn[i2][:], in_=dn[i2][:])
                        if dr == 0:
                            self.v('dve', 'tensor_tensor', [('tmpn', i2), ('dn', i2)], [('hF', c)], out=hF[:, c], in0=tmpn[i2][:, :, 0:96],
                                   in1=dn[i2][:].unsqueeze(2).to_broadcast([128, 2, 96]), op=OP.mult)
                            return
                        self.v('dve', 'tensor_tensor', [('tmpn', i2), ('dn', i2)], [('hs', i2)], out=hs[i2][:], in0=tmpn[i2][:, :, 0:96],
                               in1=dn[i2][:].unsqueeze(2).to_broadcast([128, 2, 96]), op=OP.mult)
                        self.v('dve', 'tensor_tensor', [('hs', i2), ('hF', c)], [('hs', i2)], out=hs[i2][:], in0=hs[i2][:], in1=hF[:, c], op=OP.add)
                        self.v('dve', 'tensor_tensor', [('hs', i2)], [('hq', i2)], out=hq[i2][:], in0=hs[i2][:], in1=hs[i2][:], op=OP.mult)
                        self.v('dve', 'tensor_reduce', [('hq', i2)], [('ss', i2)], out=ss[i2][:], in_=hq[i2][:], axis=AX.X, op=OP.add)
                        self.act(ss[i2][:], ss[i2][:], AF.Ln, [('ss', i2), 'epsT'], [('ss', i2)], bias=self.epsT[:], scale=1.0 / 96)
                        self.act(ss[i2][:], ss[i2][:], AF.Exp, [('ss', i2)], [('ss', i2)], scale=-0.5)
                        self.v('dve', 'tensor_tensor', [('hs', i2), ('ss', i2)], [('hq', i2)], out=hq[i2][:], in0=hs[i2][:],
                               in1=ss[i2][:].unsqueeze(2).to_broadcast([128, 2, 96]), op=OP.mult)
                        self.v('dve', 'tensor_tensor', [('hq', i2), 'gm'], [('hq', i2)], out=hq[i2][:].rearrange("p h e -> p (h e)"),
                               in0=hq[i2][:].rearrange("p h e -> p (h e)"), in1=gm[:, 192 * hp:192 * (hp + 1)], op=OP.mult)
                        self.v('dve', 'tensor_tensor', [('hq', i2), ('sg', c)], [('yb', i2)], out=yb[i2][:],
                               in0=hq[i2][:].rearrange("p h e -> p (h e)"), in1=sg[:, c, :], op=OP.mult)
                        for hh in range(2):
                            self.tr(self.ptr[i2][0:96, hh, :], yb[i2][:, hh * 96:(hh + 1) * 96], self.identb[:], [('yb', i2), 'identb'], [('ptr', i2)])
                        self.act(yhT[i2][:], self.ptr[i2][0:96, 0:2, :], AF.Identity, [('ptr', i2)], [('yhT', i2)])
                        for nh in range(2):
                            for hh in range(2):
                                self.mm(self.pb[0][:], yhT[i2][:, hh, :], woM[:, hh, nh * 512:(nh + 1) * 512], hh == 0, hh == 1,
                                        [('yhT', i2), 'woM'], [('pb', 0)])
                            self.xupd(c, nh, 0)

                    part_i(0)
                    for it in range(NT):
                        if it + 1 < NT:
                            part_i(it + 1)
                        part_d(it)

    def mla(self, l, last):
        nc, P, d = self.nc, self.P, self.d
        SCALE = 96.0 ** -0.5
        with Scope(self) as S:
            wA = S.sb("wA", [128, 8, 416], BF16)
            wuq = S.sb("wuq", [128, 2, 384], BF16)
            wuqs = S.sb("wuqs", [128, 2, 4, 96], BF16)
            wkr = S.sb("wkr", [128, 8, 96], BF16)
            wkrs = S.sb("wkrs", [128, 8, 96], BF16)
            wukv = S.sb("wukv", [128, 640], BF16)
            woA = S.sb("woA", [96, 4, DM], BF16)
            ropeC = S.sb("ropeC", [96, TOK], BF16)
            ropeS = S.sb("ropeS", [96, TOK], BF16)
            Vt = S.sb("Vt", [128, NT, 384], BF16)
            cqnT = S.sb("cqnT", [128, 2, 512], BF16)
            ckvnT = S.sb("ckvnT", [128, 512], BF16)
            cqn = [S.sb("cqn%d" % i, [128, 256], BF16) for i in range(2)]
            ckvn = [S.sb("ckvn%d" % i, [128, 128], BF16) for i in range(2)]
            rt1 = S.sb("rt1", [96, 512], F32)
            rt2 = S.sb("rt2", [96, 512], F32)
            pT = [S.sb("pT%d" % i, [128, 512], BF16) for i in range(3)]
            rc = S.sb("rc", [96, 512], F32)
            yaT = S.sb("yaT", [96, 4, 512], BF16)
            onesb = S.sb("onesb", [128, 96], BF16)
            st = [S.sb("st%d" % i, [128, 2], F32) for i in range(2)]
            rs = [S.sb("rs%d" % i, [128, 2], F32) for i in range(2)]
            gq = S.sb("gq", [128, 2, 2], F32)
            gkv = S.sb("gkv", [128, 2], F32)
            qT = self.hT[0:96, 0:4, :]
            kT = self.hT[0:96, 4:8, :]
            win = d['w_in'][l]
            self.dma('pool', wA[:], win[:, 1808:2224].rearrange("(k p) c -> p k c", p=128), [], ['wA'], 'aw')
            self.dma('pool', wuq[:], d['w_uq'][l].rearrange("(k p) c -> p k c", p=128), [], ['wuq'], 'aw')
            self.v('dve', 'memset', [], ['wuqs'], wuqs[:], 0.0)
            self.v('dve', 'memset', [], ['wkr'], wkr[:], 0.0)
            self.v('dve', 'memset', [], ['wkrs'], wkrs[:], 0.0)
            self.v('dve', 'memset', [], ['onesb'], onesb[:], 1.0)
            for c in range(2):
                self.dma('pool', wuqs[:, c, :, 64:96], d['w_uq_sw'][l][c * 128:(c + 1) * 128, :, :], ['wuqs'], ['wuqs'], 'aw')
            self.dma('pool', wkr[:, :, 64:96], win[:, 2192:2224].rearrange("(k p) c -> p k c", p=128), ['wkr'], ['wkr'], 'aw')
            self.dma('pool', wkrs[:, :, 64:96], d['w_kr_sw'][l].rearrange("(k p) c -> p k c", p=128), ['wkrs'], ['wkrs'], 'aw')
            self.dma('pool', wukv[:], d['w_ukv'][l], [], ['wukv'], 'aw')
            self.dma('pool', woA[:], d['w_out'][l][640:1024, :].rearrange("(h p) c -> p h c", p=96), [], ['woA'], 'aw')
            self.dma('sp', ropeC[64:96, :], d['ropeC'][:, :], [], ['rope'], 'aw')
            self.dma('sp', ropeS[64:96, :], d['ropeS'][:, :], [], ['rope'], 'aw')
            self.dma('sp', gq[:], d['gqfm'][:, :, :], [], ['gq'], 'aw')
            self.dma('sp', gkv[:], d['gkvfm'][:, :], [], ['gq'], 'aw')
            wv = wukv[:, :].rearrange("p (h c) -> p h c", c=160)[:, :, 64:160]
            for (o, n) in self.blocks():
                tl = list(range(o // 128, (o + n) // 128))
                hk = [('hT', t) for t in tl]
                for ti, t in enumerate(tl):
                    b1 = t % 2
                    for k in range(8):
                        self.mm(self.pb[b1][:, 0:416], self.hT[:, k, t * 128:(t + 1) * 128], wA[:, k, :], k == 0, k == 7,
                                [('hT', t), 'wA'], [('pb', b1)])
                    si = t % 2
                    self.v('dve', 'memset', [], [('st', si)], st[si][:], 0.0)
                    self.cnt += 1
                    self.act(self.junk[:, 0:256], self.pb[b1][:, 0:256], AF.Square, [('pb', b1)], [('junk', self.cnt), ('st', si)],
                             accum=st[si][:, 0:1])
                    self.cnt += 1
                    self.act(self.junk[:, 256:384], self.pb[b1][:, 256:384], AF.Square, [('pb', b1)], [('junk', self.cnt), ('st', si)],
                             accum=st[si][:, 1:2])
                    self.act(rs[si][:, 0:1], st[si][:, 0:1], AF.Sqrt, [('st', si), 'epsT'], [('rs', si)], bias=self.epsT[:], scale=1.0 / 256)
                    self.act(rs[si][:, 1:2], st[si][:, 1:2], AF.Sqrt, [('st', si), 'epsT'], [('rs', si)], bias=self.epsT[:], scale=1.0 / 128)
                    self.v('dve', 'reciprocal', [('rs', si)], [('rs', si)], out=rs[si][:], in_=rs[si][:])
                    self.v('dve', 'tensor_scalar', [('pb', b1), ('rs', si)], [('cqn', si)], out=cqn[si][:], in0=self.pb[b1][:, 0:256],
                           scalar1=rs[si][:, 0:1], scalar2=None, op0=OP.mult)
                    self.v('dve', 'tensor_scalar', [('pb', b1), ('rs', si)], [('ckvn', si)], out=ckvn[si][:], in0=self.pb[b1][:, 256:384],
                           scalar1=rs[si][:, 1:2], scalar2=None, op0=OP.mult)
                    pi = t % 2
                    for c in range(2):
                        self.tr(self.ptr[pi][:, c, :], cqn[si][:, c * 128:(c + 1) * 128], self.identb[:], [('cqn', si), 'identb'], [('ptr', pi)])
                    self.tr(self.ptr[pi][:, 2, :], ckvn[si][:], self.identb[:], [('ckvn', si), 'identb'], [('ptr', pi)])
                    for c in range(2):
                        self.act(cqnT[:, c, ti * 128:(ti + 1) * 128], self.ptr[pi][:, c, :], AF.Identity, [('ptr', pi), 'gq'], ['cqnT'],
                                 scale=gq[:, l, c:c + 1])
                    self.act(ckvnT[:, ti * 128:(ti + 1) * 128], self.ptr[pi][:, 2, :], AF.Identity, [('ptr', pi), 'gq'], ['ckvnT'],
                             scale=gkv[:, l:l + 1])
                    bv = 2 + t % 2
                    self.mm(self.pb[bv][:, 0:384], ckvnT[:, ti * 128:(ti + 1) * 128], wv, True, True, ['ckvnT', 'wukv'], [('pb', bv)])
                    self.act(Vt[:, t, :], self.pb[bv][:, 0:384], AF.Identity, [('pb', bv)], [('Vt', t)])
                for k in range(8):
                    self.mm(self.pb[4][0:96, 0:n], wkr[:, k, :], self.hT[:, k, o:o + n], k == 0, k == 7, hk + ['wkr'], [('pb', 4)])
                for k in range(8):
                    self.mm(self.pb[5][0:96, 0:n], wkrs[:, k, :], self.hT[:, k, o:o + n], k == 0, k == 7, hk + ['wkrs'], [('pb', 5)])
                for h in range(4):
                    for c in range(2):
                        self.mm(self.pb[0][0:96, 0:n], wuq[:, c, h * 96:(h + 1) * 96], cqnT[:, c, 0:n], c == 0, c == 1, ['wuq', 'cqnT'], [('pb', 0)])
                    for c in range(2):
                        self.mm(self.pb[1][0:96, 0:n], wuqs[:, c, h, :], cqnT[:, c, 0:n], c == 0, c == 1, ['wuqs', 'cqnT'], [('pb', 1)])
                    bk = 2 + h % 2
                    self.mm(self.pb[bk][0:64, 0:n], wukv[:, h * 160:h * 160 + 64], ckvnT[:, 0:n], True, True, ['wukv', 'ckvnT'], [('pb', bk)])
                    self.act(qT[0:64, h, o:o + n], self.pb[0][0:64, 0:n], AF.Identity, [('pb', 0)], hk)
                    self.act(kT[0:64, h, o:o + n], self.pb[bk][0:64, 0:n], AF.Identity, [('pb', bk)], hk)
                    self.v('dve', 'tensor_tensor', [('pb', 0), 'rope'], ['rt1'], out=rt1[64:96, 0:n], in0=self.pb[0][64:96, 0:n],
                           in1=ropeC[64:96, o:o + n], op=OP.mult)
                    self.v('dve', 'tensor_tensor', [('pb', 1), 'rope'], ['rt2'], out=rt2[64:96, 0:n], in0=self.pb[1][64:96, 0:n],
                           in1=ropeS[64:96, o:o + n], op=OP.mult)
                    self.v('pool', 'tensor_tensor', ['rt1', 'rt2'], hk, out=qT[64:96, h, o:o + n], in0=rt1[64:96, 0:n],
                           in1=rt2[64:96, 0:n], op=OP.add)
                self.v('dve', 'tensor_tensor', [('pb', 4), 'rope'], ['rt1'], out=rt1[64:96, 0:n], in0=self.pb[4][64:96, 0:n],
                       in1=ropeC[64:96, o:o + n], op=OP.mult)
                self.v('dve', 'tensor_tensor', [('pb', 5), 'rope'], ['rt2'], out=rt2[64:96, 0:n], in0=self.pb[5][64:96, 0:n],
                       in1=ropeS[64:96, o:o + n], op=OP.mult)
                for h in range(4):
                    self.v('pool', 'tensor_tensor', ['rt1', 'rt2'], hk, out=kT[64:96, h, o:o + n], in0=rt1[64:96, 0:n],
                           in1=rt2[64:96, 0:n], op=OP.add)
            qblocks = [(256 + i * 512, 512, list(range(NT))) for i in range(4)]
            if not last:
                qblocks.append((0, 256, [0, 1]))
            for (qo, qn, ktiles) in qblocks:
                qk = [('hT', t) for t in range(qo // 128, (qo + qn) // 128)]
                for h in range(4):
                    def st_mm(i):
                        kt_ = ktiles[i]
                        self.mm(self.pb[i % 2][:, 0:qn], kT[:, h, kt_ * 128:(kt_ + 1) * 128], qT[:, h, qo:qo + qn], True, True,
                                qk + [('hT', kt_)], [('pb', i % 2)])
                    st_mm(0)
                    for i, kt in enumerate(ktiles):
                        sb_ = i % 2
                        pi = i % 3
                        if i + 1 < len(ktiles):
                            st_mm(i + 1)
                        self.act(pT[pi][:, 0:qn], self.pb[sb_][:, 0:qn], AF.Exp, [('pb', sb_)], [('pT', pi)], scale=SCALE)
                        self.mm(self.pb[2][0:96, 0:qn], Vt[:, kt, h * 96:(h + 1) * 96], pT[pi][:, 0:qn], i == 0, i == len(ktiles) - 1,
                                [('Vt', kt), ('pT', pi)], [('pb', 2)])
                        self.mm(self.pb[3][0:96, 0:qn], onesb[:, :], pT[pi][:, 0:qn], i == 0, i == len(ktiles) - 1,
                                ['onesb', ('pT', pi)], [('pb', 3)])
                    self.v('dve', 'reciprocal', [('pb', 3)], ['rc'], out=rc[:, 0:qn], in_=self.pb[3][0:96, 0:qn])
                    self.v('dve', 'tensor_tensor', [('pb', 2), 'rc'], [('yaT', h)], out=yaT[:, h, 0:qn], in0=self.pb[2][0:96, 0:qn],
                           in1=rc[:, 0:qn], op=OP.mult)
                for ti, t in enumerate(range(qo // 128, (qo + qn) // 128)):
                    for nh in range(2):
                        b = 4 + nh
                        for h in range(4):
                            self.mm(self.pb[b][:], yaT[:, h, ti * 128:(ti + 1) * 128], woA[:, h, nh * 512:(nh + 1) * 512], h == 0, h == 3,
                                    [('yaT', h), 'woA'], [('pb', b)])
                        self.xupd(t, nh, b)

    def fourier(self, l, last):
        nc, P, d = self.nc, self.P, self.d
        with Scope(self) as S:
            wpf = S.sb("wpf", [128, 8, 256], BF16)
            pfT = S.sb("pfT", [128, 2, TOK], BF16)
            bd = S.sb("bd", [128, 256], BF16)
            pfCS = S.sb("pfCS", [128, NT, 2, 256], BF16)
            tw = [S.sb("tw%d" % i, [128, 2, 1024], BF16) for i in range(3)]
            yfT = S.sb("yfT", [128, 2, TOK], BF16)
            wo = S.sb("wo", [128, 2, DM], BF16)
            c256 = S.sb("c256", [128, 2, 2, 256], BF16)
            self.dma('pool', wpf[:], d['w_in'][l][:, 0:256].rearrange("(k p) c -> p k c", p=128), [], ['wpf'], 'fw')
            self.dma('pool', wo[:], d['w_out'][l][0:256, :].rearrange("(k p) c -> p k c", p=128), [], ['wo'], 'fw')
            self.dma('sp', bd[:], d['bdcs'][:, :], [], ['bd'], 'fw')
            self.dma('sp', c256[:], d['c256'][:, :, :, :], [], ['c256'], 'fw')
            nb = 0
            for j in range(2):
                for (o, n) in self.blocks():
                    b = nb % 6
                    nb += 1
                    for k in range(8):
                        self.mm(self.pb[b][:, 0:n], wpf[:, k, j * 128:(j + 1) * 128], self.hT[:, k, o:o + n], k == 0, k == 7,
                                ['wpf'] + [('hT', t) for t in range(o // 128, (o + n) // 128)], [('pb', b)])
                    self.act(pfT[:, j, o:o + n], self.pb[b][:, 0:n], AF.Identity, [('pb', b)], [('pfT', j, o)])
            for t in range(NT):
                for j in range(2):
                    b = nb % 6
                    nb += 1
                    self.mm(self.pb[b][:, 0:256], pfT[:, j, t * 128:(t + 1) * 128], bd[:], True, True,
                            [('pfT', j, (t // 4) * 512), 'bd'], [('pb', b)])
                    self.act(pfCS[:, t, j, :], self.pb[b][:, 0:256], AF.Identity, [('pb', b)], [('pfCS', t)])
            ntw = 0
            for half in range(2):
                for nt in range(16):
                    t = nt + 2
                    s = ntw % 3
                    ntw += 1
                    self.dma('sp', tw[s][:, 0, :], d['twC'][nt * 128:(nt + 1) * 128, half * 1024:(half + 1) * 1024], [], [('tw', s)], 'tw%d' % s)
                    self.dma('sp', tw[s][:, 1, :], d['twS'][nt * 128:(nt + 1) * 128, half * 1024:(half + 1) * 1024], [], [('tw', s)], 'tw%d' % s)
                    for j in range(2):
                        for q in range(2):
                            b = j * 2 + q
                            self.mm(self.pb[b][:], pfCS[:, t, j, 0:128], tw[s][:, 0, q * 512:(q + 1) * 512], nt == 0, False,
                                    [('pfCS', t), ('tw', s)], [('pb', b)])
                            self.mm(self.pb[b][:], pfCS[:, t, j, 128:256], tw[s][:, 1, q * 512:(q + 1) * 512], False, nt == 15,
                                    [('pfCS', t), ('tw', s)], [('pb', b)])
                for j in range(2):
                    for q in range(2):
                        b = j * 2 + q
                        o = 256 + half * 1024 + q * 512
                        self.act(yfT[:, j, o:o + 512], self.pb[b][:], AF.Identity, [('pb', b)], [('yfT', o // 128)])
            if not last:
                for j in range(2):
                    b = 4 + j
                    for nt in range(2):
                        self.mm(self.pb[b][:, 0:256], pfCS[:, nt, j, 0:128], c256[:, nt, 0, :], nt == 0, False,
                                [('pfCS', nt), 'c256'], [('pb', b)])
                        self.mm(self.pb[b][:, 0:256], pfCS[:, nt, j, 128:256], c256[:, nt, 1, :], False, nt == 1,
                                [('pfCS', nt), 'c256'], [('pb', b)])
                    self.act(yfT[:, j, 0:256], self.pb[b][:, 0:256], AF.Identity, [('pb', b)], [('yfT', 0)])
            nb = 0
            for t in (range(2, NT) if last else range(NT)):
                for nh in range(2):
                    b = nb % 6
                    nb += 1
                    for j in range(2):
                        self.mm(self.pb[b][:], yfT[:, j, t * 128:(t + 1) * 128], wo[:, j, nh * 512:(nh + 1) * 512], j == 0, j == 1,
                                [('yfT', 0 if t < 2 else 2 + ((t - 2) // 4) * 4), 'wo'], [('pb', b)])
                    self.xupd(t, nh, b)

    def mlp(self, l, tiles):
        nc, P, d = self.nc, self.P, self.d
        groups = [tiles[i:i + 6] for i in range(0, len(tiles), 6)]
        with Scope(self) as S:
            uT = S.sb("uT", [128, 32, 768], BF16)
            wu = [S.sb("wu%d" % i, [128, 8, 512], BF16) for i in range(2)]
            wd = [S.sb("wd%d" % i, [128, 4, 512], BF16) for i in range(2)]
            tmp = [S.sb("mt%d" % i, [128, 512], F32) for i in range(2)]
            nu = nd = nb = 0
            for g in groups:
                G = 128 * len(g)
                t0 = g[0] * 128
                subs = [(0, G)] if G <= 512 else [(0, G // 2), (G // 2, G // 2)]
                for hb in range(8):
                    s = nu % 2
                    nu += 1
                    self.dma('pool', wu[s][:], d['w_up'][l][:, hb * 512:(hb + 1) * 512].rearrange("(k p) c -> p k c", p=128),
                             [], [('wu', s)], 'wu%d' % s)
                    for cc in range(4):
                        j = hb * 4 + cc
                        for (o, n) in subs:
                            b = nb % 6
                            nb += 1
                            for k in range(8):
                                self.mm(self.pb[b][:, 0:n], wu[s][:, k, cc * 128:(cc + 1) * 128],
                                        self.hT[:, k, t0 + o:t0 + o + n], k == 0, k == 7,
                                        [('wu', s)] + [('hT', t) for t in g], [('pb', b)])
                            ri = nb % 2
                            self.act(tmp[ri][:, 0:n], self.pb[b][:, 0:n], AF.Relu, [('pb', b)], [('mt', ri)])
                            self.v('dve', 'tensor_tensor', [('mt', ri)], [('uT', j)], out=uT[:, j, o:o + n],
                                   in0=tmp[ri][:, 0:n], in1=tmp[ri][:, 0:n], op=OP.mult)
                for nh in range(2):
                    for jb in range(8):
                        s = nd % 2
                        nd += 1
                        self.dma('pool', wd[s][:],
                                 d['w_down'][l][jb * 512:(jb + 1) * 512, nh * 512:(nh + 1) * 512].rearrange("(j p) c -> p j c", p=128),
                                 [], [('wd', s)], 'wd%d' % s)
                        for jj in range(4):
                            j = jb * 4 + jj
                            for ti, t in enumerate(g):
                                self.mm(self.pb[ti][:], uT[:, j, ti * 128:(ti + 1) * 128], wd[s][:, jj, :], j == 0, j == 31,
                                        [('wd', s), ('uT', j)], [('pb', ti)])
                    for ti, t in enumerate(g):
                        self.xupd(t, nh, ti)

    def final(self):
        nc, P, d = self.nc, self.P, self.d
        with Scope(self) as S:
            gf = S.sb("gf", [128, DM], F32)
            ob = [S.sb("ob%d" % i, [128, DM], F32) for i in range(2)]
            self.dma('sp', gf[:], d['gfin'][:, :], [], ['gf'], 'cst')
            for t in range(2, NT):
                self.rms(t)
                i = t % 2
                self.act(ob[i][:], self.xs[:, t, :], AF.Identity, [('xs', t), ('rstd', t)], [('ob', i)], scale=self.rstd[:, t:t + 1])
                self.v('dve', 'tensor_tensor', [('ob', i), 'gf'], [('ob', i)], out=ob[i][:], in0=ob[i][:], in1=gf[:], op=OP.mult)
                self.dma('sp', d['out'][(t - 2) * 128:(t - 1) * 128, :], ob[i][:], [('ob', i)], [('out', t)], 'out')


def dram_decls(nc):
    d = {}

    def inp(name, shape, dt=F32):
        d[name] = nc.dram_tensor(name, list(shape), dt, kind="ExternalInput").ap()

    inp('xin', [TOK, DM])
    inp('c2', [128, 16])
    inp('w_mod', [2, 1024, 6144])
    inp('bmodfm', [128, 2, 48])
    inp('gnfm', [128, 2, 2, 8])
    inp('w_in', [2, 1024, 2224])
    inp('w_out', [2, 1024, 1024])
    inp('w_up', [2, 1024, 4096])
    inp('w_down', [2, 4096, 1024])
    inp('gfin', [128, DM])
    inp('identb', [128, 128], BF16)
    inp('identf', [128, 128])
    inp('w_uq', [2, 256, 384])
    inp('w_ukv', [2, 128, 640])
    inp('w_uq_sw', [2, 256, 4, 32])
    inp('w_kr_sw', [2, 1024, 32])
    inp('ropeC', [32, TOK], BF16)
    inp('ropeS', [32, TOK], BF16)
    inp('gqfm', [128, 2, 2])
    inp('gkvfm', [128, 2])
    inp('w_g', [2, 1024, 16])
    inp('bg', [2, 8, 2])
    inp('convdg', [2, 96, 8, 5, 96], BF16)
    inp('gmB', [128, 2, 384])
    inp('E2', [8, 2, 8])
    inp('negidf', [128, 128])
    inp('mlmask', [128, 2, 256], BF16)
    inp('bdcs', [128, 256], BF16)
    inp('twC', [2048, 2048], BF16)
    inp('twS', [2048, 2048], BF16)
    inp('c256', [128, 2, 2, 256], BF16)
    d['out'] = nc.dram_tensor('out', [2048, DM], F32, kind="ExternalOutput").ap()
    if DBGXS:
        d['dbgxs'] = nc.dram_tensor('dbgxs', [TOK, DM], F32, kind="ExternalOutput").ap()
    return d


def build_program():
    nc = bass.Bass("TRN2", target_bir_lowering=False)
    d = dram_decls(nc)
    P = Prog(nc)
    kb = K(nc, P, d)
    kb.build()
    P.plan()
    es = contextlib.ExitStack()
    with es:
        P.start_real(es)
        kb.build()
    return nc, P


def host_inputs(inputs, b):
    f = lambda a: np.ascontiguousarray(np.asarray(a, dtype=np.float32))
    x, c, ctx, c_ctx = inputs['x'], inputs['c'], inputs['ctx'], inputs['c_ctx']
    m = {}
    m['xin'] = f(np.concatenate([ctx[b], x[b]], axis=0))
    c2 = np.stack([np.asarray(c[b]).reshape(8, 128).T, np.asarray(c_ctx).reshape(8, 128).T], axis=-1)
    m['c2'] = f(c2.reshape(128, 16))
    m['w_mod'] = f(inputs['w_mod'])
    m['bmodfm'] = f(np.asarray(inputs['b_mod']).reshape(2, 48, 128).transpose(2, 0, 1))
    gn = np.stack([np.asarray(inputs['g_norm1']), np.asarray(inputs['g_norm2'])], axis=1)
    m['gnfm'] = f(gn.reshape(2, 2, 8, 128).transpose(3, 0, 1, 2))
    m['w_in'] = f(inputs['w_in'])
    m['w_out'] = f(inputs['w_out'])
    m['w_up'] = f(inputs['w_up'])
    m['w_down'] = f(inputs['w_down'])
    m['gfin'] = f(np.tile(np.asarray(inputs['g_final'])[None, :], (128, 1)))
    m['identb'] = np.eye(128, dtype=np.float32).astype(ml_dtypes.bfloat16)
    m['identf'] = np.eye(128, dtype=np.float32)
    m.update(host_consts())
    perm = np.array([a * 16 + (1 - b_) * 8 + j for a in range(2) for b_ in range(2) for j in range(8)])
    wuq = np.asarray(inputs['w_uq'])
    m['w_uq'] = f(wuq)
    m['w_ukv'] = f(inputs['w_ukv'])
    m['w_uq_sw'] = f(wuq.reshape(2, 256, 4, 96)[:, :, :, 64:96][..., perm])
    m['w_kr_sw'] = f(np.asarray(inputs['w_in'])[:, :, 2192:2224][..., perm])
    m['gqfm'] = f(np.asarray(inputs['g_q_norm']).reshape(2, 2, 128).transpose(2, 0, 1))
    m['gkvfm'] = f(np.asarray(inputs['g_kv_norm']).T)
    gp = [0, 1, 2, 3, 8, 9, 10, 11, 4, 5, 6, 7, 12, 13, 14, 15]
    m['w_g'] = f(np.asarray(inputs['w_in'])[:, :, 1792:1808][..., gp])
    m['bg'] = f(np.asarray(inputs['b_gates'])[:, gp].reshape(2, 2, 8).transpose(0, 2, 1))
    cv = np.asarray(inputs['conv_qk'], dtype=np.float32).reshape(2, 5, 8, 96)
    dgc = np.zeros((2, 96, 8, 5, 96), np.float32)
    pi_ = np.arange(96)
    dgc[:, pi_, :, :, pi_] = cv.transpose(3, 0, 2, 1)
    m['convdg'] = dgc.astype(ml_dtypes.bfloat16)
    m['gmB'] = f(np.tile(np.asarray(inputs['g_mlstm'])[None, :, :], (128, 1, 1)))
    return m


def host_consts():
    bf = lambda a: np.ascontiguousarray(a.astype(np.float32)).astype(ml_dtypes.bfloat16)
    m = {}
    i64 = np.arange(64)
    a64 = 2 * np.pi * np.outer(i64, i64) / 64
    C64, S64 = np.cos(a64) / 8.0, np.sin(a64) / 8.0
    z = np.zeros((64, 64))
    m['bdcs'] = bf(np.block([[C64, z, S64, z], [z, C64, z, S64]]))
    n = np.arange(2048, dtype=np.float64)
    aN = 2 * np.pi * (np.outer(n, n) % 2048) / 2048
    m['twC'] = bf(np.cos(aN) / np.sqrt(2048.0))
    m['twS'] = bf(-np.sin(aN) / np.sqrt(2048.0))
    E2 = np.zeros((8, 2, 8), np.float32)
    for k_ in range(4):
        E2[k_, 0, k_] = 1.0
        E2[4 + k_, 1, 4 + k_] = 1.0
    m['E2'] = E2
    m['negidf'] = -np.eye(128, dtype=np.float32)
    sidx, tidx = np.arange(128)[:, None], np.arange(128)[None, :]
    mf = np.where(sidx <= tidx, 0.0, -30000.0)
    mb_ = np.where(sidx >= tidx, 0.0, -30000.0)
    m['mlmask'] = bf(np.stack([np.concatenate([mf, mf], axis=1), np.concatenate([mb_, mb_], axis=1)], axis=1))
    tt = np.arange(2048)
    freqs = 10000.0 ** (-np.arange(8, dtype=np.float64) / 8)
    ang = np.stack([np.outer(tt // 64, freqs), np.outer(tt % 64, freqs)], axis=1)
    cosT = np.ones((TOK, 2, 2, 8)); sinT = np.zeros((TOK, 2, 2, 8))
    cosT[256:] = np.cos(ang)[:, :, None, :]
    sinT[256:, :, 0, :] = -np.sin(ang)
    sinT[256:, :, 1, :] = np.sin(ang)
    m['ropeC'] = bf(cosT.reshape(TOK, 32).T)
    m['ropeS'] = bf(sinT.reshape(TOK, 32).T)
    n = np.arange(256, dtype=np.float64)
    a2 = 2 * np.pi * (np.outer(n, n) % 256) / 256
    c = np.stack([np.cos(a2) / 16.0, -np.sin(a2) / 16.0], axis=1)
    m['c256'] = bf(c.reshape(2, 128, 2, 256).transpose(1, 0, 2, 3))
    return m


_CACHE = {}


def kernel(**inputs):
    if 'nc' not in _CACHE:
        _CACHE['nc'] = build_program()
    nc, P = _CACHE['nc']
    shared = None
    in_maps = []
    for b in range(8):
        m = host_inputs(inputs, b) if shared is None else None
        if shared is None:
            shared = m
        else:
            m = dict(shared)
            f = lambda a: np.ascontiguousarray(np.asarray(a, dtype=np.float32))
            m['xin'] = f(np.concatenate([inputs['ctx'][b], inputs['x'][b]], axis=0))
            c2 = np.stack([np.asarray(inputs['c'][b]).reshape(8, 128).T, np.asarray(inputs['c_ctx']).reshape(8, 128).T], axis=-1)
            m['c2'] = f(c2.reshape(128, 16))
        in_maps.append(m)
    res = run_bass_kernel_spmd(nc, in_maps, core_ids=list(range(8)))
    _CACHE['res'] = res
    return np.stack([np.asarray(r['out'], dtype=np.float32) for r in res.results], axis=0)
```

```python
import contextlib, bisect, math
import numpy as np
import ml_dtypes
import concourse.bass as bass
import concourse.mybir as mybir
from concourse.bass_utils import run_bass_kernel_spmd

F32 = mybir.dt.float32
BF16 = mybir.dt.bfloat16
AF = mybir.ActivationFunctionType
OP = mybir.AluOpType
AX = mybir.AxisListType

NT = 18
TOK = 2304
DM = 1024
EPS = 1e-6
DEPTH = 2
STAGE = 99
DEBUG = []
BRANCHES = 'FAM'
DBGXS = False


class Dummy:
    def __getitem__(self, k):
        return self

    def __getattr__(self, n):
        return lambda *a, **k: self


class Prog:
    ENGS = ('pe', 'dve', 'act', 'pool', 'sp')

    def __init__(self, nc):
        self.nc = nc
        self.dry = True
        self.rec = []
        self.i = 0

    def op(self, eng, fn, r=(), w=(), dma=None):
        if self.dry:
            self.rec.append(('op', eng, tuple(r), tuple(w), dma))
        else:
            i = self.i
            E = self.H[eng]
            for (s, v) in self.waits[i]:
                E.wait_ge(self.sems[s], v)
            if fn is not None:
                ins = fn()
                inc = self.incs[i]
                if inc is not None:
                    ins.then_inc(self.sems[inc[0]], inc[1])
        self.i += 1

    def barrier(self):
        if self.dry:
            self.rec.append(('bar',))
        self.i += 1

    def plan(self):
        rec = self.rec
        n = len(rec)
        lastw, readers, last_on_eng, dma_ops = {}, {}, {}, {}
        deps = [None] * n
        pend = {e: None for e in self.ENGS}
        for i, r in enumerate(rec):
            if r[0] == 'bar':
                bd = set(last_on_eng.values())
                for s, l in dma_ops.items():
                    if l:
                        bd.add(l[-1])
                for e in self.ENGS:
                    pend[e] = set(bd) | (pend[e] or set())
                lastw.clear()
                readers.clear()
                continue
            _, eng, R, W, dma = r
            d = set()
            for k in R:
                if k in lastw:
                    d.add(lastw[k])
            for k in W:
                if k in lastw:
                    d.add(lastw[k])
                d.update(readers.get(k, ()))
            if pend[eng]:
                d |= pend[eng]
                pend[eng] = None
            d.discard(i)
            if eng == 'pe':
                d = {j for j in d if not (rec[j][1] == 'pe' and rec[j][4] is None)}
            deps[i] = d
            for k in W:
                lastw[k] = i
                readers[k] = []
            for k in R:
                lst = readers.setdefault(k, [])
                if dma is None:
                    lst[:] = [j for j in lst if not (rec[j][1] == eng and rec[j][4] is None)]
                lst.append(i)
            last_on_eng[eng] = i
            if dma:
                dma_ops.setdefault(dma, []).append(i)
        sig = [False] * n
        for i in range(n):
            if deps[i]:
                for j in deps[i]:
                    if rec[j][4] is None:
                        sig[j] = True
        cnt = {e: 0 for e in self.ENGS}
        ev = [None] * n
        for i, r in enumerate(rec):
            if r[0] != 'op':
                continue
            if r[4] is None and sig[i]:
                cnt[r[1]] += 1
                ev[i] = ('E_' + r[1], cnt[r[1]])
        self.waits = [()] * n
        self.incs = [None] * n
        seen = {e: {} for e in self.ENGS}
        for i, r in enumerate(rec):
            if r[0] != 'op':
                continue
            eng = r[1]
            wl = {}
            for j in deps[i]:
                if rec[j][4] is None:
                    s, v = ev[j]
                else:
                    s = 'D_' + rec[j][4]
                    v = 16 * bisect.bisect_left(dma_ops[rec[j][4]], i)
                if v > wl.get(s, 0):
                    wl[s] = v
            ws = []
            for s, v in wl.items():
                if v > seen[eng].get(s, 0):
                    seen[eng][s] = v
                    ws.append((s, v))
            self.waits[i] = tuple(ws)
            if r[4] is not None:
                self.incs[i] = ('D_' + r[4], 16)
            elif sig[i]:
                self.incs[i] = (ev[i][0], 1)
        self.semnames = ['E_' + e for e in self.ENGS] + ['D_' + s for s in dma_ops]
        self.stats = dict(n=n, cnt=cnt, nw=sum(len(w) for w in self.waits))

    def start_real(self, es):
        nc = self.nc
        self.dry = False
        self.i = 0
        self.H = {'pe': nc.tensor, 'dve': nc.vector, 'act': nc.scalar, 'pool': nc.gpsimd, 'sp': nc.sync}
        self.sems = {s: es.enter_context(nc.semaphore(s)) for s in self.semnames}


class Scope:
    _n = 0

    def __init__(self, K):
        self.K = K
        self.es = contextlib.ExitStack()

    def __enter__(self):
        return self

    def sb(self, name, shape, dt):
        if self.K.P.dry:
            return Dummy()
        Scope._n += 1
        return self.es.enter_context(self.K.nc.sbuf_tensor("%s_%d" % (name, Scope._n), list(shape), dt))

    def __exit__(self, *a):
        self.K.P.barrier()
        self.K.semmap = {}
        self.es.close()
        return False


def which(t):
    return 1 if t < 2 else 0


class K:
    def __init__(self, nc, P, dram):
        self.nc, self.P, self.d = nc, P, dram
        self.semmap = {}

    def mm(self, out, lhsT, rhs, start, stop, r, w):
        nc = self.nc
        self.P.op('pe', lambda: nc.tensor.matmul(out, lhsT=lhsT, rhs=rhs, start=start, stop=stop), r, w)

    def tr(self, out, in_, ident, r, w):
        nc = self.nc
        self.P.op('pe', lambda: nc.tensor.transpose(out, in_, ident), r, w)

    def act(self, out, in_, func, r, w, bias=None, scale=None, accum=None):
        nc = self.nc
        kw = {}
        if bias is not None:
            kw['bias'] = bias
        if scale is not None:
            kw['scale'] = scale
        if accum is not None:
            kw['accum_out'] = accum
        self.P.op('act', lambda: nc.scalar.activation(out=out, in_=in_, func=func, **kw), r, w)

    def v(self, eng, name, r, w, *a, **kw):
        nc = self.nc
        E = nc.vector if eng == 'dve' else nc.gpsimd
        self.P.op(eng, lambda: getattr(E, name)(*a, **kw), r, w)

    def dma(self, q, out, in_, r, w, sem):
        nc = self.nc
        E = nc.sync if q == 'sp' else nc.gpsimd
        k0 = w[0] if w else sem
        if isinstance(k0, tuple) and k0[0] in ('out', 'dbgxs'):
            name = sem
        elif isinstance(k0, tuple) and k0[0] == 'xs':
            name = 'xs'
        else:
            name = "b%d" % self.semmap.setdefault((k0, q), len(self.semmap))
        self.P.op(q, lambda: E.dma_start(out=out, in_=in_), r, w, dma=name + '_' + q)

    def build(self):
        nc, P, d = self.nc, self.P, self.d
        self.semmap = {}
        with Scope(self) as S:
            self.S0 = S
            self.xs = S.sb("xs", [128, NT, DM], F32)
            self.hT = S.sb("hT", [128, 8, TOK], BF16)
            self.identb = S.sb("identb", [128, 128], BF16)
            self.identf = S.sb("identf", [128, 128], F32)
            self.onesf = S.sb("onesf", [128, 128], F32)
            self.c2 = S.sb("c2", [128, 16], F32)
            self.scb = S.sb("scb", [128, 16], F32)
            self.modv = S.sb("modv", [128, 48, 2], F32)
            self.bmod = S.sb("bmod", [128, 2, 48], F32)
            self.gn = S.sb("gn", [128, 2, 2, 8], F32)
            self.G = S.sb("G", [128, 2, 8, 2], F32)
            self.gaB = S.sb("gaB", [128, 2, DM], F32)
            self.ssq = S.sb("ssq", [128, NT], F32)
            self.rstd = S.sb("rstd", [128, NT], F32)
            self.epsT = S.sb("epsT", [128, 1], F32)
            self.xn = [S.sb("xn%d" % i, [128, DM], BF16) for i in range(2)]
            self.junk = S.sb("junk", [128, DM], BF16)
            self.dg = [S.sb("dg%d" % i, [128, 128], F32) for i in range(2)]
            self.xt = [S.sb("xt%d" % i, [128, 512], F32) for i in range(2)]
            if not P.dry:
                es = S.es
                self.pb = [es.enter_context(nc.psum_tensor("pb%d" % i, [128, 512], F32)) for i in range(6)]
                self.ptr = [es.enter_context(nc.psum_tensor("ptr%d" % i, [128, 8, 128], BF16)) for i in range(2)]
            else:
                self.pb = [Dummy()] * 6
                self.ptr = [Dummy()] * 2
            self.cnt = 0
            self.dma('sp', self.identb[:], d['identb'][:, :], [], ['identb'], 'cst')
            self.dma('sp', self.identf[:], d['identf'][:, :], [], ['identf'], 'cst')
            self.dma('sp', self.c2[:], d['c2'][:, :], [], ['c2'], 'cst')
            self.dma('sp', self.bmod[:], d['bmodfm'][:, :, :], [], ['bmod'], 'cst')
            self.dma('sp', self.gn[:], d['gnfm'][:, :, :, :], [], ['gn'], 'cst')
            self.v('dve', 'memset', [], ['onesf'], self.onesf[:], 1.0)
            self.v('dve', 'memset', [], ['epsT'], self.epsT[:], EPS)
            self.v('dve', 'memset', [], ['ssq'], self.ssq[:], 0.0)
            self.act(self.scb[:], self.c2[:], AF.Silu, ['c2'], ['scb'])
            for t in range(NT):
                self.dma('sp', self.xs[:, t, :], d['xin'][t * 128:(t + 1) * 128, :], [], [('xs', t)], 'xin')
            for l in range(DEPTH):
                last = (l == DEPTH - 1)
                self.mod_phase(l)
                if STAGE <= 1:
                    break
                self.make_gaB(2)
                self.phaseA(l, 0, range(NT))
                if STAGE >= 3:
                    self.mixers(l, last)
                if DBGXS and l == 0:
                    for t in range(NT):
                        self.dma('sp', d['dbgxs'][t * 128:(t + 1) * 128, :], self.xs[:, t, :], [('xs', t)], [('dbgxs', t)], 'out')
                    break
                self.make_gaB(5)
                tiles = list(range(2, NT)) if last else list(range(NT))
                self.phaseA(l, 1, tiles)
                self.mlp(l, tiles)
            self.final()
        P.barrier()
        P.op('sp', None)

    def mod_phase(self, l):
        nc, P, d = self.nc, self.P, self.d
        with Scope(self) as S:
            wm = [S.sb("wm%d" % i, [128, 8, 1024], F32) for i in range(2)]
            modps = self.pb[0]
            for blk in range(6):
                s = blk % 2
                self.dma('sp', wm[s][:], d['w_mod'][l][:, blk * 1024:(blk + 1) * 1024].rearrange("(k p) c -> p k c", p=128),
                         [], [('wm', s)], 'wm%d' % s)
                for jj in range(8):
                    j = blk * 8 + jj
                    for k in range(8):
                        self.mm(modps[:, 2 * j:2 * j + 2], wm[s][:, k, jj * 128:(jj + 1) * 128], self.scb[:, 2 * k:2 * k + 2],
                                k == 0, k == 7, [('wm', s), 'scb'], [('pb', 0)])
            self.v('dve', 'tensor_tensor', [('pb', 0), 'bmod'], ['modv'], out=self.modv[:],
                   in0=modps[:, 0:96].rearrange("p (j w) -> p j w", w=2),
                   in1=self.bmod[:, l, :].unsqueeze(2).to_broadcast([128, 48, 2]), op=OP.add)
            for ni, vi in ((0, 1), (1, 4)):
                self.v('dve', 'scalar_tensor_tensor', ['modv', 'gn'], ['G'], out=self.G[:, ni, :, :],
                       in0=self.modv[:, vi * 8:(vi + 1) * 8, :], scalar=1.0,
                       in1=self.gn[:, l, ni, :].unsqueeze(2).to_broadcast([128, 8, 2]), op0=OP.add, op1=OP.mult)

    def make_gaB(self, vi):
        for w in range(2):
            for kk in range(8):
                self.cnt += 1
                dgi = self.cnt % 2
                self.v('dve', 'tensor_scalar', ['modv', 'identf'], [('dg', dgi)], out=self.dg[dgi][:], in0=self.identf[:],
                       scalar1=self.modv[:, vi * 8 + kk, w:w + 1], scalar2=None, op0=OP.mult)
                bank = kk // 4
                self.mm(self.pb[bank][:, (kk % 4) * 128:(kk % 4 + 1) * 128], self.onesf[:], self.dg[dgi][:], True, True,
                        ['onesf', ('dg', dgi)], [('pb', bank)])
            for bank in range(2):
                self.act(self.gaB[:, w, bank * 512:(bank + 1) * 512], self.pb[bank][:], AF.Identity, [('pb', bank)], ['gaB'])

    def rms(self, t):
        self.cnt += 1
        self.act(self.junk[:], self.xs[:, t, :], AF.Square, [('xs', t)], [('junk', self.cnt), ('ssq', t)],
                 accum=self.ssq[:, t:t + 1])
        self.act(self.rstd[:, t:t + 1], self.ssq[:, t:t + 1], AF.Sqrt, [('ssq', t), 'epsT'], [('rstd', t)],
                 bias=self.epsT[:], scale=1.0 / DM)
        self.v('dve', 'reciprocal', [('rstd', t)], [('rstd', t)], out=self.rstd[:, t:t + 1], in_=self.rstd[:, t:t + 1])
        self.v('dve', 'memset', [('ssq', t)], [('ssq', t)], self.ssq[:, t:t + 1], 0.0)

    def phaseA(self, l, ni, tiles):
        vi = 0 if ni == 0 else 3
        for t in tiles:
            w = which(t)
            self.rms(t)
            self.cnt += 1
            xi = self.cnt % 2
            self.act(self.xn[xi][:], self.xs[:, t, :], AF.Identity, [('xs', t), ('rstd', t)], [('xn', xi)],
                     scale=self.rstd[:, t:t + 1])
            for k in range(8):
                self.tr(self.ptr[xi][:, k, :], self.xn[xi][:, k * 128:(k + 1) * 128], self.identb[:],
                        [('xn', xi), 'identb'], [('ptr', xi)])
            for k in range(8):
                self.act(self.hT[:, k, t * 128:(t + 1) * 128], self.ptr[xi][:, k, :], AF.Identity,
                         [('ptr', xi), 'G', 'modv'], [('hT', t)],
                         bias=self.modv[:, vi * 8 + k, w:w + 1], scale=self.G[:, ni, k, w:w + 1])

    def xupd(self, t, nh, b):
        w = which(t)
        self.cnt += 1
        mi = self.cnt % 2
        self.v('dve', 'tensor_tensor', [('pb', b), 'gaB'], [('xt', mi)], out=self.xt[mi][:], in0=self.pb[b][:],
               in1=self.gaB[:, w, nh * 512:(nh + 1) * 512], op=OP.mult)
        self.v('pool', 'tensor_tensor', [('xt', mi), ('xs', t)], [('xs', t)],
               out=self.xs[:, t, nh * 512:(nh + 1) * 512], in0=self.xs[:, t, nh * 512:(nh + 1) * 512],
               in1=self.xt[mi][:], op=OP.add)

    def blocks(self, lo=0, hi=TOK, step=512):
        return [(o, min(step, hi - o)) for o in range(lo, hi, step)]

    def mixers(self, l, last):
        if 'F' in BRANCHES:
            self.fourier(l, last)
        if 'M' in BRANCHES:
            self.mlstm(l, last)
        if 'A' in BRANCHES:
            self.mla(l, last)

    def mlstm(self, l, last):
        nc, P, d = self.nc, self.P, self.d
        win = d['w_in'][l]
        with Scope(self) as SO:
            acol = SO.sb("acol", [128, NT, 8], F32)
            Mcol = SO.sb("Mcol", [128, NT, 8], F32)
            T4 = SO.sb("T4", [128, 4, NT, 8], F32)
            gm = SO.sb("gm", [128, 384], F32)
            E2 = SO.sb("E2", [8, 2, 8], F32)
            negid = SO.sb("negid", [128, 128], F32)
            mlmask = SO.sb("mlmask", [128, 2, 256], BF16)
            self.dma('sp', gm[:], d['gmB'][:, l, :], [], ['gm'], 'mw')
            self.dma('sp', E2[:], d['E2'][:, :, :], [], ['E2'], 'mw')
            self.dma('sp', negid[:], d['negidf'][:, :], [], ['negid'], 'mw')
            self.dma('sp', mlmask[:], d['mlmask'][:, :, :], [], ['mlmask'], 'mw')
            with Scope(self) as S:
                wg = S.sb("wg", [128, 8, 16], BF16)
                bg = S.sb("bg", [8, 2], F32)
                nbf = S.sb("nbf", [8, 1], F32)
                IG = S.sb("IG", [8, TOK], F32)
                LF = S.sb("LF", [8, TOK], F32)
                FF = S.sb("FF", [8, TOK], F32)
                FB = S.sb("FB", [8, TOK], F32)
                AFr = S.sb("AFr", [8, TOK], F32)
                ABr = S.sb("ABr", [8, TOK], F32)
                R1 = S.sb("R1", [8, NT, 8], F32)
                R2 = S.sb("R2", [8, NT, 8], F32)
                mcol = S.sb("mcol", [128, NT, 8], F32)
                MendB = S.sb("MendB", [128, NT, 8], F32)
                MP = S.sb("MP", [128, NT, 8], F32)
                ex = S.sb("ex", [128, 4, NT, 8], F32)
                self.dma('pool', wg[:], d['w_g'][l].rearrange("(k p) c -> p k c", p=128), [], ['wg'], 'gw')
                self.dma('sp', bg[:], d['bg'][l], [], ['bg'], 'gw')
                self.v('dve', 'tensor_scalar', ['bg'], ['nbf'], out=nbf[:], in0=bg[:, 1:2], scalar1=-1.0, scalar2=None, op0=OP.mult)
                nb = 0
                for (o, n) in self.blocks():
                    hk = [('hT', t) for t in range(o // 128, (o + n) // 128)]
                    bi, bf_ = nb % 6, (nb + 1) % 6
                    nb += 2
                    for k in range(8):
                        self.mm(self.pb[bi][0:8, 0:n], wg[:, k, 0:8], self.hT[:, k, o:o + n], k == 0, k == 7, hk + ['wg'], [('pb', bi)])
                    for k in range(8):
                        self.mm(self.pb[bf_][0:8, 0:n], wg[:, k, 8:16], self.hT[:, k, o:o + n], k == 0, k == 7, hk + ['wg'], [('pb', bf_)])
                    self.act(IG[:, o:o + n], self.pb[bi][0:8, 0:n], AF.Identity, [('pb', bi), 'bg'], ['IG'], bias=bg[:, 0:1])
                    self.act(LF[:, o:o + n], self.pb[bf_][0:8, 0:n], AF.Exp, [('pb', bf_), 'nbf'], ['LF'], bias=nbf[:], scale=-1.0)
                self.act(LF[:], LF[:], AF.Ln, ['LF'], ['LF'], bias=1.0)
                self.v('dve', 'tensor_scalar', ['LF'], ['LF'], out=LF[:], in0=LF[:], scalar1=-1.0, scalar2=None, op0=OP.mult)
                one_b = lambda n: self.onesf[0:8, 0:1].to_broadcast([8, n])
                self.v('dve', 'tensor_tensor_scan', ['LF', 'onesf'], ['FF'], out=FF[:], data0=one_b(TOK), data1=LF[:], initial=0.0,
                       op0=OP.mult, op1=OP.add)
                self.v('dve', 'tensor_tensor_scan', ['LF', 'onesf'], ['FB'], out=FB[:, 255::-1], data0=one_b(256), data1=LF[:, 255::-1],
                       initial=0.0, op0=OP.mult, op1=OP.add)
                self.v('dve', 'tensor_tensor_scan', ['LF', 'onesf', 'FB'], ['FB'], out=FB[:, 2303:255:-1], data0=one_b(2048),
                       data1=LF[:, 2303:255:-1], initial=FB[:, 0:1], op0=OP.mult, op1=OP.add)
                self.v('dve', 'tensor_tensor', ['IG', 'FF'], ['AFr'], out=AFr[:], in0=IG[:], in1=FF[:], op=OP.subtract)
                self.v('dve', 'tensor_tensor', ['IG', 'FB'], ['ABr'], out=ABr[:], in0=IG[:], in1=FB[:], op=OP.subtract)
                MF, MB = IG, LF
                self.v('dve', 'tensor_tensor_scan', ['AFr'], ['IG'], out=MF[:], data0=AFr[:], data1=AFr[:], initial=0.0, op0=OP.max, op1=OP.max)
                self.v('dve', 'tensor_tensor_scan', ['ABr'], ['LF'], out=MB[:, 255::-1], data0=ABr[:, 255::-1], data1=ABr[:, 255::-1],
                       initial=0.0, op0=OP.max, op1=OP.max)
                self.v('dve', 'tensor_tensor_scan', ['ABr', 'LF'], ['LF'], out=MB[:, 2303:255:-1], data0=ABr[:, 2303:255:-1],
                       data1=ABr[:, 2303:255:-1], initial=MB[:, 0:1], op0=OP.max, op1=OP.max)
                self.v('dve', 'tensor_tensor', ['FF', 'IG'], ['FF'], out=FF[:], in0=FF[:], in1=MF[:], op=OP.add)
                self.v('dve', 'tensor_tensor', ['FB', 'LF'], ['FB'], out=FB[:], in0=FB[:], in1=MB[:], op=OP.add)
                for qi, (XF, XB, kf, kb) in enumerate(((AFr, ABr, 'AFr', 'ABr'), (MF, MB, 'IG', 'LF'), (FF, FB, 'FF', 'FB'))):
                    for c in range(NT):
                        self.mm(self.pb[qi][:, c * 8:(c + 1) * 8], XF[:, c * 128:(c + 1) * 128], E2[:, 0, :], True, False, [kf, 'E2'], [('pb', qi)])
                        self.mm(self.pb[qi][:, c * 8:(c + 1) * 8], XB[:, c * 128:(c + 1) * 128], E2[:, 1, :], False, True, [kb, 'E2'], [('pb', qi)])
                for qi, (dst, kd) in enumerate(((acol, 'acol'), (Mcol, 'Mcol'), (mcol, 'mcol'))):
                    self.act(dst[:].rearrange("p c e -> p (c e)"), self.pb[qi][:, 0:NT * 8], AF.Identity, [('pb', qi)], [kd])
                self.v('dve', 'tensor_tensor', ['E2', 'IG'], ['R1'], out=R1[:], in0=E2[:, 0, :].unsqueeze(1).to_broadcast([8, NT, 8]),
                       in1=MF[:, 127::128].unsqueeze(2).to_broadcast([8, NT, 8]), op=OP.mult)
                self.v('dve', 'tensor_tensor', ['E2', 'LF'], ['R2'], out=R2[:], in0=E2[:, 1, :].unsqueeze(1).to_broadcast([8, NT, 8]),
                       in1=MB[:, 0::128].unsqueeze(2).to_broadcast([8, NT, 8]), op=OP.mult)
                self.v('dve', 'tensor_tensor', ['R1', 'R2'], ['R1'], out=R1[:], in0=R1[:], in1=R2[:], op=OP.add)
                self.mm(self.pb[3][:, 0:NT * 8], self.onesf[0:8, :], R1[:].rearrange("p c e -> p (c e)"), True, True, ['R1', 'onesf'], [('pb', 3)])
                self.act(MendB[:].rearrange("p c e -> p (c e)"), self.pb[3][:, 0:NT * 8], AF.Identity, [('pb', 3)], ['MendB'])
                self.v('dve', 'memset', [], ['MP'], MP[:], 0.0)
                self.v('dve', 'tensor_copy', ['MendB', 'MP'], ['MP'], out=MP[:, 1:NT, 0:4], in_=MendB[:, 0:NT - 1, 0:4])
                self.v('dve', 'tensor_copy', ['MendB', 'MP'], ['MP'], out=MP[:, 0:1, 4:8], in_=MendB[:, 1:2, 4:8])
                self.v('dve', 'tensor_copy', ['MendB', 'MP'], ['MP'], out=MP[:, 2:NT - 1, 4:8], in_=MendB[:, 3:NT, 4:8])
                self.v('dve', 'tensor_copy', ['MendB', 'MP'], ['MP'], out=MP[:, NT - 1:NT, 4:8], in_=MendB[:, 0:1, 4:8])
                self.v('dve', 'tensor_tensor', ['MP', 'Mcol'], ['ex'], out=ex[:, 0], in0=MP[:], in1=Mcol[:], op=OP.subtract)
                self.v('dve', 'tensor_tensor', ['acol', 'MendB'], ['ex'], out=ex[:, 1], in0=acol[:], in1=MendB[:], op=OP.subtract)
                self.v('dve', 'tensor_tensor', ['MP', 'MendB'], ['ex'], out=ex[:, 2], in0=MP[:], in1=MendB[:], op=OP.subtract)
                self.v('dve', 'tensor_scalar', ['mcol'], ['ex'], out=ex[:, 3], in0=mcol[:], scalar1=-1.0, scalar2=None, op0=OP.mult)
                self.act(T4[:].rearrange("p a c e -> p (a c e)"), ex[:].rearrange("p a c e -> p (a c e)"), AF.Exp, ['ex'], ['T4'])
            for hp in range(2):
                self.mlstm_pair(l, last, hp, acol, Mcol, T4, gm, negid, mlmask)

    def mlstm_pair(self, l, last, hp, acol, Mcol, T4, gm, negid, mlmask):
        nc, P, d = self.nc, self.P, self.d
        win = d['w_in'][l]
        QS = 96.0 ** -0.5
        colof = lambda T: T + 2 if T < 256 else T + 6
        cblocks = [(0, 256)] + [(256 + i * 512, 512) for i in range(4)]
        with Scope(self) as SP:
            post = SP.sb("post", [96, 4, TOK], BF16)
            vp = SP.sb("vp", [128, NT, 2, 97], BF16)
            sg = SP.sb("sg", [128, NT, 192], BF16)
            S = Scope(self)
            wvo = S.sb("wvo", [128, 8, 384], BF16)
            self.dma('pool', wvo[:, :, 0:192], win[:, 1024 + 192 * hp:1024 + 192 * (hp + 1)].rearrange("(k p) c -> p k c", p=128), [], ['wvo'], 'mw')
            self.dma('pool', wvo[:, :, 192:384], win[:, 1408 + 192 * hp:1408 + 192 * (hp + 1)].rearrange("(k p) c -> p k c", p=128), [], ['wvo'], 'mw')
            self.v('dve', 'memset', [], ['vp'], vp[:], 1.0)
            with S:
                wqk = S.sb("wqk", [128, 8, 4, 96], BF16)
                dgw = S.sb("dgw", [96, 4, 5, 96], BF16)
                pre = S.sb("pre", [96, 4, 2312], BF16)
                sgt = [S.sb("sgt%d" % i, [96, 512], F32) for i in range(1)]
                for jj in range(4):
                    c0 = (256 if jj < 2 else 640) + 96 * (2 * hp + jj % 2)
                    self.dma('pool', wqk[:, :, jj, :], win[:, c0:c0 + 96].rearrange("(k p) c -> p k c", p=128), [], ['wqk'], 'mw')
                    jg = (0 if jj < 2 else 4) + 2 * hp + jj % 2
                    self.dma('sp', dgw[:, jj, :, :], d['convdg'][l, :, jg, :, :], [], ['dgw'], 'mw')
                self.v('dve', 'memset', [], ['pre'], pre[:], 0.0)
                nb = 0
                for jj in range(4):
                    for (o, n) in cblocks:
                        b = nb % 6
                        nb += 1
                        hk = [('hT', t) for t in range(o // 128, (o + n) // 128)]
                        for k in range(8):
                            self.mm(self.pb[b][0:96, 0:n], wqk[:, k, jj, :], self.hT[:, k, o:o + n], k == 0, k == 7, hk + ['wqk'], [('pb', b)])
                        self.act(pre[:, jj, colof(o):colof(o) + n], self.pb[b][0:96, 0:n], AF.Identity, [('pb', b), 'pre'], [('pre', jj)])
                for t in range(NT):
                    b = nb % 6
                    nb += 1
                    for k in range(8):
                        self.mm(self.pb[b][:, 0:384], self.hT[:, k, t * 128:(t + 1) * 128], wvo[:, k, :], k == 0, k == 7, [('hT', t), 'wvo'], [('pb', b)])
                    self.act(vp[:, t, :, 0:96], self.pb[b][:, 0:192].rearrange("p (h c) -> p h c", c=96), AF.Identity, [('pb', b), 'vp'], [('vp', t)])
                    self.act(sg[:, t, :], self.pb[b][:, 192:384], AF.Sigmoid, [('pb', b)], [('sg', t)])
                for jj in range(4):
                    for (o, n) in cblocks:
                        b = nb % 6
                        nb += 1
                        c0 = colof(o)
                        for tap in range(5):
                            self.mm(self.pb[b][0:96, 0:n], dgw[:, jj, tap, :], pre[:, jj, c0 + tap - 2:c0 + tap - 2 + n], tap == 0, tap == 4,
                                    [('pre', jj), 'dgw'], [('pb', b)])
                        tk = [('post', jj, t) for t in range(o // 128, (o + n) // 128)]
                        if jj < 2:
                            si = 0
                            self.act(sgt[si][:, 0:n], self.pb[b][0:96, 0:n], AF.Sigmoid, [('pb', b)], [('sgt', si)])
                            self.v('dve', 'scalar_tensor_tensor', [('pb', b), ('sgt', si)], tk, out=post[:, jj, o:o + n], in0=self.pb[b][0:96, 0:n],
                                   scalar=QS, in1=sgt[si][:, 0:n], op0=OP.mult, op1=OP.mult)
                        else:
                            self.act(post[:, jj, o:o + n], self.pb[b][0:96, 0:n], AF.Silu, [('pb', b)], tk)
            ktok = SP.sb("ktok", [128, NT, 2, 96], BF16)
            for t in range(NT):
                pi = t % 2
                for hh in range(2):
                    self.tr(self.ptr[pi][:, hh, 0:96], post[:, 2 + hh, t * 128:(t + 1) * 128], self.identb[0:96, 0:96],
                            [('post', 2 + hh, t), 'identb'], [('ptr', pi)])
                self.act(ktok[:, t, :, :], self.ptr[pi][:, 0:2, 0:96], AF.Identity, [('ptr', pi)], [('ktok', t)])
            hF = SP.sb("hF", [128, NT, 2, 96], BF16)
            hB = SP.sb("hB", [128, NT, 2, 96], BF16)
            woM = SP.sb("woM", [96, 2, DM], BF16)
            with Scope(self) as S:
                dgM = [S.sb("dgM%d" % i, [128, 2, 128], F32) for i in range(2)]
                WT = [S.sb("WT%d" % i, [128, 2, 128], F32) for i in range(2)]
                PT = [S.sb("PT%d" % i, [128, 2, 128], BF16) for i in range(2)]
                tmpn = [S.sb("tmpn%d" % i, [128, 2, 97], F32) for i in range(2)]
                dn = [S.sb("dn%d" % i, [128, 2], F32) for i in range(2)]
                vw = [S.sb("vw%d" % i, [128, 2, 97], BF16) for i in range(2)]
                Cst = S.sb("Cst", [96, 2, 97], F32)
                Cball = S.sb("Cball", [96, NT, 2, 97], BF16)
                self.dma('pool', woM[:], d['w_out'][l][256 + 192 * hp:256 + 192 * (hp + 1), :].rearrange("(h p) c -> p h c", p=96), [], ['woM'], 'mw')
                for dr in range(2):
                    order = list(range(NT)) if dr == 0 else [1, 0] + list(range(NT - 1, 1, -1))
                    cf0 = dr * 4 + 2 * hp
                    self.v('dve', 'memset', ['Cst'], ['Cst'], Cst[:], 0.0)
                    self.v('dve', 'memset', [('Cball', 0)], [('Cball', 0)], Cball[:, 0], 0.0)

                    def kv_part(it):
                        c = order[it]
                        i2 = it % 2
                        self.v('dve', 'tensor_tensor', [('vp', c), 'T4'], [('vw', i2)], out=vw[i2][:], in0=vp[:, c],
                               in1=T4[:, 1, c, cf0:cf0 + 2].unsqueeze(2).to_broadcast([128, 2, 97]), op=OP.mult)
                        for hh in range(2):
                            self.mm(self.pb[4 + i2][0:96, hh * 97:(hh + 1) * 97], ktok[:, c, hh, :], vw[i2][:, hh, :], True, True,
                                    [('ktok', c), ('vw', i2)], [('pb', 4 + i2)])

                    kv_part(0)
                    for it in range(NT - 1):
                        c = order[it]
                        i2 = it % 2
                        if it + 1 < NT - 1:
                            kv_part(it + 1)
                        self.v('dve', 'tensor_tensor', ['Cst', 'T4'], ['Cst'], out=Cst[:], in0=Cst[:],
                               in1=T4[0:96, 2, c, cf0:cf0 + 2].unsqueeze(2).to_broadcast([96, 2, 97]), op=OP.mult)
                        self.v('dve', 'tensor_tensor', ['Cst', ('pb', 4 + i2)], ['Cst'], out=Cst[:], in0=Cst[:],
                               in1=self.pb[4 + i2][0:96, 0:194].rearrange("p (h e) -> p h e", e=97), op=OP.add)
                        self.act(Cball[:, it + 1], Cst[:], AF.Identity, ['Cst'], [('Cball', it + 1)])

                    def part_i(it):
                        c = order[it]
                        i2 = it % 2
                        if last and c < 2:
                            return
                        tsl = slice(c * 128, (c + 1) * 128)
                        bo = 3 + i2
                        self.v('dve', 'tensor_tensor', ['negid', 'Mcol'], [('dgM', i2)], out=dgM[i2][:],
                               in0=negid[:].unsqueeze(1).to_broadcast([128, 2, 128]),
                               in1=Mcol[:, c, cf0:cf0 + 2].unsqueeze(2).to_broadcast([128, 2, 128]), op=OP.mult)
                        self.mm(self.pb[1][:, 0:256], self.onesf[:], dgM[i2][:].rearrange("p h t -> p (h t)"), True, False,
                                [('dgM', i2), 'onesf'], [('pb', 1)])
                        self.mm(self.pb[1][:, 0:256], self.identb[:], mlmask[:, dr, :], False, True, ['mlmask', 'identb'], [('pb', 1)])
                        for hh in range(2):
                            self.act(WT[i2][:, hh, :], self.pb[1][:, hh * 128:(hh + 1) * 128], AF.Exp, [('pb', 1), 'acol'], [('WT', i2)],
                                     bias=acol[:, c, cf0 + hh:cf0 + hh + 1])
                        for hh in range(2):
                            self.mm(self.pb[2][:, hh * 128:(hh + 1) * 128], post[:, 2 + hh, tsl], post[:, hh, tsl], True, True,
                                    [('post', 2 + hh, c), ('post', hh, c)], [('pb', 2)])
                        self.v('dve', 'tensor_tensor', [('pb', 2), ('WT', i2)], [('PT', i2)], out=PT[i2][:].rearrange("p h t -> p (h t)"),
                               in0=self.pb[2][:, 0:256], in1=WT[i2][:].rearrange("p h t -> p (h t)"), op=OP.mult)
                        for hh in range(2):
                            self.mm(self.pb[bo][:, hh * 97:(hh + 1) * 97], PT[i2][:, hh, :], vp[:, c, hh, :], True, True,
                                    [('PT', i2), ('vp', c)], [('pb', bo)])
                        for hh in range(2):
                            self.mm(self.pb[bo][:, 256 + hh * 97:256 + (hh + 1) * 97], post[:, hh, tsl], Cball[:, it, hh, :], True, True,
                                    [('post', hh, c), ('Cball', it)], [('pb', bo)])

                    def part_d(it):
                        c = order[it]
                        i2 = it % 2
                        if last and c < 2:
                            return
                        bo = 3 + i2
                        for hh in range(2):
                            self.act(tmpn[i2][:, hh, :], self.pb[bo][:, 256 + hh * 97:256 + (hh + 1) * 97], AF.Identity, [('pb', bo), 'T4'],
                                     [('tmpn', i2)], scale=T4[:, 0, c, cf0 + hh:cf0 + hh + 1])
                        self.v('dve', 'tensor_tensor', [('tmpn', i2), ('pb', bo)], [('tmpn', i2)], out=tmpn[i2][:], in0=tmpn[i2][:],
                               in1=self.pb[bo][:, 0:194].rearrange("p (h e) -> p h e", e=97), op=OP.add)
                        self.v('dve', 'scalar_tensor_tensor', [('tmpn', i2)], [('dn', i2)], out=dn[i2][:], in0=tmpn[i2][:, :, 96], scalar=-1.0,
                               in1=tmpn[i2][:, :, 96], op0=OP.mult, op1=OP.max)
                        self.v('dve', 'tensor_tensor', [('dn', i2), 'T4'], [('dn', i2)], out=dn[i2][:], in0=dn[i2][:],
                               in1=T4[:, 3, c, cf0:cf0 + 2], op=OP.max)
                        self.v('dve', 'reciprocal', [('dn', i2)], [('dn', i2)], out=dn[i2][:], in_=dn[i2][:])
                        if dr == 0:
                            self.v('dve', 'tensor_tensor', [('tmpn', i2), ('dn', i2)], [('hF', c)], out=hF[:, c], in0=tmpn[i2][:, :, 0:96],
                                   in1=dn[i2][:].unsqueeze(2).to_broadcast([128, 2, 96]), op=OP.mult)
                            return
                        self.v('dve', 'tensor_tensor', [('tmpn', i2), ('dn', i2)], [('hB', c)], out=hB[:, c], in0=tmpn[i2][:, :, 0:96],
                               in1=dn[i2][:].unsqueeze(2).to_broadcast([128, 2, 96]), op=OP.mult)

                    part_i(0)
                    for it in range(NT):
                        if it + 1 < NT:
                            part_i(it + 1)
                        part_d(it)

            with Scope(self) as S:
                hsC = S.sb("hsC", [128, 6, 2, 96], F32)
                hqC = S.sb("hqC", [128, 6, 2, 96], F32)
                ssC = S.sb("ssC", [128, 12], F32)
                ybC = S.sb("ybC", [128, 6, 192], BF16)
                yhT = [S.sb("yhT%d" % i, [96, 2, 128], BF16) for i in range(2)]
                otiles = list(range(2, NT)) if last else list(range(NT))
                for g0 in range(0, len(otiles), 6):
                    g = otiles[g0:g0 + 6]
                    ng = len(g)
                    c0, c1 = g[0], g[-1] + 1
                    hk = [('hF', c) for c in g] + [('hB', c) for c in g]
                    self.v('dve', 'tensor_tensor', hk, ['hsC'], out=hsC[:, 0:ng], in0=hF[:, c0:c1], in1=hB[:, c0:c1], op=OP.add)
                    self.v('dve', 'tensor_tensor', ['hsC'], ['hqC'], out=hqC[:, 0:ng], in0=hsC[:, 0:ng], in1=hsC[:, 0:ng], op=OP.mult)
                    self.v('dve', 'tensor_reduce', ['hqC'], ['ssC'], out=ssC[:, 0:2 * ng],
                           in_=hqC[:, 0:ng].rearrange("p c h e -> p (c h) e"), axis=AX.X, op=OP.add)
                    self.act(ssC[:, 0:2 * ng], ssC[:, 0:2 * ng], AF.Ln, ['ssC', 'epsT'], ['ssC'], bias=self.epsT[:], scale=1.0 / 96)
                    self.act(ssC[:, 0:2 * ng], ssC[:, 0:2 * ng], AF.Exp, ['ssC'], ['ssC'], scale=-0.5)
                    self.v('dve', 'tensor_tensor', ['hsC', 'ssC'], ['hqC'], out=hqC[:, 0:ng].rearrange("p c h e -> p (c h) e"),
                           in0=hsC[:, 0:ng].rearrange("p c h e -> p (c h) e"),
                           in1=ssC[:, 0:2 * ng].unsqueeze(2).to_broadcast([128, 2 * ng, 96]), op=OP.mult)
                    self.v('dve', 'tensor_tensor', ['hqC', 'gm'], ['hqC'], out=hqC[:, 0:ng].rearrange("p c h e -> p c (h e)"),
                           in0=hqC[:, 0:ng].rearrange("p c h e -> p c (h e)"),
                           in1=gm[:, 192 * hp:192 * (hp + 1)].unsqueeze(1).to_broadcast([128, ng, 192]), op=OP.mult)
                    self.v('dve', 'tensor_tensor', ['hqC'] + [('sg', c) for c in g], ['ybC'], out=ybC[:, 0:ng],
                           in0=hqC[:, 0:ng].rearrange("p c h e -> p c (h e)"), in1=sg[:, c0:c1, :], op=OP.mult)
                    for ci, c in enumerate(g):
                        i2 = c % 2
                        for hh in range(2):
                            self.tr(self.ptr[i2][0:96, hh, :], ybC[:, ci, hh * 96:(hh + 1) * 96], self.identb[:], ['ybC', 'identb'], [('ptr', i2)])
                        self.act(yhT[i2][:], self.ptr[i2][0:96, 0:2, :], AF.Identity, [('ptr', i2)], [('yhT', i2)])
                        for nh in range(2):
                            b = 2 * i2 + nh
                            for hh in range(2):
                                self.mm(self.pb[b][:], yhT[i2][:, hh, :], woM[:, hh, nh * 512:(nh + 1) * 512], hh == 0, hh == 1,
                                        [('yhT', i2), 'woM'], [('pb', b)])
                            self.xupd(c, nh, b)

    def mla(self, l, last):
        nc, P, d = self.nc, self.P, self.d
        SCALE = 96.0 ** -0.5
        with Scope(self) as S:
            wA = S.sb("wA", [128, 8, 416], BF16)
            wuq = S.sb("wuq", [128, 2, 384], BF16)
            wuqs = S.sb("wuqs", [128, 2, 4, 96], BF16)
            wkr = S.sb("wkr", [128, 8, 96], BF16)
            wkrs = S.sb("wkrs", [128, 8, 96], BF16)
            wukv = S.sb("wukv", [128, 640], BF16)
            woA = S.sb("woA", [96, 4, DM], BF16)
            ropeC = S.sb("ropeC", [96, TOK], BF16)
            ropeS = S.sb("ropeS", [96, TOK], BF16)
            Vt = S.sb("Vt", [128, NT, 384], BF16)
            cqnT = S.sb("cqnT", [128, 2, 512], BF16)
            ckvnT = S.sb("ckvnT", [128, 512], BF16)
            cqn = [S.sb("cqn%d" % i, [128, 256], BF16) for i in range(2)]
            ckvn = [S.sb("ckvn%d" % i, [128, 128], BF16) for i in range(2)]
            rt1 = S.sb("rt1", [96, 512], F32)
            rt2 = S.sb("rt2", [96, 512], F32)
            pT = [S.sb("pT%d" % i, [128, 512], BF16) for i in range(3)]
            rc = S.sb("rc", [96, 512], F32)
            yaT = S.sb("yaT", [96, 4, 512], BF16)
            onesb = S.sb("onesb", [128, 96], BF16)
            st = [S.sb("st%d" % i, [128, 2], F32) for i in range(2)]
            rs = [S.sb("rs%d" % i, [128, 2], F32) for i in range(2)]
            gq = S.sb("gq", [128, 2, 2], F32)
            gkv = S.sb("gkv", [128, 2], F32)
            qT = self.hT[0:96, 0:4, :]
            kT = self.hT[0:96, 4:8, :]
            win = d['w_in'][l]
            self.dma('pool', wA[:], win[:, 1808:2224].rearrange("(k p) c -> p k c", p=128), [], ['wA'], 'aw')
            self.dma('pool', wuq[:], d['w_uq'][l].rearrange("(k p) c -> p k c", p=128), [], ['wuq'], 'aw')
            self.v('dve', 'memset', [], ['wuqs'], wuqs[:], 0.0)
            self.v('dve', 'memset', [], ['wkr'], wkr[:], 0.0)
            self.v('dve', 'memset', [], ['wkrs'], wkrs[:], 0.0)
            self.v('dve', 'memset', [], ['onesb'], onesb[:], 1.0)
            for c in range(2):
                self.dma('pool', wuqs[:, c, :, 64:96], d['w_uq_sw'][l][c * 128:(c + 1) * 128, :, :], ['wuqs'], ['wuqs'], 'aw')
            self.dma('pool', wkr[:, :, 64:96], win[:, 2192:2224].rearrange("(k p) c -> p k c", p=128), ['wkr'], ['wkr'], 'aw')
            self.dma('pool', wkrs[:, :, 64:96], d['w_kr_sw'][l].rearrange("(k p) c -> p k c", p=128), ['wkrs'], ['wkrs'], 'aw')
            self.dma('pool', wukv[:], d['w_ukv'][l], [], ['wukv'], 'aw')
            self.dma('pool', woA[:], d['w_out'][l][640:1024, :].rearrange("(h p) c -> p h c", p=96), [], ['woA'], 'aw')
            self.dma('sp', ropeC[64:96, :], d['ropeC'][:, :], [], ['rope'], 'aw')
            self.dma('sp', ropeS[64:96, :], d['ropeS'][:, :], [], ['rope'], 'aw')
            self.dma('sp', gq[:], d['gqfm'][:, :, :], [], ['gq'], 'aw')
            self.dma('sp', gkv[:], d['gkvfm'][:, :], [], ['gq'], 'aw')
            wv = wukv[:, :].rearrange("p (h c) -> p h c", c=160)[:, :, 64:160]
            for (o, n) in self.blocks():
                tl = list(range(o // 128, (o + n) // 128))
                hk = [('hT', t) for t in tl]
                for ti, t in enumerate(tl):
                    b1 = t % 2
                    for k in range(8):
                        self.mm(self.pb[b1][:, 0:416], self.hT[:, k, t * 128:(t + 1) * 128], wA[:, k, :], k == 0, k == 7,
                                [('hT', t), 'wA'], [('pb', b1)])
                    si = t % 2
                    self.v('dve', 'memset', [], [('st', si)], st[si][:], 0.0)
                    self.cnt += 1
                    self.act(self.junk[:, 0:256], self.pb[b1][:, 0:256], AF.Square, [('pb', b1)], [('junk', self.cnt), ('st', si)],
                             accum=st[si][:, 0:1])
                    self.cnt += 1
                    self.act(self.junk[:, 256:384], self.pb[b1][:, 256:384], AF.Square, [('pb', b1)], [('junk', self.cnt), ('st', si)],
                             accum=st[si][:, 1:2])
                    self.act(rs[si][:, 0:1], st[si][:, 0:1], AF.Sqrt, [('st', si), 'epsT'], [('rs', si)], bias=self.epsT[:], scale=1.0 / 256)
                    self.act(rs[si][:, 1:2], st[si][:, 1:2], AF.Sqrt, [('st', si), 'epsT'], [('rs', si)], bias=self.epsT[:], scale=1.0 / 128)
                    self.v('dve', 'reciprocal', [('rs', si)], [('rs', si)], out=rs[si][:], in_=rs[si][:])
                    self.v('dve', 'tensor_scalar', [('pb', b1), ('rs', si)], [('cqn', si)], out=cqn[si][:], in0=self.pb[b1][:, 0:256],
                           scalar1=rs[si][:, 0:1], scalar2=None, op0=OP.mult)
                    self.v('dve', 'tensor_scalar', [('pb', b1), ('rs', si)], [('ckvn', si)], out=ckvn[si][:], in0=self.pb[b1][:, 256:384],
                           scalar1=rs[si][:, 1:2], scalar2=None, op0=OP.mult)
                    pi = t % 2
                    for c in range(2):
                        self.tr(self.ptr[pi][:, c, :], cqn[si][:, c * 128:(c + 1) * 128], self.identb[:], [('cqn', si), 'identb'], [('ptr', pi)])
                    self.tr(self.ptr[pi][:, 2, :], ckvn[si][:], self.identb[:], [('ckvn', si), 'identb'], [('ptr', pi)])
                    for c in range(2):
                        self.act(cqnT[:, c, ti * 128:(ti + 1) * 128], self.ptr[pi][:, c, :], AF.Identity, [('ptr', pi), 'gq'], ['cqnT'],
                                 scale=gq[:, l, c:c + 1])
                    self.act(ckvnT[:, ti * 128:(ti + 1) * 128], self.ptr[pi][:, 2, :], AF.Identity, [('ptr', pi), 'gq'], ['ckvnT'],
                             scale=gkv[:, l:l + 1])
                    bv = 2 + t % 2
                    self.mm(self.pb[bv][:, 0:384], ckvnT[:, ti * 128:(ti + 1) * 128], wv, True, True, ['ckvnT', 'wukv'], [('pb', bv)])
                    self.act(Vt[:, t, :], self.pb[bv][:, 0:384], AF.Identity, [('pb', bv)], [('Vt', t)])
                for k in range(8):
                    self.mm(self.pb[4][0:96, 0:n], wkr[:, k, :], self.hT[:, k, o:o + n], k == 0, k == 7, hk + ['wkr'], [('pb', 4)])
                for k in range(8):
                    self.mm(self.pb[5][0:96, 0:n], wkrs[:, k, :], self.hT[:, k, o:o + n], k == 0, k == 7, hk + ['wkrs'], [('pb', 5)])
                for h in range(4):
                    for c in range(2):
                        self.mm(self.pb[0][0:96, 0:n], wuq[:, c, h * 96:(h + 1) * 96], cqnT[:, c, 0:n], c == 0, c == 1, ['wuq', 'cqnT'], [('pb', 0)])
                    for c in range(2):
                        self.mm(self.pb[1][0:96, 0:n], wuqs[:, c, h, :], cqnT[:, c, 0:n], c == 0, c == 1, ['wuqs', 'cqnT'], [('pb', 1)])
                    bk = 2 + h % 2
                    self.mm(self.pb[bk][0:64, 0:n], wukv[:, h * 160:h * 160 + 64], ckvnT[:, 0:n], True, True, ['wukv', 'ckvnT'], [('pb', bk)])
                    self.act(qT[0:64, h, o:o + n], self.pb[0][0:64, 0:n], AF.Identity, [('pb', 0)], hk)
                    self.act(kT[0:64, h, o:o + n], self.pb[bk][0:64, 0:n], AF.Identity, [('pb', bk)], hk)
                    self.v('dve', 'tensor_tensor', [('pb', 0), 'rope'], ['rt1'], out=rt1[64:96, 0:n], in0=self.pb[0][64:96, 0:n],
                           in1=ropeC[64:96, o:o + n], op=OP.mult)
                    self.v('dve', 'tensor_tensor', [('pb', 1), 'rope'], ['rt2'], out=rt2[64:96, 0:n], in0=self.pb[1][64:96, 0:n],
                           in1=ropeS[64:96, o:o + n], op=OP.mult)
                    self.v('pool', 'tensor_tensor', ['rt1', 'rt2'], hk, out=qT[64:96, h, o:o + n], in0=rt1[64:96, 0:n],
                           in1=rt2[64:96, 0:n], op=OP.add)
                self.v('dve', 'tensor_tensor', [('pb', 4), 'rope'], ['rt1'], out=rt1[64:96, 0:n], in0=self.pb[4][64:96, 0:n],
                       in1=ropeC[64:96, o:o + n], op=OP.mult)
                self.v('dve', 'tensor_tensor', [('pb', 5), 'rope'], ['rt2'], out=rt2[64:96, 0:n], in0=self.pb[5][64:96, 0:n],
                       in1=ropeS[64:96, o:o + n], op=OP.mult)
                for h in range(4):
                    self.v('pool', 'tensor_tensor', ['rt1', 'rt2'], hk, out=kT[64:96, h, o:o + n], in0=rt1[64:96, 0:n],
                           in1=rt2[64:96, 0:n], op=OP.add)
            qblocks = [(256 + i * 512, 512, list(range(NT))) for i in range(4)]
            if not last:
                qblocks.append((0, 256, [0, 1]))
            for (qo, qn, ktiles) in qblocks:
                qk = [('hT', t) for t in range(qo // 128, (qo + qn) // 128)]
                for h in range(4):
                    def st_mm(i):
                        kt_ = ktiles[i]
                        self.mm(self.pb[i % 2][:, 0:qn], kT[:, h, kt_ * 128:(kt_ + 1) * 128], qT[:, h, qo:qo + qn], True, True,
                                qk + [('hT', kt_)], [('pb', i % 2)])
                    st_mm(0)
                    for i, kt in enumerate(ktiles):
                        sb_ = i % 2
                        pi = i % 3
                        if i + 1 < len(ktiles):
                            st_mm(i + 1)
                        self.act(pT[pi][:, 0:qn], self.pb[sb_][:, 0:qn], AF.Exp, [('pb', sb_)], [('pT', pi)], scale=SCALE)
                        self.mm(self.pb[2][0:96, 0:qn], Vt[:, kt, h * 96:(h + 1) * 96], pT[pi][:, 0:qn], i == 0, i == len(ktiles) - 1,
                                [('Vt', kt), ('pT', pi)], [('pb', 2)])
                        self.mm(self.pb[3][0:96, 0:qn], onesb[:, :], pT[pi][:, 0:qn], i == 0, i == len(ktiles) - 1,
                                ['onesb', ('pT', pi)], [('pb', 3)])
                    self.v('dve', 'reciprocal', [('pb', 3)], ['rc'], out=rc[:, 0:qn], in_=self.pb[3][0:96, 0:qn])
                    self.v('dve', 'tensor_tensor', [('pb', 2), 'rc'], [('yaT', h)], out=yaT[:, h, 0:qn], in0=self.pb[2][0:96, 0:qn],
                           in1=rc[:, 0:qn], op=OP.mult)
                for ti, t in enumerate(range(qo // 128, (qo + qn) // 128)):
                    for nh in range(2):
                        b = 4 + nh
                        for h in range(4):
                            self.mm(self.pb[b][:], yaT[:, h, ti * 128:(ti + 1) * 128], woA[:, h, nh * 512:(nh + 1) * 512], h == 0, h == 3,
                                    [('yaT', h), 'woA'], [('pb', b)])
                        self.xupd(t, nh, b)

    def fourier(self, l, last):
        nc, P, d = self.nc, self.P, self.d
        with Scope(self) as S:
            wpf = S.sb("wpf", [128, 8, 256], BF16)
            pfT = S.sb("pfT", [128, 2, TOK], BF16)
            bd = S.sb("bd", [128, 256], BF16)
            pfCS = S.sb("pfCS", [128, NT, 2, 256], BF16)
            tw = [S.sb("tw%d" % i, [128, 2, 1024], BF16) for i in range(3)]
            yfT = S.sb("yfT", [128, 2, TOK], BF16)
            wo = S.sb("wo", [128, 2, DM], BF16)
            c256 = S.sb("c256", [128, 2, 2, 256], BF16)
            self.dma('pool', wpf[:], d['w_in'][l][:, 0:256].rearrange("(k p) c -> p k c", p=128), [], ['wpf'], 'fw')
            self.dma('pool', wo[:], d['w_out'][l][0:256, :].rearrange("(k p) c -> p k c", p=128), [], ['wo'], 'fw')
            self.dma('sp', bd[:], d['bdcs'][:, :], [], ['bd'], 'fw')
            self.dma('sp', c256[:], d['c256'][:, :, :, :], [], ['c256'], 'fw')
            nb = 0
            for j in range(2):
                for (o, n) in self.blocks():
                    b = nb % 6
                    nb += 1
                    for k in range(8):
                        self.mm(self.pb[b][:, 0:n], wpf[:, k, j * 128:(j + 1) * 128], self.hT[:, k, o:o + n], k == 0, k == 7,
                                ['wpf'] + [('hT', t) for t in range(o // 128, (o + n) // 128)], [('pb', b)])
                    self.act(pfT[:, j, o:o + n], self.pb[b][:, 0:n], AF.Identity, [('pb', b)], [('pfT', j, o)])
            for t in range(NT):
                for j in range(2):
                    b = nb % 6
                    nb += 1
                    self.mm(self.pb[b][:, 0:256], pfT[:, j, t * 128:(t + 1) * 128], bd[:], True, True,
                            [('pfT', j, (t // 4) * 512), 'bd'], [('pb', b)])
                    self.act(pfCS[:, t, j, :], self.pb[b][:, 0:256], AF.Identity, [('pb', b)], [('pfCS', t)])
            ntw = 0
            for half in range(2):
                for nt in range(16):
                    t = nt + 2
                    s = ntw % 3
                    ntw += 1
                    self.dma('sp', tw[s][:, 0, :], d['twC'][nt * 128:(nt + 1) * 128, half * 1024:(half + 1) * 1024], [], [('tw', s)], 'tw%d' % s)
                    self.dma('sp', tw[s][:, 1, :], d['twS'][nt * 128:(nt + 1) * 128, half * 1024:(half + 1) * 1024], [], [('tw', s)], 'tw%d' % s)
                    for j in range(2):
                        for q in range(2):
                            b = j * 2 + q
                            self.mm(self.pb[b][:], pfCS[:, t, j, 0:128], tw[s][:, 0, q * 512:(q + 1) * 512], nt == 0, False,
                                    [('pfCS', t), ('tw', s)], [('pb', b)])
                            self.mm(self.pb[b][:], pfCS[:, t, j, 128:256], tw[s][:, 1, q * 512:(q + 1) * 512], False, nt == 15,
                                    [('pfCS', t), ('tw', s)], [('pb', b)])
                for j in range(2):
                    for q in range(2):
                        b = j * 2 + q
                        o = 256 + half * 1024 + q * 512
                        self.act(yfT[:, j, o:o + 512], self.pb[b][:], AF.Identity, [('pb', b)], [('yfT', o // 128)])
            if not last:
                for j in range(2):
                    b = 4 + j
                    for nt in range(2):
                        self.mm(self.pb[b][:, 0:256], pfCS[:, nt, j, 0:128], c256[:, nt, 0, :], nt == 0, False,
                                [('pfCS', nt), 'c256'], [('pb', b)])
                        self.mm(self.pb[b][:, 0:256], pfCS[:, nt, j, 128:256], c256[:, nt, 1, :], False, nt == 1,
                                [('pfCS', nt), 'c256'], [('pb', b)])
                    self.act(yfT[:, j, 0:256], self.pb[b][:, 0:256], AF.Identity, [('pb', b)], [('yfT', 0)])
            nb = 0
            for t in (range(2, NT) if last else range(NT)):
                for nh in range(2):
                    b = nb % 6
                    nb += 1
                    for j in range(2):
                        self.mm(self.pb[b][:], yfT[:, j, t * 128:(t + 1) * 128], wo[:, j, nh * 512:(nh + 1) * 512], j == 0, j == 1,
                                [('yfT', 0 if t < 2 else 2 + ((t - 2) // 4) * 4), 'wo'], [('pb', b)])
                    self.xupd(t, nh, b)

    def mlp(self, l, tiles):
        nc, P, d = self.nc, self.P, self.d
        groups = [tiles[i:i + 6] for i in range(0, len(tiles), 6)]
        with Scope(self) as S:
            uT = S.sb("uT", [128, 32, 768], BF16)
            wu = [S.sb("wu%d" % i, [128, 8, 512], BF16) for i in range(2)]
            wd = [S.sb("wd%d" % i, [128, 4, 512], BF16) for i in range(2)]
            tmp = [S.sb("mt%d" % i, [128, 512], F32) for i in range(2)]
            nu = nd = nb = 0
            for g in groups:
                G = 128 * len(g)
                t0 = g[0] * 128
                subs = [(0, G)] if G <= 512 else [(0, G // 2), (G // 2, G // 2)]
                for hb in range(8):
                    s = nu % 2
                    nu += 1
                    self.dma('pool', wu[s][:], d['w_up'][l][:, hb * 512:(hb + 1) * 512].rearrange("(k p) c -> p k c", p=128),
                             [], [('wu', s)], 'wu%d' % s)
                    for cc in range(4):
                        j = hb * 4 + cc
                        for (o, n) in subs:
                            b = nb % 6
                            nb += 1
                            for k in range(8):
                                self.mm(self.pb[b][:, 0:n], wu[s][:, k, cc * 128:(cc + 1) * 128],
                                        self.hT[:, k, t0 + o:t0 + o + n], k == 0, k == 7,
                                        [('wu', s)] + [('hT', t) for t in g], [('pb', b)])
                            ri = nb % 2
                            self.act(tmp[ri][:, 0:n], self.pb[b][:, 0:n], AF.Relu, [('pb', b)], [('mt', ri)])
                            self.v('dve', 'tensor_tensor', [('mt', ri)], [('uT', j)], out=uT[:, j, o:o + n],
                                   in0=tmp[ri][:, 0:n], in1=tmp[ri][:, 0:n], op=OP.mult)
                for nh in range(2):
                    for jb in range(8):
                        s = nd % 2
                        nd += 1
                        self.dma('pool', wd[s][:],
                                 d['w_down'][l][jb * 512:(jb + 1) * 512, nh * 512:(nh + 1) * 512].rearrange("(j p) c -> p j c", p=128),
                                 [], [('wd', s)], 'wd%d' % s)
                        for jj in range(4):
                            j = jb * 4 + jj
                            for ti, t in enumerate(g):
                                self.mm(self.pb[ti][:], uT[:, j, ti * 128:(ti + 1) * 128], wd[s][:, jj, :], j == 0, j == 31,
                                        [('wd', s), ('uT', j)], [('pb', ti)])
                    for ti, t in enumerate(g):
                        self.xupd(t, nh, ti)

    def final(self):
        nc, P, d = self.nc, self.P, self.d
        with Scope(self) as S:
            gf = S.sb("gf", [128, DM], F32)
            ob = [S.sb("ob%d" % i, [128, DM], F32) for i in range(2)]
            self.dma('sp', gf[:], d['gfin'][:, :], [], ['gf'], 'cst')
            for t in range(2, NT):
                self.rms(t)
                i = t % 2
                self.act(ob[i][:], self.xs[:, t, :], AF.Identity, [('xs', t), ('rstd', t)], [('ob', i)], scale=self.rstd[:, t:t + 1])
                self.v('dve', 'tensor_tensor', [('ob', i), 'gf'], [('ob', i)], out=ob[i][:], in0=ob[i][:], in1=gf[:], op=OP.mult)
                self.dma('sp', d['out'][(t - 2) * 128:(t - 1) * 128, :], ob[i][:], [('ob', i)], [('out', t)], 'out')


def dram_decls(nc):
    d = {}

    def inp(name, shape, dt=F32):
        d[name] = nc.dram_tensor(name, list(shape), dt, kind="ExternalInput").ap()

    inp('xin', [TOK, DM])
    inp('c2', [128, 16])
    inp('w_mod', [2, 1024, 6144])
    inp('bmodfm', [128, 2, 48])
    inp('gnfm', [128, 2, 2, 8])
    inp('w_in', [2, 1024, 2224])
    inp('w_out', [2, 1024, 1024])
    inp('w_up', [2, 1024, 4096])
    inp('w_down', [2, 4096, 1024])
    inp('gfin', [128, DM])
    inp('identb', [128, 128], BF16)
    inp('identf', [128, 128])
    inp('w_uq', [2, 256, 384])
    inp('w_ukv', [2, 128, 640])
    inp('w_uq_sw', [2, 256, 4, 32])
    inp('w_kr_sw', [2, 1024, 32])
    inp('ropeC', [32, TOK], BF16)
    inp('ropeS', [32, TOK], BF16)
    inp('gqfm', [128, 2, 2])
    inp('gkvfm', [128, 2])
    inp('w_g', [2, 1024, 16])
    inp('bg', [2, 8, 2])
    inp('convdg', [2, 96, 8, 5, 96], BF16)
    inp('gmB', [128, 2, 384])
    inp('E2', [8, 2, 8])
    inp('negidf', [128, 128])
    inp('mlmask', [128, 2, 256], BF16)
    inp('bdcs', [128, 256], BF16)
    inp('twC', [2048, 2048], BF16)
    inp('twS', [2048, 2048], BF16)
    inp('c256', [128, 2, 2, 256], BF16)
    d['out'] = nc.dram_tensor('out', [2048, DM], F32, kind="ExternalOutput").ap()
    if DBGXS:
        d['dbgxs'] = nc.dram_tensor('dbgxs', [TOK, DM], F32, kind="ExternalOutput").ap()
    return d


def build_program():
    nc = bass.Bass("TRN2", target_bir_lowering=False)
    d = dram_decls(nc)
    P = Prog(nc)
    kb = K(nc, P, d)
    kb.build()
    P.plan()
    es = contextlib.ExitStack()
    with es:
        P.start_real(es)
        kb.build()
    return nc, P


def host_inputs(inputs, b):
    f = lambda a: np.ascontiguousarray(np.asarray(a, dtype=np.float32))
    x, c, ctx, c_ctx = inputs['x'], inputs['c'], inputs['ctx'], inputs['c_ctx']
    m = {}
    m['xin'] = f(np.concatenate([ctx[b], x[b]], axis=0))
    c2 = np.stack([np.asarray(c[b]).reshape(8, 128).T, np.asarray(c_ctx).reshape(8, 128).T], axis=-1)
    m['c2'] = f(c2.reshape(128, 16))
    m['w_mod'] = f(inputs['w_mod'])
    m['bmodfm'] = f(np.asarray(inputs['b_mod']).reshape(2, 48, 128).transpose(2, 0, 1))
    gn = np.stack([np.asarray(inputs['g_norm1']), np.asarray(inputs['g_norm2'])], axis=1)
    m['gnfm'] = f(gn.reshape(2, 2, 8, 128).transpose(3, 0, 1, 2))
    m['w_in'] = f(inputs['w_in'])
    m['w_out'] = f(inputs['w_out'])
    m['w_up'] = f(inputs['w_up'])
    m['w_down'] = f(inputs['w_down'])
    m['gfin'] = f(np.tile(np.asarray(inputs['g_final'])[None, :], (128, 1)))
    m['identb'] = np.eye(128, dtype=np.float32).astype(ml_dtypes.bfloat16)
    m['identf'] = np.eye(128, dtype=np.float32)
    m.update(host_consts())
    perm = np.array([a * 16 + (1 - b_) * 8 + j for a in range(2) for b_ in range(2) for j in range(8)])
    wuq = np.asarray(inputs['w_uq'])
    m['w_uq'] = f(wuq)
    m['w_ukv'] = f(inputs['w_ukv'])
    m['w_uq_sw'] = f(wuq.reshape(2, 256, 4, 96)[:, :, :, 64:96][..., perm])
    m['w_kr_sw'] = f(np.asarray(inputs['w_in'])[:, :, 2192:2224][..., perm])
    m['gqfm'] = f(np.asarray(inputs['g_q_norm']).reshape(2, 2, 128).transpose(2, 0, 1))
    m['gkvfm'] = f(np.asarray(inputs['g_kv_norm']).T)
    gp = [0, 1, 2, 3, 8, 9, 10, 11, 4, 5, 6, 7, 12, 13, 14, 15]
    m['w_g'] = f(np.asarray(inputs['w_in'])[:, :, 1792:1808][..., gp])
    m['bg'] = f(np.asarray(inputs['b_gates'])[:, gp].reshape(2, 2, 8).transpose(0, 2, 1))
    cv = np.asarray(inputs['conv_qk'], dtype=np.float32).reshape(2, 5, 8, 96)
    dgc = np.zeros((2, 96, 8, 5, 96), np.float32)
    pi_ = np.arange(96)
    dgc[:, pi_, :, :, pi_] = cv.transpose(3, 0, 2, 1)
    m['convdg'] = dgc.astype(ml_dtypes.bfloat16)
    m['gmB'] = f(np.tile(np.asarray(inputs['g_mlstm'])[None, :, :], (128, 1, 1)))
    return m


def host_consts():
    bf = lambda a: np.ascontiguousarray(a.astype(np.float32)).astype(ml_dtypes.bfloat16)
    m = {}
    i64 = np.arange(64)
    a64 = 2 * np.pi * np.outer(i64, i64) / 64
    C64, S64 = np.cos(a64) / 8.0, np.sin(a64) / 8.0
    z = np.zeros((64, 64))
    m['bdcs'] = bf(np.block([[C64, z, S64, z], [z, C64, z, S64]]))
    n = np.arange(2048, dtype=np.float64)
    aN = 2 * np.pi * (np.outer(n, n) % 2048) / 2048
    m['twC'] = bf(np.cos(aN) / np.sqrt(2048.0))
    m['twS'] = bf(-np.sin(aN) / np.sqrt(2048.0))
    E2 = np.zeros((8, 2, 8), np.float32)
    for k_ in range(4):
        E2[k_, 0, k_] = 1.0
        E2[4 + k_, 1, 4 + k_] = 1.0
    m['E2'] = E2
    m['negidf'] = -np.eye(128, dtype=np.float32)
    sidx, tidx = np.arange(128)[:, None], np.arange(128)[None, :]
    mf = np.where(sidx <= tidx, 0.0, -30000.0)
    mb_ = np.where(sidx >= tidx, 0.0, -30000.0)
    m['mlmask'] = bf(np.stack([np.concatenate([mf, mf], axis=1), np.concatenate([mb_, mb_], axis=1)], axis=1))
    tt = np.arange(2048)
    freqs = 10000.0 ** (-np.arange(8, dtype=np.float64) / 8)
    ang = np.stack([np.outer(tt // 64, freqs), np.outer(tt % 64, freqs)], axis=1)
    cosT = np.ones((TOK, 2, 2, 8)); sinT = np.zeros((TOK, 2, 2, 8))
    cosT[256:] = np.cos(ang)[:, :, None, :]
    sinT[256:, :, 0, :] = -np.sin(ang)
    sinT[256:, :, 1, :] = np.sin(ang)
    m['ropeC'] = bf(cosT.reshape(TOK, 32).T)
    m['ropeS'] = bf(sinT.reshape(TOK, 32).T)
    n = np.arange(256, dtype=np.float64)
    a2 = 2 * np.pi * (np.outer(n, n) % 256) / 256
    c = np.stack([np.cos(a2) / 16.0, -np.sin(a2) / 16.0], axis=1)
    m['c256'] = bf(c.reshape(2, 128, 2, 256).transpose(1, 0, 2, 3))
    return m


_CACHE = {}


def kernel(**inputs):
    if 'nc' not in _CACHE:
        _CACHE['nc'] = build_program()
    nc, P = _CACHE['nc']
    shared = None
    in_maps = []
    for b in range(8):
        m = host_inputs(inputs, b) if shared is None else None
        if shared is None:
            shared = m
        else:
            m = dict(shared)
            f = lambda a: np.ascontiguousarray(np.asarray(a, dtype=np.float32))
            m['xin'] = f(np.concatenate([inputs['ctx'][b], inputs['x'][b]], axis=0))
            c2 = np.stack([np.asarray(inputs['c'][b]).reshape(8, 128).T, np.asarray(inputs['c_ctx']).reshape(8, 128).T], axis=-1)
            m['c2'] = f(c2.reshape(128, 16))
        in_maps.append(m)
    res = run_bass_kernel_spmd(nc, in_maps, core_ids=list(range(8)))
    _CACHE['res'] = res
    return np.stack([np.asarray(r['out'], dtype=np.float32) for r in res.results], axis=0)
```

```python
import contextlib, bisect, math
import numpy as np
import ml_dtypes
import concourse.bass as bass
import concourse.mybir as mybir
from concourse.bass_utils import run_bass_kernel_spmd

F32 = mybir.dt.float32
BF16 = mybir.dt.bfloat16
AF = mybir.ActivationFunctionType
OP = mybir.AluOpType
AX = mybir.AxisListType

NT = 18
TOK = 2304
DM = 1024
EPS = 1e-6
DEPTH = 2
STAGE = 99
DEBUG = []
BRANCHES = 'FAM'
DBGXS = False


class Dummy:
    def __getitem__(self, k):
        return self

    def __getattr__(self, n):
        return lambda *a, **k: self


class Prog:
    ENGS = ('pe', 'dve', 'act', 'pool', 'sp')

    def __init__(self, nc):
        self.nc = nc
        self.dry = True
        self.rec = []
        self.i = 0

    def op(self, eng, fn, r=(), w=(), dma=None):
        if self.dry:
            self.rec.append(('op', eng, tuple(r), tuple(w), dma))
        else:
            i = self.i
            E = self.H[eng]
            for (s, v) in self.waits[i]:
                E.wait_ge(self.sems[s], v)
            if fn is not None:
                ins = fn()
                inc = self.incs[i]
                if inc is not None:
                    ins.then_inc(self.sems[inc[0]], inc[1])
        self.i += 1

    def barrier(self):
        if self.dry:
            self.rec.append(('bar',))
        self.i += 1

    def plan(self):
        rec = self.rec
        n = len(rec)
        lastw, readers, last_on_eng, dma_ops = {}, {}, {}, {}
        deps = [None] * n
        pend = {e: None for e in self.ENGS}
        for i, r in enumerate(rec):
            if r[0] == 'bar':
                bd = set(last_on_eng.values())
                for s, l in dma_ops.items():
                    if l:
                        bd.add(l[-1])
                for e in self.ENGS:
                    pend[e] = set(bd) | (pend[e] or set())
                lastw.clear()
                readers.clear()
                continue
            _, eng, R, W, dma = r
            d = set()
            for k in R:
                if k in lastw:
                    d.add(lastw[k])
            for k in W:
                if k in lastw:
                    d.add(lastw[k])
                d.update(readers.get(k, ()))
            if pend[eng]:
                d |= pend[eng]
                pend[eng] = None
            d.discard(i)
            if eng == 'pe':
                d = {j for j in d if not (rec[j][1] == 'pe' and rec[j][4] is None)}
            deps[i] = d
            for k in W:
                lastw[k] = i
                readers[k] = []
            for k in R:
                lst = readers.setdefault(k, [])
                if dma is None:
                    lst[:] = [j for j in lst if not (rec[j][1] == eng and rec[j][4] is None)]
                lst.append(i)
            last_on_eng[eng] = i
            if dma:
                dma_ops.setdefault(dma, []).append(i)
        sig = [False] * n
        for i in range(n):
            if deps[i]:
                for j in deps[i]:
                    if rec[j][4] is None:
                        sig[j] = True
        cnt = {e: 0 for e in self.ENGS}
        ev = [None] * n
        for i, r in enumerate(rec):
            if r[0] != 'op':
                continue
            if r[4] is None and sig[i]:
                cnt[r[1]] += 1
                ev[i] = ('E_' + r[1], cnt[r[1]])
        self.waits = [()] * n
        self.incs = [None] * n
        seen = {e: {} for e in self.ENGS}
        for i, r in enumerate(rec):
            if r[0] != 'op':
                continue
            eng = r[1]
            wl = {}
            for j in deps[i]:
                if rec[j][4] is None:
                    s, v = ev[j]
                else:
                    s = 'D_' + rec[j][4]
                    v = 16 * bisect.bisect_left(dma_ops[rec[j][4]], i)
                if v > wl.get(s, 0):
                    wl[s] = v
            ws = []
            for s, v in wl.items():
                if v > seen[eng].get(s, 0):
                    seen[eng][s] = v
                    ws.append((s, v))
            self.waits[i] = tuple(ws)
            if r[4] is not None:
                self.incs[i] = ('D_' + r[4], 16)
            elif sig[i]:
                self.incs[i] = (ev[i][0], 1)
        self.semnames = ['E_' + e for e in self.ENGS] + ['D_' + s for s in dma_ops]
        self.stats = dict(n=n, cnt=cnt, nw=sum(len(w) for w in self.waits))

    def start_real(self, es):
        nc = self.nc
        self.dry = False
        self.i = 0
        self.H = {'pe': nc.tensor, 'dve': nc.vector, 'act': nc.scalar, 'pool': nc.gpsimd, 'sp': nc.sync}
        self.sems = {s: es.enter_context(nc.semaphore(s)) for s in self.semnames}


class Scope:
    _n = 0

    def __init__(self, K):
        self.K = K
        self.es = contextlib.ExitStack()

    def __enter__(self):
        return self

    def sb(self, name, shape, dt):
        if self.K.P.dry:
            return Dummy()
        Scope._n += 1
        return self.es.enter_context(self.K.nc.sbuf_tensor("%s_%d" % (name, Scope._n), list(shape), dt))

    def __exit__(self, *a):
        self.K.P.barrier()
        self.K.semmap = {}
        self.es.close()
        return False


def which(t):
    return 1 if t < 2 else 0


class K:
    def __init__(self, nc, P, dram):
        self.nc, self.P, self.d = nc, P, dram
        self.semmap = {}

    def mm(self, out, lhsT, rhs, start, stop, r, w):
        nc = self.nc
        self.P.op('pe', lambda: nc.tensor.matmul(out, lhsT=lhsT, rhs=rhs, start=start, stop=stop), r, w)

    def tr(self, out, in_, ident, r, w):
        nc = self.nc
        self.P.op('pe', lambda: nc.tensor.transpose(out, in_, ident), r, w)

    def act(self, out, in_, func, r, w, bias=None, scale=None, accum=None):
        nc = self.nc
        kw = {}
        if bias is not None:
            kw['bias'] = bias
        if scale is not None:
            kw['scale'] = scale
        if accum is not None:
            kw['accum_out'] = accum
        self.P.op('act', lambda: nc.scalar.activation(out=out, in_=in_, func=func, **kw), r, w)

    def v(self, eng, name, r, w, *a, **kw):
        nc = self.nc
        E = nc.vector if eng == 'dve' else nc.gpsimd
        self.P.op(eng, lambda: getattr(E, name)(*a, **kw), r, w)

    def dma(self, q, out, in_, r, w, sem):
        nc = self.nc
        E = nc.sync if q == 'sp' else nc.gpsimd
        k0 = w[0] if w else sem
        if isinstance(k0, tuple) and k0[0] in ('out', 'dbgxs'):
            name = sem
        elif isinstance(k0, tuple) and k0[0] == 'xs':
            name = 'xs'
        else:
            name = "b%d" % self.semmap.setdefault((k0, q), len(self.semmap))
        self.P.op(q, lambda: E.dma_start(out=out, in_=in_), r, w, dma=name + '_' + q)

    def build(self):
        nc, P, d = self.nc, self.P, self.d
        self.semmap = {}
        with Scope(self) as S:
            self.S0 = S
            self.xs = S.sb("xs", [128, NT, DM], F32)
            self.hT = S.sb("hT", [128, 8, TOK], BF16)
            self.identb = S.sb("identb", [128, 128], BF16)
            self.identf = S.sb("identf", [128, 128], F32)
            self.onesf = S.sb("onesf", [128, 128], F32)
            self.c2 = S.sb("c2", [128, 16], F32)
            self.scb = S.sb("scb", [128, 16], F32)
            self.modv = S.sb("modv", [128, 48, 2], F32)
            self.bmod = S.sb("bmod", [128, 2, 48], F32)
            self.gn = S.sb("gn", [128, 2, 2, 8], F32)
            self.G = S.sb("G", [128, 2, 8, 2], F32)
            self.gaB = S.sb("gaB", [128, 2, DM], F32)
            self.ssq = S.sb("ssq", [128, NT], F32)
            self.rstd = S.sb("rstd", [128, NT], F32)
            self.epsT = S.sb("epsT", [128, 1], F32)
            self.xn = [S.sb("xn%d" % i, [128, DM], BF16) for i in range(2)]
            self.junk = S.sb("junk", [128, DM], BF16)
            self.dg = [S.sb("dg%d" % i, [128, 128], F32) for i in range(2)]
            self.xt = [S.sb("xt%d" % i, [128, 512], F32) for i in range(2)]
            if not P.dry:
                es = S.es
                self.pb = [es.enter_context(nc.psum_tensor("pb%d" % i, [128, 512], F32)) for i in range(6)]
                self.ptr = [es.enter_context(nc.psum_tensor("ptr%d" % i, [128, 8, 128], BF16)) for i in range(2)]
            else:
                self.pb = [Dummy()] * 6
                self.ptr = [Dummy()] * 2
            self.cnt = 0
            self.dma('sp', self.identb[:], d['identb'][:, :], [], ['identb'], 'cst')
            self.dma('sp', self.identf[:], d['identf'][:, :], [], ['identf'], 'cst')
            self.dma('sp', self.c2[:], d['c2'][:, :], [], ['c2'], 'cst')
            self.dma('sp', self.bmod[:], d['bmodfm'][:, :, :], [], ['bmod'], 'cst')
            self.dma('sp', self.gn[:], d['gnfm'][:, :, :, :], [], ['gn'], 'cst')
            self.v('dve', 'memset', [], ['onesf'], self.onesf[:], 1.0)
            self.v('dve', 'memset', [], ['epsT'], self.epsT[:], EPS)
            self.v('dve', 'memset', [], ['ssq'], self.ssq[:], 0.0)
            self.act(self.scb[:], self.c2[:], AF.Silu, ['c2'], ['scb'])
            for t in range(NT):
                self.dma('sp', self.xs[:, t, :], d['xin'][t * 128:(t + 1) * 128, :], [], [('xs', t)], 'xin')
            for l in range(DEPTH):
                last = (l == DEPTH - 1)
                self.mod_phase(l)
                if STAGE <= 1:
                    break
                self.make_gaB(2)
                self.phaseA(l, 0, range(NT))
                if STAGE >= 3:
                    self.mixers(l, last)
                if DBGXS and l == 0:
                    for t in range(NT):
                        self.dma('sp', d['dbgxs'][t * 128:(t + 1) * 128, :], self.xs[:, t, :], [('xs', t)], [('dbgxs', t)], 'out')
                    break
                self.make_gaB(5)
                tiles = list(range(2, NT)) if last else list(range(NT))
                self.phaseA(l, 1, tiles)
                self.mlp(l, tiles)
            self.final()
        P.barrier()
        P.op('sp', None)

    def mod_phase(self, l):
        nc, P, d = self.nc, self.P, self.d
        with Scope(self) as S:
            wm = [S.sb("wm%d" % i, [128, 8, 1024], F32) for i in range(2)]
            modps = self.pb[0]
            for blk in range(6):
                s = blk % 2
                self.dma('sp', wm[s][:], d['w_mod'][l][:, blk * 1024:(blk + 1) * 1024].rearrange("(k p) c -> p k c", p=128),
                         [], [('wm', s)], 'wm%d' % s)
                for jj in range(8):
                    j = blk * 8 + jj
                    for k in range(8):
                        self.mm(modps[:, 2 * j:2 * j + 2], wm[s][:, k, jj * 128:(jj + 1) * 128], self.scb[:, 2 * k:2 * k + 2],
                                k == 0, k == 7, [('wm', s), 'scb'], [('pb', 0)])
            self.v('dve', 'tensor_tensor', [('pb', 0), 'bmod'], ['modv'], out=self.modv[:],
                   in0=modps[:, 0:96].rearrange("p (j w) -> p j w", w=2),
                   in1=self.bmod[:, l, :].unsqueeze(2).to_broadcast([128, 48, 2]), op=OP.add)
            for ni, vi in ((0, 1), (1, 4)):
                self.v('dve', 'scalar_tensor_tensor', ['modv', 'gn'], ['G'], out=self.G[:, ni, :, :],
                       in0=self.modv[:, vi * 8:(vi + 1) * 8, :], scalar=1.0,
                       in1=self.gn[:, l, ni, :].unsqueeze(2).to_broadcast([128, 8, 2]), op0=OP.add, op1=OP.mult)

    def make_gaB(self, vi):
        for w in range(2):
            for kk in range(8):
                self.cnt += 1
                dgi = self.cnt % 2
                self.v('dve', 'tensor_scalar', ['modv', 'identf'], [('dg', dgi)], out=self.dg[dgi][:], in0=self.identf[:],
                       scalar1=self.modv[:, vi * 8 + kk, w:w + 1], scalar2=None, op0=OP.mult)
                bank = kk // 4
                self.mm(self.pb[bank][:, (kk % 4) * 128:(kk % 4 + 1) * 128], self.onesf[:], self.dg[dgi][:], True, True,
                        ['onesf', ('dg', dgi)], [('pb', bank)])
            for bank in range(2):
                self.act(self.gaB[:, w, bank * 512:(bank + 1) * 512], self.pb[bank][:], AF.Identity, [('pb', bank)], ['gaB'])

    def rms(self, t):
        self.cnt += 1
        self.act(self.junk[:], self.xs[:, t, :], AF.Square, [('xs', t)], [('junk', self.cnt), ('ssq', t)],
                 accum=self.ssq[:, t:t + 1])
        self.act(self.rstd[:, t:t + 1], self.ssq[:, t:t + 1], AF.Sqrt, [('ssq', t), 'epsT'], [('rstd', t)],
                 bias=self.epsT[:], scale=1.0 / DM)
        self.v('dve', 'reciprocal', [('rstd', t)], [('rstd', t)], out=self.rstd[:, t:t + 1], in_=self.rstd[:, t:t + 1])
        self.v('dve', 'memset', [('ssq', t)], [('ssq', t)], self.ssq[:, t:t + 1], 0.0)

    def rms_all(self, tiles):
        tiles = list(tiles)
        t0, t1 = tiles[0], tiles[-1] + 1
        for t in tiles:
            self.cnt += 1
            self.act(self.junk[:], self.xs[:, t, :], AF.Square, [('xs', t)], [('junk', self.cnt), 'ssq'],
                     accum=self.ssq[:, t:t + 1])
        self.act(self.rstd[:, t0:t1], self.ssq[:, t0:t1], AF.Sqrt, ['ssq', 'epsT'], ['rstd'], bias=self.epsT[:], scale=1.0 / DM)
        self.v('dve', 'reciprocal', ['rstd'], ['rstd'], out=self.rstd[:, t0:t1], in_=self.rstd[:, t0:t1])
        self.v('dve', 'memset', ['ssq'], ['ssq'], self.ssq[:, t0:t1], 0.0)

    def phaseA(self, l, ni, tiles):
        vi = 0 if ni == 0 else 3
        tiles = list(tiles)
        self.rms_all(tiles)
        for t in tiles:
            w = which(t)
            self.cnt += 1
            xi = self.cnt % 2
            self.act(self.xn[xi][:], self.xs[:, t, :], AF.Identity, [('xs', t), 'rstd'], [('xn', xi)],
                     scale=self.rstd[:, t:t + 1])
            for k in range(8):
                self.tr(self.ptr[xi][:, k, :], self.xn[xi][:, k * 128:(k + 1) * 128], self.identb[:],
                        [('xn', xi), 'identb'], [('ptr', xi)])
            for k in range(8):
                self.v('dve', 'tensor_scalar', [('ptr', xi), 'G', 'modv'], [('hT', t)], out=self.hT[:, k, t * 128:(t + 1) * 128],
                       in0=self.ptr[xi][:, k, :], scalar1=self.G[:, ni, k, w:w + 1], scalar2=self.modv[:, vi * 8 + k, w:w + 1],
                       op0=OP.mult, op1=OP.add)

    def xupd(self, t, nh, b):
        w = which(t)
        self.cnt += 1
        mi = self.cnt % 2
        self.v('dve', 'tensor_tensor', [('pb', b), 'gaB'], [('xt', mi)], out=self.xt[mi][:], in0=self.pb[b][:],
               in1=self.gaB[:, w, nh * 512:(nh + 1) * 512], op=OP.mult)
        self.v('pool', 'tensor_tensor', [('xt', mi), ('xs', t)], [('xs', t)],
               out=self.xs[:, t, nh * 512:(nh + 1) * 512], in0=self.xs[:, t, nh * 512:(nh + 1) * 512],
               in1=self.xt[mi][:], op=OP.add)

    def blocks(self, lo=0, hi=TOK, step=512):
        return [(o, min(step, hi - o)) for o in range(lo, hi, step)]

    def mixers(self, l, last):
        if 'F' in BRANCHES:
            self.fourier(l, last)
        if 'M' in BRANCHES:
            self.mlstm(l, last)
        if 'A' in BRANCHES:
            self.mla(l, last)

    def mlstm(self, l, last):
        nc, P, d = self.nc, self.P, self.d
        win = d['w_in'][l]
        with Scope(self) as SO:
            acol = SO.sb("acol", [128, NT, 8], F32)
            Mcol = SO.sb("Mcol", [128, NT, 8], F32)
            T4 = SO.sb("T4", [128, 4, NT, 8], F32)
            gm = SO.sb("gm", [128, 384], F32)
            E2 = SO.sb("E2", [8, 2, 8], F32)
            negid = SO.sb("negid", [128, 128], F32)
            mlmask = SO.sb("mlmask", [128, 2, 256], BF16)
            self.dma('sp', gm[:], d['gmB'][:, l, :], [], ['gm'], 'mw')
            self.dma('sp', E2[:], d['E2'][:, :, :], [], ['E2'], 'mw')
            self.dma('sp', negid[:], d['negidf'][:, :], [], ['negid'], 'mw')
            self.dma('sp', mlmask[:], d['mlmask'][:, :, :], [], ['mlmask'], 'mw')
            with Scope(self) as S:
                wg = S.sb("wg", [128, 8, 16], BF16)
                bg = S.sb("bg", [8, 2], F32)
                nbf = S.sb("nbf", [8, 1], F32)
                IG = S.sb("IG", [8, TOK], F32)
                LF = S.sb("LF", [8, TOK], F32)
                FF = S.sb("FF", [8, TOK], F32)
                FB = S.sb("FB", [8, TOK], F32)
                AFr = S.sb("AFr", [8, TOK], F32)
                ABr = S.sb("ABr", [8, TOK], F32)
                R1 = S.sb("R1", [8, NT, 8], F32)
                R2 = S.sb("R2", [8, NT, 8], F32)
                mcol = S.sb("mcol", [128, NT, 8], F32)
                MendB = S.sb("MendB", [128, NT, 8], F32)
                MP = S.sb("MP", [128, NT, 8], F32)
                ex = S.sb("ex", [128, 4, NT, 8], F32)
                self.dma('pool', wg[:], d['w_g'][l].rearrange("(k p) c -> p k c", p=128), [], ['wg'], 'gw')
                self.dma('sp', bg[:], d['bg'][l], [], ['bg'], 'gw')
                self.v('dve', 'tensor_scalar', ['bg'], ['nbf'], out=nbf[:], in0=bg[:, 1:2], scalar1=-1.0, scalar2=None, op0=OP.mult)
                nb = 0
                for (o, n) in self.blocks():
                    hk = [('hT', t) for t in range(o // 128, (o + n) // 128)]
                    bi, bf_ = nb % 6, (nb + 1) % 6
                    nb += 2
                    for k in range(8):
                        self.mm(self.pb[bi][0:8, 0:n], wg[:, k, 0:8], self.hT[:, k, o:o + n], k == 0, k == 7, hk + ['wg'], [('pb', bi)])
                    for k in range(8):
                        self.mm(self.pb[bf_][0:8, 0:n], wg[:, k, 8:16], self.hT[:, k, o:o + n], k == 0, k == 7, hk + ['wg'], [('pb', bf_)])
                    self.act(IG[:, o:o + n], self.pb[bi][0:8, 0:n], AF.Identity, [('pb', bi), 'bg'], ['IG'], bias=bg[:, 0:1])
                    self.act(LF[:, o:o + n], self.pb[bf_][0:8, 0:n], AF.Exp, [('pb', bf_), 'nbf'], ['LF'], bias=nbf[:], scale=-1.0)
                self.act(LF[:], LF[:], AF.Ln, ['LF'], ['LF'], bias=1.0)
                self.v('dve', 'tensor_scalar', ['LF'], ['LF'], out=LF[:], in0=LF[:], scalar1=-1.0, scalar2=None, op0=OP.mult)
                one_b = lambda n: self.onesf[0:8, 0:1].to_broadcast([8, n])
                self.v('dve', 'tensor_tensor_scan', ['LF', 'onesf'], ['FF'], out=FF[:], data0=one_b(TOK), data1=LF[:], initial=0.0,
                       op0=OP.mult, op1=OP.add)
                self.v('dve', 'tensor_tensor_scan', ['LF', 'onesf'], ['FB'], out=FB[:, 255::-1], data0=one_b(256), data1=LF[:, 255::-1],
                       initial=0.0, op0=OP.mult, op1=OP.add)
                self.v('dve', 'tensor_tensor_scan', ['LF', 'onesf', 'FB'], ['FB'], out=FB[:, 2303:255:-1], data0=one_b(2048),
                       data1=LF[:, 2303:255:-1], initial=FB[:, 0:1], op0=OP.mult, op1=OP.add)
                self.v('dve', 'tensor_tensor', ['IG', 'FF'], ['AFr'], out=AFr[:], in0=IG[:], in1=FF[:], op=OP.subtract)
                self.v('dve', 'tensor_tensor', ['IG', 'FB'], ['ABr'], out=ABr[:], in0=IG[:], in1=FB[:], op=OP.subtract)
                MF, MB = IG, LF
                self.v('dve', 'tensor_tensor_scan', ['AFr'], ['IG'], out=MF[:], data0=AFr[:], data1=AFr[:], initial=0.0, op0=OP.max, op1=OP.max)
                self.v('dve', 'tensor_tensor_scan', ['ABr'], ['LF'], out=MB[:, 255::-1], data0=ABr[:, 255::-1], data1=ABr[:, 255::-1],
                       initial=0.0, op0=OP.max, op1=OP.max)
                self.v('dve', 'tensor_tensor_scan', ['ABr', 'LF'], ['LF'], out=MB[:, 2303:255:-1], data0=ABr[:, 2303:255:-1],
                       data1=ABr[:, 2303:255:-1], initial=MB[:, 0:1], op0=OP.max, op1=OP.max)
                self.v('dve', 'tensor_tensor', ['FF', 'IG'], ['FF'], out=FF[:], in0=FF[:], in1=MF[:], op=OP.add)
                self.v('dve', 'tensor_tensor', ['FB', 'LF'], ['FB'], out=FB[:], in0=FB[:], in1=MB[:], op=OP.add)
                for qi, (XF, XB, kf, kb) in enumerate(((AFr, ABr, 'AFr', 'ABr'), (MF, MB, 'IG', 'LF'), (FF, FB, 'FF', 'FB'))):
                    for c in range(NT):
                        self.mm(self.pb[qi][:, c * 8:(c + 1) * 8], XF[:, c * 128:(c + 1) * 128], E2[:, 0, :], True, False, [kf, 'E2'], [('pb', qi)])
                        self.mm(self.pb[qi][:, c * 8:(c + 1) * 8], XB[:, c * 128:(c + 1) * 128], E2[:, 1, :], False, True, [kb, 'E2'], [('pb', qi)])
                for qi, (dst, kd) in enumerate(((acol, 'acol'), (Mcol, 'Mcol'), (mcol, 'mcol'))):
                    self.act(dst[:].rearrange("p c e -> p (c e)"), self.pb[qi][:, 0:NT * 8], AF.Identity, [('pb', qi)], [kd])
                self.v('dve', 'tensor_tensor', ['E2', 'IG'], ['R1'], out=R1[:], in0=E2[:, 0, :].unsqueeze(1).to_broadcast([8, NT, 8]),
                       in1=MF[:, 127::128].unsqueeze(2).to_broadcast([8, NT, 8]), op=OP.mult)
                self.v('dve', 'tensor_tensor', ['E2', 'LF'], ['R2'], out=R2[:], in0=E2[:, 1, :].unsqueeze(1).to_broadcast([8, NT, 8]),
                       in1=MB[:, 0::128].unsqueeze(2).to_broadcast([8, NT, 8]), op=OP.mult)
                self.v('dve', 'tensor_tensor', ['R1', 'R2'], ['R1'], out=R1[:], in0=R1[:], in1=R2[:], op=OP.add)
                self.mm(self.pb[3][:, 0:NT * 8], self.onesf[0:8, :], R1[:].rearrange("p c e -> p (c e)"), True, True, ['R1', 'onesf'], [('pb', 3)])
                self.act(MendB[:].rearrange("p c e -> p (c e)"), self.pb[3][:, 0:NT * 8], AF.Identity, [('pb', 3)], ['MendB'])
                self.v('dve', 'memset', [], ['MP'], MP[:], 0.0)
                self.v('dve', 'tensor_copy', ['MendB', 'MP'], ['MP'], out=MP[:, 1:NT, 0:4], in_=MendB[:, 0:NT - 1, 0:4])
                self.v('dve', 'tensor_copy', ['MendB', 'MP'], ['MP'], out=MP[:, 0:1, 4:8], in_=MendB[:, 1:2, 4:8])
                self.v('dve', 'tensor_copy', ['MendB', 'MP'], ['MP'], out=MP[:, 2:NT - 1, 4:8], in_=MendB[:, 3:NT, 4:8])
                self.v('dve', 'tensor_copy', ['MendB', 'MP'], ['MP'], out=MP[:, NT - 1:NT, 4:8], in_=MendB[:, 0:1, 4:8])
                self.v('dve', 'tensor_tensor', ['MP', 'Mcol'], ['ex'], out=ex[:, 0], in0=MP[:], in1=Mcol[:], op=OP.subtract)
                self.v('dve', 'tensor_tensor', ['acol', 'MendB'], ['ex'], out=ex[:, 1], in0=acol[:], in1=MendB[:], op=OP.subtract)
                self.v('dve', 'tensor_tensor', ['MP', 'MendB'], ['ex'], out=ex[:, 2], in0=MP[:], in1=MendB[:], op=OP.subtract)
                self.v('dve', 'tensor_scalar', ['mcol'], ['ex'], out=ex[:, 3], in0=mcol[:], scalar1=-1.0, scalar2=None, op0=OP.mult)
                self.act(T4[:].rearrange("p a c e -> p (a c e)"), ex[:].rearrange("p a c e -> p (a c e)"), AF.Exp, ['ex'], ['T4'])
            for hp in range(2):
                self.mlstm_pair(l, last, hp, acol, Mcol, T4, gm, negid, mlmask)

    def mlstm_pair(self, l, last, hp, acol, Mcol, T4, gm, negid, mlmask):
        nc, P, d = self.nc, self.P, self.d
        win = d['w_in'][l]
        QS = 96.0 ** -0.5
        colof = lambda T: T + 2 if T < 256 else T + 6
        cblocks = [(0, 256)] + [(256 + i * 512, 512) for i in range(4)]
        with Scope(self) as SP:
            post = SP.sb("post", [96, 4, TOK], BF16)
            vp = SP.sb("vp", [128, NT, 2, 97], BF16)
            sg = SP.sb("sg", [128, NT, 192], BF16)
            S = Scope(self)
            wvo = S.sb("wvo", [128, 8, 384], BF16)
            self.dma('pool', wvo[:, :, 0:192], win[:, 1024 + 192 * hp:1024 + 192 * (hp + 1)].rearrange("(k p) c -> p k c", p=128), [], ['wvo'], 'mw')
            self.dma('pool', wvo[:, :, 192:384], win[:, 1408 + 192 * hp:1408 + 192 * (hp + 1)].rearrange("(k p) c -> p k c", p=128), [], ['wvo'], 'mw')
            self.v('dve', 'memset', [], ['vp'], vp[:], 1.0)
            with S:
                wqk = S.sb("wqk", [128, 8, 4, 96], BF16)
                dgw = S.sb("dgw", [96, 4, 5, 96], BF16)
                pre = S.sb("pre", [96, 4, 2312], BF16)
                sgt = [S.sb("sgt%d" % i, [96, 512], F32) for i in range(1)]
                for jj in range(4):
                    c0 = (256 if jj < 2 else 640) + 96 * (2 * hp + jj % 2)
                    self.dma('pool', wqk[:, :, jj, :], win[:, c0:c0 + 96].rearrange("(k p) c -> p k c", p=128), [], ['wqk'], 'mw')
                    jg = (0 if jj < 2 else 4) + 2 * hp + jj % 2
                    self.dma('sp', dgw[:, jj, :, :], d['convdg'][l, :, jg, :, :], [], ['dgw'], 'mw')
                self.v('dve', 'memset', [], ['pre'], pre[:], 0.0)
                nb = 0
                for jj in range(4):
                    for (o, n) in cblocks:
                        b = nb % 6
                        nb += 1
                        hk = [('hT', t) for t in range(o // 128, (o + n) // 128)]
                        for k in range(8):
                            self.mm(self.pb[b][0:96, 0:n], wqk[:, k, jj, :], self.hT[:, k, o:o + n], k == 0, k == 7, hk + ['wqk'], [('pb', b)])
                        self.act(pre[:, jj, colof(o):colof(o) + n], self.pb[b][0:96, 0:n], AF.Identity, [('pb', b), 'pre'], [('pre', jj)])
                for t in range(NT):
                    b = nb % 6
                    nb += 1
                    for k in range(8):
                        self.mm(self.pb[b][:, 0:384], self.hT[:, k, t * 128:(t + 1) * 128], wvo[:, k, :], k == 0, k == 7, [('hT', t), 'wvo'], [('pb', b)])
                    self.act(vp[:, t, :, 0:96], self.pb[b][:, 0:192].rearrange("p (h c) -> p h c", c=96), AF.Identity, [('pb', b), 'vp'], [('vp', t)])
                    self.act(sg[:, t, :], self.pb[b][:, 192:384], AF.Sigmoid, [('pb', b)], [('sg', t)])
                for jj in range(4):
                    for (o, n) in cblocks:
                        b = nb % 6
                        nb += 1
                        c0 = colof(o)
                        for tap in range(5):
                            self.mm(self.pb[b][0:96, 0:n], dgw[:, jj, tap, :], pre[:, jj, c0 + tap - 2:c0 + tap - 2 + n], tap == 0, tap == 4,
                                    [('pre', jj), 'dgw'], [('pb', b)])
                        tk = [('post', jj, t) for t in range(o // 128, (o + n) // 128)]
                        if jj < 2:
                            si = 0
                            self.act(sgt[si][:, 0:n], self.pb[b][0:96, 0:n], AF.Sigmoid, [('pb', b)], [('sgt', si)])
                            self.v('dve', 'scalar_tensor_tensor', [('pb', b), ('sgt', si)], tk, out=post[:, jj, o:o + n], in0=self.pb[b][0:96, 0:n],
                                   scalar=QS, in1=sgt[si][:, 0:n], op0=OP.mult, op1=OP.mult)
                        else:
                            self.act(post[:, jj, o:o + n], self.pb[b][0:96, 0:n], AF.Silu, [('pb', b)], tk)
            ktok = SP.sb("ktok", [128, NT, 2, 96], BF16)
            for t in range(NT):
                pi = t % 2
                for hh in range(2):
                    self.tr(self.ptr[pi][:, hh, 0:96], post[:, 2 + hh, t * 128:(t + 1) * 128], self.identb[0:96, 0:96],
                            [('post', 2 + hh, t), 'identb'], [('ptr', pi)])
                self.act(ktok[:, t, :, :], self.ptr[pi][:, 0:2, 0:96], AF.Identity, [('ptr', pi)], [('ktok', t)])
            hF = SP.sb("hF", [128, NT, 2, 96], BF16)
            hB = SP.sb("hB", [128, NT, 2, 96], BF16)
            woM = SP.sb("woM", [96, 2, DM], BF16)
            with Scope(self) as S:
                dgM = [S.sb("dgM%d" % i, [128, 2, 128], F32) for i in range(2)]
                WT = [S.sb("WT%d" % i, [128, 2, 128], F32) for i in range(2)]
                PT = [S.sb("PT%d" % i, [128, 2, 128], BF16) for i in range(2)]
                tmpn = [S.sb("tmpn%d" % i, [128, 2, 97], F32) for i in range(2)]
                dn = [S.sb("dn%d" % i, [128, 2], F32) for i in range(2)]
                vw = [S.sb("vw%d" % i, [128, 2, 97], BF16) for i in range(2)]
                Cst = S.sb("Cst", [96, 2, 97], F32)
                Cball = S.sb("Cball", [96, NT, 2, 97], BF16)
                self.dma('pool', woM[:], d['w_out'][l][256 + 192 * hp:256 + 192 * (hp + 1), :].rearrange("(h p) c -> p h c", p=96), [], ['woM'], 'mw')
                for dr in range(2):
                    order = list(range(NT)) if dr == 0 else [1, 0] + list(range(NT - 1, 1, -1))
                    cf0 = dr * 4 + 2 * hp
                    self.v('dve', 'memset', ['Cst'], ['Cst'], Cst[:], 0.0)
                    self.v('dve', 'memset', [('Cball', 0)], [('Cball', 0)], Cball[:, 0], 0.0)

                    def kv_part(it):
                        c = order[it]
                        i2 = it % 2
                        self.v('dve', 'tensor_tensor', [('vp', c), 'T4'], [('vw', i2)], out=vw[i2][:], in0=vp[:, c],
                               in1=T4[:, 1, c, cf0:cf0 + 2].unsqueeze(2).to_broadcast([128, 2, 97]), op=OP.mult)
                        for hh in range(2):
                            self.mm(self.pb[4 + i2][0:96, hh * 97:(hh + 1) * 97], ktok[:, c, hh, :], vw[i2][:, hh, :], True, True,
                                    [('ktok', c), ('vw', i2)], [('pb', 4 + i2)])

                    kv_part(0)
                    for it in range(NT - 1):
                        c = order[it]
                        i2 = it % 2
                        if it + 1 < NT - 1:
                            kv_part(it + 1)
                        self.v('dve', 'tensor_tensor', ['Cst', 'T4'], ['Cst'], out=Cst[:], in0=Cst[:],
                               in1=T4[0:96, 2, c, cf0:cf0 + 2].unsqueeze(2).to_broadcast([96, 2, 97]), op=OP.mult)
                        self.v('dve', 'tensor_tensor', ['Cst', ('pb', 4 + i2)], ['Cst'], out=Cst[:], in0=Cst[:],
                               in1=self.pb[4 + i2][0:96, 0:194].rearrange("p (h e) -> p h e", e=97), op=OP.add)
                        self.act(Cball[:, it + 1], Cst[:], AF.Identity, ['Cst'], [('Cball', it + 1)])

                    def part_i(it):
                        c = order[it]
                        i2 = it % 2
                        if last and c < 2:
                            return
                        tsl = slice(c * 128, (c + 1) * 128)
                        bo = 3 + i2
                        self.v('dve', 'tensor_tensor', ['negid', 'Mcol'], [('dgM', i2)], out=dgM[i2][:],
                               in0=negid[:].unsqueeze(1).to_broadcast([128, 2, 128]),
                               in1=Mcol[:, c, cf0:cf0 + 2].unsqueeze(2).to_broadcast([128, 2, 128]), op=OP.mult)
                        self.mm(self.pb[1][:, 0:256], self.onesf[:], dgM[i2][:].rearrange("p h t -> p (h t)"), True, False,
                                [('dgM', i2), 'onesf'], [('pb', 1)])
                        self.mm(self.pb[1][:, 0:256], self.identb[:], mlmask[:, dr, :], False, True, ['mlmask', 'identb'], [('pb', 1)])
                        for hh in range(2):
                            self.act(WT[i2][:, hh, :], self.pb[1][:, hh * 128:(hh + 1) * 128], AF.Exp, [('pb', 1), 'acol'], [('WT', i2)],
                                     bias=acol[:, c, cf0 + hh:cf0 + hh + 1])
                        for hh in range(2):
                            self.mm(self.pb[2][:, hh * 128:(hh + 1) * 128], post[:, 2 + hh, tsl], post[:, hh, tsl], True, True,
                                    [('post', 2 + hh, c), ('post', hh, c)], [('pb', 2)])
                        self.v('dve', 'tensor_tensor', [('pb', 2), ('WT', i2)], [('PT', i2)], out=PT[i2][:].rearrange("p h t -> p (h t)"),
                               in0=self.pb[2][:, 0:256], in1=WT[i2][:].rearrange("p h t -> p (h t)"), op=OP.mult)
                        for hh in range(2):
                            self.mm(self.pb[bo][:, hh * 97:(hh + 1) * 97], PT[i2][:, hh, :], vp[:, c, hh, :], True, True,
                                    [('PT', i2), ('vp', c)], [('pb', bo)])
                        for hh in range(2):
                            self.mm(self.pb[bo][:, 256 + hh * 97:256 + (hh + 1) * 97], post[:, hh, tsl], Cball[:, it, hh, :], True, True,
                                    [('post', hh, c), ('Cball', it)], [('pb', bo)])

                    def part_d(it):
                        c = order[it]
                        i2 = it % 2
                        if last and c < 2:
                            return
                        bo = 3 + i2
                        for hh in range(2):
                            self.act(tmpn[i2][:, hh, :], self.pb[bo][:, 256 + hh * 97:256 + (hh + 1) * 97], AF.Identity, [('pb', bo), 'T4'],
                                     [('tmpn', i2)], scale=T4[:, 0, c, cf0 + hh:cf0 + hh + 1])
                        self.v('dve', 'tensor_tensor', [('tmpn', i2), ('pb', bo)], [('tmpn', i2)], out=tmpn[i2][:], in0=tmpn[i2][:],
                               in1=self.pb[bo][:, 0:194].rearrange("p (h e) -> p h e", e=97), op=OP.add)
                        self.v('dve', 'scalar_tensor_tensor', [('tmpn', i2)], [('dn', i2)], out=dn[i2][:], in0=tmpn[i2][:, :, 96], scalar=-1.0,
                               in1=tmpn[i2][:, :, 96], op0=OP.mult, op1=OP.max)
                        self.v('dve', 'tensor_tensor', [('dn', i2), 'T4'], [('dn', i2)], out=dn[i2][:], in0=dn[i2][:],
                               in1=T4[:, 3, c, cf0:cf0 + 2], op=OP.max)
                        self.v('dve', 'reciprocal', [('dn', i2)], [('dn', i2)], out=dn[i2][:], in_=dn[i2][:])
                        if dr == 0:
                            self.v('dve', 'tensor_tensor', [('tmpn', i2), ('dn', i2)], [('hF', c)], out=hF[:, c], in0=tmpn[i2][:, :, 0:96],
                                   in1=dn[i2][:].unsqueeze(2).to_broadcast([128, 2, 96]), op=OP.mult)
                            return
                        self.v('dve', 'tensor_tensor', [('tmpn', i2), ('dn', i2)], [('hB', c)], out=hB[:, c], in0=tmpn[i2][:, :, 0:96],
                               in1=dn[i2][:].unsqueeze(2).to_broadcast([128, 2, 96]), op=OP.mult)

                    part_i(0)
                    for it in range(NT):
                        if it + 1 < NT:
                            part_i(it + 1)
                        part_d(it)

            with Scope(self) as S:
                hsC = S.sb("hsC", [128, 6, 2, 96], F32)
                hqC = S.sb("hqC", [128, 6, 2, 96], F32)
                ssC = S.sb("ssC", [128, 12], F32)
                ybC = S.sb("ybC", [128, 6, 192], BF16)
                yhT = [S.sb("yhT%d" % i, [96, 2, 128], BF16) for i in range(2)]
                otiles = list(range(2, NT)) if last else list(range(NT))
                for g0 in range(0, len(otiles), 6):
                    g = otiles[g0:g0 + 6]
                    ng = len(g)
                    c0, c1 = g[0], g[-1] + 1
                    hk = [('hF', c) for c in g] + [('hB', c) for c in g]
                    self.v('dve', 'tensor_tensor', hk, ['hsC'], out=hsC[:, 0:ng], in0=hF[:, c0:c1], in1=hB[:, c0:c1], op=OP.add)
                    self.v('dve', 'tensor_tensor', ['hsC'], ['hqC'], out=hqC[:, 0:ng], in0=hsC[:, 0:ng], in1=hsC[:, 0:ng], op=OP.mult)
                    self.v('dve', 'tensor_reduce', ['hqC'], ['ssC'], out=ssC[:, 0:2 * ng],
                           in_=hqC[:, 0:ng].rearrange("p c h e -> p (c h) e"), axis=AX.X, op=OP.add)
                    self.act(ssC[:, 0:2 * ng], ssC[:, 0:2 * ng], AF.Ln, ['ssC', 'epsT'], ['ssC'], bias=self.epsT[:], scale=1.0 / 96)
                    self.act(ssC[:, 0:2 * ng], ssC[:, 0:2 * ng], AF.Exp, ['ssC'], ['ssC'], scale=-0.5)
                    self.v('dve', 'tensor_tensor', ['hsC', 'ssC'], ['hqC'], out=hqC[:, 0:ng].rearrange("p c h e -> p (c h) e"),
                           in0=hsC[:, 0:ng].rearrange("p c h e -> p (c h) e"),
                           in1=ssC[:, 0:2 * ng].unsqueeze(2).to_broadcast([128, 2 * ng, 96]), op=OP.mult)
                    self.v('dve', 'tensor_tensor', ['hqC', 'gm'], ['hqC'], out=hqC[:, 0:ng].rearrange("p c h e -> p c (h e)"),
                           in0=hqC[:, 0:ng].rearrange("p c h e -> p c (h e)"),
                           in1=gm[:, 192 * hp:192 * (hp + 1)].unsqueeze(1).to_broadcast([128, ng, 192]), op=OP.mult)
                    self.v('dve', 'tensor_tensor', ['hqC'] + [('sg', c) for c in g], ['ybC'], out=ybC[:, 0:ng],
                           in0=hqC[:, 0:ng].rearrange("p c h e -> p c (h e)"), in1=sg[:, c0:c1, :], op=OP.mult)
                    for ci, c in enumerate(g):
                        i2 = c % 2
                        for hh in range(2):
                            self.tr(self.ptr[i2][0:96, hh, :], ybC[:, ci, hh * 96:(hh + 1) * 96], self.identb[:], ['ybC', 'identb'], [('ptr', i2)])
                        self.act(yhT[i2][:], self.ptr[i2][0:96, 0:2, :], AF.Identity, [('ptr', i2)], [('yhT', i2)])
                        for nh in range(2):
                            b = 2 * i2 + nh
                            for hh in range(2):
                                self.mm(self.pb[b][:], yhT[i2][:, hh, :], woM[:, hh, nh * 512:(nh + 1) * 512], hh == 0, hh == 1,
                                        [('yhT', i2), 'woM'], [('pb', b)])
                            self.xupd(c, nh, b)

    def mla(self, l, last):
        nc, P, d = self.nc, self.P, self.d
        SCALE = 96.0 ** -0.5
        with Scope(self) as S:
            wA = S.sb("wA", [128, 8, 416], BF16)
            wuq = S.sb("wuq", [128, 2, 384], BF16)
            wuqs = S.sb("wuqs", [128, 2, 4, 96], BF16)
            wkr = S.sb("wkr", [128, 8, 96], BF16)
            wkrs = S.sb("wkrs", [128, 8, 96], BF16)
            wukv = S.sb("wukv", [128, 640], BF16)
            woA = S.sb("woA", [96, 4, DM], BF16)
            ropeC = S.sb("ropeC", [96, TOK], BF16)
            ropeS = S.sb("ropeS", [96, TOK], BF16)
            Vt = S.sb("Vt", [128, NT, 384], BF16)
            cqnT = S.sb("cqnT", [128, 2, 512], BF16)
            ckvnT = S.sb("ckvnT", [128, 512], BF16)
            cqn = [S.sb("cqn%d" % i, [128, 256], BF16) for i in range(2)]
            ckvn = [S.sb("ckvn%d" % i, [128, 128], BF16) for i in range(2)]
            rt1 = S.sb("rt1", [96, 512], F32)
            rt2 = S.sb("rt2", [96, 512], F32)
            pT = [S.sb("pT%d" % i, [128, 512], BF16) for i in range(3)]
            rc = S.sb("rc", [96, 512], F32)
            yaT = S.sb("yaT", [96, 4, 512], BF16)
            onesb = S.sb("onesb", [128, 96], BF16)
            st = [S.sb("st%d" % i, [128, 2], F32) for i in range(2)]
            rs = [S.sb("rs%d" % i, [128, 2], F32) for i in range(2)]
            gq = S.sb("gq", [128, 2, 2], F32)
            gkv = S.sb("gkv", [128, 2], F32)
            qT = self.hT[0:96, 0:4, :]
            kT = self.hT[0:96, 4:8, :]
            win = d['w_in'][l]
            self.dma('pool', wA[:], win[:, 1808:2224].rearrange("(k p) c -> p k c", p=128), [], ['wA'], 'aw')
            self.dma('pool', wuq[:], d['w_uq'][l].rearrange("(k p) c -> p k c", p=128), [], ['wuq'], 'aw')
            self.v('dve', 'memset', [], ['wuqs'], wuqs[:], 0.0)
            self.v('dve', 'memset', [], ['wkr'], wkr[:], 0.0)
            self.v('dve', 'memset', [], ['wkrs'], wkrs[:], 0.0)
            self.v('dve', 'memset', [], ['onesb'], onesb[:], 1.0)
            for c in range(2):
                self.dma('pool', wuqs[:, c, :, 64:96], d['w_uq_sw'][l][c * 128:(c + 1) * 128, :, :], ['wuqs'], ['wuqs'], 'aw')
            self.dma('pool', wkr[:, :, 64:96], win[:, 2192:2224].rearrange("(k p) c -> p k c", p=128), ['wkr'], ['wkr'], 'aw')
            self.dma('pool', wkrs[:, :, 64:96], d['w_kr_sw'][l].rearrange("(k p) c -> p k c", p=128), ['wkrs'], ['wkrs'], 'aw')
            self.dma('pool', wukv[:], d['w_ukv'][l], [], ['wukv'], 'aw')
            self.dma('pool', woA[:], d['w_out'][l][640:1024, :].rearrange("(h p) c -> p h c", p=96), [], ['woA'], 'aw')
            self.dma('sp', ropeC[64:96, :], d['ropeC'][:, :], [], ['rope'], 'aw')
            self.dma('sp', ropeS[64:96, :], d['ropeS'][:, :], [], ['rope'], 'aw')
            self.dma('sp', gq[:], d['gqfm'][:, :, :], [], ['gq'], 'aw')
            self.dma('sp', gkv[:], d['gkvfm'][:, :], [], ['gq'], 'aw')
            wv = wukv[:, :].rearrange("p (h c) -> p h c", c=160)[:, :, 64:160]
            for (o, n) in self.blocks():
                tl = list(range(o // 128, (o + n) // 128))
                hk = [('hT', t) for t in tl]
                def a1_front(t):
                    b1 = t % 2
                    for k in range(8):
                        self.mm(self.pb[b1][:, 0:416], self.hT[:, k, t * 128:(t + 1) * 128], wA[:, k, :], k == 0, k == 7,
                                [('hT', t), 'wA'], [('pb', b1)])
                a1_front(tl[0])
                for ti, t in enumerate(tl):
                    b1 = t % 2
                    if ti + 1 < len(tl):
                        a1_front(tl[ti + 1])
                    si = t % 2
                    self.v('dve', 'memset', [], [('st', si)], st[si][:], 0.0)
                    self.cnt += 1
                    self.act(self.junk[:, 0:256], self.pb[b1][:, 0:256], AF.Square, [('pb', b1)], [('junk', self.cnt), ('st', si)],
                             accum=st[si][:, 0:1])
                    self.cnt += 1
                    self.act(self.junk[:, 256:384], self.pb[b1][:, 256:384], AF.Square, [('pb', b1)], [('junk', self.cnt), ('st', si)],
                             accum=st[si][:, 1:2])
                    self.act(rs[si][:, 0:1], st[si][:, 0:1], AF.Sqrt, [('st', si), 'epsT'], [('rs', si)], bias=self.epsT[:], scale=1.0 / 256)
                    self.act(rs[si][:, 1:2], st[si][:, 1:2], AF.Sqrt, [('st', si), 'epsT'], [('rs', si)], bias=self.epsT[:], scale=1.0 / 128)
                    self.v('dve', 'reciprocal', [('rs', si)], [('rs', si)], out=rs[si][:], in_=rs[si][:])
                    self.v('dve', 'tensor_scalar', [('pb', b1), ('rs', si)], [('cqn', si)], out=cqn[si][:], in0=self.pb[b1][:, 0:256],
                           scalar1=rs[si][:, 0:1], scalar2=None, op0=OP.mult)
                    self.v('dve', 'tensor_scalar', [('pb', b1), ('rs', si)], [('ckvn', si)], out=ckvn[si][:], in0=self.pb[b1][:, 256:384],
                           scalar1=rs[si][:, 1:2], scalar2=None, op0=OP.mult)
                    pi = t % 2
                    for c in range(2):
                        self.tr(self.ptr[pi][:, c, :], cqn[si][:, c * 128:(c + 1) * 128], self.identb[:], [('cqn', si), 'identb'], [('ptr', pi)])
                    self.tr(self.ptr[pi][:, 2, :], ckvn[si][:], self.identb[:], [('ckvn', si), 'identb'], [('ptr', pi)])
                    for c in range(2):
                        self.act(cqnT[:, c, ti * 128:(ti + 1) * 128], self.ptr[pi][:, c, :], AF.Identity, [('ptr', pi), 'gq'], ['cqnT'],
                                 scale=gq[:, l, c:c + 1])
                    self.act(ckvnT[:, ti * 128:(ti + 1) * 128], self.ptr[pi][:, 2, :], AF.Identity, [('ptr', pi), 'gq'], ['ckvnT'],
                             scale=gkv[:, l:l + 1])
                    bv = 2 + t % 2
                    self.mm(self.pb[bv][:, 0:384], ckvnT[:, ti * 128:(ti + 1) * 128], wv, True, True, ['ckvnT', 'wukv'], [('pb', bv)])
                    self.act(Vt[:, t, :], self.pb[bv][:, 0:384], AF.Identity, [('pb', bv)], [('Vt', t)])
                for k in range(8):
                    self.mm(self.pb[4][0:96, 0:n], wkr[:, k, :], self.hT[:, k, o:o + n], k == 0, k == 7, hk + ['wkr'], [('pb', 4)])
                for k in range(8):
                    self.mm(self.pb[5][0:96, 0:n], wkrs[:, k, :], self.hT[:, k, o:o + n], k == 0, k == 7, hk + ['wkrs'], [('pb', 5)])
                for h in range(4):
                    for c in range(2):
                        self.mm(self.pb[0][0:96, 0:n], wuq[:, c, h * 96:(h + 1) * 96], cqnT[:, c, 0:n], c == 0, c == 1, ['wuq', 'cqnT'], [('pb', 0)])
                    for c in range(2):
                        self.mm(self.pb[1][0:96, 0:n], wuqs[:, c, h, :], cqnT[:, c, 0:n], c == 0, c == 1, ['wuqs', 'cqnT'], [('pb', 1)])
                    bk = 2 + h % 2
                    self.mm(self.pb[bk][0:64, 0:n], wukv[:, h * 160:h * 160 + 64], ckvnT[:, 0:n], True, True, ['wukv', 'ckvnT'], [('pb', bk)])
                    self.act(qT[0:64, h, o:o + n], self.pb[0][0:64, 0:n], AF.Identity, [('pb', 0)], hk)
                    self.act(kT[0:64, h, o:o + n], self.pb[bk][0:64, 0:n], AF.Identity, [('pb', bk)], hk)
                    self.v('dve', 'tensor_tensor', [('pb', 0), 'rope'], ['rt1'], out=rt1[64:96, 0:n], in0=self.pb[0][64:96, 0:n],
                           in1=ropeC[64:96, o:o + n], op=OP.mult)
                    self.v('dve', 'tensor_tensor', [('pb', 1), 'rope'], ['rt2'], out=rt2[64:96, 0:n], in0=self.pb[1][64:96, 0:n],
                           in1=ropeS[64:96, o:o + n], op=OP.mult)
                    self.v('pool', 'tensor_tensor', ['rt1', 'rt2'], hk, out=qT[64:96, h, o:o + n], in0=rt1[64:96, 0:n],
                           in1=rt2[64:96, 0:n], op=OP.add)
                self.v('dve', 'tensor_tensor', [('pb', 4), 'rope'], ['rt1'], out=rt1[64:96, 0:n], in0=self.pb[4][64:96, 0:n],
                       in1=ropeC[64:96, o:o + n], op=OP.mult)
                self.v('dve', 'tensor_tensor', [('pb', 5), 'rope'], ['rt2'], out=rt2[64:96, 0:n], in0=self.pb[5][64:96, 0:n],
                       in1=ropeS[64:96, o:o + n], op=OP.mult)
                for h in range(4):
                    self.v('pool', 'tensor_tensor', ['rt1', 'rt2'], hk, out=kT[64:96, h, o:o + n], in0=rt1[64:96, 0:n],
                           in1=rt2[64:96, 0:n], op=OP.add)
            qblocks = [(256 + i * 512, 512, list(range(NT))) for i in range(4)]
            if not last:
                qblocks.append((0, 256, [0, 1]))
            for (qo, qn, ktiles) in qblocks:
                qk = [('hT', t) for t in range(qo // 128, (qo + qn) // 128)]
                for h in range(4):
                    def st_mm(i):
                        kt_ = ktiles[i]
                        self.mm(self.pb[i % 2][:, 0:qn], kT[:, h, kt_ * 128:(kt_ + 1) * 128], qT[:, h, qo:qo + qn], True, True,
                                qk + [('hT', kt_)], [('pb', i % 2)])
                    st_mm(0)
                    for i, kt in enumerate(ktiles):
                        sb_ = i % 2
                        pi = i % 3
                        if i + 1 < len(ktiles):
                            st_mm(i + 1)
                        self.act(pT[pi][:, 0:qn], self.pb[sb_][:, 0:qn], AF.Exp, [('pb', sb_)], [('pT', pi)], scale=SCALE)
                        self.mm(self.pb[2][0:96, 0:qn], Vt[:, kt, h * 96:(h + 1) * 96], pT[pi][:, 0:qn], i == 0, i == len(ktiles) - 1,
                                [('Vt', kt), ('pT', pi)], [('pb', 2)])
                        self.mm(self.pb[3][0:96, 0:qn], onesb[:, :], pT[pi][:, 0:qn], i == 0, i == len(ktiles) - 1,
                                ['onesb', ('pT', pi)], [('pb', 3)])
                    self.v('dve', 'reciprocal', [('pb', 3)], ['rc'], out=rc[:, 0:qn], in_=self.pb[3][0:96, 0:qn])
                    self.v('dve', 'tensor_tensor', [('pb', 2), 'rc'], [('yaT', h)], out=yaT[:, h, 0:qn], in0=self.pb[2][0:96, 0:qn],
                           in1=rc[:, 0:qn], op=OP.mult)
                for ti, t in enumerate(range(qo // 128, (qo + qn) // 128)):
                    for nh in range(2):
                        b = 4 + nh
                        for h in range(4):
                            self.mm(self.pb[b][:], yaT[:, h, ti * 128:(ti + 1) * 128], woA[:, h, nh * 512:(nh + 1) * 512], h == 0, h == 3,
                                    [('yaT', h), 'woA'], [('pb', b)])
                        self.xupd(t, nh, b)

    def fourier(self, l, last):
        nc, P, d = self.nc, self.P, self.d
        with Scope(self) as S:
            wpf = S.sb("wpf", [128, 8, 256], BF16)
            pfT = S.sb("pfT", [128, 2, TOK], BF16)
            bd = S.sb("bd", [128, 256], BF16)
            pfCS = S.sb("pfCS", [128, NT, 2, 256], BF16)
            tw = [S.sb("tw%d" % i, [128, 2, 1024], BF16) for i in range(3)]
            yfT = S.sb("yfT", [128, 2, TOK], BF16)
            wo = S.sb("wo", [128, 2, DM], BF16)
            c256 = S.sb("c256", [128, 2, 2, 256], BF16)
            self.dma('pool', wpf[:], d['w_in'][l][:, 0:256].rearrange("(k p) c -> p k c", p=128), [], ['wpf'], 'fw')
            self.dma('pool', wo[:], d['w_out'][l][0:256, :].rearrange("(k p) c -> p k c", p=128), [], ['wo'], 'fw')
            self.dma('sp', bd[:], d['bdcs'][:, :], [], ['bd'], 'fw')
            self.dma('sp', c256[:], d['c256'][:, :, :, :], [], ['c256'], 'fw')
            nb = 0
            for j in range(2):
                for (o, n) in self.blocks():
                    b = nb % 6
                    nb += 1
                    for k in range(8):
                        self.mm(self.pb[b][:, 0:n], wpf[:, k, j * 128:(j + 1) * 128], self.hT[:, k, o:o + n], k == 0, k == 7,
                                ['wpf'] + [('hT', t) for t in range(o // 128, (o + n) // 128)], [('pb', b)])
                    self.act(pfT[:, j, o:o + n], self.pb[b][:, 0:n], AF.Identity, [('pb', b)], [('pfT', j, o)])
            for t in range(NT):
                for j in range(2):
                    b = nb % 6
                    nb += 1
                    self.mm(self.pb[b][:, 0:256], pfT[:, j, t * 128:(t + 1) * 128], bd[:], True, True,
                            [('pfT', j, (t // 4) * 512), 'bd'], [('pb', b)])
                    self.act(pfCS[:, t, j, :], self.pb[b][:, 0:256], AF.Identity, [('pb', b)], [('pfCS', t)])
            ntw = 0
            for half in range(2):
                for nt in range(16):
                    t = nt + 2
                    s = ntw % 3
                    ntw += 1
                    self.dma('sp', tw[s][:, 0, :], d['twC'][nt * 128:(nt + 1) * 128, half * 1024:(half + 1) * 1024], [], [('tw', s)], 'tw%d' % s)
                    self.dma('sp', tw[s][:, 1, :], d['twS'][nt * 128:(nt + 1) * 128, half * 1024:(half + 1) * 1024], [], [('tw', s)], 'tw%d' % s)
                    for j in range(2):
                        for q in range(2):
                            b = j * 2 + q
                            self.mm(self.pb[b][:], pfCS[:, t, j, 0:128], tw[s][:, 0, q * 512:(q + 1) * 512], nt == 0, False,
                                    [('pfCS', t), ('tw', s)], [('pb', b)])
                            self.mm(self.pb[b][:], pfCS[:, t, j, 128:256], tw[s][:, 1, q * 512:(q + 1) * 512], False, nt == 15,
                                    [('pfCS', t), ('tw', s)], [('pb', b)])
                for j in range(2):
                    for q in range(2):
                        b = j * 2 + q
                        o = 256 + half * 1024 + q * 512
                        self.act(yfT[:, j, o:o + 512], self.pb[b][:], AF.Identity, [('pb', b)], [('yfT', o // 128)])
            if not last:
                for j in range(2):
                    b = 4 + j
                    for nt in range(2):
                        self.mm(self.pb[b][:, 0:256], pfCS[:, nt, j, 0:128], c256[:, nt, 0, :], nt == 0, False,
                                [('pfCS', nt), 'c256'], [('pb', b)])
                        self.mm(self.pb[b][:, 0:256], pfCS[:, nt, j, 128:256], c256[:, nt, 1, :], False, nt == 1,
                                [('pfCS', nt), 'c256'], [('pb', b)])
                    self.act(yfT[:, j, 0:256], self.pb[b][:, 0:256], AF.Identity, [('pb', b)], [('yfT', 0)])
            nb = 0
            for t in (range(2, NT) if last else range(NT)):
                for nh in range(2):
                    b = nb % 6
                    nb += 1
                    for j in range(2):
                        self.mm(self.pb[b][:], yfT[:, j, t * 128:(t + 1) * 128], wo[:, j, nh * 512:(nh + 1) * 512], j == 0, j == 1,
                                [('yfT', 0 if t < 2 else 2 + ((t - 2) // 4) * 4), 'wo'], [('pb', b)])
                    self.xupd(t, nh, b)

    def mlp(self, l, tiles):
        nc, P, d = self.nc, self.P, self.d
        groups = [tiles[i:i + 6] for i in range(0, len(tiles), 6)]
        with Scope(self) as S:
            uT = S.sb("uT", [128, 32, 768], BF16)
            wu = [S.sb("wu%d" % i, [128, 8, 512], BF16) for i in range(2)]
            wd = [S.sb("wd%d" % i, [128, 4, 512], BF16) for i in range(2)]
            tmp = [S.sb("mt%d" % i, [128, 512], F32) for i in range(2)]
            nu = nd = nb = 0
            for g in groups:
                G = 128 * len(g)
                t0 = g[0] * 128
                subs = [(0, G)] if G <= 512 else [(0, G // 2), (G // 2, G // 2)]
                for hb in range(8):
                    s = nu % 2
                    nu += 1
                    self.dma('pool', wu[s][:], d['w_up'][l][:, hb * 512:(hb + 1) * 512].rearrange("(k p) c -> p k c", p=128),
                             [], [('wu', s)], 'wu%d' % s)
                    for cc in range(4):
                        j = hb * 4 + cc
                        for (o, n) in subs:
                            b = nb % 6
                            nb += 1
                            for k in range(8):
                                self.mm(self.pb[b][:, 0:n], wu[s][:, k, cc * 128:(cc + 1) * 128],
                                        self.hT[:, k, t0 + o:t0 + o + n], k == 0, k == 7,
                                        [('wu', s)] + [('hT', t) for t in g], [('pb', b)])
                            ri = nb % 2
                            self.act(tmp[ri][:, 0:n], self.pb[b][:, 0:n], AF.Relu, [('pb', b)], [('mt', ri)])
                            self.v('dve', 'tensor_tensor', [('mt', ri)], [('uT', j)], out=uT[:, j, o:o + n],
                                   in0=tmp[ri][:, 0:n], in1=tmp[ri][:, 0:n], op=OP.mult)
                for nh in range(2):
                    for jb in range(8):
                        s = nd % 2
                        nd += 1
                        self.dma('pool', wd[s][:],
                                 d['w_down'][l][jb * 512:(jb + 1) * 512, nh * 512:(nh + 1) * 512].rearrange("(j p) c -> p j c", p=128),
                                 [], [('wd', s)], 'wd%d' % s)
                        for jj in range(4):
                            j = jb * 4 + jj
                            for ti, t in enumerate(g):
                                self.mm(self.pb[ti][:], uT[:, j, ti * 128:(ti + 1) * 128], wd[s][:, jj, :], j == 0, j == 31,
                                        [('wd', s), ('uT', j)], [('pb', ti)])
                    for ti, t in enumerate(g):
                        self.xupd(t, nh, ti)

    def final(self):
        nc, P, d = self.nc, self.P, self.d
        with Scope(self) as S:
            gf = S.sb("gf", [128, DM], F32)
            ob = [S.sb("ob%d" % i, [128, DM], F32) for i in range(2)]
            self.dma('sp', gf[:], d['gfin'][:, :], [], ['gf'], 'cst')
            self.rms_all(range(2, NT))
            for t in range(2, NT):
                i = t % 2
                self.act(ob[i][:], self.xs[:, t, :], AF.Identity, [('xs', t), 'rstd'], [('ob', i)], scale=self.rstd[:, t:t + 1])
                self.v('dve', 'tensor_tensor', [('ob', i), 'gf'], [('ob', i)], out=ob[i][:], in0=ob[i][:], in1=gf[:], op=OP.mult)
                self.dma('sp', d['out'][(t - 2) * 128:(t - 1) * 128, :], ob[i][:], [('ob', i)], [('out', t)], 'out')


def dram_decls(nc):
    d = {}

    def inp(name, shape, dt=F32):
        d[name] = nc.dram_tensor(name, list(shape), dt, kind="ExternalInput").ap()

    inp('xin', [TOK, DM])
    inp('c2', [128, 16])
    inp('w_mod', [2, 1024, 6144])
    inp('bmodfm', [128, 2, 48])
    inp('gnfm', [128, 2, 2, 8])
    inp('w_in', [2, 1024, 2224])
    inp('w_out', [2, 1024, 1024])
    inp('w_up', [2, 1024, 4096])
    inp('w_down', [2, 4096, 1024])
    inp('gfin', [128, DM])
    inp('identb', [128, 128], BF16)
    inp('identf', [128, 128])
    inp('w_uq', [2, 256, 384])
    inp('w_ukv', [2, 128, 640])
    inp('w_uq_sw', [2, 256, 4, 32])
    inp('w_kr_sw', [2, 1024, 32])
    inp('ropeC', [32, TOK], BF16)
    inp('ropeS', [32, TOK], BF16)
    inp('gqfm', [128, 2, 2])
    inp('gkvfm', [128, 2])
    inp('w_g', [2, 1024, 16])
    inp('bg', [2, 8, 2])
    inp('convdg', [2, 96, 8, 5, 96], BF16)
    inp('gmB', [128, 2, 384])
    inp('E2', [8, 2, 8])
    inp('negidf', [128, 128])
    inp('mlmask', [128, 2, 256], BF16)
    inp('bdcs', [128, 256], BF16)
    inp('twC', [2048, 2048], BF16)
    inp('twS', [2048, 2048], BF16)
    inp('c256', [128, 2, 2, 256], BF16)
    d['out'] = nc.dram_tensor('out', [2048, DM], F32, kind="ExternalOutput").ap()
    if DBGXS:
        d['dbgxs'] = nc.dram_tensor('dbgxs', [TOK, DM], F32, kind="ExternalOutput").ap()
    return d


def build_program():
    nc = bass.Bass("TRN2", target_bir_lowering=False)
    d = dram_decls(nc)
    P = Prog(nc)
    kb = K(nc, P, d)
    kb.build()
    P.plan()
    es = contextlib.ExitStack()
    with es:
        P.start_real(es)
        kb.build()
    return nc, P


def host_inputs(inputs, b):
    f = lambda a: np.ascontiguousarray(np.asarray(a, dtype=np.float32))
    x, c, ctx, c_ctx = inputs['x'], inputs['c'], inputs['ctx'], inputs['c_ctx']
    m = {}
    m['xin'] = f(np.concatenate([ctx[b], x[b]], axis=0))
    c2 = np.stack([np.asarray(c[b]).reshape(8, 128).T, np.asarray(c_ctx).reshape(8, 128).T], axis=-1)
    m['c2'] = f(c2.reshape(128, 16))
    m['w_mod'] = f(inputs['w_mod'])
    m['bmodfm'] = f(np.asarray(inputs['b_mod']).reshape(2, 48, 128).transpose(2, 0, 1))
    gn = np.stack([np.asarray(inputs['g_norm1']), np.asarray(inputs['g_norm2'])], axis=1)
    m['gnfm'] = f(gn.reshape(2, 2, 8, 128).transpose(3, 0, 1, 2))
    m['w_in'] = f(inputs['w_in'])
    m['w_out'] = f(inputs['w_out'])
    m['w_up'] = f(inputs['w_up'])
    m['w_down'] = f(inputs['w_down'])
    m['gfin'] = f(np.tile(np.asarray(inputs['g_final'])[None, :], (128, 1)))
    m['identb'] = np.eye(128, dtype=np.float32).astype(ml_dtypes.bfloat16)
    m['identf'] = np.eye(128, dtype=np.float32)
    m.update(host_consts())
    perm = np.array([a * 16 + (1 - b_) * 8 + j for a in range(2) for b_ in range(2) for j in range(8)])
    wuq = np.asarray(inputs['w_uq'])
    m['w_uq'] = f(wuq)
    m['w_ukv'] = f(inputs['w_ukv'])
    m['w_uq_sw'] = f(wuq.reshape(2, 256, 4, 96)[:, :, :, 64:96][..., perm])
    m['w_kr_sw'] = f(np.asarray(inputs['w_in'])[:, :, 2192:2224][..., perm])
    m['gqfm'] = f(np.asarray(inputs['g_q_norm']).reshape(2, 2, 128).transpose(2, 0, 1))
    m['gkvfm'] = f(np.asarray(inputs['g_kv_norm']).T)
    gp = [0, 1, 2, 3, 8, 9, 10, 11, 4, 5, 6, 7, 12, 13, 14, 15]
    m['w_g'] = f(np.asarray(inputs['w_in'])[:, :, 1792:1808][..., gp])
    m['bg'] = f(np.asarray(inputs['b_gates'])[:, gp].reshape(2, 2, 8).transpose(0, 2, 1))
    cv = np.asarray(inputs['conv_qk'], dtype=np.float32).reshape(2, 5, 8, 96)
    dgc = np.zeros((2, 96, 8, 5, 96), np.float32)
    pi_ = np.arange(96)
    dgc[:, pi_, :, :, pi_] = cv.transpose(3, 0, 2, 1)
    m['convdg'] = dgc.astype(ml_dtypes.bfloat16)
    m['gmB'] = f(np.tile(np.asarray(inputs['g_mlstm'])[None, :, :], (128, 1, 1)))
    return m


def host_consts():
    bf = lambda a: np.ascontiguousarray(a.astype(np.float32)).astype(ml_dtypes.bfloat16)
    m = {}
    i64 = np.arange(64)
    a64 = 2 * np.pi * np.outer(i64, i64) / 64
    C64, S64 = np.cos(a64) / 8.0, np.sin(a64) / 8.0
    z = np.zeros((64, 64))
    m['bdcs'] = bf(np.block([[C64, z, S64, z], [z, C64, z, S64]]))
    n = np.arange(2048, dtype=np.float64)
    aN = 2 * np.pi * (np.outer(n, n) % 2048) / 2048
    m['twC'] = bf(np.cos(aN) / np.sqrt(2048.0))
    m['twS'] = bf(-np.sin(aN) / np.sqrt(2048.0))
    E2 = np.zeros((8, 2, 8), np.float32)
    for k_ in range(4):
        E2[k_, 0, k_] = 1.0
        E2[4 + k_, 1, 4 + k_] = 1.0
    m['E2'] = E2
    m['negidf'] = -np.eye(128, dtype=np.float32)
    sidx, tidx = np.arange(128)[:, None], np.arange(128)[None, :]
    mf = np.where(sidx <= tidx, 0.0, -30000.0)
    mb_ = np.where(sidx >= tidx, 0.0, -30000.0)
    m['mlmask'] = bf(np.stack([np.concatenate([mf, mf], axis=1), np.concatenate([mb_, mb_], axis=1)], axis=1))
    tt = np.arange(2048)
    freqs = 10000.0 ** (-np.arange(8, dtype=np.float64) / 8)
    ang = np.stack([np.outer(tt // 64, freqs), np.outer(tt % 64, freqs)], axis=1)
    cosT = np.ones((TOK, 2, 2, 8)); sinT = np.zeros((TOK, 2, 2, 8))
    cosT[256:] = np.cos(ang)[:, :, None, :]
    sinT[256:, :, 0, :] = -np.sin(ang)
    sinT[256:, :, 1, :] = np.sin(ang)
    m['ropeC'] = bf(cosT.reshape(TOK, 32).T)
    m['ropeS'] = bf(sinT.reshape(TOK, 32).T)
    n = np.arange(256, dtype=np.float64)
    a2 = 2 * np.pi * (np.outer(n, n) % 256) / 256
    c = np.stack([np.cos(a2) / 16.0, -np.sin(a2) / 16.0], axis=1)
    m['c256'] = bf(c.reshape(2, 128, 2, 256).transpose(1, 0, 2, 3))
    return m


_CACHE = {}


def kernel(**inputs):
    if 'nc' not in _CACHE:
        _CACHE['nc'] = build_program()
    nc, P = _CACHE['nc']
    shared = None
    in_maps = []
    for b in range(8):
        m = host_inputs(inputs, b) if shared is None else None
        if shared is None:
            shared = m
        else:
            m = dict(shared)
            f = lambda a: np.ascontiguousarray(np.asarray(a, dtype=np.float32))
            m['xin'] = f(np.concatenate([inputs['ctx'][b], inputs['x'][b]], axis=0))
            c2 = np.stack([np.asarray(inputs['c'][b]).reshape(8, 128).T, np.asarray(inputs['c_ctx']).reshape(8, 128).T], axis=-1)
            m['c2'] = f(c2.reshape(128, 16))
        in_maps.append(m)
    res = run_bass_kernel_spmd(nc, in_maps, core_ids=list(range(8)))
    _CACHE['res'] = res
    return np.stack([np.asarray(r['out'], dtype=np.float32) for r in res.results], axis=0)
```

```python
import contextlib, bisect, math
import numpy as np
import ml_dtypes
import concourse.bass as bass
import concourse.mybir as mybir
from concourse.bass_utils import run_bass_kernel_spmd

F32 = mybir.dt.float32
BF16 = mybir.dt.bfloat16
AF = mybir.ActivationFunctionType
OP = mybir.AluOpType
AX = mybir.AxisListType

NT = 18
TOK = 2304
DM = 1024
EPS = 1e-6
DEPTH = 2
STAGE = 99
DEBUG = []
BRANCHES = 'FAM'
DBGXS = False


class Dummy:
    def __getitem__(self, k):
        return self

    def __getattr__(self, n):
        return lambda *a, **k: self


class Prog:
    ENGS = ('pe', 'dve', 'act', 'pool', 'sp')

    def __init__(self, nc):
        self.nc = nc
        self.dry = True
        self.rec = []
        self.i = 0

    def op(self, eng, fn, r=(), w=(), dma=None):
        if self.dry:
            self.rec.append(('op', eng, tuple(r), tuple(w), dma))
        else:
            i = self.i
            E = self.H[eng]
            for (s, v) in self.waits[i]:
                E.wait_ge(self.sems[s], v)
            if fn is not None:
                ins = fn()
                inc = self.incs[i]
                if inc is not None:
                    ins.then_inc(self.sems[inc[0]], inc[1])
        self.i += 1

    def barrier(self):
        if self.dry:
            self.rec.append(('bar',))
        self.i += 1

    def plan(self):
        rec = self.rec
        n = len(rec)
        lastw, readers, last_on_eng, dma_ops = {}, {}, {}, {}
        deps = [None] * n
        pend = {e: None for e in self.ENGS}
        for i, r in enumerate(rec):
            if r[0] == 'bar':
                bd = set(last_on_eng.values())
                for s, l in dma_ops.items():
                    if l:
                        bd.add(l[-1])
                for e in self.ENGS:
                    pend[e] = set(bd) | (pend[e] or set())
                lastw.clear()
                readers.clear()
                continue
            _, eng, R, W, dma = r
            d = set()
            for k in R:
                if k in lastw:
                    d.add(lastw[k])
            for k in W:
                if k in lastw:
                    d.add(lastw[k])
                d.update(readers.get(k, ()))
            if pend[eng]:
                d |= pend[eng]
                pend[eng] = None
            d.discard(i)
            if eng == 'pe':
                d = {j for j in d if not (rec[j][1] == 'pe' and rec[j][4] is None)}
            deps[i] = d
            for k in W:
                lastw[k] = i
                readers[k] = []
            for k in R:
                lst = readers.setdefault(k, [])
                if dma is None:
                    lst[:] = [j for j in lst if not (rec[j][1] == eng and rec[j][4] is None)]
                lst.append(i)
            last_on_eng[eng] = i
            if dma:
                dma_ops.setdefault(dma, []).append(i)
        sig = [False] * n
        for i in range(n):
            if deps[i]:
                for j in deps[i]:
                    if rec[j][4] is None:
                        sig[j] = True
        cnt = {e: 0 for e in self.ENGS}
        ev = [None] * n
        for i, r in enumerate(rec):
            if r[0] != 'op':
                continue
            if r[4] is None and sig[i]:
                cnt[r[1]] += 1
                ev[i] = ('E_' + r[1], cnt[r[1]])
        self.waits = [()] * n
        self.incs = [None] * n
        seen = {e: {} for e in self.ENGS}
        for i, r in enumerate(rec):
            if r[0] != 'op':
                continue
            eng = r[1]
            wl = {}
            for j in deps[i]:
                if rec[j][4] is None:
                    s, v = ev[j]
                else:
                    s = 'D_' + rec[j][4]
                    v = 16 * bisect.bisect_left(dma_ops[rec[j][4]], i)
                if v > wl.get(s, 0):
                    wl[s] = v
            ws = []
            for s, v in wl.items():
                if v > seen[eng].get(s, 0):
                    seen[eng][s] = v
                    ws.append((s, v))
            self.waits[i] = tuple(ws)
            if r[4] is not None:
                self.incs[i] = ('D_' + r[4], 16)
            elif sig[i]:
                self.incs[i] = (ev[i][0], 1)
        self.semnames = ['E_' + e for e in self.ENGS] + ['D_' + s for s in dma_ops]
        self.stats = dict(n=n, cnt=cnt, nw=sum(len(w) for w in self.waits))

    def start_real(self, es):
        nc = self.nc
        self.dry = False
        self.i = 0
        self.H = {'pe': nc.tensor, 'dve': nc.vector, 'act': nc.scalar, 'pool': nc.gpsimd, 'sp': nc.sync}
        self.sems = {s: es.enter_context(nc.semaphore(s)) for s in self.semnames}


class Scope:
    _n = 0

    def __init__(self, K):
        self.K = K
        self.es = contextlib.ExitStack()

    def __enter__(self):
        return self

    def sb(self, name, shape, dt):
        if self.K.P.dry:
            return Dummy()
        Scope._n += 1
        return self.es.enter_context(self.K.nc.sbuf_tensor("%s_%d" % (name, Scope._n), list(shape), dt))

    def __exit__(self, *a):
        self.K.P.barrier()
        self.K.semmap = {}
        self.es.close()
        return False


def which(t):
    return 1 if t < 2 else 0


class K:
    def __init__(self, nc, P, dram):
        self.nc, self.P, self.d = nc, P, dram
        self.semmap = {}

    def mm(self, out, lhsT, rhs, start, stop, r, w):
        nc = self.nc
        self.P.op('pe', lambda: nc.tensor.matmul(out, lhsT=lhsT, rhs=rhs, start=start, stop=stop), r, w)

    def tr(self, out, in_, ident, r, w):
        nc = self.nc
        self.P.op('pe', lambda: nc.tensor.transpose(out, in_, ident), r, w)

    def act(self, out, in_, func, r, w, bias=None, scale=None, accum=None):
        nc = self.nc
        kw = {}
        if bias is not None:
            kw['bias'] = bias
        if scale is not None:
            kw['scale'] = scale
        if accum is not None:
            kw['accum_out'] = accum
        self.P.op('act', lambda: nc.scalar.activation(out=out, in_=in_, func=func, **kw), r, w)

    def v(self, eng, name, r, w, *a, **kw):
        nc = self.nc
        E = nc.vector if eng == 'dve' else nc.gpsimd
        self.P.op(eng, lambda: getattr(E, name)(*a, **kw), r, w)

    def dma(self, q, out, in_, r, w, sem):
        nc = self.nc
        E = nc.sync if q == 'sp' else nc.gpsimd
        k0 = w[0] if w else sem
        if isinstance(k0, tuple) and k0[0] in ('out', 'dbgxs'):
            name = sem
        elif isinstance(k0, tuple) and k0[0] == 'xs':
            name = 'xs'
        else:
            name = "b%d" % self.semmap.setdefault((k0, q), len(self.semmap))
        self.P.op(q, lambda: E.dma_start(out=out, in_=in_), r, w, dma=name + '_' + q)

    def build(self):
        nc, P, d = self.nc, self.P, self.d
        self.semmap = {}
        with Scope(self) as S:
            self.S0 = S
            self.xs = S.sb("xs", [128, NT, DM], F32)
            self.hT = S.sb("hT", [128, 8, TOK], BF16)
            self.identb = S.sb("identb", [128, 128], BF16)
            self.identf = S.sb("identf", [128, 128], F32)
            self.onesf = S.sb("onesf", [128, 128], F32)
            self.c2 = S.sb("c2", [128, 16], F32)
            self.scb = S.sb("scb", [128, 16], F32)
            self.modv = S.sb("modv", [128, 48, 2], F32)
            self.bmod = S.sb("bmod", [128, 2, 48], F32)
            self.gn = S.sb("gn", [128, 2, 2, 8], F32)
            self.G = S.sb("G", [128, 2, 8, 2], F32)
            self.gaB = S.sb("gaB", [128, 2, DM], F32)
            self.ssq = S.sb("ssq", [128, NT], F32)
            self.rstd = S.sb("rstd", [128, NT], F32)
            self.epsT = S.sb("epsT", [128, 1], F32)
            self.xn = [S.sb("xn%d" % i, [128, DM], BF16) for i in range(2)]
            self.junk = S.sb("junk", [128, DM], BF16)
            self.dg = [S.sb("dg%d" % i, [128, 128], F32) for i in range(2)]
            self.xt = [S.sb("xt%d" % i, [128, 512], F32) for i in range(2)]
            if not P.dry:
                es = S.es
                self.pb = [es.enter_context(nc.psum_tensor("pb%d" % i, [128, 512], F32)) for i in range(6)]
                self.ptr = [es.enter_context(nc.psum_tensor("ptr%d" % i, [128, 8, 128], BF16)) for i in range(2)]
            else:
                self.pb = [Dummy()] * 6
                self.ptr = [Dummy()] * 2
            self.cnt = 0
            self.dma('sp', self.identb[:], d['identb'][:, :], [], ['identb'], 'cst')
            self.dma('sp', self.identf[:], d['identf'][:, :], [], ['identf'], 'cst')
            self.dma('sp', self.c2[:], d['c2'][:, :], [], ['c2'], 'cst')
            self.dma('sp', self.bmod[:], d['bmodfm'][:, :, :], [], ['bmod'], 'cst')
            self.dma('sp', self.gn[:], d['gnfm'][:, :, :, :], [], ['gn'], 'cst')
            self.v('dve', 'memset', [], ['onesf'], self.onesf[:], 1.0)
            self.v('dve', 'memset', [], ['epsT'], self.epsT[:], EPS)
            self.v('dve', 'memset', [], ['ssq'], self.ssq[:], 0.0)
            self.act(self.scb[:], self.c2[:], AF.Silu, ['c2'], ['scb'])
            for t in range(NT):
                self.dma('sp', self.xs[:, t, :], d['xin'][t * 128:(t + 1) * 128, :], [], [('xs', t)], 'xin')
            for l in range(DEPTH):
                last = (l == DEPTH - 1)
                self.mod_phase(l)
                if STAGE <= 1:
                    break
                self.make_gaB(2)
                self.phaseA(l, 0, range(NT))
                if STAGE >= 3:
                    self.mixers(l, last)
                if DBGXS and l == 0:
                    for t in range(NT):
                        self.dma('sp', d['dbgxs'][t * 128:(t + 1) * 128, :], self.xs[:, t, :], [('xs', t)], [('dbgxs', t)], 'out')
                    break
                self.make_gaB(5)
                tiles = list(range(2, NT)) if last else list(range(NT))
                self.phaseA(l, 1, tiles)
                self.mlp(l, tiles)
            self.final()
        P.barrier()
        P.op('sp', None)

    def mod_phase(self, l):
        nc, P, d = self.nc, self.P, self.d
        with Scope(self) as S:
            wm = [S.sb("wm%d" % i, [128, 8, 1024], F32) for i in range(2)]
            rowb = [S.sb("rowb%d" % i, [2, 512], F32) for i in range(2)]
            modps = self.pb[0]
            for blk in range(6):
                s = blk % 2
                self.dma('sp', wm[s][:], d['w_mod'][l][:, blk * 1024:(blk + 1) * 1024].rearrange("(k p) c -> p k c", p=128),
                         [], [('wm', s)], 'wm%d' % s)
                for hb2 in range(2):
                    rb = 1 + (blk * 2 + hb2) % 2
                    for k in range(8):
                        self.mm(self.pb[rb][0:2, :], self.scb[:, 2 * k:2 * k + 2], wm[s][:, k, hb2 * 512:(hb2 + 1) * 512],
                                k == 0, k == 7, [('wm', s), 'scb'], [('pb', rb)])
                    ri = (blk * 2 + hb2) % 2
                    self.act(rowb[ri][:], self.pb[rb][0:2, :], AF.Identity, [('pb', rb)], [('rowb', ri)])
                    for cc in range(4):
                        j = blk * 8 + hb2 * 4 + cc
                        self.mm(modps[:, 2 * j:2 * j + 2], rowb[ri][:, cc * 128:(cc + 1) * 128], self.identf[0:2, 0:2], True, True,
                                [('rowb', ri), 'identf'], [('pb', 0)])
            self.v('dve', 'tensor_tensor', [('pb', 0), 'bmod'], ['modv'], out=self.modv[:],
                   in0=modps[:, 0:96].rearrange("p (j w) -> p j w", w=2),
                   in1=self.bmod[:, l, :].unsqueeze(2).to_broadcast([128, 48, 2]), op=OP.add)
            for ni, vi in ((0, 1), (1, 4)):
                self.v('dve', 'scalar_tensor_tensor', ['modv', 'gn'], ['G'], out=self.G[:, ni, :, :],
                       in0=self.modv[:, vi * 8:(vi + 1) * 8, :], scalar=1.0,
                       in1=self.gn[:, l, ni, :].unsqueeze(2).to_broadcast([128, 8, 2]), op0=OP.add, op1=OP.mult)

    def make_gaB(self, vi):
        for w in range(2):
            for kk in range(8):
                self.cnt += 1
                dgi = self.cnt % 2
                self.v('dve', 'tensor_scalar', ['modv', 'identf'], [('dg', dgi)], out=self.dg[dgi][:], in0=self.identf[:],
                       scalar1=self.modv[:, vi * 8 + kk, w:w + 1], scalar2=None, op0=OP.mult)
                bank = kk // 4
                self.mm(self.pb[bank][:, (kk % 4) * 128:(kk % 4 + 1) * 128], self.onesf[:], self.dg[dgi][:], True, True,
                        ['onesf', ('dg', dgi)], [('pb', bank)])
            for bank in range(2):
                self.act(self.gaB[:, w, bank * 512:(bank + 1) * 512], self.pb[bank][:], AF.Identity, [('pb', bank)], ['gaB'])

    def rms(self, t):
        self.cnt += 1
        self.act(self.junk[:], self.xs[:, t, :], AF.Square, [('xs', t)], [('junk', self.cnt), ('ssq', t)],
                 accum=self.ssq[:, t:t + 1])
        self.act(self.rstd[:, t:t + 1], self.ssq[:, t:t + 1], AF.Sqrt, [('ssq', t), 'epsT'], [('rstd', t)],
                 bias=self.epsT[:], scale=1.0 / DM)
        self.v('dve', 'reciprocal', [('rstd', t)], [('rstd', t)], out=self.rstd[:, t:t + 1], in_=self.rstd[:, t:t + 1])
        self.v('dve', 'memset', [('ssq', t)], [('ssq', t)], self.ssq[:, t:t + 1], 0.0)

    def rms_all(self, tiles):
        tiles = list(tiles)
        t0, t1 = tiles[0], tiles[-1] + 1
        for t in tiles:
            self.cnt += 1
            self.act(self.junk[:], self.xs[:, t, :], AF.Square, [('xs', t)], [('junk', self.cnt), 'ssq'],
                     accum=self.ssq[:, t:t + 1])
        self.act(self.rstd[:, t0:t1], self.ssq[:, t0:t1], AF.Sqrt, ['ssq', 'epsT'], ['rstd'], bias=self.epsT[:], scale=1.0 / DM)
        self.v('dve', 'reciprocal', ['rstd'], ['rstd'], out=self.rstd[:, t0:t1], in_=self.rstd[:, t0:t1])
        self.v('dve', 'memset', ['ssq'], ['ssq'], self.ssq[:, t0:t1], 0.0)

    def phaseA(self, l, ni, tiles):
        vi = 0 if ni == 0 else 3
        tiles = list(tiles)
        self.rms_all(tiles)
        for t in tiles:
            w = which(t)
            self.cnt += 1
            xi = self.cnt % 2
            self.act(self.xn[xi][:], self.xs[:, t, :], AF.Identity, [('xs', t), 'rstd'], [('xn', xi)],
                     scale=self.rstd[:, t:t + 1])
            for k in range(8):
                self.tr(self.ptr[xi][:, k, :], self.xn[xi][:, k * 128:(k + 1) * 128], self.identb[:],
                        [('xn', xi), 'identb'], [('ptr', xi)])
            for k in range(8):
                self.v('dve', 'tensor_scalar', [('ptr', xi), 'G', 'modv'], [('hT', t)], out=self.hT[:, k, t * 128:(t + 1) * 128],
                       in0=self.ptr[xi][:, k, :], scalar1=self.G[:, ni, k, w:w + 1], scalar2=self.modv[:, vi * 8 + k, w:w + 1],
                       op0=OP.mult, op1=OP.add)

    def xupd(self, t, nh, b):
        w = which(t)
        self.cnt += 1
        mi = self.cnt % 2
        self.v('dve', 'tensor_tensor', [('pb', b), 'gaB'], [('xt', mi)], out=self.xt[mi][:], in0=self.pb[b][:],
               in1=self.gaB[:, w, nh * 512:(nh + 1) * 512], op=OP.mult)
        self.v('pool', 'tensor_tensor', [('xt', mi), ('xs', t)], [('xs', t)],
               out=self.xs[:, t, nh * 512:(nh + 1) * 512], in0=self.xs[:, t, nh * 512:(nh + 1) * 512],
               in1=self.xt[mi][:], op=OP.add)

    def blocks(self, lo=0, hi=TOK, step=512):
        return [(o, min(step, hi - o)) for o in range(lo, hi, step)]

    def mixers(self, l, last):
        if 'F' in BRANCHES:
            self.fourier(l, last)
        if 'M' in BRANCHES:
            self.mlstm(l, last)
        if 'A' in BRANCHES:
            self.mla(l, last)

    def mlstm(self, l, last):
        nc, P, d = self.nc, self.P, self.d
        win = d['w_in'][l]
        with Scope(self) as SO:
            acol = SO.sb("acol", [128, NT, 8], F32)
            Mcol = SO.sb("Mcol", [128, NT, 8], F32)
            T4 = SO.sb("T4", [128, 4, NT, 8], F32)
            gm = SO.sb("gm", [128, 384], F32)
            E2 = SO.sb("E2", [8, 2, 8], F32)
            negid = SO.sb("negid", [128, 128], F32)
            mlmask = SO.sb("mlmask", [128, 2, 256], BF16)
            self.dma('sp', gm[:], d['gmB'][:, l, :], [], ['gm'], 'mw')
            self.dma('sp', E2[:], d['E2'][:, :, :], [], ['E2'], 'mw')
            self.dma('sp', negid[:], d['negidf'][:, :], [], ['negid'], 'mw')
            self.dma('sp', mlmask[:], d['mlmask'][:, :, :], [], ['mlmask'], 'mw')
            with Scope(self) as S:
                wg = S.sb("wg", [128, 8, 16], BF16)
                bg = S.sb("bg", [8, 2], F32)
                nbf = S.sb("nbf", [8, 1], F32)
                IG = S.sb("IG", [8, TOK], F32)
                LF = S.sb("LF", [8, TOK], F32)
                FF = S.sb("FF", [8, TOK], F32)
                FB = S.sb("FB", [8, TOK], F32)
                AFr = S.sb("AFr", [8, TOK], F32)
                ABr = S.sb("ABr", [8, TOK], F32)
                R1 = S.sb("R1", [8, NT, 8], F32)
                R2 = S.sb("R2", [8, NT, 8], F32)
                mcol = S.sb("mcol", [128, NT, 8], F32)
                MendB = S.sb("MendB", [128, NT, 8], F32)
                MP = S.sb("MP", [128, NT, 8], F32)
                ex = S.sb("ex", [128, 4, NT, 8], F32)
                self.dma('pool', wg[:], d['w_g'][l].rearrange("(k p) c -> p k c", p=128), [], ['wg'], 'gw')
                self.dma('sp', bg[:], d['bg'][l], [], ['bg'], 'gw')
                self.v('dve', 'tensor_scalar', ['bg'], ['nbf'], out=nbf[:], in0=bg[:, 1:2], scalar1=-1.0, scalar2=None, op0=OP.mult)
                nb = 0
                for (o, n) in self.blocks():
                    hk = [('hT', t) for t in range(o // 128, (o + n) // 128)]
                    bi, bf_ = nb % 6, (nb + 1) % 6
                    nb += 2
                    for k in range(8):
                        self.mm(self.pb[bi][0:8, 0:n], wg[:, k, 0:8], self.hT[:, k, o:o + n], k == 0, k == 7, hk + ['wg'], [('pb', bi)])
                    for k in range(8):
                        self.mm(self.pb[bf_][0:8, 0:n], wg[:, k, 8:16], self.hT[:, k, o:o + n], k == 0, k == 7, hk + ['wg'], [('pb', bf_)])
                    self.act(IG[:, o:o + n], self.pb[bi][0:8, 0:n], AF.Identity, [('pb', bi), 'bg'], ['IG'], bias=bg[:, 0:1])
                    self.act(LF[:, o:o + n], self.pb[bf_][0:8, 0:n], AF.Exp, [('pb', bf_), 'nbf'], ['LF'], bias=nbf[:], scale=-1.0)
                self.act(LF[:], LF[:], AF.Ln, ['LF'], ['LF'], bias=1.0)
                self.v('dve', 'tensor_scalar', ['LF'], ['LF'], out=LF[:], in0=LF[:], scalar1=-1.0, scalar2=None, op0=OP.mult)
                one_b = lambda n: self.onesf[0:8, 0:1].to_broadcast([8, n])
                self.v('dve', 'tensor_tensor_scan', ['LF', 'onesf'], ['FF'], out=FF[:], data0=one_b(TOK), data1=LF[:], initial=0.0,
                       op0=OP.mult, op1=OP.add)
                self.v('dve', 'tensor_tensor_scan', ['LF', 'onesf'], ['FB'], out=FB[:, 255::-1], data0=one_b(256), data1=LF[:, 255::-1],
                       initial=0.0, op0=OP.mult, op1=OP.add)
                self.v('dve', 'tensor_tensor_scan', ['LF', 'onesf', 'FB'], ['FB'], out=FB[:, 2303:255:-1], data0=one_b(2048),
                       data1=LF[:, 2303:255:-1], initial=FB[:, 0:1], op0=OP.mult, op1=OP.add)
                self.v('dve', 'tensor_tensor', ['IG', 'FF'], ['AFr'], out=AFr[:], in0=IG[:], in1=FF[:], op=OP.subtract)
                self.v('dve', 'tensor_tensor', ['IG', 'FB'], ['ABr'], out=ABr[:], in0=IG[:], in1=FB[:], op=OP.subtract)
                MF, MB = IG, LF
                self.v('dve', 'tensor_tensor_scan', ['AFr'], ['IG'], out=MF[:], data0=AFr[:], data1=AFr[:], initial=0.0, op0=OP.max, op1=OP.max)
                self.v('dve', 'tensor_tensor_scan', ['ABr'], ['LF'], out=MB[:, 255::-1], data0=ABr[:, 255::-1], data1=ABr[:, 255::-1],
                       initial=0.0, op0=OP.max, op1=OP.max)
                self.v('dve', 'tensor_tensor_scan', ['ABr', 'LF'], ['LF'], out=MB[:, 2303:255:-1], data0=ABr[:, 2303:255:-1],
                       data1=ABr[:, 2303:255:-1], initial=MB[:, 0:1], op0=OP.max, op1=OP.max)
                self.v('dve', 'tensor_tensor', ['FF', 'IG'], ['FF'], out=FF[:], in0=FF[:], in1=MF[:], op=OP.add)
                self.v('dve', 'tensor_tensor', ['FB', 'LF'], ['FB'], out=FB[:], in0=FB[:], in1=MB[:], op=OP.add)
                for qi, (XF, XB, kf, kb) in enumerate(((AFr, ABr, 'AFr', 'ABr'), (MF, MB, 'IG', 'LF'), (FF, FB, 'FF', 'FB'))):
                    for c in range(NT):
                        self.mm(self.pb[qi][:, c * 8:(c + 1) * 8], XF[:, c * 128:(c + 1) * 128], E2[:, 0, :], True, False, [kf, 'E2'], [('pb', qi)])
                        self.mm(self.pb[qi][:, c * 8:(c + 1) * 8], XB[:, c * 128:(c + 1) * 128], E2[:, 1, :], False, True, [kb, 'E2'], [('pb', qi)])
                for qi, (dst, kd) in enumerate(((acol, 'acol'), (Mcol, 'Mcol'), (mcol, 'mcol'))):
                    self.act(dst[:].rearrange("p c e -> p (c e)"), self.pb[qi][:, 0:NT * 8], AF.Identity, [('pb', qi)], [kd])
                self.v('dve', 'tensor_tensor', ['E2', 'IG'], ['R1'], out=R1[:], in0=E2[:, 0, :].unsqueeze(1).to_broadcast([8, NT, 8]),
                       in1=MF[:, 127::128].unsqueeze(2).to_broadcast([8, NT, 8]), op=OP.mult)
                self.v('dve', 'tensor_tensor', ['E2', 'LF'], ['R2'], out=R2[:], in0=E2[:, 1, :].unsqueeze(1).to_broadcast([8, NT, 8]),
                       in1=MB[:, 0::128].unsqueeze(2).to_broadcast([8, NT, 8]), op=OP.mult)
                self.v('dve', 'tensor_tensor', ['R1', 'R2'], ['R1'], out=R1[:], in0=R1[:], in1=R2[:], op=OP.add)
                self.mm(self.pb[3][:, 0:NT * 8], self.onesf[0:8, :], R1[:].rearrange("p c e -> p (c e)"), True, True, ['R1', 'onesf'], [('pb', 3)])
                self.act(MendB[:].rearrange("p c e -> p (c e)"), self.pb[3][:, 0:NT * 8], AF.Identity, [('pb', 3)], ['MendB'])
                self.v('dve', 'memset', [], ['MP'], MP[:], 0.0)
                self.v('dve', 'tensor_copy', ['MendB', 'MP'], ['MP'], out=MP[:, 1:NT, 0:4], in_=MendB[:, 0:NT - 1, 0:4])
                self.v('dve', 'tensor_copy', ['MendB', 'MP'], ['MP'], out=MP[:, 0:1, 4:8], in_=MendB[:, 1:2, 4:8])
                self.v('dve', 'tensor_copy', ['MendB', 'MP'], ['MP'], out=MP[:, 2:NT - 1, 4:8], in_=MendB[:, 3:NT, 4:8])
                self.v('dve', 'tensor_copy', ['MendB', 'MP'], ['MP'], out=MP[:, NT - 1:NT, 4:8], in_=MendB[:, 0:1, 4:8])
                self.v('dve', 'tensor_tensor', ['MP', 'Mcol'], ['ex'], out=ex[:, 0], in0=MP[:], in1=Mcol[:], op=OP.subtract)
                self.v('dve', 'tensor_tensor', ['acol', 'MendB'], ['ex'], out=ex[:, 1], in0=acol[:], in1=MendB[:], op=OP.subtract)
                self.v('dve', 'tensor_tensor', ['MP', 'MendB'], ['ex'], out=ex[:, 2], in0=MP[:], in1=MendB[:], op=OP.subtract)
                self.v('dve', 'tensor_scalar', ['mcol'], ['ex'], out=ex[:, 3], in0=mcol[:], scalar1=-1.0, scalar2=None, op0=OP.mult)
                self.act(T4[:].rearrange("p a c e -> p (a c e)"), ex[:].rearrange("p a c e -> p (a c e)"), AF.Exp, ['ex'], ['T4'])
            for hp in range(2):
                self.mlstm_pair(l, last, hp, acol, Mcol, T4, gm, negid, mlmask)

    def mlstm_pair(self, l, last, hp, acol, Mcol, T4, gm, negid, mlmask):
        nc, P, d = self.nc, self.P, self.d
        win = d['w_in'][l]
        QS = 96.0 ** -0.5
        colof = lambda T: T + 2 if T < 256 else T + 6
        cblocks = [(0, 256)] + [(256 + i * 512, 512) for i in range(4)]
        with Scope(self) as SP:
            post = SP.sb("post", [96, 4, TOK], BF16)
            vp = SP.sb("vp", [128, NT, 2, 97], BF16)
            sg = SP.sb("sg", [128, NT, 192], BF16)
            S = Scope(self)
            wvo = S.sb("wvo", [128, 8, 384], BF16)
            self.dma('pool', wvo[:, :, 0:192], win[:, 1024 + 192 * hp:1024 + 192 * (hp + 1)].rearrange("(k p) c -> p k c", p=128), [], ['wvo'], 'mw')
            self.dma('pool', wvo[:, :, 192:384], win[:, 1408 + 192 * hp:1408 + 192 * (hp + 1)].rearrange("(k p) c -> p k c", p=128), [], ['wvo'], 'mw')
            self.v('dve', 'memset', [], ['vp'], vp[:], 1.0)
            with S:
                wqk = S.sb("wqk", [128, 8, 4, 96], BF16)
                dgw = S.sb("dgw", [96, 4, 5, 96], BF16)
                pre = S.sb("pre", [96, 4, 2312], BF16)
                sgt = [S.sb("sgt%d" % i, [96, 512], F32) for i in range(1)]
                for jj in range(4):
                    c0 = (256 if jj < 2 else 640) + 96 * (2 * hp + jj % 2)
                    self.dma('pool', wqk[:, :, jj, :], win[:, c0:c0 + 96].rearrange("(k p) c -> p k c", p=128), [], ['wqk'], 'mw')
                    jg = (0 if jj < 2 else 4) + 2 * hp + jj % 2
                    self.dma('sp', dgw[:, jj, :, :], d['convdg'][l, :, jg, :, :], [], ['dgw'], 'mw')
                self.v('dve', 'memset', [], ['pre'], pre[:], 0.0)
                nb = 0
                for jj in range(4):
                    for (o, n) in cblocks:
                        b = nb % 6
                        nb += 1
                        hk = [('hT', t) for t in range(o // 128, (o + n) // 128)]
                        for k in range(8):
                            self.mm(self.pb[b][0:96, 0:n], wqk[:, k, jj, :], self.hT[:, k, o:o + n], k == 0, k == 7, hk + ['wqk'], [('pb', b)])
                        self.act(pre[:, jj, colof(o):colof(o) + n], self.pb[b][0:96, 0:n], AF.Identity, [('pb', b), 'pre'], [('pre', jj)])
                for t in range(NT):
                    b = nb % 6
                    nb += 1
                    for k in range(8):
                        self.mm(self.pb[b][:, 0:384], self.hT[:, k, t * 128:(t + 1) * 128], wvo[:, k, :], k == 0, k == 7, [('hT', t), 'wvo'], [('pb', b)])
                    self.act(vp[:, t, :, 0:96], self.pb[b][:, 0:192].rearrange("p (h c) -> p h c", c=96), AF.Identity, [('pb', b), 'vp'], [('vp', t)])
                    self.act(sg[:, t, :], self.pb[b][:, 192:384], AF.Sigmoid, [('pb', b)], [('sg', t)])
                for jj in range(4):
                    for (o, n) in cblocks:
                        b = nb % 6
                        nb += 1
                        c0 = colof(o)
                        for tap in range(5):
                            self.mm(self.pb[b][0:96, 0:n], dgw[:, jj, tap, :], pre[:, jj, c0 + tap - 2:c0 + tap - 2 + n], tap == 0, tap == 4,
                                    [('pre', jj), 'dgw'], [('pb', b)])
                        tk = [('post', jj, t) for t in range(o // 128, (o + n) // 128)]
                        if jj < 2:
                            si = 0
                            self.act(sgt[si][:, 0:n], self.pb[b][0:96, 0:n], AF.Sigmoid, [('pb', b)], [('sgt', si)])
                            self.v('dve', 'scalar_tensor_tensor', [('pb', b), ('sgt', si)], tk, out=post[:, jj, o:o + n], in0=self.pb[b][0:96, 0:n],
                                   scalar=QS, in1=sgt[si][:, 0:n], op0=OP.mult, op1=OP.mult)
                        else:
                            self.act(post[:, jj, o:o + n], self.pb[b][0:96, 0:n], AF.Silu, [('pb', b)], tk)
            ktok = SP.sb("ktok", [128, NT, 2, 96], BF16)
            for t in range(NT):
                pi = t % 2
                for hh in range(2):
                    self.tr(self.ptr[pi][:, hh, 0:96], post[:, 2 + hh, t * 128:(t + 1) * 128], self.identb[0:96, 0:96],
                            [('post', 2 + hh, t), 'identb'], [('ptr', pi)])
                self.act(ktok[:, t, :, :], self.ptr[pi][:, 0:2, 0:96], AF.Identity, [('ptr', pi)], [('ktok', t)])
            hF = SP.sb("hF", [128, NT, 2, 96], BF16)
            hB = SP.sb("hB", [128, NT, 2, 96], BF16)
            woM = SP.sb("woM", [96, 2, DM], BF16)
            with Scope(self) as S:
                dgM = [S.sb("dgM%d" % i, [128, 2, 128], F32) for i in range(2)]
                WT = [S.sb("WT%d" % i, [128, 2, 128], F32) for i in range(2)]
                PT = [S.sb("PT%d" % i, [128, 2, 128], BF16) for i in range(2)]
                tmpn = [S.sb("tmpn%d" % i, [128, 2, 97], F32) for i in range(2)]
                dn = [S.sb("dn%d" % i, [128, 2], F32) for i in range(2)]
                vw = [S.sb("vw%d" % i, [128, 2, 97], BF16) for i in range(2)]
                Cst = S.sb("Cst", [96, 2, 97], F32)
                Cball = S.sb("Cball", [96, NT, 2, 97], BF16)
                self.dma('pool', woM[:], d['w_out'][l][256 + 192 * hp:256 + 192 * (hp + 1), :].rearrange("(h p) c -> p h c", p=96), [], ['woM'], 'mw')
                for dr in range(2):
                    order = list(range(NT)) if dr == 0 else [1, 0] + list(range(NT - 1, 1, -1))
                    cf0 = dr * 4 + 2 * hp
                    self.v('dve', 'memset', ['Cst'], ['Cst'], Cst[:], 0.0)
                    self.v('dve', 'memset', [('Cball', 0)], [('Cball', 0)], Cball[:, 0], 0.0)

                    def kv_part(it):
                        c = order[it]
                        i2 = it % 2
                        self.v('dve', 'tensor_tensor', [('vp', c), 'T4'], [('vw', i2)], out=vw[i2][:], in0=vp[:, c],
                               in1=T4[:, 1, c, cf0:cf0 + 2].unsqueeze(2).to_broadcast([128, 2, 97]), op=OP.mult)
                        for hh in range(2):
                            self.mm(self.pb[4 + i2][0:96, hh * 97:(hh + 1) * 97], ktok[:, c, hh, :], vw[i2][:, hh, :], True, True,
                                    [('ktok', c), ('vw', i2)], [('pb', 4 + i2)])

                    kv_part(0)
                    for it in range(NT - 1):
                        c = order[it]
                        i2 = it % 2
                        if it + 1 < NT - 1:
                            kv_part(it + 1)
                        self.v('dve', 'tensor_tensor', ['Cst', 'T4'], ['Cst'], out=Cst[:], in0=Cst[:],
                               in1=T4[0:96, 2, c, cf0:cf0 + 2].unsqueeze(2).to_broadcast([96, 2, 97]), op=OP.mult)
                        self.v('dve', 'tensor_tensor', ['Cst', ('pb', 4 + i2)], ['Cst'], out=Cst[:], in0=Cst[:],
                               in1=self.pb[4 + i2][0:96, 0:194].rearrange("p (h e) -> p h e", e=97), op=OP.add)
                        self.act(Cball[:, it + 1], Cst[:], AF.Identity, ['Cst'], [('Cball', it + 1)])

                    def part_i(it):
                        c = order[it]
                        i2 = it % 2
                        if last and c < 2:
                            return
                        tsl = slice(c * 128, (c + 1) * 128)
                        bo = 3 + i2
                        self.v('dve', 'tensor_tensor', ['negid', 'Mcol'], [('dgM', i2)], out=dgM[i2][:],
                               in0=negid[:].unsqueeze(1).to_broadcast([128, 2, 128]),
                               in1=Mcol[:, c, cf0:cf0 + 2].unsqueeze(2).to_broadcast([128, 2, 128]), op=OP.mult)
                        self.mm(self.pb[1][:, 0:256], self.onesf[:], dgM[i2][:].rearrange("p h t -> p (h t)"), True, False,
                                [('dgM', i2), 'onesf'], [('pb', 1)])
                        self.mm(self.pb[1][:, 0:256], self.identb[:], mlmask[:, dr, :], False, True, ['mlmask', 'identb'], [('pb', 1)])
                        for hh in range(2):
                            self.act(WT[i2][:, hh, :], self.pb[1][:, hh * 128:(hh + 1) * 128], AF.Exp, [('pb', 1), 'acol'], [('WT', i2)],
                                     bias=acol[:, c, cf0 + hh:cf0 + hh + 1])
                        for hh in range(2):
                            self.mm(self.pb[2][:, hh * 128:(hh + 1) * 128], post[:, 2 + hh, tsl], post[:, hh, tsl], True, True,
                                    [('post', 2 + hh, c), ('post', hh, c)], [('pb', 2)])
                        self.v('dve', 'tensor_tensor', [('pb', 2), ('WT', i2)], [('PT', i2)], out=PT[i2][:].rearrange("p h t -> p (h t)"),
                               in0=self.pb[2][:, 0:256], in1=WT[i2][:].rearrange("p h t -> p (h t)"), op=OP.mult)
                        for hh in range(2):
                            self.mm(self.pb[bo][:, hh * 97:(hh + 1) * 97], PT[i2][:, hh, :], vp[:, c, hh, :], True, True,
                                    [('PT', i2), ('vp', c)], [('pb', bo)])
                        for hh in range(2):
                            self.mm(self.pb[bo][:, 256 + hh * 97:256 + (hh + 1) * 97], post[:, hh, tsl], Cball[:, it, hh, :], True, True,
                                    [('post', hh, c), ('Cball', it)], [('pb', bo)])

                    def part_d(it):
                        c = order[it]
                        i2 = it % 2
                        if last and c < 2:
                            return
                        bo = 3 + i2
                        for hh in range(2):
                            self.act(tmpn[i2][:, hh, :], self.pb[bo][:, 256 + hh * 97:256 + (hh + 1) * 97], AF.Identity, [('pb', bo), 'T4'],
                                     [('tmpn', i2)], scale=T4[:, 0, c, cf0 + hh:cf0 + hh + 1])
                        self.v('dve', 'tensor_tensor', [('tmpn', i2), ('pb', bo)], [('tmpn', i2)], out=tmpn[i2][:], in0=tmpn[i2][:],
                               in1=self.pb[bo][:, 0:194].rearrange("p (h e) -> p h e", e=97), op=OP.add)
                        self.v('dve', 'scalar_tensor_tensor', [('tmpn', i2)], [('dn', i2)], out=dn[i2][:], in0=tmpn[i2][:, :, 96], scalar=-1.0,
                               in1=tmpn[i2][:, :, 96], op0=OP.mult, op1=OP.max)
                        self.v('dve', 'tensor_tensor', [('dn', i2), 'T4'], [('dn', i2)], out=dn[i2][:], in0=dn[i2][:],
                               in1=T4[:, 3, c, cf0:cf0 + 2], op=OP.max)
                        self.v('dve', 'reciprocal', [('dn', i2)], [('dn', i2)], out=dn[i2][:], in_=dn[i2][:])
                        if dr == 0:
                            self.v('dve', 'tensor_tensor', [('tmpn', i2), ('dn', i2)], [('hF', c)], out=hF[:, c], in0=tmpn[i2][:, :, 0:96],
                                   in1=dn[i2][:].unsqueeze(2).to_broadcast([128, 2, 96]), op=OP.mult)
                            return
                        self.v('dve', 'tensor_tensor', [('tmpn', i2), ('dn', i2)], [('hB', c)], out=hB[:, c], in0=tmpn[i2][:, :, 0:96],
                               in1=dn[i2][:].unsqueeze(2).to_broadcast([128, 2, 96]), op=OP.mult)

                    part_i(0)
                    for it in range(NT):
                        if it + 1 < NT:
                            part_i(it + 1)
                        part_d(it)

            with Scope(self) as S:
                hsC = S.sb("hsC", [128, 6, 2, 96], F32)
                hqC = S.sb("hqC", [128, 6, 2, 96], F32)
                ssC = S.sb("ssC", [128, 12], F32)
                ybC = S.sb("ybC", [128, 6, 192], BF16)
                yhT = [S.sb("yhT%d" % i, [96, 2, 128], BF16) for i in range(2)]
                otiles = list(range(2, NT)) if last else list(range(NT))
                for g0 in range(0, len(otiles), 6):
                    g = otiles[g0:g0 + 6]
                    ng = len(g)
                    c0, c1 = g[0], g[-1] + 1
                    hk = [('hF', c) for c in g] + [('hB', c) for c in g]
                    self.v('dve', 'tensor_tensor', hk, ['hsC'], out=hsC[:, 0:ng], in0=hF[:, c0:c1], in1=hB[:, c0:c1], op=OP.add)
                    self.v('dve', 'tensor_tensor', ['hsC'], ['hqC'], out=hqC[:, 0:ng], in0=hsC[:, 0:ng], in1=hsC[:, 0:ng], op=OP.mult)
                    self.v('dve', 'tensor_reduce', ['hqC'], ['ssC'], out=ssC[:, 0:2 * ng],
                           in_=hqC[:, 0:ng].rearrange("p c h e -> p (c h) e"), axis=AX.X, op=OP.add)
                    self.act(ssC[:, 0:2 * ng], ssC[:, 0:2 * ng], AF.Ln, ['ssC', 'epsT'], ['ssC'], bias=self.epsT[:], scale=1.0 / 96)
                    self.act(ssC[:, 0:2 * ng], ssC[:, 0:2 * ng], AF.Exp, ['ssC'], ['ssC'], scale=-0.5)
                    self.v('dve', 'tensor_tensor', ['hsC', 'ssC'], ['hqC'], out=hqC[:, 0:ng].rearrange("p c h e -> p (c h) e"),
                           in0=hsC[:, 0:ng].rearrange("p c h e -> p (c h) e"),
                           in1=ssC[:, 0:2 * ng].unsqueeze(2).to_broadcast([128, 2 * ng, 96]), op=OP.mult)
                    self.v('dve', 'tensor_tensor', ['hqC', 'gm'], ['hqC'], out=hqC[:, 0:ng].rearrange("p c h e -> p c (h e)"),
                           in0=hqC[:, 0:ng].rearrange("p c h e -> p c (h e)"),
                           in1=gm[:, 192 * hp:192 * (hp + 1)].unsqueeze(1).to_broadcast([128, ng, 192]), op=OP.mult)
                    self.v('dve', 'tensor_tensor', ['hqC'] + [('sg', c) for c in g], ['ybC'], out=ybC[:, 0:ng],
                           in0=hqC[:, 0:ng].rearrange("p c h e -> p c (h e)"), in1=sg[:, c0:c1, :], op=OP.mult)
                    for ci, c in enumerate(g):
                        i2 = c % 2
                        for hh in range(2):
                            self.tr(self.ptr[i2][0:96, hh, :], ybC[:, ci, hh * 96:(hh + 1) * 96], self.identb[:], ['ybC', 'identb'], [('ptr', i2)])
                        self.act(yhT[i2][:], self.ptr[i2][0:96, 0:2, :], AF.Identity, [('ptr', i2)], [('yhT', i2)])
                        for nh in range(2):
                            b = 2 * i2 + nh
                            for hh in range(2):
                                self.mm(self.pb[b][:], yhT[i2][:, hh, :], woM[:, hh, nh * 512:(nh + 1) * 512], hh == 0, hh == 1,
                                        [('yhT', i2), 'woM'], [('pb', b)])
                            self.xupd(c, nh, b)

    def mla(self, l, last):
        nc, P, d = self.nc, self.P, self.d
        SCALE = 96.0 ** -0.5
        with Scope(self) as S:
            wA = S.sb("wA", [128, 8, 416], BF16)
            wuq = S.sb("wuq", [128, 2, 384], BF16)
            wuqs = S.sb("wuqs", [128, 2, 4, 96], BF16)
            wkr = S.sb("wkr", [128, 8, 96], BF16)
            wkrs = S.sb("wkrs", [128, 8, 96], BF16)
            wukv = S.sb("wukv", [128, 640], BF16)
            woA = S.sb("woA", [96, 4, DM], BF16)
            ropeC = S.sb("ropeC", [96, TOK], BF16)
            ropeS = S.sb("ropeS", [96, TOK], BF16)
            Vt = S.sb("Vt", [128, NT, 384], BF16)
            cqnT = S.sb("cqnT", [128, 2, 512], BF16)
            ckvnT = S.sb("ckvnT", [128, 512], BF16)
            cqn = [S.sb("cqn%d" % i, [128, 256], BF16) for i in range(2)]
            ckvn = [S.sb("ckvn%d" % i, [128, 128], BF16) for i in range(2)]
            rt1 = S.sb("rt1", [96, 512], F32)
            rt2 = S.sb("rt2", [96, 512], F32)
            pT = [S.sb("pT%d" % i, [128, 512], BF16) for i in range(3)]
            rc = S.sb("rc", [96, 512], F32)
            yaT = S.sb("yaT", [96, 4, 512], BF16)
            onesb = S.sb("onesb", [128, 96], BF16)
            st = [S.sb("st%d" % i, [128, 2], F32) for i in range(2)]
            rs = [S.sb("rs%d" % i, [128, 2], F32) for i in range(2)]
            gq = S.sb("gq", [128, 2, 2], F32)
            gkv = S.sb("gkv", [128, 2], F32)
            qT = self.hT[0:96, 0:4, :]
            kT = self.hT[0:96, 4:8, :]
            win = d['w_in'][l]
            self.dma('pool', wA[:], win[:, 1808:2224].rearrange("(k p) c -> p k c", p=128), [], ['wA'], 'aw')
            self.dma('pool', wuq[:], d['w_uq'][l].rearrange("(k p) c -> p k c", p=128), [], ['wuq'], 'aw')
            self.v('dve', 'memset', [], ['wuqs'], wuqs[:], 0.0)
            self.v('dve', 'memset', [], ['wkr'], wkr[:], 0.0)
            self.v('dve', 'memset', [], ['wkrs'], wkrs[:], 0.0)
            self.v('dve', 'memset', [], ['onesb'], onesb[:], 1.0)
            for c in range(2):
                self.dma('pool', wuqs[:, c, :, 64:96], d['w_uq_sw'][l][c * 128:(c + 1) * 128, :, :], ['wuqs'], ['wuqs'], 'aw')
            self.dma('pool', wkr[:, :, 64:96], win[:, 2192:2224].rearrange("(k p) c -> p k c", p=128), ['wkr'], ['wkr'], 'aw')
            self.dma('pool', wkrs[:, :, 64:96], d['w_kr_sw'][l].rearrange("(k p) c -> p k c", p=128), ['wkrs'], ['wkrs'], 'aw')
            self.dma('pool', wukv[:], d['w_ukv'][l], [], ['wukv'], 'aw')
            self.dma('pool', woA[:], d['w_out'][l][640:1024, :].rearrange("(h p) c -> p h c", p=96), [], ['woA'], 'aw')
            self.dma('sp', ropeC[64:96, :], d['ropeC'][:, :], [], ['rope'], 'aw')
            self.dma('sp', ropeS[64:96, :], d['ropeS'][:, :], [], ['rope'], 'aw')
            self.dma('sp', gq[:], d['gqfm'][:, :, :], [], ['gq'], 'aw')
            self.dma('sp', gkv[:], d['gkvfm'][:, :], [], ['gq'], 'aw')
            wv = wukv[:, :].rearrange("p (h c) -> p h c", c=160)[:, :, 64:160]
            for (o, n) in self.blocks():
                tl = list(range(o // 128, (o + n) // 128))
                hk = [('hT', t) for t in tl]
                def a1_front(t):
                    b1 = t % 2
                    for k in range(8):
                        self.mm(self.pb[b1][:, 0:416], self.hT[:, k, t * 128:(t + 1) * 128], wA[:, k, :], k == 0, k == 7,
                                [('hT', t), 'wA'], [('pb', b1)])
                a1_front(tl[0])
                for ti, t in enumerate(tl):
                    b1 = t % 2
                    if ti + 1 < len(tl):
                        a1_front(tl[ti + 1])
                    si = t % 2
                    self.v('dve', 'memset', [], [('st', si)], st[si][:], 0.0)
                    self.cnt += 1
                    self.act(self.junk[:, 0:256], self.pb[b1][:, 0:256], AF.Square, [('pb', b1)], [('junk', self.cnt), ('st', si)],
                             accum=st[si][:, 0:1])
                    self.cnt += 1
                    self.act(self.junk[:, 256:384], self.pb[b1][:, 256:384], AF.Square, [('pb', b1)], [('junk', self.cnt), ('st', si)],
                             accum=st[si][:, 1:2])
                    self.act(rs[si][:, 0:1], st[si][:, 0:1], AF.Sqrt, [('st', si), 'epsT'], [('rs', si)], bias=self.epsT[:], scale=1.0 / 256)
                    self.act(rs[si][:, 1:2], st[si][:, 1:2], AF.Sqrt, [('st', si), 'epsT'], [('rs', si)], bias=self.epsT[:], scale=1.0 / 128)
                    self.v('dve', 'reciprocal', [('rs', si)], [('rs', si)], out=rs[si][:], in_=rs[si][:])
                    self.v('dve', 'tensor_scalar', [('pb', b1), ('rs', si)], [('cqn', si)], out=cqn[si][:], in0=self.pb[b1][:, 0:256],
                           scalar1=rs[si][:, 0:1], scalar2=None, op0=OP.mult)
                    self.v('dve', 'tensor_scalar', [('pb', b1), ('rs', si)], [('ckvn', si)], out=ckvn[si][:], in0=self.pb[b1][:, 256:384],
                           scalar1=rs[si][:, 1:2], scalar2=None, op0=OP.mult)
                    pi = t % 2
                    for c in range(2):
                        self.tr(self.ptr[pi][:, c, :], cqn[si][:, c * 128:(c + 1) * 128], self.identb[:], [('cqn', si), 'identb'], [('ptr', pi)])
                    self.tr(self.ptr[pi][:, 2, :], ckvn[si][:], self.identb[:], [('ckvn', si), 'identb'], [('ptr', pi)])
                    for c in range(2):
                        self.act(cqnT[:, c, ti * 128:(ti + 1) * 128], self.ptr[pi][:, c, :], AF.Identity, [('ptr', pi), 'gq'], ['cqnT'],
                                 scale=gq[:, l, c:c + 1])
                    self.act(ckvnT[:, ti * 128:(ti + 1) * 128], self.ptr[pi][:, 2, :], AF.Identity, [('ptr', pi), 'gq'], ['ckvnT'],
                             scale=gkv[:, l:l + 1])
                    bv = 2 + t % 2
                    self.mm(self.pb[bv][:, 0:384], ckvnT[:, ti * 128:(ti + 1) * 128], wv, True, True, ['ckvnT', 'wukv'], [('pb', bv)])
                    self.act(Vt[:, t, :], self.pb[bv][:, 0:384], AF.Identity, [('pb', bv)], [('Vt', t)])
                for k in range(8):
                    self.mm(self.pb[4][0:96, 0:n], wkr[:, k, :], self.hT[:, k, o:o + n], k == 0, k == 7, hk + ['wkr'], [('pb', 4)])
                for k in range(8):
                    self.mm(self.pb[5][0:96, 0:n], wkrs[:, k, :], self.hT[:, k, o:o + n], k == 0, k == 7, hk + ['wkrs'], [('pb', 5)])
                for h in range(4):
                    for c in range(2):
                        self.mm(self.pb[0][0:96, 0:n], wuq[:, c, h * 96:(h + 1) * 96], cqnT[:, c, 0:n], c == 0, c == 1, ['wuq', 'cqnT'], [('pb', 0)])
                    for c in range(2):
                        self.mm(self.pb[1][0:96, 0:n], wuqs[:, c, h, :], cqnT[:, c, 0:n], c == 0, c == 1, ['wuqs', 'cqnT'], [('pb', 1)])
                    bk = 2 + h % 2
                    self.mm(self.pb[bk][0:64, 0:n], wukv[:, h * 160:h * 160 + 64], ckvnT[:, 0:n], True, True, ['wukv', 'ckvnT'], [('pb', bk)])
                    self.act(qT[0:64, h, o:o + n], self.pb[0][0:64, 0:n], AF.Identity, [('pb', 0)], hk)
                    self.act(kT[0:64, h, o:o + n], self.pb[bk][0:64, 0:n], AF.Identity, [('pb', bk)], hk)
                    self.v('dve', 'tensor_tensor', [('pb', 0), 'rope'], ['rt1'], out=rt1[64:96, 0:n], in0=self.pb[0][64:96, 0:n],
                           in1=ropeC[64:96, o:o + n], op=OP.mult)
                    self.v('dve', 'tensor_tensor', [('pb', 1), 'rope'], ['rt2'], out=rt2[64:96, 0:n], in0=self.pb[1][64:96, 0:n],
                           in1=ropeS[64:96, o:o + n], op=OP.mult)
                    self.v('pool', 'tensor_tensor', ['rt1', 'rt2'], hk, out=qT[64:96, h, o:o + n], in0=rt1[64:96, 0:n],
                           in1=rt2[64:96, 0:n], op=OP.add)
                self.v('dve', 'tensor_tensor', [('pb', 4), 'rope'], ['rt1'], out=rt1[64:96, 0:n], in0=self.pb[4][64:96, 0:n],
                       in1=ropeC[64:96, o:o + n], op=OP.mult)
                self.v('dve', 'tensor_tensor', [('pb', 5), 'rope'], ['rt2'], out=rt2[64:96, 0:n], in0=self.pb[5][64:96, 0:n],
                       in1=ropeS[64:96, o:o + n], op=OP.mult)
                for h in range(4):
                    self.v('pool', 'tensor_tensor', ['rt1', 'rt2'], hk, out=kT[64:96, h, o:o + n], in0=rt1[64:96, 0:n],
                           in1=rt2[64:96, 0:n], op=OP.add)
            qblocks = [(256 + i * 512, 512, list(range(NT))) for i in range(4)]
            if not last:
                qblocks.append((0, 256, [0, 1]))
            for (qo, qn, ktiles) in qblocks:
                qk = [('hT', t) for t in range(qo // 128, (qo + qn) // 128)]
                for h in range(4):
                    def st_mm(i):
                        kt_ = ktiles[i]
                        self.mm(self.pb[i % 2][:, 0:qn], kT[:, h, kt_ * 128:(kt_ + 1) * 128], qT[:, h, qo:qo + qn], True, True,
                                qk + [('hT', kt_)], [('pb', i % 2)])
                    st_mm(0)
                    for i, kt in enumerate(ktiles):
                        sb_ = i % 2
                        pi = i % 3
                        if i + 1 < len(ktiles):
                            st_mm(i + 1)
                        self.act(pT[pi][:, 0:qn], self.pb[sb_][:, 0:qn], AF.Exp, [('pb', sb_)], [('pT', pi)], scale=SCALE)
                        self.mm(self.pb[2][0:96, 0:qn], Vt[:, kt, h * 96:(h + 1) * 96], pT[pi][:, 0:qn], i == 0, i == len(ktiles) - 1,
                                [('Vt', kt), ('pT', pi)], [('pb', 2)])
                        self.mm(self.pb[3][0:96, 0:qn], onesb[:, :], pT[pi][:, 0:qn], i == 0, i == len(ktiles) - 1,
                                ['onesb', ('pT', pi)], [('pb', 3)])
                    self.v('dve', 'reciprocal', [('pb', 3)], ['rc'], out=rc[:, 0:qn], in_=self.pb[3][0:96, 0:qn])
                    self.v('dve', 'tensor_tensor', [('pb', 2), 'rc'], [('yaT', h)], out=yaT[:, h, 0:qn], in0=self.pb[2][0:96, 0:qn],
                           in1=rc[:, 0:qn], op=OP.mult)
                for ti, t in enumerate(range(qo // 128, (qo + qn) // 128)):
                    for nh in range(2):
                        b = 4 + nh
                        for h in range(4):
                            self.mm(self.pb[b][:], yaT[:, h, ti * 128:(ti + 1) * 128], woA[:, h, nh * 512:(nh + 1) * 512], h == 0, h == 3,
                                    [('yaT', h), 'woA'], [('pb', b)])
                        self.xupd(t, nh, b)

    def fourier(self, l, last):
        nc, P, d = self.nc, self.P, self.d
        with Scope(self) as S:
            wpf = S.sb("wpf", [128, 8, 256], BF16)
            pfT = S.sb("pfT", [128, 2, TOK], BF16)
            bd = S.sb("bd", [128, 256], BF16)
            pfCS = S.sb("pfCS", [128, NT, 2, 256], BF16)
            tw = [S.sb("tw%d" % i, [128, 2, 1024], BF16) for i in range(3)]
            yfT = S.sb("yfT", [128, 2, TOK], BF16)
            wo = S.sb("wo", [128, 2, DM], BF16)
            c256 = S.sb("c256", [128, 2, 2, 256], BF16)
            self.dma('pool', wpf[:], d['w_in'][l][:, 0:256].rearrange("(k p) c -> p k c", p=128), [], ['wpf'], 'fw')
            self.dma('pool', wo[:], d['w_out'][l][0:256, :].rearrange("(k p) c -> p k c", p=128), [], ['wo'], 'fw')
            self.dma('sp', bd[:], d['bdcs'][:, :], [], ['bd'], 'fw')
            self.dma('sp', c256[:], d['c256'][:, :, :, :], [], ['c256'], 'fw')
            nb = 0
            for j in range(2):
                for (o, n) in self.blocks():
                    b = nb % 6
                    nb += 1
                    for k in range(8):
                        self.mm(self.pb[b][:, 0:n], wpf[:, k, j * 128:(j + 1) * 128], self.hT[:, k, o:o + n], k == 0, k == 7,
                                ['wpf'] + [('hT', t) for t in range(o // 128, (o + n) // 128)], [('pb', b)])
                    self.act(pfT[:, j, o:o + n], self.pb[b][:, 0:n], AF.Identity, [('pb', b)], [('pfT', j, o)])
            for t in range(NT):
                for j in range(2):
                    b = nb % 6
                    nb += 1
                    self.mm(self.pb[b][:, 0:256], pfT[:, j, t * 128:(t + 1) * 128], bd[:], True, True,
                            [('pfT', j, (t // 4) * 512), 'bd'], [('pb', b)])
                    self.act(pfCS[:, t, j, :], self.pb[b][:, 0:256], AF.Identity, [('pb', b)], [('pfCS', t)])
            ntw = 0
            for half in range(2):
                for nt in range(16):
                    t = nt + 2
                    s = ntw % 3
                    ntw += 1
                    self.dma('sp', tw[s][:], d['twCS'][half, nt], [], [('tw', s)], 'tw%d' % s)
                    for j in range(2):
                        for q in range(2):
                            b = j * 2 + q
                            self.mm(self.pb[b][:], pfCS[:, t, j, 0:128], tw[s][:, 0, q * 512:(q + 1) * 512], nt == 0, False,
                                    [('pfCS', t), ('tw', s)], [('pb', b)])
                            self.mm(self.pb[b][:], pfCS[:, t, j, 128:256], tw[s][:, 1, q * 512:(q + 1) * 512], False, nt == 15,
                                    [('pfCS', t), ('tw', s)], [('pb', b)])
                for j in range(2):
                    for q in range(2):
                        b = j * 2 + q
                        o = 256 + half * 1024 + q * 512
                        self.act(yfT[:, j, o:o + 512], self.pb[b][:], AF.Identity, [('pb', b)], [('yfT', o // 128)])
            if not last:
                for j in range(2):
                    b = 4 + j
                    for nt in range(2):
                        self.mm(self.pb[b][:, 0:256], pfCS[:, nt, j, 0:128], c256[:, nt, 0, :], nt == 0, False,
                                [('pfCS', nt), 'c256'], [('pb', b)])
                        self.mm(self.pb[b][:, 0:256], pfCS[:, nt, j, 128:256], c256[:, nt, 1, :], False, nt == 1,
                                [('pfCS', nt), 'c256'], [('pb', b)])
                    self.act(yfT[:, j, 0:256], self.pb[b][:, 0:256], AF.Identity, [('pb', b)], [('yfT', 0)])
            nb = 0
            for t in (range(2, NT) if last else range(NT)):
                for nh in range(2):
                    b = nb % 6
                    nb += 1
                    for j in range(2):
                        self.mm(self.pb[b][:], yfT[:, j, t * 128:(t + 1) * 128], wo[:, j, nh * 512:(nh + 1) * 512], j == 0, j == 1,
                                [('yfT', 0 if t < 2 else 2 + ((t - 2) // 4) * 4), 'wo'], [('pb', b)])
                    self.xupd(t, nh, b)

    def mlp(self, l, tiles):
        nc, P, d = self.nc, self.P, self.d
        groups = [tiles[i:i + 6] for i in range(0, len(tiles), 6)]
        with Scope(self) as S:
            uT = S.sb("uT", [128, 32, 768], BF16)
            wu = [S.sb("wu%d" % i, [128, 8, 512], BF16) for i in range(2)]
            wd = [S.sb("wd%d" % i, [128, 4, 512], BF16) for i in range(2)]
            tmp = [S.sb("mt%d" % i, [128, 512], F32) for i in range(2)]
            nu = nd = nb = 0
            for g in groups:
                G = 128 * len(g)
                t0 = g[0] * 128
                subs = [(0, G)] if G <= 512 else [(0, G // 2), (G // 2, G // 2)]
                for hb in range(8):
                    s = nu % 2
                    nu += 1
                    self.dma('pool', wu[s][:], d['w_up_r'][l, hb], [], [('wu', s)], 'wu%d' % s)
                    for cc in range(4):
                        j = hb * 4 + cc
                        for (o, n) in subs:
                            b = nb % 6
                            nb += 1
                            for k in range(8):
                                self.mm(self.pb[b][:, 0:n], wu[s][:, k, cc * 128:(cc + 1) * 128],
                                        self.hT[:, k, t0 + o:t0 + o + n], k == 0, k == 7,
                                        [('wu', s)] + [('hT', t) for t in g], [('pb', b)])
                            ri = nb % 2
                            self.act(tmp[ri][:, 0:n], self.pb[b][:, 0:n], AF.Relu, [('pb', b)], [('mt', ri)])
                            self.v('dve', 'tensor_tensor', [('mt', ri)], [('uT', j)], out=uT[:, j, o:o + n],
                                   in0=tmp[ri][:, 0:n], in1=tmp[ri][:, 0:n], op=OP.mult)
                for nh in range(2):
                    for jb in range(8):
                        s = nd % 2
                        nd += 1
                        self.dma('pool', wd[s][:], d['w_down_r'][l, nh, jb], [], [('wd', s)], 'wd%d' % s)
                        for jj in range(4):
                            j = jb * 4 + jj
                            for ti, t in enumerate(g):
                                self.mm(self.pb[ti][:], uT[:, j, ti * 128:(ti + 1) * 128], wd[s][:, jj, :], j == 0, j == 31,
                                        [('wd', s), ('uT', j)], [('pb', ti)])
                    for ti, t in enumerate(g):
                        self.xupd(t, nh, ti)

    def final(self):
        nc, P, d = self.nc, self.P, self.d
        with Scope(self) as S:
            gf = S.sb("gf", [128, DM], F32)
            ob = [S.sb("ob%d" % i, [128, DM], F32) for i in range(2)]
            self.dma('sp', gf[:], d['gfin'][:, :], [], ['gf'], 'cst')
            self.rms_all(range(2, NT))
            for t in range(2, NT):
                i = t % 2
                self.act(ob[i][:], self.xs[:, t, :], AF.Identity, [('xs', t), 'rstd'], [('ob', i)], scale=self.rstd[:, t:t + 1])
                self.v('dve', 'tensor_tensor', [('ob', i), 'gf'], [('ob', i)], out=ob[i][:], in0=ob[i][:], in1=gf[:], op=OP.mult)
                self.dma('sp', d['out'][(t - 2) * 128:(t - 1) * 128, :], ob[i][:], [('ob', i)], [('out', t)], 'out')


def dram_decls(nc):
    d = {}

    def inp(name, shape, dt=F32):
        d[name] = nc.dram_tensor(name, list(shape), dt, kind="ExternalInput").ap()

    inp('xin', [TOK, DM])
    inp('c2', [128, 16])
    inp('w_mod', [2, 1024, 6144])
    inp('bmodfm', [128, 2, 48])
    inp('gnfm', [128, 2, 2, 8])
    inp('w_in', [2, 1024, 2224])
    inp('w_out', [2, 1024, 1024])
    inp('w_up_r', [2, 8, 128, 8, 512])
    inp('w_down_r', [2, 2, 8, 128, 4, 512])
    inp('gfin', [128, DM])
    inp('identb', [128, 128], BF16)
    inp('identf', [128, 128])
    inp('w_uq', [2, 256, 384])
    inp('w_ukv', [2, 128, 640])
    inp('w_uq_sw', [2, 256, 4, 32])
    inp('w_kr_sw', [2, 1024, 32])
    inp('ropeC', [32, TOK], BF16)
    inp('ropeS', [32, TOK], BF16)
    inp('gqfm', [128, 2, 2])
    inp('gkvfm', [128, 2])
    inp('w_g', [2, 1024, 16])
    inp('bg', [2, 8, 2])
    inp('convdg', [2, 96, 8, 5, 96], BF16)
    inp('gmB', [128, 2, 384])
    inp('E2', [8, 2, 8])
    inp('negidf', [128, 128])
    inp('mlmask', [128, 2, 256], BF16)
    inp('bdcs', [128, 256], BF16)
    inp('twCS', [2, 16, 128, 2, 1024], BF16)
    inp('c256', [128, 2, 2, 256], BF16)
    d['out'] = nc.dram_tensor('out', [2048, DM], F32, kind="ExternalOutput").ap()
    if DBGXS:
        d['dbgxs'] = nc.dram_tensor('dbgxs', [TOK, DM], F32, kind="ExternalOutput").ap()
    return d


def build_program():
    nc = bass.Bass("TRN2", target_bir_lowering=False)
    d = dram_decls(nc)
    P = Prog(nc)
    kb = K(nc, P, d)
    kb.build()
    P.plan()
    es = contextlib.ExitStack()
    with es:
        P.start_real(es)
        kb.build()
    return nc, P


def host_inputs(inputs, b):
    f = lambda a: np.ascontiguousarray(np.asarray(a, dtype=np.float32))
    x, c, ctx, c_ctx = inputs['x'], inputs['c'], inputs['ctx'], inputs['c_ctx']
    m = {}
    m['xin'] = f(np.concatenate([ctx[b], x[b]], axis=0))
    c2 = np.stack([np.asarray(c[b]).reshape(8, 128).T, np.asarray(c_ctx).reshape(8, 128).T], axis=-1)
    m['c2'] = f(c2.reshape(128, 16))
    m['w_mod'] = f(inputs['w_mod'])
    m['bmodfm'] = f(np.asarray(inputs['b_mod']).reshape(2, 48, 128).transpose(2, 0, 1))
    gn = np.stack([np.asarray(inputs['g_norm1']), np.asarray(inputs['g_norm2'])], axis=1)
    m['gnfm'] = f(gn.reshape(2, 2, 8, 128).transpose(3, 0, 1, 2))
    m['w_in'] = f(inputs['w_in'])
    m['w_out'] = f(inputs['w_out'])
    m['w_up_r'] = f(np.asarray(inputs['w_up']).reshape(2, 8, 128, 8, 512).transpose(0, 3, 2, 1, 4))
    m['w_down_r'] = f(np.asarray(inputs['w_down']).reshape(2, 8, 4, 128, 2, 512).transpose(0, 4, 1, 3, 2, 5))
    m['gfin'] = f(np.tile(np.asarray(inputs['g_final'])[None, :], (128, 1)))
    m['identb'] = np.eye(128, dtype=np.float32).astype(ml_dtypes.bfloat16)
    m['identf'] = np.eye(128, dtype=np.float32)
    m.update(host_consts())
    perm = np.array([a * 16 + (1 - b_) * 8 + j for a in range(2) for b_ in range(2) for j in range(8)])
    wuq = np.asarray(inputs['w_uq'])
    m['w_uq'] = f(wuq)
    m['w_ukv'] = f(inputs['w_ukv'])
    m['w_uq_sw'] = f(wuq.reshape(2, 256, 4, 96)[:, :, :, 64:96][..., perm])
    m['w_kr_sw'] = f(np.asarray(inputs['w_in'])[:, :, 2192:2224][..., perm])
    m['gqfm'] = f(np.asarray(inputs['g_q_norm']).reshape(2, 2, 128).transpose(2, 0, 1))
    m['gkvfm'] = f(np.asarray(inputs['g_kv_norm']).T)
    gp = [0, 1, 2, 3, 8, 9, 10, 11, 4, 5, 6, 7, 12, 13, 14, 15]
    m['w_g'] = f(np.asarray(inputs['w_in'])[:, :, 1792:1808][..., gp])
    m['bg'] = f(np.asarray(inputs['b_gates'])[:, gp].reshape(2, 2, 8).transpose(0, 2, 1))
    cv = np.asarray(inputs['conv_qk'], dtype=np.float32).reshape(2, 5, 8, 96)
    dgc = np.zeros((2, 96, 8, 5, 96), np.float32)
    pi_ = np.arange(96)
    dgc[:, pi_, :, :, pi_] = cv.transpose(3, 0, 2, 1)
    m['convdg'] = dgc.astype(ml_dtypes.bfloat16)
    m['gmB'] = f(np.tile(np.asarray(inputs['g_mlstm'])[None, :, :], (128, 1, 1)))
    return m


def host_consts():
    bf = lambda a: np.ascontiguousarray(a.astype(np.float32)).astype(ml_dtypes.bfloat16)
    m = {}
    i64 = np.arange(64)
    a64 = 2 * np.pi * np.outer(i64, i64) / 64
    C64, S64 = np.cos(a64) / 8.0, np.sin(a64) / 8.0
    z = np.zeros((64, 64))
    m['bdcs'] = bf(np.block([[C64, z, S64, z], [z, C64, z, S64]]))
    n = np.arange(2048, dtype=np.float64)
    aN = 2 * np.pi * (np.outer(n, n) % 2048) / 2048
    tcs = np.stack([np.cos(aN) / np.sqrt(2048.0), -np.sin(aN) / np.sqrt(2048.0)], axis=1)
    m['twCS'] = bf(tcs.reshape(16, 128, 2, 2, 1024).transpose(3, 0, 1, 2, 4))
    E2 = np.zeros((8, 2, 8), np.float32)
    for k_ in range(4):
        E2[k_, 0, k_] = 1.0
        E2[4 + k_, 1, 4 + k_] = 1.0
    m['E2'] = E2
    m['negidf'] = -np.eye(128, dtype=np.float32)
    sidx, tidx = np.arange(128)[:, None], np.arange(128)[None, :]
    mf = np.where(sidx <= tidx, 0.0, -30000.0)
    mb_ = np.where(sidx >= tidx, 0.0, -30000.0)
    m['mlmask'] = bf(np.stack([np.concatenate([mf, mf], axis=1), np.concatenate([mb_, mb_], axis=1)], axis=1))
    tt = np.arange(2048)
    freqs = 10000.0 ** (-np.arange(8, dtype=np.float64) / 8)
    ang = np.stack([np.outer(tt // 64, freqs), np.outer(tt % 64, freqs)], axis=1)
    cosT = np.ones((TOK, 2, 2, 8)); sinT = np.zeros((TOK, 2, 2, 8))
    cosT[256:] = np.cos(ang)[:, :, None, :]
    sinT[256:, :, 0, :] = -np.sin(ang)
    sinT[256:, :, 1, :] = np.sin(ang)
    m['ropeC'] = bf(cosT.reshape(TOK, 32).T)
    m['ropeS'] = bf(sinT.reshape(TOK, 32).T)
    n = np.arange(256, dtype=np.float64)
    a2 = 2 * np.pi * (np.outer(n, n) % 256) / 256
    c = np.stack([np.cos(a2) / 16.0, -np.sin(a2) / 16.0], axis=1)
    m['c256'] = bf(c.reshape(2, 128, 2, 256).transpose(1, 0, 2, 3))
    return m


_CACHE = {}


def kernel(**inputs):
    if 'nc' not in _CACHE:
        _CACHE['nc'] = build_program()
    nc, P = _CACHE['nc']
    shared = None
    in_maps = []
    for b in range(8):
        m = host_inputs(inputs, b) if shared is None else None
        if shared is None:
            shared = m
        else:
            m = dict(shared)
            f = lambda a: np.ascontiguousarray(np.asarray(a, dtype=np.float32))
            m['xin'] = f(np.concatenate([inputs['ctx'][b], inputs['x'][b]], axis=0))
            c2 = np.stack([np.asarray(inputs['c'][b]).reshape(8, 128).T, np.asarray(inputs['c_ctx']).reshape(8, 128).T], axis=-1)
            m['c2'] = f(c2.reshape(128, 16))
        in_maps.append(m)
    res = run_bass_kernel_spmd(nc, in_maps, core_ids=list(range(8)))
    _CACHE['res'] = res
    return np.stack([np.asarray(r['out'], dtype=np.float32) for r in res.results], axis=0)
```

```python
import contextlib, bisect, math
import numpy as np
import ml_dtypes
import concourse.bass as bass
import concourse.mybir as mybir
from concourse.bass_utils import run_bass_kernel_spmd

F32 = mybir.dt.float32
BF16 = mybir.dt.bfloat16
AF = mybir.ActivationFunctionType
OP = mybir.AluOpType
AX = mybir.AxisListType

NT = 18
TOK = 2304
DM = 1024
EPS = 1e-6
DEPTH = 2
STAGE = 99
DEBUG = []
BRANCHES = 'FAM'
DBGXS = False


class Dummy:
    def __getitem__(self, k):
        return self

    def __getattr__(self, n):
        return lambda *a, **k: self


class Prog:
    ENGS = ('pe', 'dve', 'act', 'pool', 'sp')

    def __init__(self, nc):
        self.nc = nc
        self.dry = True
        self.rec = []
        self.i = 0

    def op(self, eng, fn, r=(), w=(), dma=None):
        if self.dry:
            self.rec.append(('op', eng, tuple(r), tuple(w), dma))
        else:
            i = self.i
            E = self.H[eng]
            for (s, v) in self.waits[i]:
                E.wait_ge(self.sems[s], v)
            if fn is not None:
                ins = fn()
                inc = self.incs[i]
                if inc is not None:
                    ins.then_inc(self.sems[inc[0]], inc[1])
        self.i += 1

    def barrier(self):
        if self.dry:
            self.rec.append(('bar',))
        self.i += 1

    def plan(self):
        rec = self.rec
        n = len(rec)
        lastw, readers, last_on_eng, dma_ops = {}, {}, {}, {}
        deps = [None] * n
        pend = {e: None for e in self.ENGS}
        for i, r in enumerate(rec):
            if r[0] == 'bar':
                bd = set(last_on_eng.values())
                for s, l in dma_ops.items():
                    if l:
                        bd.add(l[-1])
                for e in self.ENGS:
                    pend[e] = set(bd) | (pend[e] or set())
                lastw.clear()
                readers.clear()
                continue
            _, eng, R, W, dma = r
            d = set()
            for k in R:
                if k in lastw:
                    d.add(lastw[k])
            for k in W:
                if k in lastw:
                    d.add(lastw[k])
                d.update(readers.get(k, ()))
            if pend[eng]:
                d |= pend[eng]
                pend[eng] = None
            d.discard(i)
            if eng == 'pe':
                d = {j for j in d if not (rec[j][1] == 'pe' and rec[j][4] is None)}
            deps[i] = d
            for k in W:
                lastw[k] = i
                readers[k] = []
            for k in R:
                lst = readers.setdefault(k, [])
                if dma is None:
                    lst[:] = [j for j in lst if not (rec[j][1] == eng and rec[j][4] is None)]
                lst.append(i)
            last_on_eng[eng] = i
            if dma:
                dma_ops.setdefault(dma, []).append(i)
        sig = [False] * n
        for i in range(n):
            if deps[i]:
                for j in deps[i]:
                    if rec[j][4] is None:
                        sig[j] = True
        cnt = {e: 0 for e in self.ENGS}
        ev = [None] * n
        for i, r in enumerate(rec):
            if r[0] != 'op':
                continue
            if r[4] is None and sig[i]:
                cnt[r[1]] += 1
                ev[i] = ('E_' + r[1], cnt[r[1]])
        self.waits = [()] * n
        self.incs = [None] * n
        seen = {e: {} for e in self.ENGS}
        for i, r in enumerate(rec):
            if r[0] != 'op':
                continue
            eng = r[1]
            wl = {}
            for j in deps[i]:
                if rec[j][4] is None:
                    s, v = ev[j]
                else:
                    s = 'D_' + rec[j][4]
                    v = 16 * bisect.bisect_left(dma_ops[rec[j][4]], i)
                if v > wl.get(s, 0):
                    wl[s] = v
            ws = []
            for s, v in wl.items():
                if v > seen[eng].get(s, 0):
                    seen[eng][s] = v
                    ws.append((s, v))
            self.waits[i] = tuple(ws)
            if r[4] is not None:
                self.incs[i] = ('D_' + r[4], 16)
            elif sig[i]:
                self.incs[i] = (ev[i][0], 1)
        self.semnames = ['E_' + e for e in self.ENGS] + ['D_' + s for s in dma_ops]
        self.stats = dict(n=n, cnt=cnt, nw=sum(len(w) for w in self.waits))

    def start_real(self, es):
        nc = self.nc
        self.dry = False
        self.i = 0
        self.H = {'pe': nc.tensor, 'dve': nc.vector, 'act': nc.scalar, 'pool': nc.gpsimd, 'sp': nc.sync}
        self.sems = {s: es.enter_context(nc.semaphore(s)) for s in self.semnames}


class Scope:
    _n = 0

    def __init__(self, K):
        self.K = K
        self.es = contextlib.ExitStack()

    def __enter__(self):
        return self

    def sb(self, name, shape, dt):
        if self.K.P.dry:
            return Dummy()
        Scope._n += 1
        return self.es.enter_context(self.K.nc.sbuf_tensor("%s_%d" % (name, Scope._n), list(shape), dt))

    def __exit__(self, *a):
        self.K.P.barrier()
        self.K.semmap = {}
        self.es.close()
        return False


def which(t):
    return 1 if t < 2 else 0


class K:
    def __init__(self, nc, P, dram):
        self.nc, self.P, self.d = nc, P, dram
        self.semmap = {}

    def mm(self, out, lhsT, rhs, start, stop, r, w):
        nc = self.nc
        self.P.op('pe', lambda: nc.tensor.matmul(out, lhsT=lhsT, rhs=rhs, start=start, stop=stop), r, w)

    def tr(self, out, in_, ident, r, w):
        nc = self.nc
        self.P.op('pe', lambda: nc.tensor.transpose(out, in_, ident), r, w)

    def act(self, out, in_, func, r, w, bias=None, scale=None, accum=None):
        nc = self.nc
        kw = {}
        if bias is not None:
            kw['bias'] = bias
        if scale is not None:
            kw['scale'] = scale
        if accum is not None:
            kw['accum_out'] = accum
        self.P.op('act', lambda: nc.scalar.activation(out=out, in_=in_, func=func, **kw), r, w)

    def v(self, eng, name, r, w, *a, **kw):
        nc = self.nc
        E = nc.vector if eng == 'dve' else nc.gpsimd
        self.P.op(eng, lambda: getattr(E, name)(*a, **kw), r, w)

    def dma(self, q, out, in_, r, w, sem):
        nc = self.nc
        E = nc.sync if q == 'sp' else nc.gpsimd
        k0 = w[0] if w else sem
        if isinstance(k0, tuple) and k0[0] in ('out', 'dbgxs'):
            name = sem
        elif isinstance(k0, tuple) and k0[0] == 'xs':
            name = 'xs'
        else:
            name = "b%d" % self.semmap.setdefault((k0, q), len(self.semmap))
        self.P.op(q, lambda: E.dma_start(out=out, in_=in_), r, w, dma=name + '_' + q)

    def build(self):
        nc, P, d = self.nc, self.P, self.d
        self.semmap = {}
        with Scope(self) as S:
            self.S0 = S
            self.xs = S.sb("xs", [128, NT, DM], F32)
            self.hT = S.sb("hT", [128, 8, TOK], BF16)
            self.identb = S.sb("identb", [128, 128], BF16)
            self.identf = S.sb("identf", [128, 128], F32)
            self.onesf = S.sb("onesf", [128, 128], F32)
            self.c2 = S.sb("c2", [128, 16], F32)
            self.scb = S.sb("scb", [128, 16], F32)
            self.modv = S.sb("modv", [128, 48, 2], F32)
            self.bmod = S.sb("bmod", [128, 2, 48], F32)
            self.gn = S.sb("gn", [128, 2, 2, 8], F32)
            self.G = S.sb("G", [128, 2, 8, 2], F32)
            self.gaB = S.sb("gaB", [128, 2, DM], F32)
            self.ssq = S.sb("ssq", [128, NT], F32)
            self.rstd = S.sb("rstd", [128, NT], F32)
            self.epsT = S.sb("epsT", [128, 1], F32)
            self.xn = [S.sb("xn%d" % i, [128, DM], BF16) for i in range(2)]
            self.junk = S.sb("junk", [128, DM], BF16)
            self.dg = [S.sb("dg%d" % i, [128, 128], F32) for i in range(2)]
            self.xt = [S.sb("xt%d" % i, [128, 512], F32) for i in range(2)]
            if not P.dry:
                es = S.es
                self.pb = [es.enter_context(nc.psum_tensor("pb%d" % i, [128, 512], F32)) for i in range(6)]
                self.ptr = [es.enter_context(nc.psum_tensor("ptr%d" % i, [128, 8, 128], BF16)) for i in range(2)]
            else:
                self.pb = [Dummy()] * 6
                self.ptr = [Dummy()] * 2
            self.cnt = 0
            self.dma('sp', self.identb[:], d['identb'][:, :], [], ['identb'], 'cst')
            self.dma('sp', self.identf[:], d['identf'][:, :], [], ['identf'], 'cst')
            self.dma('sp', self.c2[:], d['c2'][:, :], [], ['c2'], 'cst')
            self.dma('sp', self.bmod[:], d['bmodfm'][:, :, :], [], ['bmod'], 'cst')
            self.dma('sp', self.gn[:], d['gnfm'][:, :, :, :], [], ['gn'], 'cst')
            self.v('dve', 'memset', [], ['onesf'], self.onesf[:], 1.0)
            self.v('dve', 'memset', [], ['epsT'], self.epsT[:], EPS)
            self.v('dve', 'memset', [], ['ssq'], self.ssq[:], 0.0)
            self.act(self.scb[:], self.c2[:], AF.Silu, ['c2'], ['scb'])
            for t in range(NT):
                self.dma('sp', self.xs[:, t, :], d['xin'][t * 128:(t + 1) * 128, :], [], [('xs', t)], 'xin')
            for l in range(DEPTH):
                last = (l == DEPTH - 1)
                self.mod_phase(l)
                if STAGE <= 1:
                    break
                self.make_gaB(2)
                self.phaseA(l, 0, range(NT))
                if STAGE >= 3:
                    self.mixers(l, last)
                if DBGXS and l == 0:
                    for t in range(NT):
                        self.dma('sp', d['dbgxs'][t * 128:(t + 1) * 128, :], self.xs[:, t, :], [('xs', t)], [('dbgxs', t)], 'out')
                    break
                self.make_gaB(5)
                tiles = list(range(2, NT)) if last else list(range(NT))
                self.phaseA(l, 1, tiles)
                self.mlp(l, tiles)
            self.final()
        P.barrier()
        P.op('sp', None)

    def mod_phase(self, l):
        nc, P, d = self.nc, self.P, self.d
        with Scope(self) as S:
            wm = [S.sb("wm%d" % i, [128, 8, 1024], F32) for i in range(2)]
            rowb = [S.sb("rowb%d" % i, [2, 512], F32) for i in range(2)]
            modps = self.pb[0]
            for blk in range(6):
                s = blk % 2
                self.dma('sp', wm[s][:], d['w_mod'][l][:, blk * 1024:(blk + 1) * 1024].rearrange("(k p) c -> p k c", p=128),
                         [], [('wm', s)], 'wm%d' % s)
                for hb2 in range(2):
                    rb = 1 + (blk * 2 + hb2) % 2
                    for k in range(8):
                        self.mm(self.pb[rb][0:2, :], self.scb[:, 2 * k:2 * k + 2], wm[s][:, k, hb2 * 512:(hb2 + 1) * 512],
                                k == 0, k == 7, [('wm', s), 'scb'], [('pb', rb)])
                    ri = (blk * 2 + hb2) % 2
                    self.act(rowb[ri][:], self.pb[rb][0:2, :], AF.Identity, [('pb', rb)], [('rowb', ri)])
                    for cc in range(4):
                        j = blk * 8 + hb2 * 4 + cc
                        self.mm(modps[:, 2 * j:2 * j + 2], rowb[ri][:, cc * 128:(cc + 1) * 128], self.identf[0:2, 0:2], True, True,
                                [('rowb', ri), 'identf'], [('pb', 0)])
            self.v('dve', 'tensor_tensor', [('pb', 0), 'bmod'], ['modv'], out=self.modv[:],
                   in0=modps[:, 0:96].rearrange("p (j w) -> p j w", w=2),
                   in1=self.bmod[:, l, :].unsqueeze(2).to_broadcast([128, 48, 2]), op=OP.add)
            for ni, vi in ((0, 1), (1, 4)):
                self.v('dve', 'scalar_tensor_tensor', ['modv', 'gn'], ['G'], out=self.G[:, ni, :, :],
                       in0=self.modv[:, vi * 8:(vi + 1) * 8, :], scalar=1.0,
                       in1=self.gn[:, l, ni, :].unsqueeze(2).to_broadcast([128, 8, 2]), op0=OP.add, op1=OP.mult)

    def make_gaB(self, vi):
        for w in range(2):
            for kk in range(8):
                self.cnt += 1
                dgi = self.cnt % 2
                self.v('dve', 'tensor_scalar', ['modv', 'identf'], [('dg', dgi)], out=self.dg[dgi][:], in0=self.identf[:],
                       scalar1=self.modv[:, vi * 8 + kk, w:w + 1], scalar2=None, op0=OP.mult)
                bank = kk // 4
                self.mm(self.pb[bank][:, (kk % 4) * 128:(kk % 4 + 1) * 128], self.onesf[:], self.dg[dgi][:], True, True,
                        ['onesf', ('dg', dgi)], [('pb', bank)])
            for bank in range(2):
                self.act(self.gaB[:, w, bank * 512:(bank + 1) * 512], self.pb[bank][:], AF.Identity, [('pb', bank)], ['gaB'])

    def rms(self, t):
        self.cnt += 1
        self.act(self.junk[:], self.xs[:, t, :], AF.Square, [('xs', t)], [('junk', self.cnt), ('ssq', t)],
                 accum=self.ssq[:, t:t + 1])
        self.act(self.rstd[:, t:t + 1], self.ssq[:, t:t + 1], AF.Sqrt, [('ssq', t), 'epsT'], [('rstd', t)],
                 bias=self.epsT[:], scale=1.0 / DM)
        self.v('dve', 'reciprocal', [('rstd', t)], [('rstd', t)], out=self.rstd[:, t:t + 1], in_=self.rstd[:, t:t + 1])
        self.v('dve', 'memset', [('ssq', t)], [('ssq', t)], self.ssq[:, t:t + 1], 0.0)

    def rms_all(self, tiles):
        tiles = list(tiles)
        t0, t1 = tiles[0], tiles[-1] + 1
        for t in tiles:
            self.cnt += 1
            self.act(self.junk[:], self.xs[:, t, :], AF.Square, [('xs', t)], [('junk', self.cnt), 'ssq'],
                     accum=self.ssq[:, t:t + 1])
        self.act(self.rstd[:, t0:t1], self.ssq[:, t0:t1], AF.Sqrt, ['ssq', 'epsT'], ['rstd'], bias=self.epsT[:], scale=1.0 / DM)
        self.v('dve', 'reciprocal', ['rstd'], ['rstd'], out=self.rstd[:, t0:t1], in_=self.rstd[:, t0:t1])
        self.v('dve', 'memset', ['ssq'], ['ssq'], self.ssq[:, t0:t1], 0.0)

    def phaseA(self, l, ni, tiles):
        vi = 0 if ni == 0 else 3
        tiles = list(tiles)
        self.rms_all(tiles)
        for t in tiles:
            w = which(t)
            self.cnt += 1
            xi = self.cnt % 2
            self.act(self.xn[xi][:], self.xs[:, t, :], AF.Identity, [('xs', t), 'rstd'], [('xn', xi)],
                     scale=self.rstd[:, t:t + 1])
            for k in range(8):
                self.tr(self.ptr[xi][:, k, :], self.xn[xi][:, k * 128:(k + 1) * 128], self.identb[:],
                        [('xn', xi), 'identb'], [('ptr', xi)])
            for k in range(8):
                self.v('dve', 'tensor_scalar', [('ptr', xi), 'G', 'modv'], [('hT', t)], out=self.hT[:, k, t * 128:(t + 1) * 128],
                       in0=self.ptr[xi][:, k, :], scalar1=self.G[:, ni, k, w:w + 1], scalar2=self.modv[:, vi * 8 + k, w:w + 1],
                       op0=OP.mult, op1=OP.add)

    def xupd(self, t, nh, b):
        w = which(t)
        self.cnt += 1
        mi = self.cnt % 2
        self.v('dve', 'tensor_tensor', [('pb', b), 'gaB'], [('xt', mi)], out=self.xt[mi][:], in0=self.pb[b][:],
               in1=self.gaB[:, w, nh * 512:(nh + 1) * 512], op=OP.mult)
        self.v('pool', 'tensor_tensor', [('xt', mi), ('xs', t)], [('xs', t)],
               out=self.xs[:, t, nh * 512:(nh + 1) * 512], in0=self.xs[:, t, nh * 512:(nh + 1) * 512],
               in1=self.xt[mi][:], op=OP.add)

    def blocks(self, lo=0, hi=TOK, step=512):
        return [(o, min(step, hi - o)) for o in range(lo, hi, step)]

    def mixers(self, l, last):
        if 'F' in BRANCHES:
            self.fourier(l, last)
        if 'M' in BRANCHES:
            self.mlstm(l, last)
        if 'A' in BRANCHES:
            self.mla(l, last)

    def mlstm(self, l, last):
        nc, P, d = self.nc, self.P, self.d
        win = d['w_in'][l]
        with Scope(self) as SO:
            acol = SO.sb("acol", [128, NT, 8], F32)
            Mcol = SO.sb("Mcol", [128, NT, 8], F32)
            T4 = SO.sb("T4", [128, 4, NT, 8], F32)
            gm = SO.sb("gm", [128, 384], F32)
            E2 = SO.sb("E2", [8, 2, 8], F32)
            negid = SO.sb("negid", [128, 128], F32)
            mlmask = SO.sb("mlmask", [128, 2, 256], BF16)
            self.dma('sp', gm[:], d['gmB'][:, l, :], [], ['gm'], 'mw')
            self.dma('sp', E2[:], d['E2'][:, :, :], [], ['E2'], 'mw')
            self.dma('sp', negid[:], d['negidf'][:, :], [], ['negid'], 'mw')
            self.dma('sp', mlmask[:], d['mlmask'][:, :, :], [], ['mlmask'], 'mw')
            with Scope(self) as S:
                wg = S.sb("wg", [128, 8, 16], BF16)
                bg = S.sb("bg", [8, 2], F32)
                nbf = S.sb("nbf", [8, 1], F32)
                IG = S.sb("IG", [8, TOK], F32)
                LF = S.sb("LF", [8, TOK], F32)
                FF = S.sb("FF", [8, TOK], F32)
                FB = S.sb("FB", [8, TOK], F32)
                AFr = S.sb("AFr", [8, TOK], F32)
                ABr = S.sb("ABr", [8, TOK], F32)
                R1 = S.sb("R1", [8, NT, 8], F32)
                R2 = S.sb("R2", [8, NT, 8], F32)
                mcol = S.sb("mcol", [128, NT, 8], F32)
                MendB = S.sb("MendB", [128, NT, 8], F32)
                MP = S.sb("MP", [128, NT, 8], F32)
                ex = S.sb("ex", [128, 4, NT, 8], F32)
                self.dma('pool', wg[:], d['w_g'][l].rearrange("(k p) c -> p k c", p=128), [], ['wg'], 'gw')
                self.dma('sp', bg[:], d['bg'][l], [], ['bg'], 'gw')
                self.v('dve', 'tensor_scalar', ['bg'], ['nbf'], out=nbf[:], in0=bg[:, 1:2], scalar1=-1.0, scalar2=None, op0=OP.mult)
                nb = 0
                for (o, n) in self.blocks():
                    hk = [('hT', t) for t in range(o // 128, (o + n) // 128)]
                    bi, bf_ = nb % 6, (nb + 1) % 6
                    nb += 2
                    for k in range(8):
                        self.mm(self.pb[bi][0:8, 0:n], wg[:, k, 0:8], self.hT[:, k, o:o + n], k == 0, k == 7, hk + ['wg'], [('pb', bi)])
                    for k in range(8):
                        self.mm(self.pb[bf_][0:8, 0:n], wg[:, k, 8:16], self.hT[:, k, o:o + n], k == 0, k == 7, hk + ['wg'], [('pb', bf_)])
                    self.act(IG[:, o:o + n], self.pb[bi][0:8, 0:n], AF.Identity, [('pb', bi), 'bg'], ['IG'], bias=bg[:, 0:1])
                    self.act(LF[:, o:o + n], self.pb[bf_][0:8, 0:n], AF.Exp, [('pb', bf_), 'nbf'], ['LF'], bias=nbf[:], scale=-1.0)
                self.act(LF[:], LF[:], AF.Ln, ['LF'], ['LF'], bias=1.0)
                self.v('dve', 'tensor_scalar', ['LF'], ['LF'], out=LF[:], in0=LF[:], scalar1=-1.0, scalar2=None, op0=OP.mult)
                one_b = lambda n: self.onesf[0:8, 0:1].to_broadcast([8, n])
                self.v('dve', 'tensor_tensor_scan', ['LF', 'onesf'], ['FF'], out=FF[:], data0=one_b(TOK), data1=LF[:], initial=0.0,
                       op0=OP.mult, op1=OP.add)
                self.v('dve', 'tensor_tensor_scan', ['LF', 'onesf'], ['FB'], out=FB[:, 255::-1], data0=one_b(256), data1=LF[:, 255::-1],
                       initial=0.0, op0=OP.mult, op1=OP.add)
                self.v('dve', 'tensor_tensor_scan', ['LF', 'onesf', 'FB'], ['FB'], out=FB[:, 2303:255:-1], data0=one_b(2048),
                       data1=LF[:, 2303:255:-1], initial=FB[:, 0:1], op0=OP.mult, op1=OP.add)
                self.v('dve', 'tensor_tensor', ['IG', 'FF'], ['AFr'], out=AFr[:], in0=IG[:], in1=FF[:], op=OP.subtract)
                self.v('dve', 'tensor_tensor', ['IG', 'FB'], ['ABr'], out=ABr[:], in0=IG[:], in1=FB[:], op=OP.subtract)
                MF, MB = IG, LF
                self.v('dve', 'tensor_tensor_scan', ['AFr'], ['IG'], out=MF[:], data0=AFr[:], data1=AFr[:], initial=0.0, op0=OP.max, op1=OP.max)
                self.v('dve', 'tensor_tensor_scan', ['ABr'], ['LF'], out=MB[:, 255::-1], data0=ABr[:, 255::-1], data1=ABr[:, 255::-1],
                       initial=0.0, op0=OP.max, op1=OP.max)
                self.v('dve', 'tensor_tensor_scan', ['ABr', 'LF'], ['LF'], out=MB[:, 2303:255:-1], data0=ABr[:, 2303:255:-1],
                       data1=ABr[:, 2303:255:-1], initial=MB[:, 0:1], op0=OP.max, op1=OP.max)
                self.v('dve', 'tensor_tensor', ['FF', 'IG'], ['FF'], out=FF[:], in0=FF[:], in1=MF[:], op=OP.add)
                self.v('dve', 'tensor_tensor', ['FB', 'LF'], ['FB'], out=FB[:], in0=FB[:], in1=MB[:], op=OP.add)
                for qi, (XF, XB, kf, kb) in enumerate(((AFr, ABr, 'AFr', 'ABr'), (MF, MB, 'IG', 'LF'), (FF, FB, 'FF', 'FB'))):
                    for c in range(NT):
                        self.mm(self.pb[qi][:, c * 8:(c + 1) * 8], XF[:, c * 128:(c + 1) * 128], E2[:, 0, :], True, False, [kf, 'E2'], [('pb', qi)])
                        self.mm(self.pb[qi][:, c * 8:(c + 1) * 8], XB[:, c * 128:(c + 1) * 128], E2[:, 1, :], False, True, [kb, 'E2'], [('pb', qi)])
                for qi, (dst, kd) in enumerate(((acol, 'acol'), (Mcol, 'Mcol'), (mcol, 'mcol'))):
                    self.act(dst[:].rearrange("p c e -> p (c e)"), self.pb[qi][:, 0:NT * 8], AF.Identity, [('pb', qi)], [kd])
                self.v('dve', 'tensor_tensor', ['E2', 'IG'], ['R1'], out=R1[:], in0=E2[:, 0, :].unsqueeze(1).to_broadcast([8, NT, 8]),
                       in1=MF[:, 127::128].unsqueeze(2).to_broadcast([8, NT, 8]), op=OP.mult)
                self.v('dve', 'tensor_tensor', ['E2', 'LF'], ['R2'], out=R2[:], in0=E2[:, 1, :].unsqueeze(1).to_broadcast([8, NT, 8]),
                       in1=MB[:, 0::128].unsqueeze(2).to_broadcast([8, NT, 8]), op=OP.mult)
                self.v('dve', 'tensor_tensor', ['R1', 'R2'], ['R1'], out=R1[:], in0=R1[:], in1=R2[:], op=OP.add)
                self.mm(self.pb[3][:, 0:NT * 8], self.onesf[0:8, :], R1[:].rearrange("p c e -> p (c e)"), True, True, ['R1', 'onesf'], [('pb', 3)])
                self.act(MendB[:].rearrange("p c e -> p (c e)"), self.pb[3][:, 0:NT * 8], AF.Identity, [('pb', 3)], ['MendB'])
                self.v('dve', 'memset', [], ['MP'], MP[:], 0.0)
                self.v('dve', 'tensor_copy', ['MendB', 'MP'], ['MP'], out=MP[:, 1:NT, 0:4], in_=MendB[:, 0:NT - 1, 0:4])
                self.v('dve', 'tensor_copy', ['MendB', 'MP'], ['MP'], out=MP[:, 0:1, 4:8], in_=MendB[:, 1:2, 4:8])
                self.v('dve', 'tensor_copy', ['MendB', 'MP'], ['MP'], out=MP[:, 2:NT - 1, 4:8], in_=MendB[:, 3:NT, 4:8])
                self.v('dve', 'tensor_copy', ['MendB', 'MP'], ['MP'], out=MP[:, NT - 1:NT, 4:8], in_=MendB[:, 0:1, 4:8])
                self.v('dve', 'tensor_tensor', ['MP', 'Mcol'], ['ex'], out=ex[:, 0], in0=MP[:], in1=Mcol[:], op=OP.subtract)
                self.v('dve', 'tensor_tensor', ['acol', 'MendB'], ['ex'], out=ex[:, 1], in0=acol[:], in1=MendB[:], op=OP.subtract)
                self.v('dve', 'tensor_tensor', ['MP', 'MendB'], ['ex'], out=ex[:, 2], in0=MP[:], in1=MendB[:], op=OP.subtract)
                self.v('dve', 'tensor_scalar', ['mcol'], ['ex'], out=ex[:, 3], in0=mcol[:], scalar1=-1.0, scalar2=None, op0=OP.mult)
                self.act(T4[:].rearrange("p a c e -> p (a c e)"), ex[:].rearrange("p a c e -> p (a c e)"), AF.Exp, ['ex'], ['T4'])
            for hp in range(2):
                self.mlstm_pair(l, last, hp, acol, Mcol, T4, gm, negid, mlmask)

    def mlstm_pair(self, l, last, hp, acol, Mcol, T4, gm, negid, mlmask):
        nc, P, d = self.nc, self.P, self.d
        win = d['w_in'][l]
        QS = 96.0 ** -0.5
        colof = lambda T: T + 2 if T < 256 else T + 6
        cblocks = [(0, 256)] + [(256 + i * 512, 512) for i in range(4)]
        with Scope(self) as SP:
            post = SP.sb("post", [96, 4, TOK], BF16)
            vp = SP.sb("vp", [128, NT, 2, 97], BF16)
            sg = SP.sb("sg", [128, NT, 192], BF16)
            S = Scope(self)
            wvo = S.sb("wvo", [128, 8, 384], BF16)
            self.dma('pool', wvo[:, :, 0:192], win[:, 1024 + 192 * hp:1024 + 192 * (hp + 1)].rearrange("(k p) c -> p k c", p=128), [], ['wvo'], 'mw')
            self.dma('pool', wvo[:, :, 192:384], win[:, 1408 + 192 * hp:1408 + 192 * (hp + 1)].rearrange("(k p) c -> p k c", p=128), [], ['wvo'], 'mw')
            self.v('dve', 'memset', [], ['vp'], vp[:], 1.0)
            with S:
                wqk = S.sb("wqk", [128, 8, 4, 96], BF16)
                dgw = S.sb("dgw", [96, 4, 5, 96], BF16)
                pre = S.sb("pre", [96, 4, 2312], BF16)
                sgt = [S.sb("sgt%d" % i, [96, 512], F32) for i in range(1)]
                for jj in range(4):
                    c0 = (256 if jj < 2 else 640) + 96 * (2 * hp + jj % 2)
                    self.dma('pool', wqk[:, :, jj, :], win[:, c0:c0 + 96].rearrange("(k p) c -> p k c", p=128), [], ['wqk'], 'mw')
                    jg = (0 if jj < 2 else 4) + 2 * hp + jj % 2
                    self.dma('sp', dgw[:, jj, :, :], d['convdg'][l, :, jg, :, :], [], ['dgw'], 'mw')
                self.v('dve', 'memset', [], ['pre'], pre[:], 0.0)
                nb = 0
                for jj in range(4):
                    for (o, n) in cblocks:
                        b = nb % 6
                        nb += 1
                        hk = [('hT', t) for t in range(o // 128, (o + n) // 128)]
                        for k in range(8):
                            self.mm(self.pb[b][0:96, 0:n], wqk[:, k, jj, :], self.hT[:, k, o:o + n], k == 0, k == 7, hk + ['wqk'], [('pb', b)])
                        self.act(pre[:, jj, colof(o):colof(o) + n], self.pb[b][0:96, 0:n], AF.Identity, [('pb', b), 'pre'], [('pre', jj)])
                for t in range(NT):
                    b = nb % 6
                    nb += 1
                    for k in range(8):
                        self.mm(self.pb[b][:, 0:384], self.hT[:, k, t * 128:(t + 1) * 128], wvo[:, k, :], k == 0, k == 7, [('hT', t), 'wvo'], [('pb', b)])
                    self.act(vp[:, t, :, 0:96], self.pb[b][:, 0:192].rearrange("p (h c) -> p h c", c=96), AF.Identity, [('pb', b), 'vp'], [('vp', t)])
                    self.act(sg[:, t, :], self.pb[b][:, 192:384], AF.Sigmoid, [('pb', b)], [('sg', t)])
                for jj in range(4):
                    for (o, n) in cblocks:
                        b = nb % 6
                        nb += 1
                        c0 = colof(o)
                        for tap in range(5):
                            self.mm(self.pb[b][0:96, 0:n], dgw[:, jj, tap, :], pre[:, jj, c0 + tap - 2:c0 + tap - 2 + n], tap == 0, tap == 4,
                                    [('pre', jj), 'dgw'], [('pb', b)])
                        tk = [('post', jj, t) for t in range(o // 128, (o + n) // 128)]
                        if jj < 2:
                            si = 0
                            self.act(sgt[si][:, 0:n], self.pb[b][0:96, 0:n], AF.Sigmoid, [('pb', b)], [('sgt', si)])
                            self.v('dve', 'scalar_tensor_tensor', [('pb', b), ('sgt', si)], tk, out=post[:, jj, o:o + n], in0=self.pb[b][0:96, 0:n],
                                   scalar=QS, in1=sgt[si][:, 0:n], op0=OP.mult, op1=OP.mult)
                        else:
                            self.act(post[:, jj, o:o + n], self.pb[b][0:96, 0:n], AF.Silu, [('pb', b)], tk)
            ktok = SP.sb("ktok", [128, NT, 2, 96], BF16)
            for t in range(NT):
                pi = t % 2
                for hh in range(2):
                    self.tr(self.ptr[pi][:, hh, 0:96], post[:, 2 + hh, t * 128:(t + 1) * 128], self.identb[0:96, 0:96],
                            [('post', 2 + hh, t), 'identb'], [('ptr', pi)])
                self.act(ktok[:, t, :, :], self.ptr[pi][:, 0:2, 0:96], AF.Identity, [('ptr', pi)], [('ktok', t)])
            hF = SP.sb("hF", [128, NT, 2, 96], BF16)
            hB = SP.sb("hB", [128, NT, 2, 96], BF16)
            with Scope(self) as S:
                dgM = [S.sb("dgM%d" % i, [128, 2, 128], F32) for i in range(3)]
                WT = [S.sb("WT%d" % i, [128, 2, 128], F32) for i in range(3)]
                PT = [S.sb("PT%d" % i, [128, 2, 128], BF16) for i in range(3)]
                tmpn = [S.sb("tmpn%d" % i, [128, 2, 97], F32) for i in range(2)]
                dn = [S.sb("dn%d" % i, [128, 2], F32) for i in range(2)]
                vw = [S.sb("vw%d" % i, [128, 2, 97], BF16) for i in range(2)]
                Cst = S.sb("Cst", [96, 2, 97], F32)
                Cball = S.sb("Cball", [96, NT, 2, 97], BF16)
                for dr in range(2):
                    order = list(range(NT)) if dr == 0 else [1, 0] + list(range(NT - 1, 1, -1))
                    cf0 = dr * 4 + 2 * hp
                    self.v('dve', 'memset', ['Cst'], ['Cst'], Cst[:], 0.0)
                    self.v('dve', 'memset', [('Cball', 0)], [('Cball', 0)], Cball[:, 0], 0.0)

                    def kv_part(it):
                        c = order[it]
                        i2 = it % 2
                        self.v('dve', 'tensor_tensor', [('vp', c), 'T4'], [('vw', i2)], out=vw[i2][:], in0=vp[:, c],
                               in1=T4[:, 1, c, cf0:cf0 + 2].unsqueeze(2).to_broadcast([128, 2, 97]), op=OP.mult)
                        for hh in range(2):
                            self.mm(self.pb[4 + i2][0:96, hh * 97:(hh + 1) * 97], ktok[:, c, hh, :], vw[i2][:, hh, :], True, True,
                                    [('ktok', c), ('vw', i2)], [('pb', 4 + i2)])

                    kv_part(0)
                    for it in range(NT - 1):
                        c = order[it]
                        i2 = it % 2
                        if it + 1 < NT - 1:
                            kv_part(it + 1)
                        self.v('dve', 'tensor_tensor', ['Cst', 'T4'], ['Cst'], out=Cst[:], in0=Cst[:],
                               in1=T4[0:96, 2, c, cf0:cf0 + 2].unsqueeze(2).to_broadcast([96, 2, 97]), op=OP.mult)
                        self.v('dve', 'tensor_tensor', ['Cst', ('pb', 4 + i2)], ['Cst'], out=Cst[:], in0=Cst[:],
                               in1=self.pb[4 + i2][0:96, 0:194].rearrange("p (h e) -> p h e", e=97), op=OP.add)
                        self.act(Cball[:, it + 1], Cst[:], AF.Identity, ['Cst'], [('Cball', it + 1)])

                    def part_i(it):
                        c = order[it]
                        i2 = it % 3
                        if last and c < 2:
                            return
                        tsl = slice(c * 128, (c + 1) * 128)
                        bo = 3 + it % 3
                        for hh in range(2):
                            self.act(dgM[i2][:, hh, :], negid[:], AF.Identity, ['negid', 'Mcol'], [('dgM', i2)],
                                     scale=Mcol[:, c, cf0 + hh:cf0 + hh + 1])
                        self.mm(self.pb[1][:, 0:256], self.onesf[:], dgM[i2][:].rearrange("p h t -> p (h t)"), True, False,
                                [('dgM', i2), 'onesf'], [('pb', 1)])
                        self.mm(self.pb[1][:, 0:256], self.identb[:], mlmask[:, dr, :], False, True, ['mlmask', 'identb'], [('pb', 1)])
                        for hh in range(2):
                            self.act(WT[i2][:, hh, :], self.pb[1][:, hh * 128:(hh + 1) * 128], AF.Exp, [('pb', 1), 'acol'], [('WT', i2)],
                                     bias=acol[:, c, cf0 + hh:cf0 + hh + 1])
                        for hh in range(2):
                            self.mm(self.pb[2][:, hh * 128:(hh + 1) * 128], post[:, 2 + hh, tsl], post[:, hh, tsl], True, True,
                                    [('post', 2 + hh, c), ('post', hh, c)], [('pb', 2)])
                        self.v('dve', 'tensor_tensor', [('pb', 2), ('WT', i2)], [('PT', i2)], out=PT[i2][:].rearrange("p h t -> p (h t)"),
                               in0=self.pb[2][:, 0:256], in1=WT[i2][:].rearrange("p h t -> p (h t)"), op=OP.mult)
                        for hh in range(2):
                            self.mm(self.pb[bo][:, hh * 97:(hh + 1) * 97], PT[i2][:, hh, :], vp[:, c, hh, :], True, True,
                                    [('PT', i2), ('vp', c)], [('pb', bo)])
                        for hh in range(2):
                            self.mm(self.pb[bo][:, 256 + hh * 97:256 + (hh + 1) * 97], post[:, hh, tsl], Cball[:, it, hh, :], True, True,
                                    [('post', hh, c), ('Cball', it)], [('pb', bo)])

                    def part_d(it):
                        c = order[it]
                        i2 = it % 2
                        if last and c < 2:
                            return
                        bo = 3 + it % 3
                        for hh in range(2):
                            self.act(tmpn[i2][:, hh, :], self.pb[bo][:, 256 + hh * 97:256 + (hh + 1) * 97], AF.Identity, [('pb', bo), 'T4'],
                                     [('tmpn', i2)], scale=T4[:, 0, c, cf0 + hh:cf0 + hh + 1])
                        self.v('dve', 'tensor_tensor', [('tmpn', i2), ('pb', bo)], [('tmpn', i2)], out=tmpn[i2][:], in0=tmpn[i2][:],
                               in1=self.pb[bo][:, 0:194].rearrange("p (h e) -> p h e", e=97), op=OP.add)
                        self.v('dve', 'scalar_tensor_tensor', [('tmpn', i2)], [('dn', i2)], out=dn[i2][:], in0=tmpn[i2][:, :, 96], scalar=-1.0,
                               in1=tmpn[i2][:, :, 96], op0=OP.mult, op1=OP.max)
                        self.v('dve', 'tensor_tensor', [('dn', i2), 'T4'], [('dn', i2)], out=dn[i2][:], in0=dn[i2][:],
                               in1=T4[:, 3, c, cf0:cf0 + 2], op=OP.max)
                        self.v('dve', 'reciprocal', [('dn', i2)], [('dn', i2)], out=dn[i2][:], in_=dn[i2][:])
                        if dr == 0:
                            self.v('dve', 'tensor_tensor', [('tmpn', i2), ('dn', i2)], [('hF', c)], out=hF[:, c], in0=tmpn[i2][:, :, 0:96],
                                   in1=dn[i2][:].unsqueeze(2).to_broadcast([128, 2, 96]), op=OP.mult)
                            return
                        self.v('dve', 'tensor_tensor', [('tmpn', i2), ('dn', i2)], [('hB', c)], out=hB[:, c], in0=tmpn[i2][:, :, 0:96],
                               in1=dn[i2][:].unsqueeze(2).to_broadcast([128, 2, 96]), op=OP.mult)

                    part_i(0)
                    part_i(1)
                    for it in range(NT):
                        if it + 2 < NT:
                            part_i(it + 2)
                        part_d(it)

            with Scope(self) as S:
                hsC = S.sb("hsC", [128, 6, 2, 96], F32)
                hqC = S.sb("hqC", [128, 6, 2, 96], F32)
                ssC = S.sb("ssC", [128, 12], F32)
                ybC = S.sb("ybC", [128, 6, 192], BF16)
                yhT = [S.sb("yhT%d" % i, [96, 2, 128], BF16) for i in range(2)]
                woM = S.sb("woM", [96, 2, DM], BF16)
                self.dma('pool', woM[:], d['w_out'][l][256 + 192 * hp:256 + 192 * (hp + 1), :].rearrange("(h p) c -> p h c", p=96), [], ['woM'], 'mw')
                otiles = list(range(2, NT)) if last else list(range(NT))
                for g0 in range(0, len(otiles), 6):
                    g = otiles[g0:g0 + 6]
                    ng = len(g)
                    c0, c1 = g[0], g[-1] + 1
                    hk = [('hF', c) for c in g] + [('hB', c) for c in g]
                    self.v('dve', 'tensor_tensor', hk, ['hsC'], out=hsC[:, 0:ng], in0=hF[:, c0:c1], in1=hB[:, c0:c1], op=OP.add)
                    self.v('dve', 'tensor_tensor', ['hsC'], ['hqC'], out=hqC[:, 0:ng], in0=hsC[:, 0:ng], in1=hsC[:, 0:ng], op=OP.mult)
                    self.v('dve', 'tensor_reduce', ['hqC'], ['ssC'], out=ssC[:, 0:2 * ng],
                           in_=hqC[:, 0:ng].rearrange("p c h e -> p (c h) e"), axis=AX.X, op=OP.add)
                    self.act(ssC[:, 0:2 * ng], ssC[:, 0:2 * ng], AF.Ln, ['ssC', 'epsT'], ['ssC'], bias=self.epsT[:], scale=1.0 / 96)
                    self.act(ssC[:, 0:2 * ng], ssC[:, 0:2 * ng], AF.Exp, ['ssC'], ['ssC'], scale=-0.5)
                    self.v('dve', 'tensor_tensor', ['hsC', 'ssC'], ['hqC'], out=hqC[:, 0:ng].rearrange("p c h e -> p (c h) e"),
                           in0=hsC[:, 0:ng].rearrange("p c h e -> p (c h) e"),
                           in1=ssC[:, 0:2 * ng].unsqueeze(2).to_broadcast([128, 2 * ng, 96]), op=OP.mult)
                    self.v('dve', 'tensor_tensor', ['hqC', 'gm'], ['hqC'], out=hqC[:, 0:ng].rearrange("p c h e -> p c (h e)"),
                           in0=hqC[:, 0:ng].rearrange("p c h e -> p c (h e)"),
                           in1=gm[:, 192 * hp:192 * (hp + 1)].unsqueeze(1).to_broadcast([128, ng, 192]), op=OP.mult)
                    self.v('dve', 'tensor_tensor', ['hqC'] + [('sg', c) for c in g], ['ybC'], out=ybC[:, 0:ng],
                           in0=hqC[:, 0:ng].rearrange("p c h e -> p c (h e)"), in1=sg[:, c0:c1, :], op=OP.mult)
                    for ci, c in enumerate(g):
                        i2 = c % 2
                        for hh in range(2):
                            self.tr(self.ptr[i2][0:96, hh, :], ybC[:, ci, hh * 96:(hh + 1) * 96], self.identb[:], ['ybC', 'identb'], [('ptr', i2)])
                        self.act(yhT[i2][:], self.ptr[i2][0:96, 0:2, :], AF.Identity, [('ptr', i2)], [('yhT', i2)])
                        for nh in range(2):
                            b = 2 * i2 + nh
                            for hh in range(2):
                                self.mm(self.pb[b][:], yhT[i2][:, hh, :], woM[:, hh, nh * 512:(nh + 1) * 512], hh == 0, hh == 1,
                                        [('yhT', i2), 'woM'], [('pb', b)])
                            self.xupd(c, nh, b)

    def mla(self, l, last):
        nc, P, d = self.nc, self.P, self.d
        SCALE = 96.0 ** -0.5
        with Scope(self) as S:
            wA = S.sb("wA", [128, 8, 416], BF16)
            wuq = S.sb("wuq", [128, 2, 384], BF16)
            wuqs = S.sb("wuqs", [128, 2, 4, 96], BF16)
            wkr = S.sb("wkr", [128, 8, 96], BF16)
            wkrs = S.sb("wkrs", [128, 8, 96], BF16)
            wukv = S.sb("wukv", [128, 640], BF16)
            woA = S.sb("woA", [96, 4, DM], BF16)
            ropeC = S.sb("ropeC", [96, TOK], BF16)
            ropeS = S.sb("ropeS", [96, TOK], BF16)
            Vt = S.sb("Vt", [128, NT, 384], BF16)
            cqnT = S.sb("cqnT", [128, 2, 512], BF16)
            ckvnT = S.sb("ckvnT", [128, 512], BF16)
            cqn = [S.sb("cqn%d" % i, [128, 256], BF16) for i in range(2)]
            ckvn = [S.sb("ckvn%d" % i, [128, 128], BF16) for i in range(2)]
            rt1 = S.sb("rt1", [96, 512], F32)
            rt2 = S.sb("rt2", [96, 512], F32)
            pT = [S.sb("pT%d" % i, [128, 512], BF16) for i in range(3)]
            rc = S.sb("rc", [96, 512], F32)
            yaT = S.sb("yaT", [96, 4, 512], BF16)
            onesb = S.sb("onesb", [128, 96], BF16)
            st = [S.sb("st%d" % i, [128, 2], F32) for i in range(2)]
            rs = [S.sb("rs%d" % i, [128, 2], F32) for i in range(2)]
            gq = S.sb("gq", [128, 2, 2], F32)
            gkv = S.sb("gkv", [128, 2], F32)
            qT = self.hT[0:96, 0:4, :]
            kT = self.hT[0:96, 4:8, :]
            win = d['w_in'][l]
            self.dma('pool', wA[:], win[:, 1808:2224].rearrange("(k p) c -> p k c", p=128), [], ['wA'], 'aw')
            self.dma('pool', wuq[:], d['w_uq'][l].rearrange("(k p) c -> p k c", p=128), [], ['wuq'], 'aw')
            self.v('dve', 'memset', [], ['wuqs'], wuqs[:], 0.0)
            self.v('dve', 'memset', [], ['wkr'], wkr[:], 0.0)
            self.v('dve', 'memset', [], ['wkrs'], wkrs[:], 0.0)
            self.v('dve', 'memset', [], ['onesb'], onesb[:], 1.0)
            for c in range(2):
                self.dma('pool', wuqs[:, c, :, 64:96], d['w_uq_sw'][l][c * 128:(c + 1) * 128, :, :], ['wuqs'], ['wuqs'], 'aw')
            self.dma('pool', wkr[:, :, 64:96], win[:, 2192:2224].rearrange("(k p) c -> p k c", p=128), ['wkr'], ['wkr'], 'aw')
            self.dma('pool', wkrs[:, :, 64:96], d['w_kr_sw'][l].rearrange("(k p) c -> p k c", p=128), ['wkrs'], ['wkrs'], 'aw')
            self.dma('pool', wukv[:], d['w_ukv'][l], [], ['wukv'], 'aw')
            self.dma('pool', woA[:], d['w_out'][l][640:1024, :].rearrange("(h p) c -> p h c", p=96), [], ['woA'], 'aw')
            self.dma('sp', ropeC[64:96, :], d['ropeC'][:, :], [], ['rope'], 'aw')
            self.dma('sp', ropeS[64:96, :], d['ropeS'][:, :], [], ['rope'], 'aw')
            self.dma('sp', gq[:], d['gqfm'][:, :, :], [], ['gq'], 'aw')
            self.dma('sp', gkv[:], d['gkvfm'][:, :], [], ['gq'], 'aw')
            wv = wukv[:, :].rearrange("p (h c) -> p h c", c=160)[:, :, 64:160]
            for (o, n) in self.blocks():
                tl = list(range(o // 128, (o + n) // 128))
                hk = [('hT', t) for t in tl]
                def a1_front(t):
                    b1 = t % 2
                    for k in range(8):
                        self.mm(self.pb[b1][:, 0:416], self.hT[:, k, t * 128:(t + 1) * 128], wA[:, k, :], k == 0, k == 7,
                                [('hT', t), 'wA'], [('pb', b1)])
                a1_front(tl[0])
                for ti, t in enumerate(tl):
                    b1 = t % 2
                    if ti + 1 < len(tl):
                        a1_front(tl[ti + 1])
                    si = t % 2
                    self.v('dve', 'memset', [], [('st', si)], st[si][:], 0.0)
                    self.cnt += 1
                    self.act(self.junk[:, 0:256], self.pb[b1][:, 0:256], AF.Square, [('pb', b1)], [('junk', self.cnt), ('st', si)],
                             accum=st[si][:, 0:1])
                    self.cnt += 1
                    self.act(self.junk[:, 256:384], self.pb[b1][:, 256:384], AF.Square, [('pb', b1)], [('junk', self.cnt), ('st', si)],
                             accum=st[si][:, 1:2])
                    self.act(rs[si][:, 0:1], st[si][:, 0:1], AF.Sqrt, [('st', si), 'epsT'], [('rs', si)], bias=self.epsT[:], scale=1.0 / 256)
                    self.act(rs[si][:, 1:2], st[si][:, 1:2], AF.Sqrt, [('st', si), 'epsT'], [('rs', si)], bias=self.epsT[:], scale=1.0 / 128)
                    self.v('dve', 'reciprocal', [('rs', si)], [('rs', si)], out=rs[si][:], in_=rs[si][:])
                    self.v('dve', 'tensor_scalar', [('pb', b1), ('rs', si)], [('cqn', si)], out=cqn[si][:], in0=self.pb[b1][:, 0:256],
                           scalar1=rs[si][:, 0:1], scalar2=None, op0=OP.mult)
                    self.v('dve', 'tensor_scalar', [('pb', b1), ('rs', si)], [('ckvn', si)], out=ckvn[si][:], in0=self.pb[b1][:, 256:384],
                           scalar1=rs[si][:, 1:2], scalar2=None, op0=OP.mult)
                    pi = t % 2
                    for c in range(2):
                        self.tr(self.ptr[pi][:, c, :], cqn[si][:, c * 128:(c + 1) * 128], self.identb[:], [('cqn', si), 'identb'], [('ptr', pi)])
                    self.tr(self.ptr[pi][:, 2, :], ckvn[si][:], self.identb[:], [('ckvn', si), 'identb'], [('ptr', pi)])
                    for c in range(2):
                        self.act(cqnT[:, c, ti * 128:(ti + 1) * 128], self.ptr[pi][:, c, :], AF.Identity, [('ptr', pi), 'gq'], ['cqnT'],
                                 scale=gq[:, l, c:c + 1])
                    self.act(ckvnT[:, ti * 128:(ti + 1) * 128], self.ptr[pi][:, 2, :], AF.Identity, [('ptr', pi), 'gq'], ['ckvnT'],
                             scale=gkv[:, l:l + 1])
                    bv = 2 + t % 2
                    self.mm(self.pb[bv][:, 0:384], ckvnT[:, ti * 128:(ti + 1) * 128], wv, True, True, ['ckvnT', 'wukv'], [('pb', bv)])
                    self.act(Vt[:, t, :], self.pb[bv][:, 0:384], AF.Identity, [('pb', bv)], [('Vt', t)])
                for k in range(8):
                    self.mm(self.pb[4][0:96, 0:n], wkr[:, k, :], self.hT[:, k, o:o + n], k == 0, k == 7, hk + ['wkr'], [('pb', 4)])
                for k in range(8):
                    self.mm(self.pb[5][0:96, 0:n], wkrs[:, k, :], self.hT[:, k, o:o + n], k == 0, k == 7, hk + ['wkrs'], [('pb', 5)])
                for h in range(4):
                    for c in range(2):
                        self.mm(self.pb[0][0:96, 0:n], wuq[:, c, h * 96:(h + 1) * 96], cqnT[:, c, 0:n], c == 0, c == 1, ['wuq', 'cqnT'], [('pb', 0)])
                    for c in range(2):
                        self.mm(self.pb[1][0:96, 0:n], wuqs[:, c, h, :], cqnT[:, c, 0:n], c == 0, c == 1, ['wuqs', 'cqnT'], [('pb', 1)])
                    bk = 2 + h % 2
                    self.mm(self.pb[bk][0:64, 0:n], wukv[:, h * 160:h * 160 + 64], ckvnT[:, 0:n], True, True, ['wukv', 'ckvnT'], [('pb', bk)])
                    self.act(qT[0:64, h, o:o + n], self.pb[0][0:64, 0:n], AF.Identity, [('pb', 0)], hk)
                    self.act(kT[0:64, h, o:o + n], self.pb[bk][0:64, 0:n], AF.Identity, [('pb', bk)], hk)
                    self.v('dve', 'tensor_tensor', [('pb', 0), 'rope'], ['rt1'], out=rt1[64:96, 0:n], in0=self.pb[0][64:96, 0:n],
                           in1=ropeC[64:96, o:o + n], op=OP.mult)
                    self.v('dve', 'tensor_tensor', [('pb', 1), 'rope'], ['rt2'], out=rt2[64:96, 0:n], in0=self.pb[1][64:96, 0:n],
                           in1=ropeS[64:96, o:o + n], op=OP.mult)
                    self.v('pool', 'tensor_tensor', ['rt1', 'rt2'], hk, out=qT[64:96, h, o:o + n], in0=rt1[64:96, 0:n],
                           in1=rt2[64:96, 0:n], op=OP.add)
                self.v('dve', 'tensor_tensor', [('pb', 4), 'rope'], ['rt1'], out=rt1[64:96, 0:n], in0=self.pb[4][64:96, 0:n],
                       in1=ropeC[64:96, o:o + n], op=OP.mult)
                self.v('dve', 'tensor_tensor', [('pb', 5), 'rope'], ['rt2'], out=rt2[64:96, 0:n], in0=self.pb[5][64:96, 0:n],
                       in1=ropeS[64:96, o:o + n], op=OP.mult)
                for h in range(4):
                    self.v('pool', 'tensor_tensor', ['rt1', 'rt2'], hk, out=kT[64:96, h, o:o + n], in0=rt1[64:96, 0:n],
                           in1=rt2[64:96, 0:n], op=OP.add)
            qblocks = [(256 + i * 512, 512, list(range(NT))) for i in range(4)]
            if not last:
                qblocks.append((0, 256, [0, 1]))
            for (qo, qn, ktiles) in qblocks:
                qk = [('hT', t) for t in range(qo // 128, (qo + qn) // 128)]
                for h in range(4):
                    def st_mm(i):
                        kt_ = ktiles[i]
                        self.mm(self.pb[i % 2][:, 0:qn], kT[:, h, kt_ * 128:(kt_ + 1) * 128], qT[:, h, qo:qo + qn], True, True,
                                qk + [('hT', kt_)], [('pb', i % 2)])
                    st_mm(0)
                    for i, kt in enumerate(ktiles):
                        sb_ = i % 2
                        pi = i % 3
                        if i + 1 < len(ktiles):
                            st_mm(i + 1)
                        self.act(pT[pi][:, 0:qn], self.pb[sb_][:, 0:qn], AF.Exp, [('pb', sb_)], [('pT', pi)], scale=SCALE)
                        self.mm(self.pb[2][0:96, 0:qn], Vt[:, kt, h * 96:(h + 1) * 96], pT[pi][:, 0:qn], i == 0, i == len(ktiles) - 1,
                                [('Vt', kt), ('pT', pi)], [('pb', 2)])
                        self.mm(self.pb[3][0:96, 0:qn], onesb[:, :], pT[pi][:, 0:qn], i == 0, i == len(ktiles) - 1,
                                ['onesb', ('pT', pi)], [('pb', 3)])
                    self.v('dve', 'reciprocal', [('pb', 3)], ['rc'], out=rc[:, 0:qn], in_=self.pb[3][0:96, 0:qn])
                    self.v('dve', 'tensor_tensor', [('pb', 2), 'rc'], [('yaT', h)], out=yaT[:, h, 0:qn], in0=self.pb[2][0:96, 0:qn],
                           in1=rc[:, 0:qn], op=OP.mult)
                for ti, t in enumerate(range(qo // 128, (qo + qn) // 128)):
                    for nh in range(2):
                        b = 4 + nh
                        for h in range(4):
                            self.mm(self.pb[b][:], yaT[:, h, ti * 128:(ti + 1) * 128], woA[:, h, nh * 512:(nh + 1) * 512], h == 0, h == 3,
                                    [('yaT', h), 'woA'], [('pb', b)])
                        self.xupd(t, nh, b)

    def fourier(self, l, last):
        nc, P, d = self.nc, self.P, self.d
        with Scope(self) as S:
            wpf = S.sb("wpf", [128, 8, 256], BF16)
            pfT = S.sb("pfT", [128, 2, TOK], BF16)
            bd = S.sb("bd", [128, 256], BF16)
            pfCS = S.sb("pfCS", [128, NT, 2, 256], BF16)
            tw = [S.sb("tw%d" % i, [128, 2, 1024], BF16) for i in range(3)]
            yfT = S.sb("yfT", [128, 2, TOK], BF16)
            wo = S.sb("wo", [128, 2, DM], BF16)
            c256 = S.sb("c256", [128, 2, 2, 256], BF16)
            self.dma('pool', wpf[:], d['w_in'][l][:, 0:256].rearrange("(k p) c -> p k c", p=128), [], ['wpf'], 'fw')
            self.dma('pool', wo[:], d['w_out'][l][0:256, :].rearrange("(k p) c -> p k c", p=128), [], ['wo'], 'fw')
            self.dma('sp', bd[:], d['bdcs'][:, :], [], ['bd'], 'fw')
            self.dma('sp', c256[:], d['c256'][:, :, :, :], [], ['c256'], 'fw')
            nb = 0
            for j in range(2):
                for (o, n) in self.blocks():
                    b = nb % 6
                    nb += 1
                    for k in range(8):
                        self.mm(self.pb[b][:, 0:n], wpf[:, k, j * 128:(j + 1) * 128], self.hT[:, k, o:o + n], k == 0, k == 7,
                                ['wpf'] + [('hT', t) for t in range(o // 128, (o + n) // 128)], [('pb', b)])
                    self.act(pfT[:, j, o:o + n], self.pb[b][:, 0:n], AF.Identity, [('pb', b)], [('pfT', j, o)])
            for t in range(NT):
                for j in range(2):
                    b = nb % 6
                    nb += 1
                    self.mm(self.pb[b][:, 0:256], pfT[:, j, t * 128:(t + 1) * 128], bd[:], True, True,
                            [('pfT', j, (t // 4) * 512), 'bd'], [('pb', b)])
                    self.act(pfCS[:, t, j, :], self.pb[b][:, 0:256], AF.Identity, [('pb', b)], [('pfCS', t)])
            ntw = 0
            for half in range(2):
                for nt in range(16):
                    t = nt + 2
                    s = ntw % 3
                    ntw += 1
                    self.dma('sp', tw[s][:], d['twCS'][half, nt], [], [('tw', s)], 'tw%d' % s)
                    for j in range(2):
                        for q in range(2):
                            b = j * 2 + q
                            self.mm(self.pb[b][:], pfCS[:, t, j, 0:128], tw[s][:, 0, q * 512:(q + 1) * 512], nt == 0, False,
                                    [('pfCS', t), ('tw', s)], [('pb', b)])
                            self.mm(self.pb[b][:], pfCS[:, t, j, 128:256], tw[s][:, 1, q * 512:(q + 1) * 512], False, nt == 15,
                                    [('pfCS', t), ('tw', s)], [('pb', b)])
                for j in range(2):
                    for q in range(2):
                        b = j * 2 + q
                        o = 256 + half * 1024 + q * 512
                        self.act(yfT[:, j, o:o + 512], self.pb[b][:], AF.Identity, [('pb', b)], [('yfT', o // 128)])
            if not last:
                for j in range(2):
                    b = 4 + j
                    for nt in range(2):
                        self.mm(self.pb[b][:, 0:256], pfCS[:, nt, j, 0:128], c256[:, nt, 0, :], nt == 0, False,
                                [('pfCS', nt), 'c256'], [('pb', b)])
                        self.mm(self.pb[b][:, 0:256], pfCS[:, nt, j, 128:256], c256[:, nt, 1, :], False, nt == 1,
                                [('pfCS', nt), 'c256'], [('pb', b)])
                    self.act(yfT[:, j, 0:256], self.pb[b][:, 0:256], AF.Identity, [('pb', b)], [('yfT', 0)])
            nb = 0
            for t in (range(2, NT) if last else range(NT)):
                for nh in range(2):
                    b = nb % 6
                    nb += 1
                    for j in range(2):
                        self.mm(self.pb[b][:], yfT[:, j, t * 128:(t + 1) * 128], wo[:, j, nh * 512:(nh + 1) * 512], j == 0, j == 1,
                                [('yfT', 0 if t < 2 else 2 + ((t - 2) // 4) * 4), 'wo'], [('pb', b)])
                    self.xupd(t, nh, b)

    def mlp(self, l, tiles):
        nc, P, d = self.nc, self.P, self.d
        groups = [tiles[i:i + 6] for i in range(0, len(tiles), 6)]
        with Scope(self) as S:
            uT = S.sb("uT", [128, 32, 768], BF16)
            wu = [S.sb("wu%d" % i, [128, 8, 512], BF16) for i in range(2)]
            wd = [S.sb("wd%d" % i, [128, 4, 512], BF16) for i in range(2)]
            tmp = [S.sb("mt%d" % i, [128, 512], F32) for i in range(2)]
            nu = nd = nb = 0
            for g in groups:
                G = 128 * len(g)
                t0 = g[0] * 128
                subs = [(0, G)] if G <= 512 else [(0, G // 2), (G // 2, G // 2)]
                for hb in range(8):
                    s = nu % 2
                    nu += 1
                    self.dma('pool', wu[s][:], d['w_up_r'][l, hb], [], [('wu', s)], 'wu%d' % s)
                    for cc in range(4):
                        j = hb * 4 + cc
                        for (o, n) in subs:
                            b = nb % 6
                            nb += 1
                            for k in range(8):
                                self.mm(self.pb[b][:, 0:n], wu[s][:, k, cc * 128:(cc + 1) * 128],
                                        self.hT[:, k, t0 + o:t0 + o + n], k == 0, k == 7,
                                        [('wu', s)] + [('hT', t) for t in g], [('pb', b)])
                            ri = nb % 2
                            self.act(tmp[ri][:, 0:n], self.pb[b][:, 0:n], AF.Relu, [('pb', b)], [('mt', ri)])
                            self.v('dve', 'tensor_tensor', [('mt', ri)], [('uT', j)], out=uT[:, j, o:o + n],
                                   in0=tmp[ri][:, 0:n], in1=tmp[ri][:, 0:n], op=OP.mult)
                for nh in range(2):
                    for jb in range(8):
                        s = nd % 2
                        nd += 1
                        self.dma('pool', wd[s][:], d['w_down_r'][l, nh, jb], [], [('wd', s)], 'wd%d' % s)
                        for jj in range(4):
                            j = jb * 4 + jj
                            for ti, t in enumerate(g):
                                self.mm(self.pb[ti][:], uT[:, j, ti * 128:(ti + 1) * 128], wd[s][:, jj, :], j == 0, j == 31,
                                        [('wd', s), ('uT', j)], [('pb', ti)])
                    for ti, t in enumerate(g):
                        self.xupd(t, nh, ti)

    def final(self):
        nc, P, d = self.nc, self.P, self.d
        with Scope(self) as S:
            gf = S.sb("gf", [128, DM], F32)
            ob = [S.sb("ob%d" % i, [128, DM], F32) for i in range(2)]
            self.dma('sp', gf[:], d['gfin'][:, :], [], ['gf'], 'cst')
            self.rms_all(range(2, NT))
            for t in range(2, NT):
                i = t % 2
                self.act(ob[i][:], self.xs[:, t, :], AF.Identity, [('xs', t), 'rstd'], [('ob', i)], scale=self.rstd[:, t:t + 1])
                self.v('dve', 'tensor_tensor', [('ob', i), 'gf'], [('ob', i)], out=ob[i][:], in0=ob[i][:], in1=gf[:], op=OP.mult)
                self.dma('sp', d['out'][(t - 2) * 128:(t - 1) * 128, :], ob[i][:], [('ob', i)], [('out', t)], 'out')


def dram_decls(nc):
    d = {}

    def inp(name, shape, dt=F32):
        d[name] = nc.dram_tensor(name, list(shape), dt, kind="ExternalInput").ap()

    inp('xin', [TOK, DM])
    inp('c2', [128, 16])
    inp('w_mod', [2, 1024, 6144])
    inp('bmodfm', [128, 2, 48])
    inp('gnfm', [128, 2, 2, 8])
    inp('w_in', [2, 1024, 2224])
    inp('w_out', [2, 1024, 1024])
    inp('w_up_r', [2, 8, 128, 8, 512])
    inp('w_down_r', [2, 2, 8, 128, 4, 512])
    inp('gfin', [128, DM])
    inp('identb', [128, 128], BF16)
    inp('identf', [128, 128])
    inp('w_uq', [2, 256, 384])
    inp('w_ukv', [2, 128, 640])
    inp('w_uq_sw', [2, 256, 4, 32])
    inp('w_kr_sw', [2, 1024, 32])
    inp('ropeC', [32, TOK], BF16)
    inp('ropeS', [32, TOK], BF16)
    inp('gqfm', [128, 2, 2])
    inp('gkvfm', [128, 2])
    inp('w_g', [2, 1024, 16])
    inp('bg', [2, 8, 2])
    inp('convdg', [2, 96, 8, 5, 96], BF16)
    inp('gmB', [128, 2, 384])
    inp('E2', [8, 2, 8])
    inp('negidf', [128, 128])
    inp('mlmask', [128, 2, 256], BF16)
    inp('bdcs', [128, 256], BF16)
    inp('twCS', [2, 16, 128, 2, 1024], BF16)
    inp('c256', [128, 2, 2, 256], BF16)
    d['out'] = nc.dram_tensor('out', [2048, DM], F32, kind="ExternalOutput").ap()
    if DBGXS:
        d['dbgxs'] = nc.dram_tensor('dbgxs', [TOK, DM], F32, kind="ExternalOutput").ap()
    return d


def build_program():
    nc = bass.Bass("TRN2", target_bir_lowering=False)
    d = dram_decls(nc)
    P = Prog(nc)
    kb = K(nc, P, d)
    kb.build()
    P.plan()
    es = contextlib.ExitStack()
    with es:
        P.start_real(es)
        kb.build()
    return nc, P


def host_inputs(inputs, b):
    f = lambda a: np.ascontiguousarray(np.asarray(a, dtype=np.float32))
    x, c, ctx, c_ctx = inputs['x'], inputs['c'], inputs['ctx'], inputs['c_ctx']
    m = {}
    m['xin'] = f(np.concatenate([ctx[b], x[b]], axis=0))
    c2 = np.stack([np.asarray(c[b]).reshape(8, 128).T, np.asarray(c_ctx).reshape(8, 128).T], axis=-1)
    m['c2'] = f(c2.reshape(128, 16))
    m['w_mod'] = f(inputs['w_mod'])
    m['bmodfm'] = f(np.asarray(inputs['b_mod']).reshape(2, 48, 128).transpose(2, 0, 1))
    gn = np.stack([np.asarray(inputs['g_norm1']), np.asarray(inputs['g_norm2'])], axis=1)
    m['gnfm'] = f(gn.reshape(2, 2, 8, 128).transpose(3, 0, 1, 2))
    m['w_in'] = f(inputs['w_in'])
    m['w_out'] = f(inputs['w_out'])
    m['w_up_r'] = f(np.asarray(inputs['w_up']).reshape(2, 8, 128, 8, 512).transpose(0, 3, 2, 1, 4))
    m['w_down_r'] = f(np.asarray(inputs['w_down']).reshape(2, 8, 4, 128, 2, 512).transpose(0, 4, 1, 3, 2, 5))
    m['gfin'] = f(np.tile(np.asarray(inputs['g_final'])[None, :], (128, 1)))
    m['identb'] = np.eye(128, dtype=np.float32).astype(ml_dtypes.bfloat16)
    m['identf'] = np.eye(128, dtype=np.float32)
    m.update(host_consts())
    perm = np.array([a * 16 + (1 - b_) * 8 + j for a in range(2) for b_ in range(2) for j in range(8)])
    wuq = np.asarray(inputs['w_uq'])
    m['w_uq'] = f(wuq)
    m['w_ukv'] = f(inputs['w_ukv'])
    m['w_uq_sw'] = f(wuq.reshape(2, 256, 4, 96)[:, :, :, 64:96][..., perm])
    m['w_kr_sw'] = f(np.asarray(inputs['w_in'])[:, :, 2192:2224][..., perm])
    m['gqfm'] = f(np.asarray(inputs['g_q_norm']).reshape(2, 2, 128).transpose(2, 0, 1))
    m['gkvfm'] = f(np.asarray(inputs['g_kv_norm']).T)
    gp = [0, 1, 2, 3, 8, 9, 10, 11, 4, 5, 6, 7, 12, 13, 14, 15]
    m['w_g'] = f(np.asarray(inputs['w_in'])[:, :, 1792:1808][..., gp])
    m['bg'] = f(np.asarray(inputs['b_gates'])[:, gp].reshape(2, 2, 8).transpose(0, 2, 1))
    cv = np.asarray(inputs['conv_qk'], dtype=np.float32).reshape(2, 5, 8, 96)
    dgc = np.zeros((2, 96, 8, 5, 96), np.float32)
    pi_ = np.arange(96)
    dgc[:, pi_, :, :, pi_] = cv.transpose(3, 0, 2, 1)
    m['convdg'] = dgc.astype(ml_dtypes.bfloat16)
    m['gmB'] = f(np.tile(np.asarray(inputs['g_mlstm'])[None, :, :], (128, 1, 1)))
    return m


def host_consts():
    bf = lambda a: np.ascontiguousarray(a.astype(np.float32)).astype(ml_dtypes.bfloat16)
    m = {}
    i64 = np.arange(64)
    a64 = 2 * np.pi * np.outer(i64, i64) / 64
    C64, S64 = np.cos(a64) / 8.0, np.sin(a64) / 8.0
    z = np.zeros((64, 64))
    m['bdcs'] = bf(np.block([[C64, z, S64, z], [z, C64, z, S64]]))
    n = np.arange(2048, dtype=np.float64)
    aN = 2 * np.pi * (np.outer(n, n) % 2048) / 2048
    tcs = np.stack([np.cos(aN) / np.sqrt(2048.0), -np.sin(aN) / np.sqrt(2048.0)], axis=1)
    m['twCS'] = bf(tcs.reshape(16, 128, 2, 2, 1024).transpose(3, 0, 1, 2, 4))
    E2 = np.zeros((8, 2, 8), np.float32)
    for k_ in range(4):
        E2[k_, 0, k_] = 1.0
        E2[4 + k_, 1, 4 + k_] = 1.0
    m['E2'] = E2
    m['negidf'] = -np.eye(128, dtype=np.float32)
    sidx, tidx = np.arange(128)[:, None], np.arange(128)[None, :]
    mf = np.where(sidx <= tidx, 0.0, -30000.0)
    mb_ = np.where(sidx >= tidx, 0.0, -30000.0)
    m['mlmask'] = bf(np.stack([np.concatenate([mf, mf], axis=1), np.concatenate([mb_, mb_], axis=1)], axis=1))
    tt = np.arange(2048)
    freqs = 10000.0 ** (-np.arange(8, dtype=np.float64) / 8)
    ang = np.stack([np.outer(tt // 64, freqs), np.outer(tt % 64, freqs)], axis=1)
    cosT = np.ones((TOK, 2, 2, 8)); sinT = np.zeros((TOK, 2, 2, 8))
    cosT[256:] = np.cos(ang)[:, :, None, :]
    sinT[256:, :, 0, :] = -np.sin(ang)
    sinT[256:, :, 1, :] = np.sin(ang)
    m['ropeC'] = bf(cosT.reshape(TOK, 32).T)
    m['ropeS'] = bf(sinT.reshape(TOK, 32).T)
    n = np.arange(256, dtype=np.float64)
    a2 = 2 * np.pi * (np.outer(n, n) % 256) / 256
    c = np.stack([np.cos(a2) / 16.0, -np.sin(a2) / 16.0], axis=1)
    m['c256'] = bf(c.reshape(2, 128, 2, 256).transpose(1, 0, 2, 3))
    return m


_CACHE = {}


def kernel(**inputs):
    if 'nc' not in _CACHE:
        _CACHE['nc'] = build_program()
    nc, P = _CACHE['nc']
    shared = None
    in_maps = []
    for b in range(8):
        m = host_inputs(inputs, b) if shared is None else None
        if shared is None:
            shared = m
        else:
            m = dict(shared)
            f = lambda a: np.ascontiguousarray(np.asarray(a, dtype=np.float32))
            m['xin'] = f(np.concatenate([inputs['ctx'][b], inputs['x'][b]], axis=0))
            c2 = np.stack([np.asarray(inputs['c'][b]).reshape(8, 128).T, np.asarray(inputs['c_ctx']).reshape(8, 128).T], axis=-1)
            m['c2'] = f(c2.reshape(128, 16))
        in_maps.append(m)
    res = run_bass_kernel_spmd(nc, in_maps, core_ids=list(range(8)))
    _CACHE['res'] = res
    return np.stack([np.asarray(r['out'], dtype=np.float32) for r in res.results], axis=0)
```

```python
import contextlib, bisect, math
import numpy as np
import ml_dtypes
import concourse.bass as bass
import concourse.mybir as mybir
from concourse.bass_utils import run_bass_kernel_spmd

F32 = mybir.dt.float32
BF16 = mybir.dt.bfloat16
AF = mybir.ActivationFunctionType
OP = mybir.AluOpType
AX = mybir.AxisListType

NT = 18
TOK = 2304
DM = 1024
EPS = 1e-6
DEPTH = 2
STAGE = 99
DEBUG = []
BRANCHES = 'FAM'
DBGXS = False


class Dummy:
    def __getitem__(self, k):
        return self

    def __getattr__(self, n):
        return lambda *a, **k: self


class Prog:
    ENGS = ('pe', 'dve', 'act', 'pool', 'sp')

    def __init__(self, nc):
        self.nc = nc
        self.dry = True
        self.rec = []
        self.i = 0

    def op(self, eng, fn, r=(), w=(), dma=None):
        if self.dry:
            self.rec.append(('op', eng, tuple(r), tuple(w), dma))
        else:
            i = self.i
            E = self.H[eng]
            for (s, v) in self.waits[i]:
                E.wait_ge(self.sems[s], v)
            if fn is not None:
                ins = fn()
                inc = self.incs[i]
                if inc is not None:
                    ins.then_inc(self.sems[inc[0]], inc[1])
        self.i += 1

    def barrier(self):
        if self.dry:
            self.rec.append(('bar',))
        self.i += 1

    def plan(self):
        rec = self.rec
        n = len(rec)
        lastw, readers, last_on_eng, dma_ops = {}, {}, {}, {}
        deps = [None] * n
        pend = {e: None for e in self.ENGS}
        for i, r in enumerate(rec):
            if r[0] == 'bar':
                bd = set(last_on_eng.values())
                for s, l in dma_ops.items():
                    if l:
                        bd.add(l[-1])
                for e in self.ENGS:
                    pend[e] = set(bd) | (pend[e] or set())
                lastw.clear()
                readers.clear()
                continue
            _, eng, R, W, dma = r
            d = set()
            for k in R:
                if k in lastw:
                    d.add(lastw[k])
            for k in W:
                if k in lastw:
                    d.add(lastw[k])
                d.update(readers.get(k, ()))
            if pend[eng]:
                d |= pend[eng]
                pend[eng] = None
            d.discard(i)
            if eng == 'pe':
                d = {j for j in d if not (rec[j][1] == 'pe' and rec[j][4] is None)}
            deps[i] = d
            for k in W:
                lastw[k] = i
                readers[k] = []
            for k in R:
                lst = readers.setdefault(k, [])
                if dma is None:
                    lst[:] = [j for j in lst if not (rec[j][1] == eng and rec[j][4] is None)]
                lst.append(i)
            last_on_eng[eng] = i
            if dma:
                dma_ops.setdefault(dma, []).append(i)
        sig = [False] * n
        for i in range(n):
            if deps[i]:
                for j in deps[i]:
                    if rec[j][4] is None:
                        sig[j] = True
        cnt = {e: 0 for e in self.ENGS}
        ev = [None] * n
        for i, r in enumerate(rec):
            if r[0] != 'op':
                continue
            if r[4] is None and sig[i]:
                cnt[r[1]] += 1
                ev[i] = ('E_' + r[1], cnt[r[1]])
        self.waits = [()] * n
        self.incs = [None] * n
        seen = {e: {} for e in self.ENGS}
        for i, r in enumerate(rec):
            if r[0] != 'op':
                continue
            eng = r[1]
            wl = {}
            for j in deps[i]:
                if rec[j][4] is None:
                    s, v = ev[j]
                else:
                    s = 'D_' + rec[j][4]
                    v = 16 * bisect.bisect_left(dma_ops[rec[j][4]], i)
                if v > wl.get(s, 0):
                    wl[s] = v
            ws = []
            for s, v in wl.items():
                if v > seen[eng].get(s, 0):
                    seen[eng][s] = v
                    ws.append((s, v))
            self.waits[i] = tuple(ws)
            if r[4] is not None:
                self.incs[i] = ('D_' + r[4], 16)
            elif sig[i]:
                self.incs[i] = (ev[i][0], 1)
        self.semnames = ['E_' + e for e in self.ENGS] + ['D_' + s for s in dma_ops]
        self.stats = dict(n=n, cnt=cnt, nw=sum(len(w) for w in self.waits))

    def start_real(self, es):
        nc = self.nc
        self.dry = False
        self.i = 0
        self.H = {'pe': nc.tensor, 'dve': nc.vector, 'act': nc.scalar, 'pool': nc.gpsimd, 'sp': nc.sync}
        self.sems = {s: es.enter_context(nc.semaphore(s)) for s in self.semnames}


class Scope:
    _n = 0

    def __init__(self, K):
        self.K = K
        self.es = contextlib.ExitStack()

    def __enter__(self):
        return self

    def sb(self, name, shape, dt):
        if self.K.P.dry:
            return Dummy()
        Scope._n += 1
        return self.es.enter_context(self.K.nc.sbuf_tensor("%s_%d" % (name, Scope._n), list(shape), dt))

    def __exit__(self, *a):
        self.K.P.barrier()
        self.K.semmap = {}
        self.es.close()
        return False


def which(t):
    return 1 if t < 2 else 0


class K:
    def __init__(self, nc, P, dram):
        self.nc, self.P, self.d = nc, P, dram
        self.semmap = {}

    def mm(self, out, lhsT, rhs, start, stop, r, w):
        nc = self.nc
        self.P.op('pe', lambda: nc.tensor.matmul(out, lhsT=lhsT, rhs=rhs, start=start, stop=stop), r, w)

    def tr(self, out, in_, ident, r, w):
        nc = self.nc
        self.P.op('pe', lambda: nc.tensor.transpose(out, in_, ident), r, w)

    def act(self, out, in_, func, r, w, bias=None, scale=None, accum=None):
        nc = self.nc
        kw = {}
        if bias is not None:
            kw['bias'] = bias
        if scale is not None:
            kw['scale'] = scale
        if accum is not None:
            kw['accum_out'] = accum
        self.P.op('act', lambda: nc.scalar.activation(out=out, in_=in_, func=func, **kw), r, w)

    def v(self, eng, name, r, w, *a, **kw):
        nc = self.nc
        E = nc.vector if eng == 'dve' else nc.gpsimd
        self.P.op(eng, lambda: getattr(E, name)(*a, **kw), r, w)

    def dma(self, q, out, in_, r, w, sem):
        nc = self.nc
        E = nc.sync if q == 'sp' else nc.gpsimd
        k0 = w[0] if w else sem
        if isinstance(k0, tuple) and k0[0] in ('out', 'dbgxs'):
            name = sem
        elif isinstance(k0, tuple) and k0[0] == 'xs':
            name = 'xs'
        else:
            name = "b%d" % self.semmap.setdefault((k0, q), len(self.semmap))
        self.P.op(q, lambda: E.dma_start(out=out, in_=in_), r, w, dma=name + '_' + q)

    def build(self):
        nc, P, d = self.nc, self.P, self.d
        self.semmap = {}
        with Scope(self) as S:
            self.S0 = S
            self.xs = S.sb("xs", [128, NT, DM], F32)
            self.hT = S.sb("hT", [128, 8, TOK], BF16)
            self.identb = S.sb("identb", [128, 128], BF16)
            self.identf = S.sb("identf", [128, 128], F32)
            self.onesf = S.sb("onesf", [128, 128], F32)
            self.c2 = S.sb("c2", [128, 16], F32)
            self.scb = S.sb("scb", [128, 16], F32)
            self.modv = S.sb("modv", [128, 48, 2], F32)
            self.bmod = S.sb("bmod", [128, 2, 48], F32)
            self.gn = S.sb("gn", [128, 2, 2, 8], F32)
            self.G = S.sb("G", [128, 2, 8, 2], F32)
            self.gaB = S.sb("gaB", [128, 2, DM], F32)
            self.ssq = S.sb("ssq", [128, NT], F32)
            self.rstd = S.sb("rstd", [128, NT], F32)
            self.epsT = S.sb("epsT", [128, 1], F32)
            self.xn = [S.sb("xn%d" % i, [128, DM], BF16) for i in range(2)]
            self.junk = S.sb("junk", [128, DM], BF16)
            self.dg = [S.sb("dg%d" % i, [128, 128], F32) for i in range(2)]
            self.xt = [S.sb("xt%d" % i, [128, 512], F32) for i in range(2)]
            if not P.dry:
                es = S.es
                self.pb = [es.enter_context(nc.psum_tensor("pb%d" % i, [128, 512], F32)) for i in range(6)]
                self.ptr = [es.enter_context(nc.psum_tensor("ptr%d" % i, [128, 8, 128], BF16)) for i in range(2)]
            else:
                self.pb = [Dummy()] * 6
                self.ptr = [Dummy()] * 2
            self.cnt = 0
            self.dma('sp', self.identb[:], d['identb'][:, :], [], ['identb'], 'cst')
            self.dma('sp', self.identf[:], d['identf'][:, :], [], ['identf'], 'cst')
            self.dma('sp', self.c2[:], d['c2'][:, :], [], ['c2'], 'cst')
            self.dma('sp', self.bmod[:], d['bmodfm'][:, :, :], [], ['bmod'], 'cst')
            self.dma('sp', self.gn[:], d['gnfm'][:, :, :, :], [], ['gn'], 'cst')
            self.v('dve', 'memset', [], ['onesf'], self.onesf[:], 1.0)
            self.v('dve', 'memset', [], ['epsT'], self.epsT[:], EPS)
            self.v('dve', 'memset', [], ['ssq'], self.ssq[:], 0.0)
            self.act(self.scb[:], self.c2[:], AF.Silu, ['c2'], ['scb'])
            for t in range(NT):
                self.dma('sp', self.xs[:, t, :], d['xin'][t * 128:(t + 1) * 128, :], [], [('xs', t)], 'xin')
            for l in range(DEPTH):
                last = (l == DEPTH - 1)
                self.mod_phase(l)
                if STAGE <= 1:
                    break
                self.make_gaB(2)
                self.phaseA(l, 0, range(NT))
                if STAGE >= 3:
                    self.mixers(l, last)
                if DBGXS and l == 0:
                    for t in range(NT):
                        self.dma('sp', d['dbgxs'][t * 128:(t + 1) * 128, :], self.xs[:, t, :], [('xs', t)], [('dbgxs', t)], 'out')
                    break
                self.make_gaB(5)
                tiles = list(range(2, NT)) if last else list(range(NT))
                self.phaseA(l, 1, tiles)
                self.mlp(l, tiles)
            self.final()
        P.barrier()
        P.op('sp', None)

    def mod_phase(self, l):
        nc, P, d = self.nc, self.P, self.d
        with Scope(self) as S:
            wm = [S.sb("wm%d" % i, [128, 8, 1024], F32) for i in range(2)]
            rowb = [S.sb("rowb%d" % i, [2, 512], F32) for i in range(2)]
            modps = self.pb[0]
            for blk in range(6):
                s = blk % 2
                self.dma('sp', wm[s][:], d['w_mod'][l][:, blk * 1024:(blk + 1) * 1024].rearrange("(k p) c -> p k c", p=128),
                         [], [('wm', s)], 'wm%d' % s)
                for hb2 in range(2):
                    rb = 1 + (blk * 2 + hb2) % 2
                    for k in range(8):
                        self.mm(self.pb[rb][0:2, :], self.scb[:, 2 * k:2 * k + 2], wm[s][:, k, hb2 * 512:(hb2 + 1) * 512],
                                k == 0, k == 7, [('wm', s), 'scb'], [('pb', rb)])
                    ri = (blk * 2 + hb2) % 2
                    self.act(rowb[ri][:], self.pb[rb][0:2, :], AF.Identity, [('pb', rb)], [('rowb', ri)])
                    for cc in range(4):
                        j = blk * 8 + hb2 * 4 + cc
                        self.mm(modps[:, 2 * j:2 * j + 2], rowb[ri][:, cc * 128:(cc + 1) * 128], self.identf[0:2, 0:2], True, True,
                                [('rowb', ri), 'identf'], [('pb', 0)])
            self.v('dve', 'tensor_tensor', [('pb', 0), 'bmod'], ['modv'], out=self.modv[:],
                   in0=modps[:, 0:96].rearrange("p (j w) -> p j w", w=2),
                   in1=self.bmod[:, l, :].unsqueeze(2).to_broadcast([128, 48, 2]), op=OP.add)
            for ni, vi in ((0, 1), (1, 4)):
                self.v('dve', 'scalar_tensor_tensor', ['modv', 'gn'], ['G'], out=self.G[:, ni, :, :],
                       in0=self.modv[:, vi * 8:(vi + 1) * 8, :], scalar=1.0,
                       in1=self.gn[:, l, ni, :].unsqueeze(2).to_broadcast([128, 8, 2]), op0=OP.add, op1=OP.mult)

    def make_gaB(self, vi):
        for w in range(2):
            for kk in range(8):
                self.cnt += 1
                dgi = self.cnt % 2
                self.v('dve', 'tensor_scalar', ['modv', 'identf'], [('dg', dgi)], out=self.dg[dgi][:], in0=self.identf[:],
                       scalar1=self.modv[:, vi * 8 + kk, w:w + 1], scalar2=None, op0=OP.mult)
                bank = kk // 4
                self.mm(self.pb[bank][:, (kk % 4) * 128:(kk % 4 + 1) * 128], self.onesf[:], self.dg[dgi][:], True, True,
                        ['onesf', ('dg', dgi)], [('pb', bank)])
            for bank in range(2):
                self.act(self.gaB[:, w, bank * 512:(bank + 1) * 512], self.pb[bank][:], AF.Identity, [('pb', bank)], ['gaB'])

    def rms(self, t):
        self.cnt += 1
        self.act(self.junk[:], self.xs[:, t, :], AF.Square, [('xs', t)], [('junk', self.cnt), ('ssq', t)],
                 accum=self.ssq[:, t:t + 1])
        self.act(self.rstd[:, t:t + 1], self.ssq[:, t:t + 1], AF.Sqrt, [('ssq', t), 'epsT'], [('rstd', t)],
                 bias=self.epsT[:], scale=1.0 / DM)
        self.v('dve', 'reciprocal', [('rstd', t)], [('rstd', t)], out=self.rstd[:, t:t + 1], in_=self.rstd[:, t:t + 1])
        self.v('dve', 'memset', [('ssq', t)], [('ssq', t)], self.ssq[:, t:t + 1], 0.0)

    def rms_all(self, tiles):
        tiles = list(tiles)
        t0, t1 = tiles[0], tiles[-1] + 1
        for t in tiles:
            self.cnt += 1
            self.act(self.junk[:], self.xs[:, t, :], AF.Square, [('xs', t)], [('junk', self.cnt), 'ssq'],
                     accum=self.ssq[:, t:t + 1])
        self.act(self.rstd[:, t0:t1], self.ssq[:, t0:t1], AF.Sqrt, ['ssq', 'epsT'], ['rstd'], bias=self.epsT[:], scale=1.0 / DM)
        self.v('dve', 'reciprocal', ['rstd'], ['rstd'], out=self.rstd[:, t0:t1], in_=self.rstd[:, t0:t1])
        self.v('dve', 'memset', ['ssq'], ['ssq'], self.ssq[:, t0:t1], 0.0)

    def phaseA(self, l, ni, tiles):
        vi = 0 if ni == 0 else 3
        tiles = list(tiles)
        self.rms_all(tiles)
        for t in tiles:
            w = which(t)
            self.cnt += 1
            xi = self.cnt % 2
            self.act(self.xn[xi][:], self.xs[:, t, :], AF.Identity, [('xs', t), 'rstd'], [('xn', xi)],
                     scale=self.rstd[:, t:t + 1])
            for k in range(8):
                self.tr(self.ptr[xi][:, k, :], self.xn[xi][:, k * 128:(k + 1) * 128], self.identb[:],
                        [('xn', xi), 'identb'], [('ptr', xi)])
            for k in range(8):
                self.v('dve', 'tensor_scalar', [('ptr', xi), 'G', 'modv'], [('hT', t)], out=self.hT[:, k, t * 128:(t + 1) * 128],
                       in0=self.ptr[xi][:, k, :], scalar1=self.G[:, ni, k, w:w + 1], scalar2=self.modv[:, vi * 8 + k, w:w + 1],
                       op0=OP.mult, op1=OP.add)

    def xupd(self, t, nh, b):
        w = which(t)
        self.cnt += 1
        mi = self.cnt % 2
        self.v('dve', 'tensor_tensor', [('pb', b), 'gaB'], [('xt', mi)], out=self.xt[mi][:], in0=self.pb[b][:],
               in1=self.gaB[:, w, nh * 512:(nh + 1) * 512], op=OP.mult)
        self.v('pool', 'tensor_tensor', [('xt', mi), ('xs', t)], [('xs', t)],
               out=self.xs[:, t, nh * 512:(nh + 1) * 512], in0=self.xs[:, t, nh * 512:(nh + 1) * 512],
               in1=self.xt[mi][:], op=OP.add)

    def xadd(self, t, nh, b):
        self.v('dve', 'tensor_tensor', [('pb', b), ('xs', t)], [('xs', t)], out=self.xs[:, t, nh * 512:(nh + 1) * 512],
               in0=self.xs[:, t, nh * 512:(nh + 1) * 512], in1=self.pb[b][:], op=OP.add)

    def blocks(self, lo=0, hi=TOK, step=512):
        return [(o, min(step, hi - o)) for o in range(lo, hi, step)]

    def mixers(self, l, last):
        if 'F' in BRANCHES:
            self.fourier(l, last)
        if 'M' in BRANCHES:
            self.mlstm(l, last)
        if 'A' in BRANCHES:
            self.mla(l, last)

    def mlstm(self, l, last):
        nc, P, d = self.nc, self.P, self.d
        win = d['w_in'][l]
        with Scope(self) as SO:
            acol = SO.sb("acol", [128, NT, 8], F32)
            Mcol = SO.sb("Mcol", [128, NT, 8], F32)
            T4 = SO.sb("T4", [128, 4, NT, 8], F32)
            gm = SO.sb("gm", [128, 384], F32)
            E2 = SO.sb("E2", [8, 2, 8], F32)
            negid = SO.sb("negid", [128, 128], F32)
            mlmask = SO.sb("mlmask", [128, 2, 256], BF16)
            self.dma('sp', gm[:], d['gmB'][:, l, :], [], ['gm'], 'mw')
            self.dma('sp', E2[:], d['E2'][:, :, :], [], ['E2'], 'mw')
            self.dma('sp', negid[:], d['negidf'][:, :], [], ['negid'], 'mw')
            self.dma('sp', mlmask[:], d['mlmask'][:, :, :], [], ['mlmask'], 'mw')
            with Scope(self) as S:
                wg = S.sb("wg", [128, 8, 16], BF16)
                bg = S.sb("bg", [8, 2], F32)
                nbf = S.sb("nbf", [8, 1], F32)
                IG = S.sb("IG", [8, TOK], F32)
                LF = S.sb("LF", [8, TOK], F32)
                FF = S.sb("FF", [8, TOK], F32)
                FB = S.sb("FB", [8, TOK], F32)
                AFr = S.sb("AFr", [8, TOK], F32)
                ABr = S.sb("ABr", [8, TOK], F32)
                R1 = S.sb("R1", [8, NT, 8], F32)
                R2 = S.sb("R2", [8, NT, 8], F32)
                mcol = S.sb("mcol", [128, NT, 8], F32)
                MendB = S.sb("MendB", [128, NT, 8], F32)
                MP = S.sb("MP", [128, NT, 8], F32)
                ex = S.sb("ex", [128, 4, NT, 8], F32)
                self.dma('pool', wg[:], d['w_g'][l].rearrange("(k p) c -> p k c", p=128), [], ['wg'], 'gw')
                self.dma('sp', bg[:], d['bg'][l], [], ['bg'], 'gw')
                self.v('dve', 'tensor_scalar', ['bg'], ['nbf'], out=nbf[:], in0=bg[:, 1:2], scalar1=-1.0, scalar2=None, op0=OP.mult)
                nb = 0
                for (o, n) in self.blocks():
                    hk = [('hT', t) for t in range(o // 128, (o + n) // 128)]
                    bi, bf_ = nb % 6, (nb + 1) % 6
                    nb += 2
                    for k in range(8):
                        self.mm(self.pb[bi][0:8, 0:n], wg[:, k, 0:8], self.hT[:, k, o:o + n], k == 0, k == 7, hk + ['wg'], [('pb', bi)])
                    for k in range(8):
                        self.mm(self.pb[bf_][0:8, 0:n], wg[:, k, 8:16], self.hT[:, k, o:o + n], k == 0, k == 7, hk + ['wg'], [('pb', bf_)])
                    self.act(IG[:, o:o + n], self.pb[bi][0:8, 0:n], AF.Identity, [('pb', bi), 'bg'], ['IG'], bias=bg[:, 0:1])
                    self.act(LF[:, o:o + n], self.pb[bf_][0:8, 0:n], AF.Exp, [('pb', bf_), 'nbf'], ['LF'], bias=nbf[:], scale=-1.0)
                self.act(LF[:], LF[:], AF.Ln, ['LF'], ['LF'], bias=1.0)
                self.v('dve', 'tensor_scalar', ['LF'], ['LF'], out=LF[:], in0=LF[:], scalar1=-1.0, scalar2=None, op0=OP.mult)
                one_b = lambda n: self.onesf[0:8, 0:1].to_broadcast([8, n])
                self.v('dve', 'tensor_tensor_scan', ['LF', 'onesf'], ['FF'], out=FF[:], data0=one_b(TOK), data1=LF[:], initial=0.0,
                       op0=OP.mult, op1=OP.add)
                self.v('dve', 'tensor_tensor_scan', ['LF', 'onesf'], ['FB'], out=FB[:, 255::-1], data0=one_b(256), data1=LF[:, 255::-1],
                       initial=0.0, op0=OP.mult, op1=OP.add)
                self.v('dve', 'tensor_tensor_scan', ['LF', 'onesf', 'FB'], ['FB'], out=FB[:, 2303:255:-1], data0=one_b(2048),
                       data1=LF[:, 2303:255:-1], initial=FB[:, 0:1], op0=OP.mult, op1=OP.add)
                self.v('dve', 'tensor_tensor', ['IG', 'FF'], ['AFr'], out=AFr[:], in0=IG[:], in1=FF[:], op=OP.subtract)
                self.v('dve', 'tensor_tensor', ['IG', 'FB'], ['ABr'], out=ABr[:], in0=IG[:], in1=FB[:], op=OP.subtract)
                MF, MB = IG, LF
                self.v('dve', 'tensor_tensor_scan', ['AFr'], ['IG'], out=MF[:], data0=AFr[:], data1=AFr[:], initial=0.0, op0=OP.max, op1=OP.max)
                self.v('dve', 'tensor_tensor_scan', ['ABr'], ['LF'], out=MB[:, 255::-1], data0=ABr[:, 255::-1], data1=ABr[:, 255::-1],
                       initial=0.0, op0=OP.max, op1=OP.max)
                self.v('dve', 'tensor_tensor_scan', ['ABr', 'LF'], ['LF'], out=MB[:, 2303:255:-1], data0=ABr[:, 2303:255:-1],
                       data1=ABr[:, 2303:255:-1], initial=MB[:, 0:1], op0=OP.max, op1=OP.max)
                self.v('dve', 'tensor_tensor', ['FF', 'IG'], ['FF'], out=FF[:], in0=FF[:], in1=MF[:], op=OP.add)
                self.v('dve', 'tensor_tensor', ['FB', 'LF'], ['FB'], out=FB[:], in0=FB[:], in1=MB[:], op=OP.add)
                for qi, (XF, XB, kf, kb) in enumerate(((AFr, ABr, 'AFr', 'ABr'), (MF, MB, 'IG', 'LF'), (FF, FB, 'FF', 'FB'))):
                    for c in range(NT):
                        self.mm(self.pb[qi][:, c * 8:(c + 1) * 8], XF[:, c * 128:(c + 1) * 128], E2[:, 0, :], True, False, [kf, 'E2'], [('pb', qi)])
                        self.mm(self.pb[qi][:, c * 8:(c + 1) * 8], XB[:, c * 128:(c + 1) * 128], E2[:, 1, :], False, True, [kb, 'E2'], [('pb', qi)])
                for qi, (dst, kd) in enumerate(((acol, 'acol'), (Mcol, 'Mcol'), (mcol, 'mcol'))):
                    self.act(dst[:].rearrange("p c e -> p (c e)"), self.pb[qi][:, 0:NT * 8], AF.Identity, [('pb', qi)], [kd])
                self.v('dve', 'tensor_tensor', ['E2', 'IG'], ['R1'], out=R1[:], in0=E2[:, 0, :].unsqueeze(1).to_broadcast([8, NT, 8]),
                       in1=MF[:, 127::128].unsqueeze(2).to_broadcast([8, NT, 8]), op=OP.mult)
                self.v('dve', 'tensor_tensor', ['E2', 'LF'], ['R2'], out=R2[:], in0=E2[:, 1, :].unsqueeze(1).to_broadcast([8, NT, 8]),
                       in1=MB[:, 0::128].unsqueeze(2).to_broadcast([8, NT, 8]), op=OP.mult)
                self.v('dve', 'tensor_tensor', ['R1', 'R2'], ['R1'], out=R1[:], in0=R1[:], in1=R2[:], op=OP.add)
                self.mm(self.pb[3][:, 0:NT * 8], self.onesf[0:8, :], R1[:].rearrange("p c e -> p (c e)"), True, True, ['R1', 'onesf'], [('pb', 3)])
                self.act(MendB[:].rearrange("p c e -> p (c e)"), self.pb[3][:, 0:NT * 8], AF.Identity, [('pb', 3)], ['MendB'])
                self.v('dve', 'memset', [], ['MP'], MP[:], 0.0)
                self.v('dve', 'tensor_copy', ['MendB', 'MP'], ['MP'], out=MP[:, 1:NT, 0:4], in_=MendB[:, 0:NT - 1, 0:4])
                self.v('dve', 'tensor_copy', ['MendB', 'MP'], ['MP'], out=MP[:, 0:1, 4:8], in_=MendB[:, 1:2, 4:8])
                self.v('dve', 'tensor_copy', ['MendB', 'MP'], ['MP'], out=MP[:, 2:NT - 1, 4:8], in_=MendB[:, 3:NT, 4:8])
                self.v('dve', 'tensor_copy', ['MendB', 'MP'], ['MP'], out=MP[:, NT - 1:NT, 4:8], in_=MendB[:, 0:1, 4:8])
                self.v('dve', 'tensor_tensor', ['MP', 'Mcol'], ['ex'], out=ex[:, 0], in0=MP[:], in1=Mcol[:], op=OP.subtract)
                self.v('dve', 'tensor_tensor', ['acol', 'MendB'], ['ex'], out=ex[:, 1], in0=acol[:], in1=MendB[:], op=OP.subtract)
                self.v('dve', 'tensor_tensor', ['MP', 'MendB'], ['ex'], out=ex[:, 2], in0=MP[:], in1=MendB[:], op=OP.subtract)
                self.v('dve', 'tensor_scalar', ['mcol'], ['ex'], out=ex[:, 3], in0=mcol[:], scalar1=-1.0, scalar2=None, op0=OP.mult)
                self.act(T4[:].rearrange("p a c e -> p (a c e)"), ex[:].rearrange("p a c e -> p (a c e)"), AF.Exp, ['ex'], ['T4'])
            for hp in range(2):
                self.mlstm_pair(l, last, hp, acol, Mcol, T4, gm, negid, mlmask)

    def mlstm_pair(self, l, last, hp, acol, Mcol, T4, gm, negid, mlmask):
        nc, P, d = self.nc, self.P, self.d
        win = d['w_in'][l]
        QS = 96.0 ** -0.5
        colof = lambda T: T + 2 if T < 256 else T + 6
        cblocks = [(0, 256)] + [(256 + i * 512, 512) for i in range(4)]
        with Scope(self) as SP:
            post = SP.sb("post", [96, 4, TOK], BF16)
            vp = SP.sb("vp", [128, NT, 2, 97], BF16)
            sg = SP.sb("sg", [128, NT, 192], BF16)
            S = Scope(self)
            wvo = S.sb("wvo", [128, 8, 384], BF16)
            self.dma('pool', wvo[:, :, 0:192], win[:, 1024 + 192 * hp:1024 + 192 * (hp + 1)].rearrange("(k p) c -> p k c", p=128), [], ['wvo'], 'mw')
            self.dma('pool', wvo[:, :, 192:384], win[:, 1408 + 192 * hp:1408 + 192 * (hp + 1)].rearrange("(k p) c -> p k c", p=128), [], ['wvo'], 'mw')
            self.v('dve', 'memset', [], ['vp'], vp[:], 1.0)
            with S:
                wqk = S.sb("wqk", [128, 8, 4, 96], BF16)
                dgw = S.sb("dgw", [96, 4, 5, 96], BF16)
                pre = S.sb("pre", [96, 4, 2312], BF16)
                sgt = [S.sb("sgt%d" % i, [96, 512], F32) for i in range(1)]
                for jj in range(4):
                    c0 = (256 if jj < 2 else 640) + 96 * (2 * hp + jj % 2)
                    self.dma('pool', wqk[:, :, jj, :], win[:, c0:c0 + 96].rearrange("(k p) c -> p k c", p=128), [], ['wqk'], 'mw')
                    jg = (0 if jj < 2 else 4) + 2 * hp + jj % 2
                    self.dma('sp', dgw[:, jj, :, :], d['convdg'][l, :, jg, :, :], [], ['dgw'], 'mw')
                self.v('dve', 'memset', [], ['pre'], pre[:], 0.0)
                nb = 0
                for jj in range(4):
                    for (o, n) in cblocks:
                        b = nb % 6
                        nb += 1
                        hk = [('hT', t) for t in range(o // 128, (o + n) // 128)]
                        for k in range(8):
                            self.mm(self.pb[b][0:96, 0:n], wqk[:, k, jj, :], self.hT[:, k, o:o + n], k == 0, k == 7, hk + ['wqk'], [('pb', b)])
                        self.act(pre[:, jj, colof(o):colof(o) + n], self.pb[b][0:96, 0:n], AF.Identity, [('pb', b), 'pre'], [('pre', jj)])
                for t in range(NT):
                    b = nb % 6
                    nb += 1
                    for k in range(8):
                        self.mm(self.pb[b][:, 0:384], self.hT[:, k, t * 128:(t + 1) * 128], wvo[:, k, :], k == 0, k == 7, [('hT', t), 'wvo'], [('pb', b)])
                    self.act(vp[:, t, :, 0:96], self.pb[b][:, 0:192].rearrange("p (h c) -> p h c", c=96), AF.Identity, [('pb', b), 'vp'], [('vp', t)])
                    self.act(sg[:, t, :], self.pb[b][:, 192:384], AF.Sigmoid, [('pb', b)], [('sg', t)])
                for jj in range(4):
                    for (o, n) in cblocks:
                        b = nb % 6
                        nb += 1
                        c0 = colof(o)
                        for tap in range(5):
                            self.mm(self.pb[b][0:96, 0:n], dgw[:, jj, tap, :], pre[:, jj, c0 + tap - 2:c0 + tap - 2 + n], tap == 0, tap == 4,
                                    [('pre', jj), 'dgw'], [('pb', b)])
                        tk = [('post', jj, t) for t in range(o // 128, (o + n) // 128)]
                        if jj < 2:
                            si = 0
                            self.act(sgt[si][:, 0:n], self.pb[b][0:96, 0:n], AF.Sigmoid, [('pb', b)], [('sgt', si)])
                            self.v('dve', 'scalar_tensor_tensor', [('pb', b), ('sgt', si)], tk, out=post[:, jj, o:o + n], in0=self.pb[b][0:96, 0:n],
                                   scalar=QS, in1=sgt[si][:, 0:n], op0=OP.mult, op1=OP.mult)
                        else:
                            self.act(post[:, jj, o:o + n], self.pb[b][0:96, 0:n], AF.Silu, [('pb', b)], tk)
            ktok = SP.sb("ktok", [128, NT, 2, 96], BF16)
            for t in range(NT):
                pi = t % 2
                for hh in range(2):
                    self.tr(self.ptr[pi][:, hh, 0:96], post[:, 2 + hh, t * 128:(t + 1) * 128], self.identb[0:96, 0:96],
                            [('post', 2 + hh, t), 'identb'], [('ptr', pi)])
                self.act(ktok[:, t, :, :], self.ptr[pi][:, 0:2, 0:96], AF.Identity, [('ptr', pi)], [('ktok', t)])
            hF = SP.sb("hF", [128, NT, 2, 96], BF16)
            hB = SP.sb("hB", [128, NT, 2, 96], BF16)
            with Scope(self) as S:
                dgM = [S.sb("dgM%d" % i, [128, 2, 128], F32) for i in range(3)]
                WT = [S.sb("WT%d" % i, [128, 2, 128], F32) for i in range(3)]
                PT = [S.sb("PT%d" % i, [128, 2, 128], BF16) for i in range(3)]
                tmpn = [S.sb("tmpn%d" % i, [128, 2, 97], F32) for i in range(2)]
                dn = [S.sb("dn%d" % i, [128, 2], F32) for i in range(2)]
                vw = [S.sb("vw%d" % i, [128, 2, 97], BF16) for i in range(2)]
                Cst = S.sb("Cst", [96, 2, 97], F32)
                Cball = S.sb("Cball", [96, NT, 2, 97], BF16)
                for dr in range(2):
                    order = list(range(NT)) if dr == 0 else [1, 0] + list(range(NT - 1, 1, -1))
                    cf0 = dr * 4 + 2 * hp
                    self.v('dve', 'memset', ['Cst'], ['Cst'], Cst[:], 0.0)
                    self.v('dve', 'memset', [('Cball', 0)], [('Cball', 0)], Cball[:, 0], 0.0)

                    def kv_part(it):
                        c = order[it]
                        i2 = it % 2
                        self.v('dve', 'tensor_tensor', [('vp', c), 'T4'], [('vw', i2)], out=vw[i2][:], in0=vp[:, c],
                               in1=T4[:, 1, c, cf0:cf0 + 2].unsqueeze(2).to_broadcast([128, 2, 97]), op=OP.mult)
                        for hh in range(2):
                            self.mm(self.pb[4 + i2][0:96, hh * 97:(hh + 1) * 97], ktok[:, c, hh, :], vw[i2][:, hh, :], True, True,
                                    [('ktok', c), ('vw', i2)], [('pb', 4 + i2)])

                    kv_part(0)
                    for it in range(NT - 1):
                        c = order[it]
                        i2 = it % 2
                        if it + 1 < NT - 1:
                            kv_part(it + 1)
                        self.v('dve', 'tensor_tensor', ['Cst', 'T4'], ['Cst'], out=Cst[:], in0=Cst[:],
                               in1=T4[0:96, 2, c, cf0:cf0 + 2].unsqueeze(2).to_broadcast([96, 2, 97]), op=OP.mult)
                        self.v('dve', 'tensor_tensor', ['Cst', ('pb', 4 + i2)], ['Cst'], out=Cst[:], in0=Cst[:],
                               in1=self.pb[4 + i2][0:96, 0:194].rearrange("p (h e) -> p h e", e=97), op=OP.add)
                        self.act(Cball[:, it + 1], Cst[:], AF.Identity, ['Cst'], [('Cball', it + 1)])

                    def part_i(it):
                        c = order[it]
                        i2 = it % 3
                        if last and c < 2:
                            return
                        tsl = slice(c * 128, (c + 1) * 128)
                        bo = 3 + it % 3
                        for hh in range(2):
                            self.act(dgM[i2][:, hh, :], negid[:], AF.Identity, ['negid', 'Mcol'], [('dgM', i2)],
                                     scale=Mcol[:, c, cf0 + hh:cf0 + hh + 1])
                        self.mm(self.pb[1][:, 0:256], self.onesf[:], dgM[i2][:].rearrange("p h t -> p (h t)"), True, False,
                                [('dgM', i2), 'onesf'], [('pb', 1)])
                        self.mm(self.pb[1][:, 0:256], self.identb[:], mlmask[:, dr, :], False, True, ['mlmask', 'identb'], [('pb', 1)])
                        for hh in range(2):
                            self.act(WT[i2][:, hh, :], self.pb[1][:, hh * 128:(hh + 1) * 128], AF.Exp, [('pb', 1), 'acol'], [('WT', i2)],
                                     bias=acol[:, c, cf0 + hh:cf0 + hh + 1])
                        for hh in range(2):
                            self.mm(self.pb[2][:, hh * 128:(hh + 1) * 128], post[:, 2 + hh, tsl], post[:, hh, tsl], True, True,
                                    [('post', 2 + hh, c), ('post', hh, c)], [('pb', 2)])
                        self.v('dve', 'tensor_tensor', [('pb', 2), ('WT', i2)], [('PT', i2)], out=PT[i2][:].rearrange("p h t -> p (h t)"),
                               in0=self.pb[2][:, 0:256], in1=WT[i2][:].rearrange("p h t -> p (h t)"), op=OP.mult)
                        for hh in range(2):
                            self.mm(self.pb[bo][:, hh * 97:(hh + 1) * 97], PT[i2][:, hh, :], vp[:, c, hh, :], True, True,
                                    [('PT', i2), ('vp', c)], [('pb', bo)])
                        for hh in range(2):
                            self.mm(self.pb[bo][:, 256 + hh * 97:256 + (hh + 1) * 97], post[:, hh, tsl], Cball[:, it, hh, :], True, True,
                                    [('post', hh, c), ('Cball', it)], [('pb', bo)])

                    def part_d(it):
                        c = order[it]
                        i2 = it % 2
                        if last and c < 2:
                            return
                        bo = 3 + it % 3
                        for hh in range(2):
                            self.act(tmpn[i2][:, hh, :], self.pb[bo][:, 256 + hh * 97:256 + (hh + 1) * 97], AF.Identity, [('pb', bo), 'T4'],
                                     [('tmpn', i2)], scale=T4[:, 0, c, cf0 + hh:cf0 + hh + 1])
                        self.v('dve', 'tensor_tensor', [('tmpn', i2), ('pb', bo)], [('tmpn', i2)], out=tmpn[i2][:], in0=tmpn[i2][:],
                               in1=self.pb[bo][:, 0:194].rearrange("p (h e) -> p h e", e=97), op=OP.add)
                        self.v('dve', 'scalar_tensor_tensor', [('tmpn', i2)], [('dn', i2)], out=dn[i2][:], in0=tmpn[i2][:, :, 96], scalar=-1.0,
                               in1=tmpn[i2][:, :, 96], op0=OP.mult, op1=OP.max)
                        self.v('dve', 'tensor_tensor', [('dn', i2), 'T4'], [('dn', i2)], out=dn[i2][:], in0=dn[i2][:],
                               in1=T4[:, 3, c, cf0:cf0 + 2], op=OP.max)
                        self.v('dve', 'reciprocal', [('dn', i2)], [('dn', i2)], out=dn[i2][:], in_=dn[i2][:])
                        if dr == 0:
                            self.v('dve', 'tensor_tensor', [('tmpn', i2), ('dn', i2)], [('hF', c)], out=hF[:, c], in0=tmpn[i2][:, :, 0:96],
                                   in1=dn[i2][:].unsqueeze(2).to_broadcast([128, 2, 96]), op=OP.mult)
                            return
                        self.v('dve', 'tensor_tensor', [('tmpn', i2), ('dn', i2)], [('hB', c)], out=hB[:, c], in0=tmpn[i2][:, :, 0:96],
                               in1=dn[i2][:].unsqueeze(2).to_broadcast([128, 2, 96]), op=OP.mult)

                    part_i(0)
                    part_i(1)
                    for it in range(NT):
                        if it + 2 < NT:
                            part_i(it + 2)
                        part_d(it)

            with Scope(self) as S:
                hsC = S.sb("hsC", [128, 4, 2, 96], F32)
                hqC = S.sb("hqC", [128, 4, 2, 96], F32)
                ssC = S.sb("ssC", [128, 8], F32)
                ybC = S.sb("ybC", [128, 4, 192], BF16)
                yhT = [S.sb("yhT%d" % i, [96, 2, 128], BF16) for i in range(2)]
                woM = S.sb("woM", [96, 2, DM], BF16)
                self.dma('pool', woM[:], d['w_out'][l][256 + 192 * hp:256 + 192 * (hp + 1), :].rearrange("(h p) c -> p h c", p=96), [], ['woM'], 'mw')
                woMs = [woM]
                if not last:
                    woMs.append(S.sb("woMc", [96, 2, DM], BF16))
                    self.v('dve', 'tensor_tensor', ['woM', 'gaB'], [('woMs', 1)], out=woMs[1][:], in0=woM[:],
                           in1=self.gaB[0:96, 1, :].unsqueeze(1).to_broadcast([96, 2, DM]), op=OP.mult)
                self.v('dve', 'tensor_tensor', ['woM', 'gaB'], ['woM', ('woMs', 0)], out=woM[:], in0=woM[:],
                       in1=self.gaB[0:96, 0, :].unsqueeze(1).to_broadcast([96, 2, DM]), op=OP.mult)
                otiles = list(range(2, NT)) if last else list(range(NT))
                for g0 in range(0, len(otiles), 4):
                    g = otiles[g0:g0 + 4]
                    ng = len(g)
                    c0, c1 = g[0], g[-1] + 1
                    hk = [('hF', c) for c in g] + [('hB', c) for c in g]
                    self.v('dve', 'tensor_tensor', hk, ['hsC'], out=hsC[:, 0:ng], in0=hF[:, c0:c1], in1=hB[:, c0:c1], op=OP.add)
                    self.v('dve', 'tensor_tensor', ['hsC'], ['hqC'], out=hqC[:, 0:ng], in0=hsC[:, 0:ng], in1=hsC[:, 0:ng], op=OP.mult)
                    self.v('dve', 'tensor_reduce', ['hqC'], ['ssC'], out=ssC[:, 0:2 * ng],
                           in_=hqC[:, 0:ng].rearrange("p c h e -> p (c h) e"), axis=AX.X, op=OP.add)
                    self.act(ssC[:, 0:2 * ng], ssC[:, 0:2 * ng], AF.Ln, ['ssC', 'epsT'], ['ssC'], bias=self.epsT[:], scale=1.0 / 96)
                    self.act(ssC[:, 0:2 * ng], ssC[:, 0:2 * ng], AF.Exp, ['ssC'], ['ssC'], scale=-0.5)
                    self.v('dve', 'tensor_tensor', ['hsC', 'ssC'], ['hqC'], out=hqC[:, 0:ng].rearrange("p c h e -> p (c h) e"),
                           in0=hsC[:, 0:ng].rearrange("p c h e -> p (c h) e"),
                           in1=ssC[:, 0:2 * ng].unsqueeze(2).to_broadcast([128, 2 * ng, 96]), op=OP.mult)
                    self.v('dve', 'tensor_tensor', ['hqC', 'gm'], ['hqC'], out=hqC[:, 0:ng].rearrange("p c h e -> p c (h e)"),
                           in0=hqC[:, 0:ng].rearrange("p c h e -> p c (h e)"),
                           in1=gm[:, 192 * hp:192 * (hp + 1)].unsqueeze(1).to_broadcast([128, ng, 192]), op=OP.mult)
                    self.v('dve', 'tensor_tensor', ['hqC'] + [('sg', c) for c in g], ['ybC'], out=ybC[:, 0:ng],
                           in0=hqC[:, 0:ng].rearrange("p c h e -> p c (h e)"), in1=sg[:, c0:c1, :], op=OP.mult)
                    for ci, c in enumerate(g):
                        i2 = c % 2
                        for hh in range(2):
                            self.tr(self.ptr[i2][0:96, hh, :], ybC[:, ci, hh * 96:(hh + 1) * 96], self.identb[:], ['ybC', 'identb'], [('ptr', i2)])
                        self.act(yhT[i2][:], self.ptr[i2][0:96, 0:2, :], AF.Identity, [('ptr', i2)], [('yhT', i2)])
                        for nh in range(2):
                            b = 2 * i2 + nh
                            for hh in range(2):
                                self.mm(self.pb[b][:], yhT[i2][:, hh, :], woMs[which(c)][:, hh, nh * 512:(nh + 1) * 512], hh == 0, hh == 1,
                                        [('yhT', i2), ('woMs', which(c))], [('pb', b)])
                            self.xadd(c, nh, b)

    def mla(self, l, last):
        nc, P, d = self.nc, self.P, self.d
        SCALE = 96.0 ** -0.5
        with Scope(self) as S:
            wA = S.sb("wA", [128, 8, 416], BF16)
            wuq = S.sb("wuq", [128, 2, 384], BF16)
            wuqs = S.sb("wuqs", [128, 2, 4, 96], BF16)
            wkr = S.sb("wkr", [128, 8, 96], BF16)
            wkrs = S.sb("wkrs", [128, 8, 96], BF16)
            wukv = S.sb("wukv", [128, 640], BF16)
            woA = S.sb("woA", [96, 4, DM], BF16)
            ropeC = S.sb("ropeC", [96, TOK], BF16)
            ropeS = S.sb("ropeS", [96, TOK], BF16)
            Vt = S.sb("Vt", [128, NT, 384], BF16)
            cqnT = S.sb("cqnT", [128, 2, 512], BF16)
            ckvnT = S.sb("ckvnT", [128, 512], BF16)
            cqn = [S.sb("cqn%d" % i, [128, 256], BF16) for i in range(2)]
            ckvn = [S.sb("ckvn%d" % i, [128, 128], BF16) for i in range(2)]
            rt1 = S.sb("rt1", [96, 512], F32)
            rt2 = S.sb("rt2", [96, 512], F32)
            pT = [S.sb("pT%d" % i, [128, 512], BF16) for i in range(3)]
            rc = S.sb("rc", [96, 512], F32)
            yaT = S.sb("yaT", [96, 4, 512], BF16)
            onesb = S.sb("onesb", [128, 96], BF16)
            st = [S.sb("st%d" % i, [128, 2], F32) for i in range(2)]
            rs = [S.sb("rs%d" % i, [128, 2], F32) for i in range(2)]
            gq = S.sb("gq", [128, 2, 2], F32)
            gkv = S.sb("gkv", [128, 2], F32)
            qT = self.hT[0:96, 0:4, :]
            kT = self.hT[0:96, 4:8, :]
            win = d['w_in'][l]
            self.dma('pool', wA[:], win[:, 1808:2224].rearrange("(k p) c -> p k c", p=128), [], ['wA'], 'aw')
            self.dma('pool', wuq[:], d['w_uq'][l].rearrange("(k p) c -> p k c", p=128), [], ['wuq'], 'aw')
            self.v('dve', 'memset', [], ['wuqs'], wuqs[:], 0.0)
            self.v('dve', 'memset', [], ['wkr'], wkr[:], 0.0)
            self.v('dve', 'memset', [], ['wkrs'], wkrs[:], 0.0)
            self.v('dve', 'memset', [], ['onesb'], onesb[:], 1.0)
            for c in range(2):
                self.dma('pool', wuqs[:, c, :, 64:96], d['w_uq_sw'][l][c * 128:(c + 1) * 128, :, :], ['wuqs'], ['wuqs'], 'aw')
            self.dma('pool', wkr[:, :, 64:96], win[:, 2192:2224].rearrange("(k p) c -> p k c", p=128), ['wkr'], ['wkr'], 'aw')
            self.dma('pool', wkrs[:, :, 64:96], d['w_kr_sw'][l].rearrange("(k p) c -> p k c", p=128), ['wkrs'], ['wkrs'], 'aw')
            self.dma('pool', wukv[:], d['w_ukv'][l], [], ['wukv'], 'aw')
            self.dma('pool', woA[:], d['w_out'][l][640:1024, :].rearrange("(h p) c -> p h c", p=96), [], ['woA'], 'aw')
            self.dma('sp', ropeC[64:96, :], d['ropeC'][:, :], [], ['rope'], 'aw')
            self.dma('sp', ropeS[64:96, :], d['ropeS'][:, :], [], ['rope'], 'aw')
            self.dma('sp', gq[:], d['gqfm'][:, :, :], [], ['gq'], 'aw')
            self.dma('sp', gkv[:], d['gkvfm'][:, :], [], ['gq'], 'aw')
            wv = wukv[:, :].rearrange("p (h c) -> p h c", c=160)[:, :, 64:160]
            for (o, n) in self.blocks():
                tl = list(range(o // 128, (o + n) // 128))
                hk = [('hT', t) for t in tl]
                def a1_front(t):
                    b1 = t % 2
                    for k in range(8):
                        self.mm(self.pb[b1][:, 0:416], self.hT[:, k, t * 128:(t + 1) * 128], wA[:, k, :], k == 0, k == 7,
                                [('hT', t), 'wA'], [('pb', b1)])
                a1_front(tl[0])
                for ti, t in enumerate(tl):
                    b1 = t % 2
                    if ti + 1 < len(tl):
                        a1_front(tl[ti + 1])
                    si = t % 2
                    self.v('dve', 'memset', [], [('st', si)], st[si][:], 0.0)
                    self.cnt += 1
                    self.act(self.junk[:, 0:256], self.pb[b1][:, 0:256], AF.Square, [('pb', b1)], [('junk', self.cnt), ('st', si)],
                             accum=st[si][:, 0:1])
                    self.cnt += 1
                    self.act(self.junk[:, 256:384], self.pb[b1][:, 256:384], AF.Square, [('pb', b1)], [('junk', self.cnt), ('st', si)],
                             accum=st[si][:, 1:2])
                    self.act(rs[si][:, 0:1], st[si][:, 0:1], AF.Sqrt, [('st', si), 'epsT'], [('rs', si)], bias=self.epsT[:], scale=1.0 / 256)
                    self.act(rs[si][:, 1:2], st[si][:, 1:2], AF.Sqrt, [('st', si), 'epsT'], [('rs', si)], bias=self.epsT[:], scale=1.0 / 128)
                    self.v('dve', 'reciprocal', [('rs', si)], [('rs', si)], out=rs[si][:], in_=rs[si][:])
                    self.v('dve', 'tensor_scalar', [('pb', b1), ('rs', si)], [('cqn', si)], out=cqn[si][:], in0=self.pb[b1][:, 0:256],
                           scalar1=rs[si][:, 0:1], scalar2=None, op0=OP.mult)
                    self.v('dve', 'tensor_scalar', [('pb', b1), ('rs', si)], [('ckvn', si)], out=ckvn[si][:], in0=self.pb[b1][:, 256:384],
                           scalar1=rs[si][:, 1:2], scalar2=None, op0=OP.mult)
                    pi = t % 2
                    for c in range(2):
                        self.tr(self.ptr[pi][:, c, :], cqn[si][:, c * 128:(c + 1) * 128], self.identb[:], [('cqn', si), 'identb'], [('ptr', pi)])
                    self.tr(self.ptr[pi][:, 2, :], ckvn[si][:], self.identb[:], [('ckvn', si), 'identb'], [('ptr', pi)])
                    for c in range(2):
                        self.act(cqnT[:, c, ti * 128:(ti + 1) * 128], self.ptr[pi][:, c, :], AF.Identity, [('ptr', pi), 'gq'], ['cqnT'],
                                 scale=gq[:, l, c:c + 1])
                    self.act(ckvnT[:, ti * 128:(ti + 1) * 128], self.ptr[pi][:, 2, :], AF.Identity, [('ptr', pi), 'gq'], ['ckvnT'],
                             scale=gkv[:, l:l + 1])
                    bv = 2 + t % 2
                    self.mm(self.pb[bv][:, 0:384], ckvnT[:, ti * 128:(ti + 1) * 128], wv, True, True, ['ckvnT', 'wukv'], [('pb', bv)])
                    self.act(Vt[:, t, :], self.pb[bv][:, 0:384], AF.Identity, [('pb', bv)], [('Vt', t)])
                for k in range(8):
                    self.mm(self.pb[4][0:96, 0:n], wkr[:, k, :], self.hT[:, k, o:o + n], k == 0, k == 7, hk + ['wkr'], [('pb', 4)])
                for k in range(8):
                    self.mm(self.pb[5][0:96, 0:n], wkrs[:, k, :], self.hT[:, k, o:o + n], k == 0, k == 7, hk + ['wkrs'], [('pb', 5)])
                for h in range(4):
                    for c in range(2):
                        self.mm(self.pb[0][0:96, 0:n], wuq[:, c, h * 96:(h + 1) * 96], cqnT[:, c, 0:n], c == 0, c == 1, ['wuq', 'cqnT'], [('pb', 0)])
                    for c in range(2):
                        self.mm(self.pb[1][0:96, 0:n], wuqs[:, c, h, :], cqnT[:, c, 0:n], c == 0, c == 1, ['wuqs', 'cqnT'], [('pb', 1)])
                    bk = 2 + h % 2
                    self.mm(self.pb[bk][0:64, 0:n], wukv[:, h * 160:h * 160 + 64], ckvnT[:, 0:n], True, True, ['wukv', 'ckvnT'], [('pb', bk)])
                    self.act(qT[0:64, h, o:o + n], self.pb[0][0:64, 0:n], AF.Identity, [('pb', 0)], hk)
                    self.act(kT[0:64, h, o:o + n], self.pb[bk][0:64, 0:n], AF.Identity, [('pb', bk)], hk)
                    self.v('dve', 'tensor_tensor', [('pb', 0), 'rope'], ['rt1'], out=rt1[64:96, 0:n], in0=self.pb[0][64:96, 0:n],
                           in1=ropeC[64:96, o:o + n], op=OP.mult)
                    self.v('dve', 'tensor_tensor', [('pb', 1), 'rope'], ['rt2'], out=rt2[64:96, 0:n], in0=self.pb[1][64:96, 0:n],
                           in1=ropeS[64:96, o:o + n], op=OP.mult)
                    self.v('pool', 'tensor_tensor', ['rt1', 'rt2'], hk, out=qT[64:96, h, o:o + n], in0=rt1[64:96, 0:n],
                           in1=rt2[64:96, 0:n], op=OP.add)
                self.v('dve', 'tensor_tensor', [('pb', 4), 'rope'], ['rt1'], out=rt1[64:96, 0:n], in0=self.pb[4][64:96, 0:n],
                       in1=ropeC[64:96, o:o + n], op=OP.mult)
                self.v('dve', 'tensor_tensor', [('pb', 5), 'rope'], ['rt2'], out=rt2[64:96, 0:n], in0=self.pb[5][64:96, 0:n],
                       in1=ropeS[64:96, o:o + n], op=OP.mult)
                for h in range(4):
                    self.v('pool', 'tensor_tensor', ['rt1', 'rt2'], hk, out=kT[64:96, h, o:o + n], in0=rt1[64:96, 0:n],
                           in1=rt2[64:96, 0:n], op=OP.add)
            qblocks = [(256 + i * 512, 512, list(range(NT))) for i in range(4)]
            if not last:
                qblocks.append((0, 256, [0, 1]))
            for (qo, qn, ktiles) in qblocks:
                qk = [('hT', t) for t in range(qo // 128, (qo + qn) // 128)]
                for h in range(4):
                    def st_mm(i):
                        kt_ = ktiles[i]
                        self.mm(self.pb[i % 2][:, 0:qn], kT[:, h, kt_ * 128:(kt_ + 1) * 128], qT[:, h, qo:qo + qn], True, True,
                                qk + [('hT', kt_)], [('pb', i % 2)])
                    st_mm(0)
                    for i, kt in enumerate(ktiles):
                        sb_ = i % 2
                        pi = i % 3
                        if i + 1 < len(ktiles):
                            st_mm(i + 1)
                        self.act(pT[pi][:, 0:qn], self.pb[sb_][:, 0:qn], AF.Exp, [('pb', sb_)], [('pT', pi)], scale=SCALE)
                        self.mm(self.pb[2][0:96, 0:qn], Vt[:, kt, h * 96:(h + 1) * 96], pT[pi][:, 0:qn], i == 0, i == len(ktiles) - 1,
                                [('Vt', kt), ('pT', pi)], [('pb', 2)])
                        self.mm(self.pb[3][0:96, 0:qn], onesb[:, :], pT[pi][:, 0:qn], i == 0, i == len(ktiles) - 1,
                                ['onesb', ('pT', pi)], [('pb', 3)])
                    self.v('dve', 'reciprocal', [('pb', 3)], ['rc'], out=rc[:, 0:qn], in_=self.pb[3][0:96, 0:qn])
                    self.v('dve', 'tensor_tensor', [('pb', 2), 'rc'], [('yaT', h)], out=yaT[:, h, 0:qn], in0=self.pb[2][0:96, 0:qn],
                           in1=rc[:, 0:qn], op=OP.mult)
                for ti, t in enumerate(range(qo // 128, (qo + qn) // 128)):
                    for nh in range(2):
                        b = 4 + nh
                        for h in range(4):
                            self.mm(self.pb[b][:], yaT[:, h, ti * 128:(ti + 1) * 128], woA[:, h, nh * 512:(nh + 1) * 512], h == 0, h == 3,
                                    [('yaT', h), 'woA'], [('pb', b)])
                        self.xupd(t, nh, b)

    def fourier(self, l, last):
        nc, P, d = self.nc, self.P, self.d
        with Scope(self) as S:
            wpf = S.sb("wpf", [128, 8, 256], BF16)
            pfT = S.sb("pfT", [128, 2, TOK], BF16)
            bd = S.sb("bd", [128, 256], BF16)
            pfCS = S.sb("pfCS", [128, NT, 2, 256], BF16)
            tw = [S.sb("tw%d" % i, [128, 2, 1024], BF16) for i in range(3)]
            yfT = S.sb("yfT", [128, 2, TOK], BF16)
            wo = S.sb("wo", [128, 2, DM], BF16)
            c256 = S.sb("c256", [128, 2, 2, 256], BF16)
            self.dma('pool', wpf[:], d['w_in'][l][:, 0:256].rearrange("(k p) c -> p k c", p=128), [], ['wpf'], 'fw')
            self.dma('pool', wo[:], d['w_out'][l][0:256, :].rearrange("(k p) c -> p k c", p=128), [], ['wo'], 'fw')
            self.dma('sp', bd[:], d['bdcs'][:, :], [], ['bd'], 'fw')
            self.dma('sp', c256[:], d['c256'][:, :, :, :], [], ['c256'], 'fw')
            nb = 0
            for j in range(2):
                for (o, n) in self.blocks():
                    b = nb % 6
                    nb += 1
                    for k in range(8):
                        self.mm(self.pb[b][:, 0:n], wpf[:, k, j * 128:(j + 1) * 128], self.hT[:, k, o:o + n], k == 0, k == 7,
                                ['wpf'] + [('hT', t) for t in range(o // 128, (o + n) // 128)], [('pb', b)])
                    self.act(pfT[:, j, o:o + n], self.pb[b][:, 0:n], AF.Identity, [('pb', b)], [('pfT', j, o)])
            for t in range(NT):
                for j in range(2):
                    b = nb % 6
                    nb += 1
                    self.mm(self.pb[b][:, 0:256], pfT[:, j, t * 128:(t + 1) * 128], bd[:], True, True,
                            [('pfT', j, (t // 4) * 512), 'bd'], [('pb', b)])
                    self.act(pfCS[:, t, j, :], self.pb[b][:, 0:256], AF.Identity, [('pb', b)], [('pfCS', t)])
            ntw = 0
            for half in range(2):
                for nt in range(16):
                    t = nt + 2
                    s = ntw % 3
                    ntw += 1
                    self.dma('sp', tw[s][:], d['twCS'][half, nt], [], [('tw', s)], 'tw%d' % s)
                    for j in range(2):
                        for q in range(2):
                            b = j * 2 + q
                            self.mm(self.pb[b][:], pfCS[:, t, j, 0:128], tw[s][:, 0, q * 512:(q + 1) * 512], nt == 0, False,
                                    [('pfCS', t), ('tw', s)], [('pb', b)])
                            self.mm(self.pb[b][:], pfCS[:, t, j, 128:256], tw[s][:, 1, q * 512:(q + 1) * 512], False, nt == 15,
                                    [('pfCS', t), ('tw', s)], [('pb', b)])
                for j in range(2):
                    for q in range(2):
                        b = j * 2 + q
                        o = 256 + half * 1024 + q * 512
                        self.act(yfT[:, j, o:o + 512], self.pb[b][:], AF.Identity, [('pb', b)], [('yfT', o // 128)])
            if not last:
                for j in range(2):
                    b = 4 + j
                    for nt in range(2):
                        self.mm(self.pb[b][:, 0:256], pfCS[:, nt, j, 0:128], c256[:, nt, 0, :], nt == 0, False,
                                [('pfCS', nt), 'c256'], [('pb', b)])
                        self.mm(self.pb[b][:, 0:256], pfCS[:, nt, j, 128:256], c256[:, nt, 1, :], False, nt == 1,
                                [('pfCS', nt), 'c256'], [('pb', b)])
                    self.act(yfT[:, j, 0:256], self.pb[b][:, 0:256], AF.Identity, [('pb', b)], [('yfT', 0)])
            nb = 0
            wos = [S.sb("wos%d" % w, [128, 2, DM], BF16) for w in range(1 if last else 2)]
            for w in range(len(wos)):
                self.v('dve', 'tensor_tensor', ['wo', 'gaB'], [('wos', w)], out=wos[w][:], in0=wo[:],
                       in1=self.gaB[:, w, :].unsqueeze(1).to_broadcast([128, 2, DM]), op=OP.mult)
            for t in (range(2, NT) if last else range(NT)):
                for nh in range(2):
                    b = nb % 6
                    nb += 1
                    for j in range(2):
                        self.mm(self.pb[b][:], yfT[:, j, t * 128:(t + 1) * 128], wos[which(t)][:, j, nh * 512:(nh + 1) * 512], j == 0, j == 1,
                                [('yfT', 0 if t < 2 else 2 + ((t - 2) // 4) * 4), ('wos', which(t))], [('pb', b)])
                    self.xadd(t, nh, b)

    def mlp(self, l, tiles):
        nc, P, d = self.nc, self.P, self.d
        groups = [tiles[i:i + 6] for i in range(0, len(tiles), 6)]
        with Scope(self) as S:
            uT = S.sb("uT", [128, 32, 768], BF16)
            wu = [S.sb("wu%d" % i, [128, 8, 512], BF16) for i in range(2)]
            wd = [S.sb("wd%d" % i, [128, 4, 512], BF16) for i in range(2)]
            tmp = [S.sb("mt%d" % i, [128, 512], F32) for i in range(2)]
            nu = nd = nb = 0
            for g in groups:
                G = 128 * len(g)
                t0 = g[0] * 128
                subs = [(0, G)] if G <= 512 else [(0, G // 2), (G // 2, G // 2)]
                for hb in range(8):
                    s = nu % 2
                    nu += 1
                    self.dma('pool', wu[s][:], d['w_up_r'][l, hb], [], [('wu', s)], 'wu%d' % s)
                    for cc in range(4):
                        j = hb * 4 + cc
                        for (o, n) in subs:
                            b = nb % 6
                            nb += 1
                            for k in range(8):
                                self.mm(self.pb[b][:, 0:n], wu[s][:, k, cc * 128:(cc + 1) * 128],
                                        self.hT[:, k, t0 + o:t0 + o + n], k == 0, k == 7,
                                        [('wu', s)] + [('hT', t) for t in g], [('pb', b)])
                            ri = nb % 2
                            self.act(tmp[ri][:, 0:n], self.pb[b][:, 0:n], AF.Relu, [('pb', b)], [('mt', ri)])
                            self.v('dve', 'tensor_tensor', [('mt', ri)], [('uT', j)], out=uT[:, j, o:o + n],
                                   in0=tmp[ri][:, 0:n], in1=tmp[ri][:, 0:n], op=OP.mult)
                for nh in range(2):
                    for jb in range(8):
                        s = nd % 2
                        nd += 1
                        self.dma('pool', wd[s][:], d['w_down_r'][l, nh, jb], [], [('wd', s)], 'wd%d' % s)
                        for jj in range(4):
                            j = jb * 4 + jj
                            for ti, t in enumerate(g):
                                self.mm(self.pb[ti][:], uT[:, j, ti * 128:(ti + 1) * 128], wd[s][:, jj, :], j == 0, j == 31,
                                        [('wd', s), ('uT', j)], [('pb', ti)])
                    for ti, t in enumerate(g):
                        self.xupd(t, nh, ti)

    def final(self):
        nc, P, d = self.nc, self.P, self.d
        with Scope(self) as S:
            gf = S.sb("gf", [128, DM], F32)
            ob = [S.sb("ob%d" % i, [128, DM], F32) for i in range(2)]
            self.dma('sp', gf[:], d['gfin'][:, :], [], ['gf'], 'cst')
            self.rms_all(range(2, NT))
            for t in range(2, NT):
                i = t % 2
                self.act(ob[i][:], self.xs[:, t, :], AF.Identity, [('xs', t), 'rstd'], [('ob', i)], scale=self.rstd[:, t:t + 1])
                self.v('dve', 'tensor_tensor', [('ob', i), 'gf'], [('ob', i)], out=ob[i][:], in0=ob[i][:], in1=gf[:], op=OP.mult)
                self.dma('sp', d['out'][(t - 2) * 128:(t - 1) * 128, :], ob[i][:], [('ob', i)], [('out', t)], 'out')


def dram_decls(nc):
    d = {}

    def inp(name, shape, dt=F32):
        d[name] = nc.dram_tensor(name, list(shape), dt, kind="ExternalInput").ap()

    inp('xin', [TOK, DM])
    inp('c2', [128, 16])
    inp('w_mod', [2, 1024, 6144])
    inp('bmodfm', [128, 2, 48])
    inp('gnfm', [128, 2, 2, 8])
    inp('w_in', [2, 1024, 2224])
    inp('w_out', [2, 1024, 1024])
    inp('w_up_r', [2, 8, 128, 8, 512])
    inp('w_down_r', [2, 2, 8, 128, 4, 512])
    inp('gfin', [128, DM])
    inp('identb', [128, 128], BF16)
    inp('identf', [128, 128])
    inp('w_uq', [2, 256, 384])
    inp('w_ukv', [2, 128, 640])
    inp('w_uq_sw', [2, 256, 4, 32])
    inp('w_kr_sw', [2, 1024, 32])
    inp('ropeC', [32, TOK], BF16)
    inp('ropeS', [32, TOK], BF16)
    inp('gqfm', [128, 2, 2])
    inp('gkvfm', [128, 2])
    inp('w_g', [2, 1024, 16])
    inp('bg', [2, 8, 2])
    inp('convdg', [2, 96, 8, 5, 96], BF16)
    inp('gmB', [128, 2, 384])
    inp('E2', [8, 2, 8])
    inp('negidf', [128, 128])
    inp('mlmask', [128, 2, 256], BF16)
    inp('bdcs', [128, 256], BF16)
    inp('twCS', [2, 16, 128, 2, 1024], BF16)
    inp('c256', [128, 2, 2, 256], BF16)
    d['out'] = nc.dram_tensor('out', [2048, DM], F32, kind="ExternalOutput").ap()
    if DBGXS:
        d['dbgxs'] = nc.dram_tensor('dbgxs', [TOK, DM], F32, kind="ExternalOutput").ap()
    return d


def build_program():
    nc = bass.Bass("TRN2", target_bir_lowering=False)
    d = dram_decls(nc)
    P = Prog(nc)
    kb = K(nc, P, d)
    kb.build()
    P.plan()
    es = contextlib.ExitStack()
    with es:
        P.start_real(es)
        kb.build()
    return nc, P


def host_inputs(inputs, b):
    f = lambda a: np.ascontiguousarray(np.asarray(a, dtype=np.float32))
    x, c, ctx, c_ctx = inputs['x'], inputs['c'], inputs['ctx'], inputs['c_ctx']
    m = {}
    m['xin'] = f(np.concatenate([ctx[b], x[b]], axis=0))
    c2 = np.stack([np.asarray(c[b]).reshape(8, 128).T, np.asarray(c_ctx).reshape(8, 128).T], axis=-1)
    m['c2'] = f(c2.reshape(128, 16))
    m['w_mod'] = f(inputs['w_mod'])
    m['bmodfm'] = f(np.asarray(inputs['b_mod']).reshape(2, 48, 128).transpose(2, 0, 1))
    gn = np.stack([np.asarray(inputs['g_norm1']), np.asarray(inputs['g_norm2'])], axis=1)
    m['gnfm'] = f(gn.reshape(2, 2, 8, 128).transpose(3, 0, 1, 2))
    m['w_in'] = f(inputs['w_in'])
    m['w_out'] = f(inputs['w_out'])
    m['w_up_r'] = f(np.asarray(inputs['w_up']).reshape(2, 8, 128, 8, 512).transpose(0, 3, 2, 1, 4))
    m['w_down_r'] = f(np.asarray(inputs['w_down']).reshape(2, 8, 4, 128, 2, 512).transpose(0, 4, 1, 3, 2, 5))
    m['gfin'] = f(np.tile(np.asarray(inputs['g_final'])[None, :], (128, 1)))
    m['identb'] = np.eye(128, dtype=np.float32).astype(ml_dtypes.bfloat16)
    m['identf'] = np.eye(128, dtype=np.float32)
    m.update(host_consts())
    perm = np.array([a * 16 + (1 - b_) * 8 + j for a in range(2) for b_ in range(2) for j in range(8)])
    wuq = np.asarray(inputs['w_uq'])
    m['w_uq'] = f(wuq)
    m['w_ukv'] = f(inputs['w_ukv'])
    m['w_uq_sw'] = f(wuq.reshape(2, 256, 4, 96)[:, :, :, 64:96][..., perm])
    m['w_kr_sw'] = f(np.asarray(inputs['w_in'])[:, :, 2192:2224][..., perm])
    m['gqfm'] = f(np.asarray(inputs['g_q_norm']).reshape(2, 2, 128).transpose(2, 0, 1))
    m['gkvfm'] = f(np.asarray(inputs['g_kv_norm']).T)
    gp = [0, 1, 2, 3, 8, 9, 10, 11, 4, 5, 6, 7, 12, 13, 14, 15]
    m['w_g'] = f(np.asarray(inputs['w_in'])[:, :, 1792:1808][..., gp])
    m['bg'] = f(np.asarray(inputs['b_gates'])[:, gp].reshape(2, 2, 8).transpose(0, 2, 1))
    cv = np.asarray(inputs['conv_qk'], dtype=np.float32).reshape(2, 5, 8, 96)
    dgc = np.zeros((2, 96, 8, 5, 96), np.float32)
    pi_ = np.arange(96)
    dgc[:, pi_, :, :, pi_] = cv.transpose(3, 0, 2, 1)
    m['convdg'] = dgc.astype(ml_dtypes.bfloat16)
    m['gmB'] = f(np.tile(np.asarray(inputs['g_mlstm'])[None, :, :], (128, 1, 1)))
    return m


def host_consts():
    bf = lambda a: np.ascontiguousarray(a.astype(np.float32)).astype(ml_dtypes.bfloat16)
    m = {}
    i64 = np.arange(64)
    a64 = 2 * np.pi * np.outer(i64, i64) / 64
    C64, S64 = np.cos(a64) / 8.0, np.sin(a64) / 8.0
    z = np.zeros((64, 64))
    m['bdcs'] = bf(np.block([[C64, z, S64, z], [z, C64, z, S64]]))
    n = np.arange(2048, dtype=np.float64)
    aN = 2 * np.pi * (np.outer(n, n) % 2048) / 2048
    tcs = np.stack([np.cos(aN) / np.sqrt(2048.0), -np.sin(aN) / np.sqrt(2048.0)], axis=1)
    m['twCS'] = bf(tcs.reshape(16, 128, 2, 2, 1024).transpose(3, 0, 1, 2, 4))
    E2 = np.zeros((8, 2, 8), np.float32)
    for k_ in range(4):
        E2[k_, 0, k_] = 1.0
        E2[4 + k_, 1, 4 + k_] = 1.0
    m['E2'] = E2
    m['negidf'] = -np.eye(128, dtype=np.float32)
    sidx, tidx = np.arange(128)[:, None], np.arange(128)[None, :]
    mf = np.where(sidx <= tidx, 0.0, -30000.0)
    mb_ = np.where(sidx >= tidx, 0.0, -30000.0)
    m['mlmask'] = bf(np.stack([np.concatenate([mf, mf], axis=1), np.concatenate([mb_, mb_], axis=1)], axis=1))
    tt = np.arange(2048)
    freqs = 10000.0 ** (-np.arange(8, dtype=np.float64) / 8)
    ang = np.stack([np.outer(tt // 64, freqs), np.outer(tt % 64, freqs)], axis=1)
    cosT = np.ones((TOK, 2, 2, 8)); sinT = np.zeros((TOK, 2, 2, 8))
    cosT[256:] = np.cos(ang)[:, :, None, :]
    sinT[256:, :, 0, :] = -np.sin(ang)
    sinT[256:, :, 1, :] = np.sin(ang)
    m['ropeC'] = bf(cosT.reshape(TOK, 32).T)
    m['ropeS'] = bf(sinT.reshape(TOK, 32).T)
    n = np.arange(256, dtype=np.float64)
    a2 = 2 * np.pi * (np.outer(n, n) % 256) / 256
    c = np.stack([np.cos(a2) / 16.0, -np.sin(a2) / 16.0], axis=1)
    m['c256'] = bf(c.reshape(2, 128, 2, 256).transpose(1, 0, 2, 3))
    return m


_CACHE = {}


def kernel(**inputs):
    if 'nc' not in _CACHE:
        _CACHE['nc'] = build_program()
    nc, P = _CACHE['nc']
    shared = None
    in_maps = []
    for b in range(8):
        m = host_inputs(inputs, b) if shared is None else None
        if shared is None:
            shared = m
        else:
            m = dict(shared)
            f = lambda a: np.ascontiguousarray(np.asarray(a, dtype=np.float32))
            m['xin'] = f(np.concatenate([inputs['ctx'][b], inputs['x'][b]], axis=0))
            c2 = np.stack([np.asarray(inputs['c'][b]).reshape(8, 128).T, np.asarray(inputs['c_ctx']).reshape(8, 128).T], axis=-1)
            m['c2'] = f(c2.reshape(128, 16))
        in_maps.append(m)
    res = run_bass_kernel_spmd(nc, in_maps, core_ids=list(range(8)))
    _CACHE['res'] = res
    return np.stack([np.asarray(r['out'], dtype=np.float32) for r in res.results], axis=0)
```

```python
import contextlib, bisect, math
import numpy as np
import ml_dtypes
import concourse.bass as bass
import concourse.mybir as mybir
from concourse.bass_utils import run_bass_kernel_spmd

F32 = mybir.dt.float32
BF16 = mybir.dt.bfloat16
AF = mybir.ActivationFunctionType
OP = mybir.AluOpType
AX = mybir.AxisListType

NT = 18
TOK = 2304
DM = 1024
EPS = 1e-6
DEPTH = 2
STAGE = 99
DEBUG = []
BRANCHES = 'FAM'
DBGXS = False


class Dummy:
    def __getitem__(self, k):
        return self

    def __getattr__(self, n):
        return lambda *a, **k: self


class Prog:
    ENGS = ('pe', 'dve', 'act', 'pool', 'sp')

    def __init__(self, nc):
        self.nc = nc
        self.dry = True
        self.rec = []
        self.i = 0

    def op(self, eng, fn, r=(), w=(), dma=None):
        if self.dry:
            self.rec.append(('op', eng, tuple(r), tuple(w), dma))
        else:
            i = self.i
            E = self.H[eng]
            for (s, v) in self.waits[i]:
                E.wait_ge(self.sems[s], v)
            if fn is not None:
                ins = fn()
                inc = self.incs[i]
                if inc is not None:
                    ins.then_inc(self.sems[inc[0]], inc[1])
        self.i += 1

    def barrier(self):
        if self.dry:
            self.rec.append(('bar',))
        self.i += 1

    def plan(self):
        rec = self.rec
        n = len(rec)
        lastw, readers, last_on_eng, dma_ops = {}, {}, {}, {}
        deps = [None] * n
        pend = {e: None for e in self.ENGS}
        for i, r in enumerate(rec):
            if r[0] == 'bar':
                bd = set(last_on_eng.values())
                for s, l in dma_ops.items():
                    if l:
                        bd.add(l[-1])
                for e in self.ENGS:
                    pend[e] = set(bd) | (pend[e] or set())
                lastw.clear()
                readers.clear()
                continue
            _, eng, R, W, dma = r
            d = set()
            for k in R:
                if k in lastw:
                    d.add(lastw[k])
            for k in W:
                if k in lastw:
                    d.add(lastw[k])
                d.update(readers.get(k, ()))
            if pend[eng]:
                d |= pend[eng]
                pend[eng] = None
            d.discard(i)
            if eng == 'pe':
                d = {j for j in d if not (rec[j][1] == 'pe' and rec[j][4] is None)}
            deps[i] = d
            for k in W:
                lastw[k] = i
                readers[k] = []
            for k in R:
                lst = readers.setdefault(k, [])
                if dma is None:
                    lst[:] = [j for j in lst if not (rec[j][1] == eng and rec[j][4] is None)]
                lst.append(i)
            last_on_eng[eng] = i
            if dma:
                dma_ops.setdefault(dma, []).append(i)
        sig = [False] * n
        for i in range(n):
            if deps[i]:
                for j in deps[i]:
                    if rec[j][4] is None:
                        sig[j] = True
        cnt = {e: 0 for e in self.ENGS}
        ev = [None] * n
        for i, r in enumerate(rec):
            if r[0] != 'op':
                continue
            if r[4] is None and sig[i]:
                cnt[r[1]] += 1
                ev[i] = ('E_' + r[1], cnt[r[1]])
        self.waits = [()] * n
        self.incs = [None] * n
        seen = {e: {} for e in self.ENGS}
        for i, r in enumerate(rec):
            if r[0] != 'op':
                continue
            eng = r[1]
            wl = {}
            for j in deps[i]:
                if rec[j][4] is None:
                    s, v = ev[j]
                else:
                    s = 'D_' + rec[j][4]
                    v = 16 * bisect.bisect_left(dma_ops[rec[j][4]], i)
                if v > wl.get(s, 0):
                    wl[s] = v
            ws = []
            for s, v in wl.items():
                if v > seen[eng].get(s, 0):
                    seen[eng][s] = v
                    ws.append((s, v))
            self.waits[i] = tuple(ws)
            if r[4] is not None:
                self.incs[i] = ('D_' + r[4], 16)
            elif sig[i]:
                self.incs[i] = (ev[i][0], 1)
        self.semnames = ['E_' + e for e in self.ENGS] + ['D_' + s for s in dma_ops]
        self.stats = dict(n=n, cnt=cnt, nw=sum(len(w) for w in self.waits))

    def start_real(self, es):
        nc = self.nc
        self.dry = False
        self.i = 0
        self.H = {'pe': nc.tensor, 'dve': nc.vector, 'act': nc.scalar, 'pool': nc.gpsimd, 'sp': nc.sync}
        self.sems = {s: es.enter_context(nc.semaphore(s)) for s in self.semnames}


class Scope:
    _n = 0

    def __init__(self, K):
        self.K = K
        self.es = contextlib.ExitStack()

    def __enter__(self):
        return self

    def sb(self, name, shape, dt):
        if self.K.P.dry:
            return Dummy()
        Scope._n += 1
        return self.es.enter_context(self.K.nc.sbuf_tensor("%s_%d" % (name, Scope._n), list(shape), dt))

    def __exit__(self, *a):
        self.K.P.barrier()
        self.K.semmap = {}
        self.es.close()
        return False


def which(t):
    return 1 if t < 2 else 0


class K:
    def __init__(self, nc, P, dram):
        self.nc, self.P, self.d = nc, P, dram
        self.semmap = {}

    def mm(self, out, lhsT, rhs, start, stop, r, w):
        nc = self.nc
        self.P.op('pe', lambda: nc.tensor.matmul(out, lhsT=lhsT, rhs=rhs, start=start, stop=stop), r, w)

    def tr(self, out, in_, ident, r, w):
        nc = self.nc
        self.P.op('pe', lambda: nc.tensor.transpose(out, in_, ident), r, w)

    def act(self, out, in_, func, r, w, bias=None, scale=None, accum=None):
        nc = self.nc
        kw = {}
        if bias is not None:
            kw['bias'] = bias
        if scale is not None:
            kw['scale'] = scale
        if accum is not None:
            kw['accum_out'] = accum
        self.P.op('act', lambda: nc.scalar.activation(out=out, in_=in_, func=func, **kw), r, w)

    def v(self, eng, name, r, w, *a, **kw):
        nc = self.nc
        E = nc.vector if eng == 'dve' else nc.gpsimd
        self.P.op(eng, lambda: getattr(E, name)(*a, **kw), r, w)

    def dma(self, q, out, in_, r, w, sem):
        nc = self.nc
        E = nc.sync if q == 'sp' else nc.gpsimd
        k0 = w[0] if w else sem
        if isinstance(k0, tuple) and k0[0] in ('out', 'dbgxs'):
            name = sem
        elif isinstance(k0, tuple) and k0[0] == 'xs':
            name = 'xs'
        else:
            name = "b%d" % self.semmap.setdefault((k0, q), len(self.semmap))
        self.P.op(q, lambda: E.dma_start(out=out, in_=in_), r, w, dma=name + '_' + q)

    def build(self):
        nc, P, d = self.nc, self.P, self.d
        self.semmap = {}
        with Scope(self) as S:
            self.S0 = S
            self.xs = S.sb("xs", [128, NT, DM], F32)
            self.hT = S.sb("hT", [128, 8, TOK], BF16)
            self.identb = S.sb("identb", [128, 128], BF16)
            self.identf = S.sb("identf", [128, 128], F32)
            self.onesf = S.sb("onesf", [128, 128], F32)
            self.c2 = S.sb("c2", [128, 16], F32)
            self.scb = S.sb("scb", [128, 16], F32)
            self.modv = S.sb("modv", [128, 48, 2], F32)
            self.bmod = S.sb("bmod", [128, 2, 48], F32)
            self.gn = S.sb("gn", [128, 2, 2, 8], F32)
            self.G = S.sb("G", [128, 2, 8, 2], F32)
            self.gaB = S.sb("gaB", [128, 2, DM], F32)
            self.ssq = S.sb("ssq", [128, NT], F32)
            self.rstd = S.sb("rstd", [128, NT], F32)
            self.epsT = S.sb("epsT", [128, 1], F32)
            self.xn = [S.sb("xn%d" % i, [128, DM], BF16) for i in range(2)]
            self.junk = S.sb("junk", [128, DM], BF16)
            self.dg = [S.sb("dg%d" % i, [128, 128], F32) for i in range(2)]
            self.xt = [S.sb("xt%d" % i, [128, 512], F32) for i in range(2)]
            if not P.dry:
                es = S.es
                self.pb = [es.enter_context(nc.psum_tensor("pb%d" % i, [128, 512], F32)) for i in range(6)]
                self.ptr = [es.enter_context(nc.psum_tensor("ptr%d" % i, [128, 8, 128], BF16)) for i in range(2)]
            else:
                self.pb = [Dummy()] * 6
                self.ptr = [Dummy()] * 2
            self.cnt = 0
            self.dma('sp', self.identb[:], d['identb'][:, :], [], ['identb'], 'cst')
            self.dma('sp', self.identf[:], d['identf'][:, :], [], ['identf'], 'cst')
            self.dma('sp', self.c2[:], d['c2'][:, :], [], ['c2'], 'cst')
            self.dma('sp', self.bmod[:], d['bmodfm'][:, :, :], [], ['bmod'], 'cst')
            self.dma('sp', self.gn[:], d['gnfm'][:, :, :, :], [], ['gn'], 'cst')
            self.v('dve', 'memset', [], ['onesf'], self.onesf[:], 1.0)
            self.v('dve', 'memset', [], ['epsT'], self.epsT[:], EPS)
            self.v('dve', 'memset', [], ['ssq'], self.ssq[:], 0.0)
            self.act(self.scb[:], self.c2[:], AF.Silu, ['c2'], ['scb'])
            for t in range(NT):
                self.dma('sp', self.xs[:, t, :], d['xin'][t * 128:(t + 1) * 128, :], [], [('xs', t)], 'xin')
            for l in range(DEPTH):
                last = (l == DEPTH - 1)
                self.mod_phase(l)
                if STAGE <= 1:
                    break
                self.make_gaB(2)
                self.phaseA(l, 0, range(NT))
                if STAGE >= 3:
                    self.mixers(l, last)
                if DBGXS and l == 0:
                    for t in range(NT):
                        self.dma('sp', d['dbgxs'][t * 128:(t + 1) * 128, :], self.xs[:, t, :], [('xs', t)], [('dbgxs', t)], 'out')
                    break
                self.make_gaB(5)
                tiles = list(range(2, NT)) if last else list(range(NT))
                self.phaseA(l, 1, tiles)
                self.mlp(l, tiles)
            self.final()
        P.barrier()
        P.op('sp', None)

    def mod_phase(self, l):
        nc, P, d = self.nc, self.P, self.d
        with Scope(self) as S:
            wm = [S.sb("wm%d" % i, [128, 8, 1024], F32) for i in range(2)]
            rowb = [S.sb("rowb%d" % i, [2, 512], F32) for i in range(2)]
            modps = self.pb[0]
            for blk in range(6):
                s = blk % 2
                self.dma('sp', wm[s][:], d['w_mod'][l][:, blk * 1024:(blk + 1) * 1024].rearrange("(k p) c -> p k c", p=128),
                         [], [('wm', s)], 'wm%d' % s)
                for hb2 in range(2):
                    rb = 1 + (blk * 2 + hb2) % 2
                    for k in range(8):
                        self.mm(self.pb[rb][0:2, :], self.scb[:, 2 * k:2 * k + 2], wm[s][:, k, hb2 * 512:(hb2 + 1) * 512],
                                k == 0, k == 7, [('wm', s), 'scb'], [('pb', rb)])
                    ri = (blk * 2 + hb2) % 2
                    self.act(rowb[ri][:], self.pb[rb][0:2, :], AF.Identity, [('pb', rb)], [('rowb', ri)])
                    for cc in range(4):
                        j = blk * 8 + hb2 * 4 + cc
                        self.mm(modps[:, 2 * j:2 * j + 2], rowb[ri][:, cc * 128:(cc + 1) * 128], self.identf[0:2, 0:2], True, True,
                                [('rowb', ri), 'identf'], [('pb', 0)])
            self.v('dve', 'tensor_tensor', [('pb', 0), 'bmod'], ['modv'], out=self.modv[:],
                   in0=modps[:, 0:96].rearrange("p (j w) -> p j w", w=2),
                   in1=self.bmod[:, l, :].unsqueeze(2).to_broadcast([128, 48, 2]), op=OP.add)
            for ni, vi in ((0, 1), (1, 4)):
                self.v('dve', 'scalar_tensor_tensor', ['modv', 'gn'], ['G'], out=self.G[:, ni, :, :],
                       in0=self.modv[:, vi * 8:(vi + 1) * 8, :], scalar=1.0,
                       in1=self.gn[:, l, ni, :].unsqueeze(2).to_broadcast([128, 8, 2]), op0=OP.add, op1=OP.mult)

    def make_gaB(self, vi):
        for w in range(2):
            for kk in range(8):
                self.cnt += 1
                dgi = self.cnt % 2
                self.v('dve', 'tensor_scalar', ['modv', 'identf'], [('dg', dgi)], out=self.dg[dgi][:], in0=self.identf[:],
                       scalar1=self.modv[:, vi * 8 + kk, w:w + 1], scalar2=None, op0=OP.mult)
                bank = kk // 4
                self.mm(self.pb[bank][:, (kk % 4) * 128:(kk % 4 + 1) * 128], self.onesf[:], self.dg[dgi][:], True, True,
                        ['onesf', ('dg', dgi)], [('pb', bank)])
            for bank in range(2):
                self.act(self.gaB[:, w, bank * 512:(bank + 1) * 512], self.pb[bank][:], AF.Identity, [('pb', bank)], ['gaB'])

    def rms(self, t):
        self.cnt += 1
        self.act(self.junk[:], self.xs[:, t, :], AF.Square, [('xs', t)], [('junk', self.cnt), ('ssq', t)],
                 accum=self.ssq[:, t:t + 1])
        self.act(self.rstd[:, t:t + 1], self.ssq[:, t:t + 1], AF.Sqrt, [('ssq', t), 'epsT'], [('rstd', t)],
                 bias=self.epsT[:], scale=1.0 / DM)
        self.v('dve', 'reciprocal', [('rstd', t)], [('rstd', t)], out=self.rstd[:, t:t + 1], in_=self.rstd[:, t:t + 1])
        self.v('dve', 'memset', [('ssq', t)], [('ssq', t)], self.ssq[:, t:t + 1], 0.0)

    def rms_all(self, tiles):
        tiles = list(tiles)
        t0, t1 = tiles[0], tiles[-1] + 1
        for t in tiles:
            self.cnt += 1
            self.act(self.junk[:], self.xs[:, t, :], AF.Square, [('xs', t)], [('junk', self.cnt), 'ssq'],
                     accum=self.ssq[:, t:t + 1])
        self.act(self.rstd[:, t0:t1], self.ssq[:, t0:t1], AF.Sqrt, ['ssq', 'epsT'], ['rstd'], bias=self.epsT[:], scale=1.0 / DM)
        self.v('dve', 'reciprocal', ['rstd'], ['rstd'], out=self.rstd[:, t0:t1], in_=self.rstd[:, t0:t1])
        self.v('dve', 'memset', ['ssq'], ['ssq'], self.ssq[:, t0:t1], 0.0)

    def phaseA(self, l, ni, tiles):
        vi = 0 if ni == 0 else 3
        tiles = list(tiles)
        self.rms_all(tiles)
        for t in tiles:
            w = which(t)
            self.cnt += 1
            xi = self.cnt % 2
            self.act(self.xn[xi][:], self.xs[:, t, :], AF.Identity, [('xs', t), 'rstd'], [('xn', xi)],
                     scale=self.rstd[:, t:t + 1])
            for k in range(8):
                self.tr(self.ptr[xi][:, k, :], self.xn[xi][:, k * 128:(k + 1) * 128], self.identb[:],
                        [('xn', xi), 'identb'], [('ptr', xi)])
            for k in range(8):
                self.v('dve', 'tensor_scalar', [('ptr', xi), 'G', 'modv'], [('hT', t)], out=self.hT[:, k, t * 128:(t + 1) * 128],
                       in0=self.ptr[xi][:, k, :], scalar1=self.G[:, ni, k, w:w + 1], scalar2=self.modv[:, vi * 8 + k, w:w + 1],
                       op0=OP.mult, op1=OP.add)

    def xupd(self, t, nh, b):
        w = which(t)
        self.cnt += 1
        mi = self.cnt % 2
        self.v('dve', 'tensor_tensor', [('pb', b), 'gaB'], [('xt', mi)], out=self.xt[mi][:], in0=self.pb[b][:],
               in1=self.gaB[:, w, nh * 512:(nh + 1) * 512], op=OP.mult)
        self.v('pool', 'tensor_tensor', [('xt', mi), ('xs', t)], [('xs', t)],
               out=self.xs[:, t, nh * 512:(nh + 1) * 512], in0=self.xs[:, t, nh * 512:(nh + 1) * 512],
               in1=self.xt[mi][:], op=OP.add)

    def xadd(self, t, nh, b):
        self.v('dve', 'tensor_tensor', [('pb', b), ('xs', t)], [('xs', t)], out=self.xs[:, t, nh * 512:(nh + 1) * 512],
               in0=self.xs[:, t, nh * 512:(nh + 1) * 512], in1=self.pb[b][:], op=OP.add)

    def blocks(self, lo=0, hi=TOK, step=512):
        return [(o, min(step, hi - o)) for o in range(lo, hi, step)]

    def mixers(self, l, last):
        if 'F' in BRANCHES:
            self.fourier(l, last)
        if 'M' in BRANCHES:
            self.mlstm(l, last)
        if 'A' in BRANCHES:
            self.mla(l, last)

    def mlstm(self, l, last):
        nc, P, d = self.nc, self.P, self.d
        win = d['w_in'][l]
        with Scope(self) as SO:
            acol = SO.sb("acol", [128, NT, 8], F32)
            Mcol = SO.sb("Mcol", [128, NT, 8], F32)
            T4 = SO.sb("T4", [128, 5, NT, 8], F32)
            gm = SO.sb("gm", [128, 384], F32)
            E2 = SO.sb("E2", [8, 2, 8], F32)
            negid = SO.sb("negid", [128, 128], F32)
            mlmask = SO.sb("mlmask", [128, 2, 256], BF16)
            self.dma('sp', gm[:], d['gmB'][:, l, :], [], ['gm'], 'mw')
            self.dma('sp', E2[:], d['E2'][:, :, :], [], ['E2'], 'mw')
            self.dma('sp', negid[:], d['negidf'][:, :], [], ['negid'], 'mw')
            self.dma('sp', mlmask[:], d['mlmask'][:, :, :], [], ['mlmask'], 'mw')
            with Scope(self) as S:
                wg = S.sb("wg", [128, 8, 16], BF16)
                bg = S.sb("bg", [8, 2], F32)
                nbf = S.sb("nbf", [8, 1], F32)
                IG = S.sb("IG", [8, TOK], F32)
                LF = S.sb("LF", [8, TOK], F32)
                FF = S.sb("FF", [8, TOK], F32)
                FB = S.sb("FB", [8, TOK], F32)
                AFr = S.sb("AFr", [8, TOK], F32)
                ABr = S.sb("ABr", [8, TOK], F32)
                R1 = S.sb("R1", [8, NT, 8], F32)
                R2 = S.sb("R2", [8, NT, 8], F32)
                mcol = S.sb("mcol", [128, NT, 8], F32)
                MendB = S.sb("MendB", [128, NT, 8], F32)
                MP = S.sb("MP", [128, NT, 8], F32)
                ex = S.sb("ex", [128, 5, NT, 8], F32)
                self.dma('pool', wg[:], d['w_g'][l].rearrange("(k p) c -> p k c", p=128), [], ['wg'], 'gw')
                self.dma('sp', bg[:], d['bg'][l], [], ['bg'], 'gw')
                self.v('dve', 'tensor_scalar', ['bg'], ['nbf'], out=nbf[:], in0=bg[:, 1:2], scalar1=-1.0, scalar2=None, op0=OP.mult)
                nb = 0
                for (o, n) in self.blocks():
                    hk = [('hT', t) for t in range(o // 128, (o + n) // 128)]
                    bi, bf_ = nb % 6, (nb + 1) % 6
                    nb += 2
                    for k in range(8):
                        self.mm(self.pb[bi][0:8, 0:n], wg[:, k, 0:8], self.hT[:, k, o:o + n], k == 0, k == 7, hk + ['wg'], [('pb', bi)])
                    for k in range(8):
                        self.mm(self.pb[bf_][0:8, 0:n], wg[:, k, 8:16], self.hT[:, k, o:o + n], k == 0, k == 7, hk + ['wg'], [('pb', bf_)])
                    self.act(IG[:, o:o + n], self.pb[bi][0:8, 0:n], AF.Identity, [('pb', bi), 'bg'], ['IG'], bias=bg[:, 0:1])
                    self.act(LF[:, o:o + n], self.pb[bf_][0:8, 0:n], AF.Exp, [('pb', bf_), 'nbf'], ['LF'], bias=nbf[:], scale=-1.0)
                self.act(LF[:], LF[:], AF.Ln, ['LF'], ['LF'], bias=1.0)
                self.v('dve', 'tensor_scalar', ['LF'], ['LF'], out=LF[:], in0=LF[:], scalar1=-1.0, scalar2=None, op0=OP.mult)
                one_b = lambda n: self.onesf[0:8, 0:1].to_broadcast([8, n])
                self.v('dve', 'tensor_tensor_scan', ['LF', 'onesf'], ['FF'], out=FF[:], data0=one_b(TOK), data1=LF[:], initial=0.0,
                       op0=OP.mult, op1=OP.add)
                self.v('dve', 'tensor_tensor_scan', ['LF', 'onesf'], ['FB'], out=FB[:, 255::-1], data0=one_b(256), data1=LF[:, 255::-1],
                       initial=0.0, op0=OP.mult, op1=OP.add)
                self.v('dve', 'tensor_tensor_scan', ['LF', 'onesf', 'FB'], ['FB'], out=FB[:, 2303:255:-1], data0=one_b(2048),
                       data1=LF[:, 2303:255:-1], initial=FB[:, 0:1], op0=OP.mult, op1=OP.add)
                self.v('dve', 'tensor_tensor', ['IG', 'FF'], ['AFr'], out=AFr[:], in0=IG[:], in1=FF[:], op=OP.subtract)
                self.v('dve', 'tensor_tensor', ['IG', 'FB'], ['ABr'], out=ABr[:], in0=IG[:], in1=FB[:], op=OP.subtract)
                MF, MB = IG, LF
                self.v('dve', 'tensor_tensor_scan', ['AFr'], ['IG'], out=MF[:], data0=AFr[:], data1=AFr[:], initial=0.0, op0=OP.max, op1=OP.max)
                self.v('dve', 'tensor_tensor_scan', ['ABr'], ['LF'], out=MB[:, 255::-1], data0=ABr[:, 255::-1], data1=ABr[:, 255::-1],
                       initial=0.0, op0=OP.max, op1=OP.max)
                self.v('dve', 'tensor_tensor_scan', ['ABr', 'LF'], ['LF'], out=MB[:, 2303:255:-1], data0=ABr[:, 2303:255:-1],
                       data1=ABr[:, 2303:255:-1], initial=MB[:, 0:1], op0=OP.max, op1=OP.max)
                self.v('dve', 'tensor_tensor', ['FF', 'IG'], ['FF'], out=FF[:], in0=FF[:], in1=MF[:], op=OP.add)
                self.v('dve', 'tensor_tensor', ['FB', 'LF'], ['FB'], out=FB[:], in0=FB[:], in1=MB[:], op=OP.add)
                for qi, (XF, XB, kf, kb) in enumerate(((AFr, ABr, 'AFr', 'ABr'), (MF, MB, 'IG', 'LF'), (FF, FB, 'FF', 'FB'))):
                    for c in range(NT):
                        self.mm(self.pb[qi][:, c * 8:(c + 1) * 8], XF[:, c * 128:(c + 1) * 128], E2[:, 0, :], True, False, [kf, 'E2'], [('pb', qi)])
                        self.mm(self.pb[qi][:, c * 8:(c + 1) * 8], XB[:, c * 128:(c + 1) * 128], E2[:, 1, :], False, True, [kb, 'E2'], [('pb', qi)])
                for qi, (dst, kd) in enumerate(((acol, 'acol'), (Mcol, 'Mcol'), (mcol, 'mcol'))):
                    self.act(dst[:].rearrange("p c e -> p (c e)"), self.pb[qi][:, 0:NT * 8], AF.Identity, [('pb', qi)], [kd])
                self.v('dve', 'tensor_tensor', ['E2', 'IG'], ['R1'], out=R1[:], in0=E2[:, 0, :].unsqueeze(1).to_broadcast([8, NT, 8]),
                       in1=MF[:, 127::128].unsqueeze(2).to_broadcast([8, NT, 8]), op=OP.mult)
                self.v('dve', 'tensor_tensor', ['E2', 'LF'], ['R2'], out=R2[:], in0=E2[:, 1, :].unsqueeze(1).to_broadcast([8, NT, 8]),
                       in1=MB[:, 0::128].unsqueeze(2).to_broadcast([8, NT, 8]), op=OP.mult)
                self.v('dve', 'tensor_tensor', ['R1', 'R2'], ['R1'], out=R1[:], in0=R1[:], in1=R2[:], op=OP.add)
                self.mm(self.pb[3][:, 0:NT * 8], self.onesf[0:8, :], R1[:].rearrange("p c e -> p (c e)"), True, True, ['R1', 'onesf'], [('pb', 3)])
                self.act(MendB[:].rearrange("p c e -> p (c e)"), self.pb[3][:, 0:NT * 8], AF.Identity, [('pb', 3)], ['MendB'])
                self.v('dve', 'memset', [], ['MP'], MP[:], 0.0)
                self.v('dve', 'tensor_copy', ['MendB', 'MP'], ['MP'], out=MP[:, 1:NT, 0:4], in_=MendB[:, 0:NT - 1, 0:4])
                self.v('dve', 'tensor_copy', ['MendB', 'MP'], ['MP'], out=MP[:, 0:1, 4:8], in_=MendB[:, 1:2, 4:8])
                self.v('dve', 'tensor_copy', ['MendB', 'MP'], ['MP'], out=MP[:, 2:NT - 1, 4:8], in_=MendB[:, 3:NT, 4:8])
                self.v('dve', 'tensor_copy', ['MendB', 'MP'], ['MP'], out=MP[:, NT - 1:NT, 4:8], in_=MendB[:, 0:1, 4:8])
                self.v('dve', 'tensor_tensor', ['MP', 'Mcol'], ['ex'], out=ex[:, 0], in0=MP[:], in1=Mcol[:], op=OP.subtract)
                self.v('dve', 'tensor_tensor', ['acol', 'MendB'], ['ex'], out=ex[:, 1], in0=acol[:], in1=MendB[:], op=OP.subtract)
                self.v('dve', 'tensor_tensor', ['MP', 'MendB'], ['ex'], out=ex[:, 2], in0=MP[:], in1=MendB[:], op=OP.subtract)
                self.v('dve', 'tensor_scalar', ['mcol'], ['ex'], out=ex[:, 3], in0=mcol[:], scalar1=-1.0, scalar2=None, op0=OP.mult)
                self.v('dve', 'tensor_tensor', ['acol', 'MP'], ['ex'], out=ex[:, 4], in0=acol[:], in1=MP[:], op=OP.subtract)
                self.act(T4[:].rearrange("p a c e -> p (a c e)"), ex[:].rearrange("p a c e -> p (a c e)"), AF.Exp, ['ex'], ['T4'])
            for hp in range(2):
                self.mlstm_pair(l, last, hp, acol, Mcol, T4, gm, negid, mlmask)

    def mlstm_pair(self, l, last, hp, acol, Mcol, T4, gm, negid, mlmask):
        nc, P, d = self.nc, self.P, self.d
        win = d['w_in'][l]
        QS = 96.0 ** -0.5
        colof = lambda T: T + 2 if T < 256 else T + 6
        cblocks = [(0, 256)] + [(256 + i * 512, 512) for i in range(4)]
        with Scope(self) as SP:
            post = SP.sb("post", [96, 4, TOK], BF16)
            vp = SP.sb("vp", [128, NT, 2, 97], BF16)
            sg = SP.sb("sg", [128, NT, 192], BF16)
            S = Scope(self)
            wvo = S.sb("wvo", [128, 8, 384], BF16)
            self.dma('pool', wvo[:, :, 0:192], win[:, 1024 + 192 * hp:1024 + 192 * (hp + 1)].rearrange("(k p) c -> p k c", p=128), [], ['wvo'], 'mw')
            self.dma('pool', wvo[:, :, 192:384], win[:, 1408 + 192 * hp:1408 + 192 * (hp + 1)].rearrange("(k p) c -> p k c", p=128), [], ['wvo'], 'mw')
            self.v('dve', 'memset', [], ['vp'], vp[:], 1.0)
            with S:
                wqk = S.sb("wqk", [128, 8, 4, 96], BF16)
                dgw = S.sb("dgw", [96, 4, 5, 96], BF16)
                pre = S.sb("pre", [96, 4, 2312], BF16)
                sgt = [S.sb("sgt%d" % i, [96, 512], F32) for i in range(1)]
                for jj in range(4):
                    c0 = (256 if jj < 2 else 640) + 96 * (2 * hp + jj % 2)
                    self.dma('pool', wqk[:, :, jj, :], win[:, c0:c0 + 96].rearrange("(k p) c -> p k c", p=128), [], ['wqk'], 'mw')
                    jg = (0 if jj < 2 else 4) + 2 * hp + jj % 2
                    self.dma('sp', dgw[:, jj, :, :], d['convdg'][l, :, jg, :, :], [], ['dgw'], 'mw')
                self.v('dve', 'memset', [], ['pre'], pre[:], 0.0)
                nb = 0
                for jj in range(4):
                    for (o, n) in cblocks:
                        b = nb % 6
                        nb += 1
                        hk = [('hT', t) for t in range(o // 128, (o + n) // 128)]
                        for k in range(8):
                            self.mm(self.pb[b][0:96, 0:n], wqk[:, k, jj, :], self.hT[:, k, o:o + n], k == 0, k == 7, hk + ['wqk'], [('pb', b)])
                        self.act(pre[:, jj, colof(o):colof(o) + n], self.pb[b][0:96, 0:n], AF.Identity, [('pb', b), 'pre'], [('pre', jj)])
                for t in range(NT):
                    b = nb % 6
                    nb += 1
                    for k in range(8):
                        self.mm(self.pb[b][:, 0:384], self.hT[:, k, t * 128:(t + 1) * 128], wvo[:, k, :], k == 0, k == 7, [('hT', t), 'wvo'], [('pb', b)])
                    self.act(vp[:, t, :, 0:96], self.pb[b][:, 0:192].rearrange("p (h c) -> p h c", c=96), AF.Identity, [('pb', b), 'vp'], [('vp', t)])
                    self.act(sg[:, t, :], self.pb[b][:, 192:384], AF.Sigmoid, [('pb', b)], [('sg', t)])
                for jj in range(4):
                    for (o, n) in cblocks:
                        b = nb % 6
                        nb += 1
                        c0 = colof(o)
                        for tap in range(5):
                            self.mm(self.pb[b][0:96, 0:n], dgw[:, jj, tap, :], pre[:, jj, c0 + tap - 2:c0 + tap - 2 + n], tap == 0, tap == 4,
                                    [('pre', jj), 'dgw'], [('pb', b)])
                        tk = [('post', jj, t) for t in range(o // 128, (o + n) // 128)]
                        if jj < 2:
                            si = 0
                            self.act(sgt[si][:, 0:n], self.pb[b][0:96, 0:n], AF.Sigmoid, [('pb', b)], [('sgt', si)])
                            self.v('dve', 'scalar_tensor_tensor', [('pb', b), ('sgt', si)], tk, out=post[:, jj, o:o + n], in0=self.pb[b][0:96, 0:n],
                                   scalar=QS, in1=sgt[si][:, 0:n], op0=OP.mult, op1=OP.mult)
                        else:
                            self.act(post[:, jj, o:o + n], self.pb[b][0:96, 0:n], AF.Silu, [('pb', b)], tk)
            ktok = SP.sb("ktok", [128, NT, 2, 96], BF16)
            for t in range(NT):
                pi = t % 2
                for hh in range(2):
                    self.tr(self.ptr[pi][:, hh, 0:96], post[:, 2 + hh, t * 128:(t + 1) * 128], self.identb[0:96, 0:96],
                            [('post', 2 + hh, t), 'identb'], [('ptr', pi)])
                self.act(ktok[:, t, :, :], self.ptr[pi][:, 0:2, 0:96], AF.Identity, [('ptr', pi)], [('ktok', t)])
            hF = SP.sb("hF", [128, NT, 2, 96], BF16)
            hB = SP.sb("hB", [128, NT, 2, 96], BF16)
            with Scope(self) as S:
                vu = [S.sb("vu%d" % i, [128, 2, 97], BF16) for i in range(3)]
                PT = [S.sb("PT%d" % i, [128, 2, 128], BF16) for i in range(3)]
                tmpn = [S.sb("tmpn%d" % i, [128, 2, 97], F32) for i in range(2)]
                dn = [S.sb("dn%d" % i, [128, 2], F32) for i in range(2)]
                vw = [S.sb("vw%d" % i, [128, 2, 97], BF16) for i in range(2)]
                Cst = S.sb("Cst", [96, 2, 97], F32)
                Cball = S.sb("Cball", [96, NT, 2, 97], BF16)
                for dr in range(2):
                    order = list(range(NT)) if dr == 0 else [1, 0] + list(range(NT - 1, 1, -1))
                    cf0 = dr * 4 + 2 * hp
                    self.v('dve', 'memset', ['Cst'], ['Cst'], Cst[:], 0.0)
                    self.v('dve', 'memset', [('Cball', 0)], [('Cball', 0)], Cball[:, 0], 0.0)

                    def kv_part(it):
                        c = order[it]
                        i2 = it % 2
                        self.v('dve', 'tensor_tensor', [('vp', c), 'T4'], [('vw', i2)], out=vw[i2][:], in0=vp[:, c],
                               in1=T4[:, 1, c, cf0:cf0 + 2].unsqueeze(2).to_broadcast([128, 2, 97]), op=OP.mult)
                        for hh in range(2):
                            self.mm(self.pb[4 + i2][0:96, hh * 97:(hh + 1) * 97], ktok[:, c, hh, :], vw[i2][:, hh, :], True, True,
                                    [('ktok', c), ('vw', i2)], [('pb', 4 + i2)])

                    kv_part(0)
                    for it in range(NT - 1):
                        c = order[it]
                        i2 = it % 2
                        if it + 1 < NT - 1:
                            kv_part(it + 1)
                        self.v('dve', 'tensor_tensor', ['Cst', 'T4'], ['Cst'], out=Cst[:], in0=Cst[:],
                               in1=T4[0:96, 2, c, cf0:cf0 + 2].unsqueeze(2).to_broadcast([96, 2, 97]), op=OP.mult)
                        self.v('dve', 'tensor_tensor', ['Cst', ('pb', 4 + i2)], ['Cst'], out=Cst[:], in0=Cst[:],
                               in1=self.pb[4 + i2][0:96, 0:194].rearrange("p (h e) -> p h e", e=97), op=OP.add)
                        self.act(Cball[:, it + 1], Cst[:], AF.Identity, ['Cst'], [('Cball', it + 1)])

                    def part_i(it):
                        c = order[it]
                        i3 = it % 3
                        if last and c < 2:
                            return
                        tsl = slice(c * 128, (c + 1) * 128)
                        bo = 3 + it % 3
                        for hh in range(2):
                            self.mm(self.pb[2][:, hh * 128:(hh + 1) * 128], post[:, 2 + hh, tsl], post[:, hh, tsl], True, True,
                                    [('post', 2 + hh, c), ('post', hh, c)], [('pb', 2)])
                        self.v('dve', 'tensor_tensor', [('pb', 2), 'mlmask'], [('PT', i3)], out=PT[i3][:].rearrange("p h t -> p (h t)"),
                               in0=self.pb[2][:, 0:256], in1=mlmask[:, dr, :], op=OP.mult)
                        for hh in range(2):
                            self.act(vu[i3][:, hh, :], vp[:, c, hh, :], AF.Identity, [('vp', c), 'T4'], [('vu', i3)],
                                     scale=T4[:, 4, c, cf0 + hh:cf0 + hh + 1])
                        for hh in range(2):
                            self.mm(self.pb[bo][:, hh * 97:(hh + 1) * 97], PT[i3][:, hh, :], vu[i3][:, hh, :], True, False,
                                    [('PT', i3), ('vu', i3)], [('pb', bo)])
                            self.mm(self.pb[bo][:, hh * 97:(hh + 1) * 97], post[:, hh, tsl], Cball[:, it, hh, :], False, True,
                                    [('post', hh, c), ('Cball', it)], [('pb', bo)])

                    def part_d(it):
                        c = order[it]
                        i2 = it % 2
                        if last and c < 2:
                            return
                        bo = 3 + it % 3
                        self.v('dve', 'tensor_tensor', [('pb', bo), 'T4'], [('tmpn', i2)], out=tmpn[i2][:],
                               in0=self.pb[bo][:, 0:194].rearrange("p (h e) -> p h e", e=97),
                               in1=T4[:, 0, c, cf0:cf0 + 2].unsqueeze(2).to_broadcast([128, 2, 97]), op=OP.mult)
                        self.v('dve', 'scalar_tensor_tensor', [('tmpn', i2)], [('dn', i2)], out=dn[i2][:], in0=tmpn[i2][:, :, 96], scalar=-1.0,
                               in1=tmpn[i2][:, :, 96], op0=OP.mult, op1=OP.max)
                        self.v('dve', 'tensor_tensor', [('dn', i2), 'T4'], [('dn', i2)], out=dn[i2][:], in0=dn[i2][:],
                               in1=T4[:, 3, c, cf0:cf0 + 2], op=OP.max)
                        self.v('dve', 'reciprocal', [('dn', i2)], [('dn', i2)], out=dn[i2][:], in_=dn[i2][:])
                        hdst, hkey = (hF[:, c], ('hF', c)) if dr == 0 else (hB[:, c], ('hB', c))
                        self.v('dve', 'tensor_tensor', [('tmpn', i2), ('dn', i2)], [hkey], out=hdst, in0=tmpn[i2][:, :, 0:96],
                               in1=dn[i2][:].unsqueeze(2).to_broadcast([128, 2, 96]), op=OP.mult)

                    part_i(0)
                    part_i(1)
                    for it in range(NT):
                        if it + 2 < NT:
                            part_i(it + 2)
                        part_d(it)

            with Scope(self) as S:
                hsC = S.sb("hsC", [128, 4, 2, 96], F32)
                hqC = S.sb("hqC", [128, 4, 2, 96], F32)
                ssC = S.sb("ssC", [128, 8], F32)
                ybC = S.sb("ybC", [128, 4, 192], BF16)
                yhT = [S.sb("yhT%d" % i, [96, 2, 128], BF16) for i in range(2)]
                woM = S.sb("woM", [96, 2, DM], BF16)
                self.dma('pool', woM[:], d['w_out'][l][256 + 192 * hp:256 + 192 * (hp + 1), :].rearrange("(h p) c -> p h c", p=96), [], ['woM'], 'mw')
                woMs = [woM]
                if not last:
                    woMs.append(S.sb("woMc", [96, 2, DM], BF16))
                    self.v('dve', 'tensor_tensor', ['woM', 'gaB'], [('woMs', 1)], out=woMs[1][:], in0=woM[:],
                           in1=self.gaB[0:96, 1, :].unsqueeze(1).to_broadcast([96, 2, DM]), op=OP.mult)
                self.v('dve', 'tensor_tensor', ['woM', 'gaB'], ['woM', ('woMs', 0)], out=woM[:], in0=woM[:],
                       in1=self.gaB[0:96, 0, :].unsqueeze(1).to_broadcast([96, 2, DM]), op=OP.mult)
                otiles = list(range(2, NT)) if last else list(range(NT))
                for g0 in range(0, len(otiles), 4):
                    g = otiles[g0:g0 + 4]
                    ng = len(g)
                    c0, c1 = g[0], g[-1] + 1
                    hk = [('hF', c) for c in g] + [('hB', c) for c in g]
                    self.v('dve', 'tensor_tensor', hk, ['hsC'], out=hsC[:, 0:ng], in0=hF[:, c0:c1], in1=hB[:, c0:c1], op=OP.add)
                    self.v('dve', 'tensor_tensor', ['hsC'], ['hqC'], out=hqC[:, 0:ng], in0=hsC[:, 0:ng], in1=hsC[:, 0:ng], op=OP.mult)
                    self.v('dve', 'tensor_reduce', ['hqC'], ['ssC'], out=ssC[:, 0:2 * ng],
                           in_=hqC[:, 0:ng].rearrange("p c h e -> p (c h) e"), axis=AX.X, op=OP.add)
                    self.act(ssC[:, 0:2 * ng], ssC[:, 0:2 * ng], AF.Ln, ['ssC', 'epsT'], ['ssC'], bias=self.epsT[:], scale=1.0 / 96)
                    self.act(ssC[:, 0:2 * ng], ssC[:, 0:2 * ng], AF.Exp, ['ssC'], ['ssC'], scale=-0.5)
                    self.v('dve', 'tensor_tensor', ['hsC', 'ssC'], ['hqC'], out=hqC[:, 0:ng].rearrange("p c h e -> p (c h) e"),
                           in0=hsC[:, 0:ng].rearrange("p c h e -> p (c h) e"),
                           in1=ssC[:, 0:2 * ng].unsqueeze(2).to_broadcast([128, 2 * ng, 96]), op=OP.mult)
                    self.v('dve', 'tensor_tensor', ['hqC', 'gm'], ['hqC'], out=hqC[:, 0:ng].rearrange("p c h e -> p c (h e)"),
                           in0=hqC[:, 0:ng].rearrange("p c h e -> p c (h e)"),
                           in1=gm[:, 192 * hp:192 * (hp + 1)].unsqueeze(1).to_broadcast([128, ng, 192]), op=OP.mult)
                    self.v('dve', 'tensor_tensor', ['hqC'] + [('sg', c) for c in g], ['ybC'], out=ybC[:, 0:ng],
                           in0=hqC[:, 0:ng].rearrange("p c h e -> p c (h e)"), in1=sg[:, c0:c1, :], op=OP.mult)
                    for ci, c in enumerate(g):
                        i2 = c % 2
                        for hh in range(2):
                            self.tr(self.ptr[i2][0:96, hh, :], ybC[:, ci, hh * 96:(hh + 1) * 96], self.identb[:], ['ybC', 'identb'], [('ptr', i2)])
                        self.act(yhT[i2][:], self.ptr[i2][0:96, 0:2, :], AF.Identity, [('ptr', i2)], [('yhT', i2)])
                        for nh in range(2):
                            b = 2 * i2 + nh
                            for hh in range(2):
                                self.mm(self.pb[b][:], yhT[i2][:, hh, :], woMs[which(c)][:, hh, nh * 512:(nh + 1) * 512], hh == 0, hh == 1,
                                        [('yhT', i2), ('woMs', which(c))], [('pb', b)])
                            self.xadd(c, nh, b)

    def mla(self, l, last):
        nc, P, d = self.nc, self.P, self.d
        SCALE = 96.0 ** -0.5
        with Scope(self) as S:
            wA = S.sb("wA", [128, 8, 416], BF16)
            wuq = S.sb("wuq", [128, 2, 384], BF16)
            wuqs = S.sb("wuqs", [128, 2, 4, 96], BF16)
            wkr = S.sb("wkr", [128, 8, 96], BF16)
            wkrs = S.sb("wkrs", [128, 8, 96], BF16)
            wukv = S.sb("wukv", [128, 640], BF16)
            woA = S.sb("woA", [96, 4, DM], BF16)
            ropeC = S.sb("ropeC", [96, TOK], BF16)
            ropeS = S.sb("ropeS", [96, TOK], BF16)
            Vt = S.sb("Vt", [128, NT, 384], BF16)
            cqnT = S.sb("cqnT", [128, 2, 512], BF16)
            ckvnT = S.sb("ckvnT", [128, 512], BF16)
            cqn = [S.sb("cqn%d" % i, [128, 256], BF16) for i in range(2)]
            ckvn = [S.sb("ckvn%d" % i, [128, 128], BF16) for i in range(2)]
            rt1 = S.sb("rt1", [96, 512], F32)
            rt2 = S.sb("rt2", [96, 512], F32)
            pT = [S.sb("pT%d" % i, [128, 512], BF16) for i in range(3)]
            rc = S.sb("rc", [96, 512], F32)
            yaT = S.sb("yaT", [96, 4, 512], BF16)
            onesb = S.sb("onesb", [128, 96], BF16)
            st = [S.sb("st%d" % i, [128, 2], F32) for i in range(2)]
            rs = [S.sb("rs%d" % i, [128, 2], F32) for i in range(2)]
            gq = S.sb("gq", [128, 2, 2], F32)
            gkv = S.sb("gkv", [128, 2], F32)
            qT = self.hT[0:96, 0:4, :]
            kT = self.hT[0:96, 4:8, :]
            win = d['w_in'][l]
            self.dma('pool', wA[:], win[:, 1808:2224].rearrange("(k p) c -> p k c", p=128), [], ['wA'], 'aw')
            self.dma('pool', wuq[:], d['w_uq'][l].rearrange("(k p) c -> p k c", p=128), [], ['wuq'], 'aw')
            self.v('dve', 'memset', [], ['wuqs'], wuqs[:], 0.0)
            self.v('dve', 'memset', [], ['wkr'], wkr[:], 0.0)
            self.v('dve', 'memset', [], ['wkrs'], wkrs[:], 0.0)
            self.v('dve', 'memset', [], ['onesb'], onesb[:], 1.0)
            for c in range(2):
                self.dma('pool', wuqs[:, c, :, 64:96], d['w_uq_sw'][l][c * 128:(c + 1) * 128, :, :], ['wuqs'], ['wuqs'], 'aw')
            self.dma('pool', wkr[:, :, 64:96], win[:, 2192:2224].rearrange("(k p) c -> p k c", p=128), ['wkr'], ['wkr'], 'aw')
            self.dma('pool', wkrs[:, :, 64:96], d['w_kr_sw'][l].rearrange("(k p) c -> p k c", p=128), ['wkrs'], ['wkrs'], 'aw')
            self.dma('pool', wukv[:], d['w_ukv'][l], [], ['wukv'], 'aw')
            self.dma('pool', woA[:], d['w_out'][l][640:1024, :].rearrange("(h p) c -> p h c", p=96), [], ['woA'], 'aw')
            self.dma('sp', ropeC[64:96, :], d['ropeC'][:, :], [], ['rope'], 'aw')
            self.dma('sp', ropeS[64:96, :], d['ropeS'][:, :], [], ['rope'], 'aw')
            self.dma('sp', gq[:], d['gqfm'][:, :, :], [], ['gq'], 'aw')
            self.dma('sp', gkv[:], d['gkvfm'][:, :], [], ['gq'], 'aw')
            wv = wukv[:, :].rearrange("p (h c) -> p h c", c=160)[:, :, 64:160]
            for (o, n) in self.blocks():
                tl = list(range(o // 128, (o + n) // 128))
                hk = [('hT', t) for t in tl]
                def a1_front(t):
                    b1 = t % 2
                    for k in range(8):
                        self.mm(self.pb[b1][:, 0:416], self.hT[:, k, t * 128:(t + 1) * 128], wA[:, k, :], k == 0, k == 7,
                                [('hT', t), 'wA'], [('pb', b1)])
                a1_front(tl[0])
                for ti, t in enumerate(tl):
                    b1 = t % 2
                    if ti + 1 < len(tl):
                        a1_front(tl[ti + 1])
                    si = t % 2
                    self.v('dve', 'memset', [], [('st', si)], st[si][:], 0.0)
                    self.cnt += 1
                    self.act(self.junk[:, 0:256], self.pb[b1][:, 0:256], AF.Square, [('pb', b1)], [('junk', self.cnt), ('st', si)],
                             accum=st[si][:, 0:1])
                    self.cnt += 1
                    self.act(self.junk[:, 256:384], self.pb[b1][:, 256:384], AF.Square, [('pb', b1)], [('junk', self.cnt), ('st', si)],
                             accum=st[si][:, 1:2])
                    self.act(rs[si][:, 0:1], st[si][:, 0:1], AF.Sqrt, [('st', si), 'epsT'], [('rs', si)], bias=self.epsT[:], scale=1.0 / 256)
                    self.act(rs[si][:, 1:2], st[si][:, 1:2], AF.Sqrt, [('st', si), 'epsT'], [('rs', si)], bias=self.epsT[:], scale=1.0 / 128)
                    self.v('dve', 'reciprocal', [('rs', si)], [('rs', si)], out=rs[si][:], in_=rs[si][:])
                    self.v('dve', 'tensor_scalar', [('pb', b1), ('rs', si)], [('cqn', si)], out=cqn[si][:], in0=self.pb[b1][:, 0:256],
                           scalar1=rs[si][:, 0:1], scalar2=None, op0=OP.mult)
                    self.v('dve', 'tensor_scalar', [('pb', b1), ('rs', si)], [('ckvn', si)], out=ckvn[si][:], in0=self.pb[b1][:, 256:384],
                           scalar1=rs[si][:, 1:2], scalar2=None, op0=OP.mult)
                    pi = t % 2
                    for c in range(2):
                        self.tr(self.ptr[pi][:, c, :], cqn[si][:, c * 128:(c + 1) * 128], self.identb[:], [('cqn', si), 'identb'], [('ptr', pi)])
                    self.tr(self.ptr[pi][:, 2, :], ckvn[si][:], self.identb[:], [('ckvn', si), 'identb'], [('ptr', pi)])
                    for c in range(2):
                        self.act(cqnT[:, c, ti * 128:(ti + 1) * 128], self.ptr[pi][:, c, :], AF.Identity, [('ptr', pi), 'gq'], ['cqnT'],
                                 scale=gq[:, l, c:c + 1])
                    self.act(ckvnT[:, ti * 128:(ti + 1) * 128], self.ptr[pi][:, 2, :], AF.Identity, [('ptr', pi), 'gq'], ['ckvnT'],
                             scale=gkv[:, l:l + 1])
                    bv = 2 + t % 2
                    self.mm(self.pb[bv][:, 0:384], ckvnT[:, ti * 128:(ti + 1) * 128], wv, True, True, ['ckvnT', 'wukv'], [('pb', bv)])
                    self.act(Vt[:, t, :], self.pb[bv][:, 0:384], AF.Identity, [('pb', bv)], [('Vt', t)])
                for k in range(8):
                    self.mm(self.pb[4][0:96, 0:n], wkr[:, k, :], self.hT[:, k, o:o + n], k == 0, k == 7, hk + ['wkr'], [('pb', 4)])
                for k in range(8):
                    self.mm(self.pb[5][0:96, 0:n], wkrs[:, k, :], self.hT[:, k, o:o + n], k == 0, k == 7, hk + ['wkrs'], [('pb', 5)])
                for h in range(4):
                    for c in range(2):
                        self.mm(self.pb[0][0:96, 0:n], wuq[:, c, h * 96:(h + 1) * 96], cqnT[:, c, 0:n], c == 0, c == 1, ['wuq', 'cqnT'], [('pb', 0)])
                    for c in range(2):
                        self.mm(self.pb[1][0:96, 0:n], wuqs[:, c, h, :], cqnT[:, c, 0:n], c == 0, c == 1, ['wuqs', 'cqnT'], [('pb', 1)])
                    bk = 2 + h % 2
                    self.mm(self.pb[bk][0:64, 0:n], wukv[:, h * 160:h * 160 + 64], ckvnT[:, 0:n], True, True, ['wukv', 'ckvnT'], [('pb', bk)])
                    self.act(qT[0:64, h, o:o + n], self.pb[0][0:64, 0:n], AF.Identity, [('pb', 0)], hk)
                    self.act(kT[0:64, h, o:o + n], self.pb[bk][0:64, 0:n], AF.Identity, [('pb', bk)], hk)
                    self.v('dve', 'tensor_tensor', [('pb', 0), 'rope'], ['rt1'], out=rt1[64:96, 0:n], in0=self.pb[0][64:96, 0:n],
                           in1=ropeC[64:96, o:o + n], op=OP.mult)
                    self.v('dve', 'tensor_tensor', [('pb', 1), 'rope'], ['rt2'], out=rt2[64:96, 0:n], in0=self.pb[1][64:96, 0:n],
                           in1=ropeS[64:96, o:o + n], op=OP.mult)
                    self.v('pool', 'tensor_tensor', ['rt1', 'rt2'], hk, out=qT[64:96, h, o:o + n], in0=rt1[64:96, 0:n],
                           in1=rt2[64:96, 0:n], op=OP.add)
                self.v('dve', 'tensor_tensor', [('pb', 4), 'rope'], ['rt1'], out=rt1[64:96, 0:n], in0=self.pb[4][64:96, 0:n],
                       in1=ropeC[64:96, o:o + n], op=OP.mult)
                self.v('dve', 'tensor_tensor', [('pb', 5), 'rope'], ['rt2'], out=rt2[64:96, 0:n], in0=self.pb[5][64:96, 0:n],
                       in1=ropeS[64:96, o:o + n], op=OP.mult)
                for h in range(4):
                    self.v('pool', 'tensor_tensor', ['rt1', 'rt2'], hk, out=kT[64:96, h, o:o + n], in0=rt1[64:96, 0:n],
                           in1=rt2[64:96, 0:n], op=OP.add)
            qblocks = [(256 + i * 512, 512, list(range(NT))) for i in range(4)]
            if not last:
                qblocks.append((0, 256, [0, 1]))
            for (qo, qn, ktiles) in qblocks:
                qk = [('hT', t) for t in range(qo // 128, (qo + qn) // 128)]
                for h in range(4):
                    def st_mm(i):
                        kt_ = ktiles[i]
                        self.mm(self.pb[i % 2][:, 0:qn], kT[:, h, kt_ * 128:(kt_ + 1) * 128], qT[:, h, qo:qo + qn], True, True,
                                qk + [('hT', kt_)], [('pb', i % 2)])
                    st_mm(0)
                    for i, kt in enumerate(ktiles):
                        sb_ = i % 2
                        pi = i % 3
                        if i + 1 < len(ktiles):
                            st_mm(i + 1)
                        self.act(pT[pi][:, 0:qn], self.pb[sb_][:, 0:qn], AF.Exp, [('pb', sb_)], [('pT', pi)], scale=SCALE)
                        self.mm(self.pb[2][0:96, 0:qn], Vt[:, kt, h * 96:(h + 1) * 96], pT[pi][:, 0:qn], i == 0, i == len(ktiles) - 1,
                                [('Vt', kt), ('pT', pi)], [('pb', 2)])
                        self.mm(self.pb[3][0:96, 0:qn], onesb[:, :], pT[pi][:, 0:qn], i == 0, i == len(ktiles) - 1,
                                ['onesb', ('pT', pi)], [('pb', 3)])
                    self.v('dve', 'reciprocal', [('pb', 3)], ['rc'], out=rc[:, 0:qn], in_=self.pb[3][0:96, 0:qn])
                    self.v('dve', 'tensor_tensor', [('pb', 2), 'rc'], [('yaT', h)], out=yaT[:, h, 0:qn], in0=self.pb[2][0:96, 0:qn],
                           in1=rc[:, 0:qn], op=OP.mult)
                for ti, t in enumerate(range(qo // 128, (qo + qn) // 128)):
                    for nh in range(2):
                        b = 4 + nh
                        for h in range(4):
                            self.mm(self.pb[b][:], yaT[:, h, ti * 128:(ti + 1) * 128], woA[:, h, nh * 512:(nh + 1) * 512], h == 0, h == 3,
                                    [('yaT', h), 'woA'], [('pb', b)])
                        self.xupd(t, nh, b)

    def fourier(self, l, last):
        nc, P, d = self.nc, self.P, self.d
        with Scope(self) as S:
            wpf = S.sb("wpf", [128, 8, 256], BF16)
            pfT = S.sb("pfT", [128, 2, TOK], BF16)
            bd = S.sb("bd", [128, 256], BF16)
            pfCS = S.sb("pfCS", [128, NT, 2, 256], BF16)
            tw = [S.sb("tw%d" % i, [128, 2, 1024], BF16) for i in range(3)]
            yfT = S.sb("yfT", [128, 2, TOK], BF16)
            wo = S.sb("wo", [128, 2, DM], BF16)
            c256 = S.sb("c256", [128, 2, 2, 256], BF16)
            self.dma('pool', wpf[:], d['w_in'][l][:, 0:256].rearrange("(k p) c -> p k c", p=128), [], ['wpf'], 'fw')
            self.dma('pool', wo[:], d['w_out'][l][0:256, :].rearrange("(k p) c -> p k c", p=128), [], ['wo'], 'fw')
            self.dma('sp', bd[:], d['bdcs'][:, :], [], ['bd'], 'fw')
            self.dma('sp', c256[:], d['c256'][:, :, :, :], [], ['c256'], 'fw')
            nb = 0
            for j in range(2):
                for (o, n) in self.blocks():
                    b = nb % 6
                    nb += 1
                    for k in range(8):
                        self.mm(self.pb[b][:, 0:n], wpf[:, k, j * 128:(j + 1) * 128], self.hT[:, k, o:o + n], k == 0, k == 7,
                                ['wpf'] + [('hT', t) for t in range(o // 128, (o + n) // 128)], [('pb', b)])
                    self.act(pfT[:, j, o:o + n], self.pb[b][:, 0:n], AF.Identity, [('pb', b)], [('pfT', j, o)])
            for t in range(NT):
                for j in range(2):
                    b = nb % 6
                    nb += 1
                    self.mm(self.pb[b][:, 0:256], pfT[:, j, t * 128:(t + 1) * 128], bd[:], True, True,
                            [('pfT', j, (t // 4) * 512), 'bd'], [('pb', b)])
                    self.act(pfCS[:, t, j, :], self.pb[b][:, 0:256], AF.Identity, [('pb', b)], [('pfCS', t)])
            ntw = 0
            for half in range(2):
                for nt in range(16):
                    t = nt + 2
                    s = ntw % 3
                    ntw += 1
                    self.dma('sp', tw[s][:], d['twCS'][half, nt], [], [('tw', s)], 'tw%d' % s)
                    for j in range(2):
                        for q in range(2):
                            b = j * 2 + q
                            self.mm(self.pb[b][:], pfCS[:, t, j, 0:128], tw[s][:, 0, q * 512:(q + 1) * 512], nt == 0, False,
                                    [('pfCS', t), ('tw', s)], [('pb', b)])
                            self.mm(self.pb[b][:], pfCS[:, t, j, 128:256], tw[s][:, 1, q * 512:(q + 1) * 512], False, nt == 15,
                                    [('pfCS', t), ('tw', s)], [('pb', b)])
                for j in range(2):
                    for q in range(2):
                        b = j * 2 + q
                        o = 256 + half * 1024 + q * 512
                        self.act(yfT[:, j, o:o + 512], self.pb[b][:], AF.Identity, [('pb', b)], [('yfT', o // 128)])
            if not last:
                for j in range(2):
                    b = 4 + j
                    for nt in range(2):
                        self.mm(self.pb[b][:, 0:256], pfCS[:, nt, j, 0:128], c256[:, nt, 0, :], nt == 0, False,
                                [('pfCS', nt), 'c256'], [('pb', b)])
                        self.mm(self.pb[b][:, 0:256], pfCS[:, nt, j, 128:256], c256[:, nt, 1, :], False, nt == 1,
                                [('pfCS', nt), 'c256'], [('pb', b)])
                    self.act(yfT[:, j, 0:256], self.pb[b][:, 0:256], AF.Identity, [('pb', b)], [('yfT', 0)])
            nb = 0
            wos = [S.sb("wos%d" % w, [128, 2, DM], BF16) for w in range(1 if last else 2)]
            for w in range(len(wos)):
                self.v('dve', 'tensor_tensor', ['wo', 'gaB'], [('wos', w)], out=wos[w][:], in0=wo[:],
                       in1=self.gaB[:, w, :].unsqueeze(1).to_broadcast([128, 2, DM]), op=OP.mult)
            for t in (range(2, NT) if last else range(NT)):
                for nh in range(2):
                    b = nb % 6
                    nb += 1
                    for j in range(2):
                        self.mm(self.pb[b][:], yfT[:, j, t * 128:(t + 1) * 128], wos[which(t)][:, j, nh * 512:(nh + 1) * 512], j == 0, j == 1,
                                [('yfT', 0 if t < 2 else 2 + ((t - 2) // 4) * 4), ('wos', which(t))], [('pb', b)])
                    self.xadd(t, nh, b)

    def mlp(self, l, tiles):
        nc, P, d = self.nc, self.P, self.d
        groups = [tiles[i:i + 6] for i in range(0, len(tiles), 6)]
        with Scope(self) as S:
            uT = S.sb("uT", [128, 32, 768], BF16)
            wu = [S.sb("wu%d" % i, [128, 8, 512], BF16) for i in range(2)]
            wd = [S.sb("wd%d" % i, [128, 4, 512], BF16) for i in range(2)]
            tmp = [S.sb("mt%d" % i, [128, 512], F32) for i in range(2)]
            nu = nd = nb = 0
            for g in groups:
                G = 128 * len(g)
                t0 = g[0] * 128
                subs = [(0, G)] if G <= 512 else [(0, G // 2), (G // 2, G // 2)]
                for hb in range(8):
                    s = nu % 2
                    nu += 1
                    self.dma('pool', wu[s][:], d['w_up_r'][l, hb], [], [('wu', s)], 'wu%d' % s)
                    for cc in range(4):
                        j = hb * 4 + cc
                        for (o, n) in subs:
                            b = nb % 6
                            nb += 1
                            for k in range(8):
                                self.mm(self.pb[b][:, 0:n], wu[s][:, k, cc * 128:(cc + 1) * 128],
                                        self.hT[:, k, t0 + o:t0 + o + n], k == 0, k == 7,
                                        [('wu', s)] + [('hT', t) for t in g], [('pb', b)])
                            ri = nb % 2
                            self.act(tmp[ri][:, 0:n], self.pb[b][:, 0:n], AF.Relu, [('pb', b)], [('mt', ri)])
                            self.v('dve', 'tensor_tensor', [('mt', ri)], [('uT', j)], out=uT[:, j, o:o + n],
                                   in0=tmp[ri][:, 0:n], in1=tmp[ri][:, 0:n], op=OP.mult)
                for nh in range(2):
                    for jb in range(8):
                        s = nd % 2
                        nd += 1
                        self.dma('pool', wd[s][:], d['w_down_r'][l, nh, jb], [], [('wd', s)], 'wd%d' % s)
                        for jj in range(4):
                            j = jb * 4 + jj
                            for ti, t in enumerate(g):
                                self.mm(self.pb[ti][:], uT[:, j, ti * 128:(ti + 1) * 128], wd[s][:, jj, :], j == 0, j == 31,
                                        [('wd', s), ('uT', j)], [('pb', ti)])
                    for ti, t in enumerate(g):
                        self.xupd(t, nh, ti)

    def final(self):
        nc, P, d = self.nc, self.P, self.d
        with Scope(self) as S:
            gf = S.sb("gf", [128, DM], F32)
            ob = [S.sb("ob%d" % i, [128, DM], F32) for i in range(2)]
            self.dma('sp', gf[:], d['gfin'][:, :], [], ['gf'], 'cst')
            self.rms_all(range(2, NT))
            for t in range(2, NT):
                i = t % 2
                self.act(ob[i][:], self.xs[:, t, :], AF.Identity, [('xs', t), 'rstd'], [('ob', i)], scale=self.rstd[:, t:t + 1])
                self.v('dve', 'tensor_tensor', [('ob', i), 'gf'], [('ob', i)], out=ob[i][:], in0=ob[i][:], in1=gf[:], op=OP.mult)
                self.dma('sp', d['out'][(t - 2) * 128:(t - 1) * 128, :], ob[i][:], [('ob', i)], [('out', t)], 'out')


def dram_decls(nc):
    d = {}

    def inp(name, shape, dt=F32):
        d[name] = nc.dram_tensor(name, list(shape), dt, kind="ExternalInput").ap()

    inp('xin', [TOK, DM])
    inp('c2', [128, 16])
    inp('w_mod', [2, 1024, 6144])
    inp('bmodfm', [128, 2, 48])
    inp('gnfm', [128, 2, 2, 8])
    inp('w_in', [2, 1024, 2224])
    inp('w_out', [2, 1024, 1024])
    inp('w_up_r', [2, 8, 128, 8, 512])
    inp('w_down_r', [2, 2, 8, 128, 4, 512])
    inp('gfin', [128, DM])
    inp('identb', [128, 128], BF16)
    inp('identf', [128, 128])
    inp('w_uq', [2, 256, 384])
    inp('w_ukv', [2, 128, 640])
    inp('w_uq_sw', [2, 256, 4, 32])
    inp('w_kr_sw', [2, 1024, 32])
    inp('ropeC', [32, TOK], BF16)
    inp('ropeS', [32, TOK], BF16)
    inp('gqfm', [128, 2, 2])
    inp('gkvfm', [128, 2])
    inp('w_g', [2, 1024, 16])
    inp('bg', [2, 8, 2])
    inp('convdg', [2, 96, 8, 5, 96], BF16)
    inp('gmB', [128, 2, 384])
    inp('E2', [8, 2, 8])
    inp('negidf', [128, 128])
    inp('mlmask', [128, 2, 256], BF16)
    inp('bdcs', [128, 256], BF16)
    inp('twCS', [2, 16, 128, 2, 1024], BF16)
    inp('c256', [128, 2, 2, 256], BF16)
    d['out'] = nc.dram_tensor('out', [2048, DM], F32, kind="ExternalOutput").ap()
    if DBGXS:
        d['dbgxs'] = nc.dram_tensor('dbgxs', [TOK, DM], F32, kind="ExternalOutput").ap()
    return d


def build_program():
    nc = bass.Bass("TRN2", target_bir_lowering=False)
    d = dram_decls(nc)
    P = Prog(nc)
    kb = K(nc, P, d)
    kb.build()
    P.plan()
    es = contextlib.ExitStack()
    with es:
        P.start_real(es)
        kb.build()
    return nc, P


def host_inputs(inputs, b):
    f = lambda a: np.ascontiguousarray(np.asarray(a, dtype=np.float32))
    x, c, ctx, c_ctx = inputs['x'], inputs['c'], inputs['ctx'], inputs['c_ctx']
    m = {}
    m['xin'] = f(np.concatenate([ctx[b], x[b]], axis=0))
    c2 = np.stack([np.asarray(c[b]).reshape(8, 128).T, np.asarray(c_ctx).reshape(8, 128).T], axis=-1)
    m['c2'] = f(c2.reshape(128, 16))
    m['w_mod'] = f(inputs['w_mod'])
    m['bmodfm'] = f(np.asarray(inputs['b_mod']).reshape(2, 48, 128).transpose(2, 0, 1))
    gn = np.stack([np.asarray(inputs['g_norm1']), np.asarray(inputs['g_norm2'])], axis=1)
    m['gnfm'] = f(gn.reshape(2, 2, 8, 128).transpose(3, 0, 1, 2))
    m['w_in'] = f(inputs['w_in'])
    m['w_out'] = f(inputs['w_out'])
    m['w_up_r'] = f(np.asarray(inputs['w_up']).reshape(2, 8, 128, 8, 512).transpose(0, 3, 2, 1, 4))
    m['w_down_r'] = f(np.asarray(inputs['w_down']).reshape(2, 8, 4, 128, 2, 512).transpose(0, 4, 1, 3, 2, 5))
    m['gfin'] = f(np.tile(np.asarray(inputs['g_final'])[None, :], (128, 1)))
    m['identb'] = np.eye(128, dtype=np.float32).astype(ml_dtypes.bfloat16)
    m['identf'] = np.eye(128, dtype=np.float32)
    m.update(host_consts())
    perm = np.array([a * 16 + (1 - b_) * 8 + j for a in range(2) for b_ in range(2) for j in range(8)])
    wuq = np.asarray(inputs['w_uq'])
    m['w_uq'] = f(wuq)
    m['w_ukv'] = f(inputs['w_ukv'])
    m['w_uq_sw'] = f(wuq.reshape(2, 256, 4, 96)[:, :, :, 64:96][..., perm])
    m['w_kr_sw'] = f(np.asarray(inputs['w_in'])[:, :, 2192:2224][..., perm])
    m['gqfm'] = f(np.asarray(inputs['g_q_norm']).reshape(2, 2, 128).transpose(2, 0, 1))
    m['gkvfm'] = f(np.asarray(inputs['g_kv_norm']).T)
    gp = [0, 1, 2, 3, 8, 9, 10, 11, 4, 5, 6, 7, 12, 13, 14, 15]
    m['w_g'] = f(np.asarray(inputs['w_in'])[:, :, 1792:1808][..., gp])
    m['bg'] = f(np.asarray(inputs['b_gates'])[:, gp].reshape(2, 2, 8).transpose(0, 2, 1))
    cv = np.asarray(inputs['conv_qk'], dtype=np.float32).reshape(2, 5, 8, 96)
    dgc = np.zeros((2, 96, 8, 5, 96), np.float32)
    pi_ = np.arange(96)
    dgc[:, pi_, :, :, pi_] = cv.transpose(3, 0, 2, 1)
    m['convdg'] = dgc.astype(ml_dtypes.bfloat16)
    m['gmB'] = f(np.tile(np.asarray(inputs['g_mlstm'])[None, :, :], (128, 1, 1)))
    return m


def host_consts():
    bf = lambda a: np.ascontiguousarray(a.astype(np.float32)).astype(ml_dtypes.bfloat16)
    m = {}
    i64 = np.arange(64)
    a64 = 2 * np.pi * np.outer(i64, i64) / 64
    C64, S64 = np.cos(a64) / 8.0, np.sin(a64) / 8.0
    z = np.zeros((64, 64))
    m['bdcs'] = bf(np.block([[C64, z, S64, z], [z, C64, z, S64]]))
    n = np.arange(2048, dtype=np.float64)
    aN = 2 * np.pi * (np.outer(n, n) % 2048) / 2048
    tcs = np.stack([np.cos(aN) / np.sqrt(2048.0), -np.sin(aN) / np.sqrt(2048.0)], axis=1)
    m['twCS'] = bf(tcs.reshape(16, 128, 2, 2, 1024).transpose(3, 0, 1, 2, 4))
    E2 = np.zeros((8, 2, 8), np.float32)
    for k_ in range(4):
        E2[k_, 0, k_] = 1.0
        E2[4 + k_, 1, 4 + k_] = 1.0
    m['E2'] = E2
    m['negidf'] = -np.eye(128, dtype=np.float32)
    sidx, tidx = np.arange(128)[:, None], np.arange(128)[None, :]
    mf = np.where(sidx <= tidx, 1.0, 0.0)
    mb_ = np.where(sidx >= tidx, 1.0, 0.0)
    m['mlmask'] = bf(np.stack([np.concatenate([mf, mf], axis=1), np.concatenate([mb_, mb_], axis=1)], axis=1))
    tt = np.arange(2048)
    freqs = 10000.0 ** (-np.arange(8, dtype=np.float64) / 8)
    ang = np.stack([np.outer(tt // 64, freqs), np.outer(tt % 64, freqs)], axis=1)
    cosT = np.ones((TOK, 2, 2, 8)); sinT = np.zeros((TOK, 2, 2, 8))
    cosT[256:] = np.cos(ang)[:, :, None, :]
    sinT[256:, :, 0, :] = -np.sin(ang)
    sinT[256:, :, 1, :] = np.sin(ang)
    m['ropeC'] = bf(cosT.reshape(TOK, 32).T)
    m['ropeS'] = bf(sinT.reshape(TOK, 32).T)
    n = np.arange(256, dtype=np.float64)
    a2 = 2 * np.pi * (np.outer(n, n) % 256) / 256
    c = np.stack([np.cos(a2) / 16.0, -np.sin(a2) / 16.0], axis=1)
    m['c256'] = bf(c.reshape(2, 128, 2, 256).transpose(1, 0, 2, 3))
    return m


_CACHE = {}


def kernel(**inputs):
    if 'nc' not in _CACHE:
        _CACHE['nc'] = build_program()
    nc, P = _CACHE['nc']
    shared = None
    in_maps = []
    for b in range(8):
        m = host_inputs(inputs, b) if shared is None else None
        if shared is None:
            shared = m
        else:
            m = dict(shared)
            f = lambda a: np.ascontiguousarray(np.asarray(a, dtype=np.float32))
            m['xin'] = f(np.concatenate([inputs['ctx'][b], inputs['x'][b]], axis=0))
            c2 = np.stack([np.asarray(inputs['c'][b]).reshape(8, 128).T, np.asarray(inputs['c_ctx']).reshape(8, 128).T], axis=-1)
            m['c2'] = f(c2.reshape(128, 16))
        in_maps.append(m)
    res = run_bass_kernel_spmd(nc, in_maps, core_ids=list(range(8)))
    _CACHE['res'] = res
    return np.stack([np.asarray(r['out'], dtype=np.float32) for r in res.results], axis=0)
```
